# Optimizing a Trainium2 kernel written in Bass

```python
import math
import jax
import jax.numpy as jnp
from jax import lax
import numpy as np

D_MODEL = 1024
BATCH = 4
SEQ = 4096
DEPTH = 4

GRID_W = 64
CTX_LEN = 256

ATT_HEADS = 6
ATT_KV_HEADS = 2
ATT_GROUP = ATT_HEADS // ATT_KV_HEADS
HEAD_DIM = 64
ATT_WIDTH = ATT_HEADS * HEAD_DIM
KV_WIDTH = ATT_KV_HEADS * HEAD_DIM
WINDOW = 128
BLOCK = 128
ROPE_BASE = 10000.0

HY_CH = 384
HY_ORDER = 2
POS_BANDS = 16
POS_EMB = 1 + 2 * POS_BANDS
FILT_HID = 64
N_FILT = 2 * HY_ORDER * HY_CH
DECAY_TARGET = 1e-2
MAX_DECAY_PCT = 0.3
MIN_DECAY_PCT = 1.5

ML_HEADS = 4
ML_DK = 64
ML_DV = 64
ML_WIDTH = ML_HEADS * ML_DV
CHUNK = 128

D_MIX = ATT_WIDTH + HY_CH + ML_WIDTH
PROJ_SIZES = (ATT_WIDTH, KV_WIDTH, KV_WIDTH, 3 * HY_CH, ML_HEADS * ML_DK, ML_HEADS * ML_DK, ML_WIDTH, ML_WIDTH, 4 * ML_HEADS)
P_IN = sum(PROJ_SIZES)
SPLIT_POINTS = tuple(sum(PROJ_SIZES[:i + 1]) for i in range(len(PROJ_SIZES) - 1))
D_FF = -(-8 * D_MODEL // (3 * 256)) * 256
NEG_INF = -1e30
EPS = 1e-6

kernel_name = 'hybrid_hyena_swa_mlstm_dit'


def rms_norm(x, w):
    xf = x.astype(jnp.float32)
    y = xf * lax.rsqrt(jnp.mean(xf * xf, axis=-1, keepdims=True) + EPS)
    return (y * w.astype(jnp.float32)).astype(x.dtype)


def modulate(x, w, shift, scale):
    return rms_norm(x, w) * (1 + scale) + shift


def swiglu(h, wg, wu, wd):
    return (jax.nn.silu(h @ wg) * (h @ wu)) @ wd


def axial_rope_tables(n_rows):
    row = jnp.repeat(jnp.arange(n_rows), GRID_W)
    col = jnp.tile(jnp.arange(GRID_W), n_rows)
    nf = HEAD_DIM // 4
    inv_freq = ROPE_BASE ** (-jnp.arange(nf, dtype=jnp.float32) / nf)
    ang = jnp.stack([row[:, None] * inv_freq, col[:, None] * inv_freq], axis=1)
    return jnp.cos(ang), jnp.sin(ang)


def apply_rope(x, cos, sin):
    B, L, H, _ = x.shape
    xr = x.astype(jnp.float32).reshape(B, L, H, 2, 2, HEAD_DIM // 4)
    x1, x2 = xr[..., 0, :], xr[..., 1, :]
    c, s = cos[None, :, None], sin[None, :, None]
    out = jnp.stack([x1 * c - x2 * s, x2 * c + x1 * s], axis=-2)
    return out.reshape(x.shape).astype(x.dtype)


def short_conv(u, w, b):
    L = u.shape[1]
    up = jnp.pad(u, ((0, 0), (1, 1), (0, 0)))
    return up[:, :L] * w[0] + up[:, 1:L + 1] * w[1] + up[:, 2:] * w[2] + b


def hyena_filters(L, w1, b1, freq, w2, b2, w3, decay):
    f32 = jnp.float32
    t = jnp.linspace(0.0, 1.0, L, dtype=f32)[:, None]
    ang = (2.0 * math.pi / L) * jnp.arange(L, dtype=f32)[:, None]
    bands = jnp.linspace(1e-4, POS_BANDS - 1, POS_BANDS, dtype=f32)[None, :]
    feats = jnp.concatenate([t, jnp.cos(bands * ang), -jnp.sin(bands * ang)], axis=-1)
    fr = freq.astype(f32)
    z = jnp.sin(fr * (feats @ w1.astype(f32) + b1.astype(f32)))
    z = jnp.sin(fr * (z @ w2.astype(f32) + b2.astype(f32)))
    filt = (z @ w3.astype(f32)) * jnp.exp(-t * jnp.abs(decay.astype(f32)))
    filt = filt.reshape(L, 2, HY_ORDER, HY_CH)
    fwd, bwd = filt[:, 0], filt[:, 1]
    circ = jnp.concatenate([fwd, jnp.zeros((1, HY_ORDER, HY_CH), f32), jnp.flip(bwd[:L - 1], axis=0)], axis=0)
    return jnp.fft.rfft(circ, axis=0)


def hyena_mix(u, filt_fft, skip):
    L = u.shape[1]
    v, x1, x2 = jnp.split(u.astype(jnp.float32), 3, axis=-1)
    sk = skip.astype(jnp.float32)

    def long_conv(s, o):
        y = jnp.fft.irfft(jnp.fft.rfft(s, n=2 * L, axis=1) * filt_fft[None, :, o], n=2 * L, axis=1)[:, :L]
        return y + s * sk[o]

    return (x2 * long_conv(x1 * long_conv(v, 0), 1)).astype(u.dtype)


def attn_heads(aq, ak, av, q_norm_w, k_norm_w):
    B, L = aq.shape[:2]
    q = rms_norm(aq.reshape(B, L, ATT_HEADS, HEAD_DIM), q_norm_w)
    k = rms_norm(ak.reshape(B, L, ATT_KV_HEADS, HEAD_DIM), k_norm_w)
    v = av.reshape(B, L, ATT_KV_HEADS, HEAD_DIM)
    return q, k, v


def band_blocks(t, nb):
    B = t.shape[0]
    tp = jnp.pad(t, ((0, 0), (BLOCK, BLOCK), (0, 0), (0, 0)))
    tb = tp.reshape(B, nb + 2, BLOCK, *t.shape[2:])
    return jnp.concatenate([tb[:, :-2], tb[:, 1:-1], tb[:, 2:]], axis=2)


def windowed_attention(q, k, v, kc, vc, sink):
    B, S = q.shape[:2]
    C = kc.shape[1]
    nb = S // BLOCK
    scale = HEAD_DIM ** -0.5
    qb = q.reshape(B, nb, BLOCK, ATT_KV_HEADS, ATT_GROUP, HEAD_DIM)
    kb, vb = band_blocks(k, nb), band_blocks(v, nb)
    qi = jnp.arange(BLOCK)[:, None]
    kj = jnp.arange(3 * BLOCK)[None, :]
    kpos = jnp.arange(nb)[:, None, None] * BLOCK + (kj - BLOCK)[None]
    mask = (jnp.abs(kj - BLOCK - qi) <= WINDOW)[None] & (kpos >= 0) & (kpos < S)
    s_loc = jnp.einsum('bnqgrd,bnkgd->bngrqk', qb, kb).astype(jnp.float32) * scale
    s_loc = jnp.where(mask[None, :, None, None], s_loc, NEG_INF)
    s_ctx = jnp.einsum('bnqgrd,bcgd->bngrqc', qb, kc).astype(jnp.float32) * scale
    s_sink = jnp.broadcast_to(sink.astype(jnp.float32).reshape(ATT_KV_HEADS, ATT_GROUP)[None, None, :, :, None, None], s_loc.shape[:-1] + (1,))
    p = jax.nn.softmax(jnp.concatenate([s_loc, s_ctx, s_sink], axis=-1), axis=-1).astype(v.dtype)
    out = jnp.einsum('bngrqk,bnkgd->bnqgrd', p[..., :3 * BLOCK], vb) + jnp.einsum('bngrqc,bcgd->bnqgrd', p[..., 3 * BLOCK:3 * BLOCK + C], vc)
    return out.reshape(B, S, ATT_WIDTH)


def context_attention(qc, kc, vc, sink):
    B, C = qc.shape[:2]
    qg = qc.reshape(B, C, ATT_KV_HEADS, ATT_GROUP, HEAD_DIM)
    s = jnp.einsum('bqgrd,bkgd->bgrqk', qg, kc).astype(jnp.float32) * HEAD_DIM ** -0.5
    s_sink = jnp.broadcast_to(sink.astype(jnp.float32).reshape(ATT_KV_HEADS, ATT_GROUP)[None, :, :, None, None], s.shape[:-1] + (1,))
    p = jax.nn.softmax(jnp.concatenate([s, s_sink], axis=-1), axis=-1)[..., :C].astype(vc.dtype)
    return jnp.einsum('bgrqk,bkgd->bqgrd', p, vc).reshape(B, C, ATT_WIDTH)


def mlstm_prep(mq, mk, mv, mg, gate_b):
    B, L = mq.shape[:2]
    f32 = jnp.float32
    q = mq.astype(f32).reshape(B, L, ML_HEADS, ML_DK)
    k = mk.astype(f32).reshape(B, L, ML_HEADS, ML_DK) * (ML_DK ** -0.5)
    v = mv.astype(f32).reshape(B, L, ML_HEADS, ML_DV)
    g = mg.astype(f32) + gate_b.astype(f32)
    i_f, f_f, i_b, f_b = jnp.split(g, 4, axis=-1)
    return (q, k, v, i_f, jax.nn.log_sigmoid(f_f), i_b, jax.nn.log_sigmoid(f_b))


def mlstm_scan(q, k, v, log_i, log_f, state):
    B, L = q.shape[:2]
    nc = L // CHUNK

    def to_chunks(t):
        return jnp.moveaxis(t.reshape(B, nc, CHUNK, *t.shape[2:]), 1, 0)

    xs = (to_chunks(q), to_chunks(k), to_chunks(v), to_chunks(log_i), to_chunks(log_f))
    tri = jnp.tril(jnp.ones((CHUNK, CHUNK), dtype=bool))

    def step(carry, inp):
        Cm, n, m = carry
        qh, kh, vh, li, lf = inp
        b = jnp.cumsum(lf, axis=1).transpose(0, 2, 1)
        li = li.transpose(0, 2, 1)
        D = jnp.where(tri, b[..., :, None] - b[..., None, :] + li[..., None, :], -jnp.inf)
        inter = b + m[..., None]
        m_t = jnp.maximum(inter, jnp.max(D, axis=-1))
        w_intra = jnp.exp(D - m_t[..., None])
        w_inter = jnp.exp(inter - m_t)
        s = jnp.einsum('bthd,bshd->bhts', qh, kh) * w_intra
        num = jnp.einsum('bhts,bshe->bthe', s, vh) + w_inter.transpose(0, 2, 1)[..., None] * jnp.einsum('bthd,bhde->bthe', qh, Cm)
        den = jnp.sum(s, axis=-1) + w_inter * jnp.einsum('bthd,bhd->bht', qh, n)
        den = jnp.maximum(jnp.abs(den), jnp.exp(-m_t))
        h = num / den.transpose(0, 2, 1)[..., None]
        bT = b[..., -1]
        g = bT[..., None] - b + li
        m_new = jnp.maximum(bT + m, jnp.max(g, axis=-1))
        ws = jnp.exp(g - m_new[..., None])
        dec = jnp.exp(bT + m - m_new)
        C_new = dec[..., None, None] * Cm + jnp.einsum('bhs,bshd,bshe->bhde', ws, kh, vh)
        n_new = dec[..., None] * n + jnp.einsum('bhs,bshd->bhd', ws, kh)
        return (C_new, n_new, m_new), h

    state, hs = lax.scan(step, state, xs)
    return jnp.moveaxis(hs, 0, 1).reshape(B, L, ML_HEADS, ML_DV), state


def mlstm_bidirectional(lat, ctx):
    q, k, v, i_f, lf_f, i_b, lf_b = lat
    qc, kc, vc, ic_f, lfc_f, ic_b, lfc_b = ctx
    B = q.shape[0]
    zero = (jnp.zeros((B, ML_HEADS, ML_DK, ML_DV), jnp.float32), jnp.zeros((B, ML_HEADS, ML_DK), jnp.float32), jnp.zeros((B, ML_HEADS), jnp.float32))

    def fl(t):
        return jnp.flip(t, axis=1)

    hc_f, st_f = mlstm_scan(qc, kc, vc, ic_f, lfc_f, zero)
    h_f, _ = mlstm_scan(q, k, v, i_f, lf_f, st_f)
    hc_b, st_b = mlstm_scan(fl(qc), fl(kc), fl(vc), fl(ic_b), fl(lfc_b), zero)
    h_b, _ = mlstm_scan(fl(q), fl(k), fl(v), fl(i_b), fl(lf_b), st_b)
    return h_f + fl(h_b), hc_f + fl(hc_b)


def mlstm_output(h, o, norm_w):
    B, L = h.shape[:2]
    hn = rms_norm(h, norm_w.reshape(ML_HEADS, ML_DV)).reshape(B, L, ML_WIDTH)
    return (hn * jax.nn.sigmoid(o.astype(jnp.float32))).astype(o.dtype)


def token_mixer(h, hc, cos, sin, w_in, w_out, q_norm_w, k_norm_w, attn_sink, hy_conv_w, hy_conv_b, hy_w1, hy_b1, hy_freq, hy_w2, hy_b2, hy_w3, hy_decay, hy_skip, ml_gate_b, ml_norm_w, with_ctx_out):
    aq, ak, av, hy, mq, mk, mv, mo, mg = jnp.split(h @ w_in, SPLIT_POINTS, axis=-1)
    aqc, akc, avc, hyc, mqc, mkc, mvc, moc, mgc = jnp.split(hc @ w_in, SPLIT_POINTS, axis=-1)
    q, k, v = attn_heads(aq, ak, av, q_norm_w, k_norm_w)
    q, k = apply_rope(q, cos, sin), apply_rope(k, cos, sin)
    qc, kc, vc = attn_heads(aqc, akc, avc, q_norm_w, k_norm_w)
    att = windowed_attention(q, k, v, kc, vc, attn_sink)
    hyo = hyena_mix(short_conv(hy, hy_conv_w, hy_conv_b), hyena_filters(h.shape[1], hy_w1, hy_b1, hy_freq, hy_w2, hy_b2, hy_w3, hy_decay), hy_skip)
    ml_lat, ml_ctx = mlstm_bidirectional(mlstm_prep(mq, mk, mv, mg, ml_gate_b), mlstm_prep(mqc, mkc, mvc, mgc, ml_gate_b))
    mlo = mlstm_output(ml_lat, mo, ml_norm_w)
    y = jnp.concatenate([att, hyo, mlo], axis=-1) @ w_out
    if not with_ctx_out:
        return y, None
    attc = context_attention(qc, kc, vc, attn_sink)
    hyoc = hyena_mix(short_conv(hyc, hy_conv_w, hy_conv_b), hyena_filters(hc.shape[1], hy_w1, hy_b1, hy_freq, hy_w2, hy_b2, hy_w3, hy_decay), hy_skip)
    mloc = mlstm_output(ml_ctx, moc, ml_norm_w)
    yc = jnp.concatenate([attc, hyoc, mloc], axis=-1) @ w_out
    return y, yc


def setup_inputs(seed: int = 0) -> dict:
    key = jax.random.key(seed)
    ks = iter(jax.random.split(key, 40))

    def nrm(shape, scale):
        return scale * jax.random.normal(next(ks), shape, jnp.float32)

    L = DEPTH
    ig = nrm((L, 2, ML_HEADS), 0.1)
    fg = jnp.linspace(3.0, 6.0, ML_HEADS, dtype=jnp.float32) + nrm((L, 2, ML_HEADS), 0.1)
    ml_gate_b = jnp.stack([ig[:, 0], fg[:, 0], ig[:, 1], fg[:, 1]], axis=1).reshape(L, 4 * ML_HEADS)
    decay0 = jnp.linspace(math.log(DECAY_TARGET) / MIN_DECAY_PCT, math.log(DECAY_TARGET) / MAX_DECAY_PCT, N_FILT, dtype=jnp.float32)
    return {
        'x': nrm((BATCH, SEQ, D_MODEL), 1.0),
        'c': nrm((BATCH, D_MODEL), 1.0),
        'ctx': nrm((BATCH, CTX_LEN, D_MODEL), 1.0),
        'c_ctx': nrm((D_MODEL,), 1.0),
        'w_mod': nrm((L, D_MODEL, 6 * D_MODEL), 0.5 * D_MODEL ** -0.5),
        'b_mod': nrm((L, 6 * D_MODEL), 0.02),
        'norm1_w': 1.0 + nrm((L, D_MODEL), 0.02),
        'norm2_w': 1.0 + nrm((L, D_MODEL), 0.02),
        'w_in': nrm((L, D_MODEL, P_IN), D_MODEL ** -0.5),
        'w_out': nrm((L, D_MIX, D_MODEL), D_MIX ** -0.5),
        'q_norm_w': 1.0 + nrm((L, HEAD_DIM), 0.02),
        'k_norm_w': 1.0 + nrm((L, HEAD_DIM), 0.02),
        'attn_sink': nrm((L, ATT_HEADS), 0.5),
        'hy_conv_w': nrm((L, 3, 3 * HY_CH), 3 ** -0.5),
        'hy_conv_b': nrm((L, 3 * HY_CH), 0.02),
        'hy_w1': nrm((L, POS_EMB, FILT_HID), POS_EMB ** -0.5),
        'hy_b1': nrm((L, FILT_HID), 0.1),
        'hy_freq': 1.0 + nrm((L, FILT_HID), 0.1),
        'hy_w2': nrm((L, FILT_HID, FILT_HID), FILT_HID ** -0.5),
        'hy_b2': nrm((L, FILT_HID), 0.1),
        'hy_w3': nrm((L, FILT_HID, N_FILT), 0.05 * FILT_HID ** -0.5),
        'hy_decay': decay0[None] + nrm((L, N_FILT), 0.1),
        'hy_skip': nrm((L, HY_ORDER, HY_CH), 0.5),
        'ml_gate_b': ml_gate_b,
        'ml_norm_w': 1.0 + nrm((L, ML_WIDTH), 0.02),
        'ffn_w_gate': nrm((L, D_MODEL, D_FF), D_MODEL ** -0.5),
        'ffn_w_up': nrm((L, D_MODEL, D_FF), D_MODEL ** -0.5),
        'ffn_w_down': nrm((L, D_FF, D_MODEL), D_FF ** -0.5),
    }


def reference(x, c, ctx, c_ctx, w_mod, b_mod, norm1_w, norm2_w, w_in, w_out, q_norm_w, k_norm_w, attn_sink, hy_conv_w, hy_conv_b, hy_w1, hy_b1, hy_freq, hy_w2, hy_b2, hy_w3, hy_decay, hy_skip, ml_gate_b, ml_norm_w, ffn_w_gate, ffn_w_up, ffn_w_down):
    n_rows = x.shape[1] // GRID_W
    cos, sin = axial_rope_tables(n_rows)
    s_lat = jax.nn.silu(c)
    s_ctx = jax.nn.silu(c_ctx)
    for l in range(DEPTH):
        last = l == DEPTH - 1
        mod = (s_lat @ w_mod[l] + b_mod[l])[:, None, :]
        mod_c = s_ctx @ w_mod[l] + b_mod[l]
        sh1, sc1, g1, sh2, sc2, g2 = jnp.split(mod, 6, axis=-1)
        sh1c, sc1c, g1c, sh2c, sc2c, g2c = jnp.split(mod_c, 6, axis=-1)
        h = modulate(x, norm1_w[l], sh1, sc1)
        hc = modulate(ctx, norm1_w[l], sh1c, sc1c)
        y, yc = token_mixer(h, hc, cos, sin, w_in[l], w_out[l], q_norm_w[l], k_norm_w[l], attn_sink[l], hy_conv_w[l], hy_conv_b[l], hy_w1[l], hy_b1[l], hy_freq[l], hy_w2[l], hy_b2[l], hy_w3[l], hy_decay[l], hy_skip[l], ml_gate_b[l], ml_norm_w[l], not last)
        x = x + g1 * y
        x = x + g2 * swiglu(modulate(x, norm2_w[l], sh2, sc2), ffn_w_gate[l], ffn_w_up[l], ffn_w_down[l])
        if not last:
            ctx = ctx + g1c * yc
            ctx = ctx + g2c * swiglu(modulate(ctx, norm2_w[l], sh2c, sc2c), ffn_w_gate[l], ffn_w_up[l], ffn_w_down[l])
    return x
```

```python
from contextlib import ExitStack
import math
import numpy as np
import ml_dtypes
import concourse.bass as bass
import concourse.mybir as mybir
from concourse.bass_utils import run_bass_kernel_spmd

F32 = mybir.dt.float32
BF16 = mybir.dt.bfloat16
ALU = mybir.AluOpType
AF = mybir.ActivationFunctionType
AX = mybir.AxisListType

D = 1024
SEQ = 4096
CTX = 256
DEPTH = 4
NLAT = SEQ // 2
NCTX = CTX // 2
NT = NLAT + NCTX
NTOK = SEQ + CTX
P_IN = 2832
D_FF = 2816
NFF = D_FF // 128
EPS = 1e-6
O_AQ, O_AK, O_AV, O_HY, O_MQ, O_MK, O_MV, O_MO, O_MG = 0, 384, 512, 640, 1792, 2048, 2304, 2560, 2816

ENGS = ["sync", "scalar", "vector", "gpsimd", "tensor"]
NDS = 8
SES_SKIP = ()


class SemBank:
    def __init__(self, nc, es, nsets=2):
        self.sets = []
        for si in range(nsets):
            self.sets.append({
                "s": {e: es.enter_context(nc.semaphore(f"s{si}_{e}")) for e in ENGS},
                "d": {e: [es.enter_context(nc.semaphore(f"d{si}_{e}{i}")) for i in range(NDS)] for e in ENGS}})
        self.phase = 0


class Sched:
    def __init__(self, nc, es, bank=None, pfx="", same_engine_sync=True):
        self.nc = nc
        self.es = es
        self.pfx = pfx
        self.q = {e: [] for e in ENGS}
        if bank is None:
            bank = SemBank(nc, es, nsets=1)
        cur = bank.sets[bank.phase % len(bank.sets)]
        self.other = bank.sets[(bank.phase + 1) % len(bank.sets)] if len(bank.sets) > 1 else None
        bank.phase += 1
        self.sem = cur["s"]
        self.dsem = cur["d"]
        if len(bank.sets) == 1:
            if not hasattr(bank, "state"):
                bank.state = ({e: 0 for e in ENGS}, {e: [0] * NDS for e in ENGS}, {e: 0 for e in ENGS}, {e: {} for e in ENGS})
            self.cnt, self.dcnt, self.dnext, self.seen = bank.state
        else:
            self.cnt = {e: 0 for e in ENGS}
            self.dcnt = {e: [0] * NDS for e in ENGS}
            self.dnext = {e: 0 for e in ENGS}
            self.seen = {e: {} for e in ENGS}
        self.res = {}
        self.ses = same_engine_sync

    def sb(self, name, shape, dt=F32):
        return self.es.enter_context(self.nc.sbuf_tensor("sb_" + self.pfx + name, list(shape), dt))

    def ps(self, name, shape, dt=F32):
        return self.es.enter_context(self.nc.psum_tensor("ps_" + self.pfx + name, list(shape), dt))

    def _deps(self, eng, reads, writes):
        deps = {}

        def add(tok):
            sem, val, name = tok
            if name not in deps or deps[name][1] < val:
                deps[name] = tok

        for k in reads:
            r = self.res.get(k)
            if r and r[0] is not None:
                add(r[0])
        for k in writes:
            r = self.res.get(k)
            if r:
                if r[0] is not None:
                    add(r[0])
                for t in r[1]:
                    add(t)
        waits = []
        for name, (sem, val, _) in deps.items():
            if name == eng:
                if eng == "tensor" or not self.ses or val > self.cnt[eng] or eng in SES_SKIP:
                    continue
            if self.seen[eng].get(name, 0) < val:
                self.seen[eng][name] = val
                waits.append((sem, val))
        return waits

    def _record(self, tok, reads, writes):
        for k in reads:
            r = self.res.setdefault(k, [None, []])
            r[1].append(tok)
        for k in writes:
            self.res[k] = [tok, []]

    def op(self, eng, fn, reads=(), writes=(), inc=True):
        waits = self._deps(eng, reads, writes)
        if inc:
            self.cnt[eng] += 1
            tok = (self.sem[eng], self.cnt[eng], eng)
        else:
            tok = (self.sem[eng], self.cnt[eng] + 1, eng)
        self.q[eng].append((waits, fn, (self.sem[eng], 1) if inc else None))
        self._record(tok, reads, writes)

    def dma(self, eng, out, in_, reads=(), writes=(), **kw):
        waits = self._deps(eng, reads, writes)
        slot = self.dnext[eng]
        self.dnext[eng] = (slot + 1) % NDS
        name = f"d_{eng}{slot}"
        prev = self.dcnt[eng][slot]
        if prev > 0 and self.seen[eng].get(name, 0) < prev:
            self.seen[eng][name] = prev
            waits.append((self.dsem[eng][slot], prev))
        self.dcnt[eng][slot] = prev + 16
        tok = (self.dsem[eng][slot], prev + 16, name)
        self.q[eng].append((waits, lambda e: e.dma_start(out=out, in_=in_, **kw), (self.dsem[eng][slot], 16)))
        self._record(tok, reads, writes)

    def emit(self):
        for e in ENGS:
            for i in range(NDS):
                v = self.dcnt[e][i]
                if v > 0:
                    self.q["sync"].append(([(self.dsem[e][i], v)], None, None))
        for e in ENGS:
            if e != "sync" and self.cnt[e] > 0:
                self.q["sync"].append(([(self.sem[e], self.cnt[e])], None, None))
        if self.other is not None:
            clr = list(self.other["s"].values()) + [x for l_ in self.other["d"].values() for x in l_]
            self.q["gpsimd"] = [([], (lambda e, sm=sm: e.sem_clear(sm)), None) for sm in clr] + self.q["gpsimd"]
        with self.nc.Block() as block:
            for eng in ENGS:
                if not self.q[eng]:
                    continue

                def body(e, eng=eng):
                    for waits, fn, inc in self.q[eng]:
                        for sem, val in waits:
                            e.wait_ge(sem, val)
                        if fn is not None:
                            ins = fn(e)
                            if inc is not None:
                                ins.then_inc(inc[0], inc[1])

                getattr(block, eng)(body)


class K:
    def __init__(self, S):
        self.S = S

    def act(self, out, in_, func, r, w, bias=None, scale=None, eng="scalar"):
        kw = {}
        if bias is not None:
            kw["bias"] = bias
        if scale is not None:
            kw["scale"] = scale
        self.S.op("scalar", lambda e: e.activation(out=out, in_=in_, func=func, **kw), reads=r, writes=w)

    def tt(self, out, a, b, op, r, w, eng="vector"):
        self.S.op(eng, lambda e: e.tensor_tensor(out=out, in0=a, in1=b, op=op), reads=r, writes=w)

    def ts(self, out, a, s1, s2, op0, op1, r, w, eng="vector"):
        if op1 is None:
            self.S.op(eng, lambda e: e.tensor_scalar(out=out, in0=a, scalar1=s1, scalar2=None, op0=op0), reads=r, writes=w)
        else:
            self.S.op(eng, lambda e: e.tensor_scalar(out=out, in0=a, scalar1=s1, scalar2=s2, op0=op0, op1=op1), reads=r, writes=w)

    def stt(self, out, a, s, b, op0, op1, r, w):
        self.S.op("vector", lambda e: e.scalar_tensor_tensor(out=out, in0=a, scalar=s, in1=b, op0=op0, op1=op1), reads=r, writes=w)

    def copy(self, eng, out, in_, r, w):
        if eng == "scalar":
            self.S.op("scalar", lambda e: e.copy(out=out, in_=in_), reads=r, writes=w)
        else:
            self.S.op(eng, lambda e: e.tensor_copy(out=out, in_=in_), reads=r, writes=w)

    def recip(self, out, in_, r, w):
        self.S.op("vector", lambda e: e.reciprocal(out=out, in_=in_), reads=r, writes=w)

    def mm(self, out, lhsT, rhs, start, stop, r, w, inc=None):
        self.S.op("tensor", lambda e: e.matmul(out, lhsT=lhsT, rhs=rhs, start=start, stop=stop), reads=r, writes=w,
                  inc=stop if inc is None else inc)

    def memset(self, eng, ap, val, w):
        self.S.op(eng, lambda e: e.memset(ap, val), writes=w)


def dram_in(nc, name, shape, dt=F32):
    return nc.dram_tensor(name, list(shape), dt, kind="ExternalInput").ap()


def dram_out(nc, name, shape, dt=F32):
    return nc.dram_tensor(name, list(shape), dt, kind="ExternalOutput").ap()


def bcast_rows(ap1d, nparts):
    return ap1d.partition_broadcast(nparts)


def token_tiles(width):
    tiles = []
    t = 0
    while t < NLAT:
        tiles.append((t, width, 0))
        t += width
    tiles.append((NLAT, NCTX, 1))
    return tiles


def emit_mod_vectors(S, k, sc_d, wmod_d, bmod_d):
    sraw = S.sb("sraw", [128, 8, 2])
    sbf = S.sb("sbf", [128, 8, 2], BF16)
    bm = S.sb("bm", [128, 48])
    modT = S.sb("modT", [128, 48, 2])
    S.dma("sync", sraw[:].rearrange("p j c -> p (j c)"), sc_d, writes=["sraw"])
    S.dma("sync", bm[:], bmod_d, writes=["bm"])
    k.act(sbf[:], sraw[:], AF.Silu, ["sraw"], ["sbf"])
    wv = wmod_d.rearrange("(kc p) n -> p kc n", p=128)
    pm = S.ps("pmod", [128, 512])
    bufs = [S.sb(f"wmodb{i}", [128, 8, 1024], BF16) for i in range(2)]
    for g in range(6):
        wt = bufs[g % 2]
        key = f"wmodb{g % 2}"
        S.dma("gpsimd", wt[:], wv[:, :, g * 1024:(g + 1) * 1024], writes=[key])
        for j in range(8):
            jj = g * 8 + j
            for kc in range(8):
                k.mm(pm[:, 2 * jj:2 * jj + 2], wt[:, kc, j * 128:(j + 1) * 128], sbf[:, kc, :], kc == 0, kc == 7,
                     [key, "sbf"], ["pmod"])
    pv = pm[:, 0:96].rearrange("p (j c) -> p j c", c=2)
    for c in range(2):
        k.tt(modT[:, :, c], pv[:, :, c], bm[:], ALU.add, ["pmod", "bm"], ["modT"])
    return modT


def emit_A(nc, bank, pfx, io, gcols, grows, dbg=None):
    sc_d, wmod_d, bmod_d, n1_d, win_d = io["sc"], io["w_mod"], io["b_mod"], io["norm1_w"], io["w_in"]
    qkn_d, gb_d, cos_d, sin_d, rm_d, bo_d = io["qkn"], io["gate_b"], io["cosT"], io["sinT"], io["rm2"], io["blk1"]
    modT_o, fmT_o, tm_o = io["modT"], io["fm"], io["tm"]
    with ExitStack() as es:
        S = Sched(nc, es, bank, pfx)
        k = K(S)
        modT = emit_mod_vectors(S, k, sc_d, wmod_d, bmod_d)
        S.dma("sync", modT_o, modT[:].rearrange("p j c -> p (j c)"), reads=["modT"])
        n1 = S.sb("n1", [128, 8])
        S.dma("sync", n1[:], n1_d, writes=["n1"])
        qkn = S.sb("qkn", [128, 2]); S.dma("sync", qkn[:], qkn_d, writes=["qkn"])
        gb = S.sb("gb", [16, 1]); S.dma("sync", gb[:], gb_d, writes=["gb"])
        cosT = S.sb("cosT", [128, NT]); sinT = S.sb("sinT", [128, NT])
        rm2 = S.sb("rm2", [128, 128], BF16); S.dma("gpsimd", rm2[:], rm_d, writes=["rm2"])
        blk1 = S.sb("blk1", [128, 128], BF16); S.dma("gpsimd", blk1[:], bo_d, writes=["blk1"])
        ones = S.sb("ones", [128, 128], BF16); k.memset("gpsimd", ones[:], 1.0, ["ones"])
        A1 = S.sb("A1", [128, 8, 2])
        for c in range(2):
            k.stt(A1[:, :, c], modT[:, 8:16, c], 1.0, n1[:], ALU.add, ALU.mult, ["modT", "n1"], ["A1"])
        wv = win_d.rearrange("(kc p) n -> p kc n", p=128)
        fm_cols = [(O_AQ, 384), (O_AK, 128), (O_MQ, 256), (O_MK, 256), (O_MG, 16), (O_HY + 768, 384)]
        Wfm = S.sb("Wfm", [128, 8, 1424], BF16)
        off = 0
        for (c0, n) in fm_cols:
            S.dma("gpsimd", Wfm[:, :, off:off + n], wv[:, :, c0:c0 + n], writes=[f"Wfm{off}"])
            off += n
        tm_cols = [(O_AV, 128), (O_MK, 256), (O_MV, 256), (O_MO, 256), (O_HY, 1152)]
        Wtm = S.sb("Wtm", [128, 8, 2048], BF16)
        off = 0
        for (c0, n) in tm_cols:
            S.dma("gpsimd", Wtm[:, :, off:off + n], wv[:, :, c0:c0 + n], writes=[f"Wtm{off}"])
            off += n
        WFM_KEYS = ["Wfm0", "Wfm384", "Wfm512", "Wfm768", "Wfm1024", "Wfm1040"]
        WTM_KEYS = ["Wtm0", "Wtm128", "Wtm384", "Wtm640", "Wtm896"]
        k.ts(Wfm[:, :, 768:1024], Wfm[:, :, 768:1024], 0.125, None, ALU.mult, None, ["Wfm768"], ["Wfm768"], eng="gpsimd")
        k.ts(Wtm[:, :, 128:384], Wtm[:, :, 128:384], 0.125, None, ALU.mult, None, ["Wtm128"], ["Wtm128"], eng="gpsimd")
        fm_tiles = [(0, 128, "q"), (128, 128, "q"), (256, 128, "q"), (384, 128, "k"),
                    (512, 128, "c"), (640, 128, "c"), (768, 128, "c"), (896, 128, "c"),
                    (1024, 16, "g"), (1040, 128, "c"), (1168, 128, "c"), (1296, 128, "c")]
        xt = [S.sb(f"xt{i}", [128, 8, 512]) for i in range(2)]
        sq = S.sb("sq", [128, 8, 512], BF16)
        tmp = S.sb("tmpn", [128, 2, 512])
        hT = [S.sb(f"hT{i}", [128, 8, 512], BF16) for i in range(2)]
        rs = S.sb("rs", [128, 512]); rstd = S.sb("rstd", [128, 512])
        ss_ps = S.ps("ss_ps", [128, 512])
        pf = [S.ps(f"pf{i}", [128, 512]) for i in range(2)]
        pt = [S.ps(f"pt{i}", [128, 512]) for i in range(2)]
        pq = [S.ps(f"pq{i}", [128, 512]) for i in range(2)]
        stg_f = [S.sb(f"stgf{i}", [128, 512]) for i in range(3)]
        stg_t = [S.sb(f"stgt{i}", [128, 2048]) for i in range(1)]
        qsq = S.sb("qsq", [128, 512], BF16); qw = S.sb("qw", [128, 512], BF16)
        qrs = S.sb("qrs", [128, 512]); qri = S.sb("qri", [128, 512]); qt1 = S.sb("qt1", [128, 512]); qt2 = S.sb("qt2", [128, 512])
        nf = 0; ntm = 0; nst = 0
        tiles_all = [(u_, t0, n, which) for u_ in range(len(io["xT"])) for (t0, n, which) in token_tiles(512)]
        for ti, (u_, t0, n, which) in enumerate(tiles_all):
            if t0 == 0:
                xv = io["xT"][u_].rearrange("(j p) t -> p j t", p=128)
                gcol, grow = gcols[u_], grows[u_]
                S.dma("sync", cosT[:], cos_d[u_], writes=["cosT"]); S.dma("sync", sinT[:], sin_d[u_], writes=["sinT"])
            xb = xt[ti % 2]; xk = f"xt{ti % 2}"; hb = hT[ti % 2]; hk = f"hT{ti % 2}"
            S.dma("sync", xb[:, :, :n], xv[:, :, t0:t0 + n], writes=[xk])
            k.act(sq[:, :, :n], xb[:, :, :n], AF.Square, [xk], ["sq"])
            for j in range(8):
                k.mm(ss_ps[:, :n], ones[:], sq[:, j, :n], j == 0, j == 7, ["ones", "sq"], ["ss_ps"])
            k.act(rs[:, :n], ss_ps[:, :n], AF.Sqrt, ["ss_ps"], ["rs"], bias=EPS, scale=1.0 / D)
            k.recip(rstd[:, :n], rs[:, :n], ["rs"], ["rstd"])
            for j in range(8):
                k.stt(tmp[:, j % 2, :n], xb[:, j, :n], A1[:, j, which:which + 1], rstd[:, :n], ALU.mult, ALU.mult,
                      [xk, "A1", "rstd"], [f"tmpn{j % 2}"])
                k.act(hb[:, j, :n], tmp[:, j % 2, :n], AF.Identity, [f"tmpn{j % 2}", "modT"], [hk + f"_{j}"],
                      bias=modT[:, j, which:which + 1], scale=1.0)
            hkeys = [hk + f"_{j}" for j in range(8)]
            if dbg == "norm":
                break
            for (c0, M, kind) in fm_tiles:
                if dbg == "fmc" and kind != "c":
                    continue
                if dbg == "fmq" and kind != "q":
                    continue
                if dbg == "fmg" and kind != "g":
                    continue
                ps = pf[nf % 2]; pk = f"pf{nf % 2}"; nf += 1
                for kc in range(8):
                    k.mm(ps[:, :n], Wfm[:, kc, c0:c0 + 128], hb[:, kc, :n], kc == 0, kc == 7, WFM_KEYS + hkeys, [pk])
                st = stg_f[nst % 3]; sk = f"stgf{nst % 3}"; nst += 1
                if kind == "c":
                    k.copy("scalar", st[:M, :n], ps[:M, :n], [pk], [sk])
                elif kind == "g":
                    k.ts(st[:M, :n], ps[:M, :n], gb[:, 0:1], None, ALU.add, None, [pk, "gb"], [sk])
                else:
                    nw = qkn[:, 0:1] if kind == "q" else qkn[:, 1:2]
                    p2 = pq[0]; p3 = pq[1]
                    import os
                    QS = int(os.environ.get("QSTEPS", "99"))
                    steps = [
                        lambda: k.act(qsq[:, :n], ps[:, :n], AF.Square, [pk], ["qsq"]),
                        lambda: k.act(qw[:, :n], ps[:, :n], AF.Identity, [pk, "qkn"], ["qw"], scale=nw),
                        lambda: k.mm(p2[:, :n], blk1[:], qsq[:, :n], True, True, ["blk1", "qsq"], ["pq0"]),
                        lambda: k.mm(p3[:, :n], rm2[:], qw[:, :n], True, True, ["rm2", "qw"], ["pq1"]),
                        lambda: k.act(qrs[:, :n], p2[:, :n], AF.Sqrt, ["pq0"], ["qrs"], bias=EPS, scale=1.0 / 64),
                        lambda: k.recip(qri[:, :n], qrs[:, :n], ["qrs"], ["qri"]),
                        lambda: k.tt(qt1[:, :n], qw[:, :n], cosT[:, t0:t0 + n], ALU.mult, ["qw", "cosT"], ["qt1"]),
                        lambda: k.tt(qt2[:, :n], p3[:, :n], sinT[:, t0:t0 + n], ALU.mult, ["pq1", "sinT"], ["qt2"]),
                        lambda: k.tt(qt1[:, :n], qt1[:, :n], qt2[:, :n], ALU.add, ["qt1", "qt2"], ["qt1"]),
                        lambda: k.tt(st[:, :n], qt1[:, :n], qri[:, :n], ALU.mult, ["qt1", "qri"], [sk]),
                    ]
                    for f_ in steps[:QS]:
                        f_()
                S.dma("sync", fmT_o[c0:c0 + M, gcol(t0):gcol(t0) + n], st[:M, :n], reads=[sk])
            if dbg in ("fm", "fmc", "fmq", "fmg"):
                break
            for s0 in range(0, n, 128):
                st = stg_t[0]; sk = "stgt0"; ntm += 1
                for g in range(4):
                    ps = pt[(ntm * 4 + g) % 2]; pk = f"pt{(ntm * 4 + g) % 2}"
                    for kc in range(8):
                        k.mm(ps[:, :], hb[:, kc, s0:s0 + 128], Wtm[:, kc, g * 512:(g + 1) * 512], kc == 0, kc == 7,
                             WTM_KEYS + hkeys, [pk])
                    k.copy("scalar" if g % 2 == 0 else "vector", st[:, g * 512:(g + 1) * 512], ps[:, :], [pk], [sk + f"_{g}"])
                S.dma("sync", tm_o[grow(t0 + s0):grow(t0 + s0) + 128, :], st[:], reads=[sk + f"_{g}" for g in range(4)])
        S.emit()


def rope_tables():
    t = np.arange(SEQ)
    row = (t // 64).astype(np.float64)
    col = (t % 64).astype(np.float64)
    nf = 16
    inv = 10000.0 ** (-np.arange(nf, dtype=np.float64) / nf)
    cos = np.zeros((64, SEQ), np.float32)
    sin = np.zeros((64, SEQ), np.float32)
    for a, pos in enumerate((row, col)):
        ang = (pos[None, :].astype(np.float32) * inv[:, None].astype(np.float32)).astype(np.float32)
        for half in range(2):
            cos[a * 32 + half * 16:a * 32 + half * 16 + 16] = np.cos(ang)
            sin[a * 32 + half * 16:a * 32 + half * 16 + 16] = np.sin(ang)
    return cos, sin


def rope_consts():
    rm = np.zeros((64, 64), np.float32)
    for d in range(64):
        if (d % 32) < 16:
            rm[d + 16, d] = -1.0
        else:
            rm[d - 16, d] = 1.0
    rm2 = np.zeros((128, 128), np.float32)
    rm2[:64, :64] = rm
    rm2[64:, 64:] = rm
    blk = np.zeros((128, 128), np.float32)
    blk[:64, :64] = 1.0
    blk[64:, 64:] = 1.0
    return rm2, blk


def pcol(v):
    v = np.asarray(v, np.float32)
    j = v.shape[0] // 128
    return np.ascontiguousarray(np.moveaxis(v.reshape(j, 128, *v.shape[1:]), 0, 1))


def core_bh(core):
    return core // 2, core % 2


def a_inputs(l, core, x_cur, ctx_cur, inp, consts):
    b, hh = core_bh(core)
    cos, sin = consts["rope"]
    xT = np.concatenate([x_cur[b, hh * NLAT:(hh + 1) * NLAT, :].T, ctx_cur[b, hh * NCTX:(hh + 1) * NCTX, :].T], axis=1)
    cosT = np.ones((128, NT), np.float32)
    sinT = np.zeros((128, NT), np.float32)
    cosT[:64, :NLAT] = cos[:, hh * NLAT:(hh + 1) * NLAT]; cosT[64:, :NLAT] = cosT[:64, :NLAT]
    sinT[:64, :NLAT] = sin[:, hh * NLAT:(hh + 1) * NLAT]; sinT[64:, :NLAT] = sinT[:64, :NLAT]
    return {
        "xT": np.ascontiguousarray(xT, dtype=np.float32),
        "sc": pcol(np.stack([inp["c"][b], inp["c_ctx"]], axis=1)).reshape(128, 16),
        "w_mod": inp["w_mod"][l], "b_mod": pcol(inp["b_mod"][l]), "norm1_w": pcol(inp["norm1_w"][l]), "w_in": inp["w_in"][l],
        "qkn": np.ascontiguousarray(np.stack([np.tile(inp["q_norm_w"][l], 2), np.tile(inp["k_norm_w"][l], 2)], axis=1)),
        "gate_b": np.ascontiguousarray(inp["ml_gate_b"][l].reshape(16, 1)),
        "cosT": cosT, "sinT": sinT, "rm2": consts["rm2"], "blk1": consts["blk1"],
    }


def emit_E(nc, bank, pfx, io, gcols, dbg=None):
    mix_d, mod_d, n2_d = io["mix"], io["modT"], io["norm2_w"]
    wo_d, wg_d, wu_d, wd_d = io["w_out"], io["w_gate"], io["w_up"], io["w_down"]
    with ExitStack() as es:
        S = Sched(nc, es, bank, pfx)
        k = K(S)
        modT = S.sb("modT", [128, 48, 2]); S.dma("sync", modT[:].rearrange("p j c -> p (j c)"), mod_d, writes=["modT"])
        n2 = S.sb("n2", [128, 8]); S.dma("sync", n2[:], n2_d, writes=["n2"])
        ones = S.sb("ones", [128, 128], BF16); k.memset("gpsimd", ones[:], 1.0, ["ones"])
        A2 = S.sb("A2", [128, 8, 2])
        for c in range(2):
            k.stt(A2[:, :, c], modT[:, 32:40, c], 1.0, n2[:], ALU.add, ALU.mult, ["modT", "n2"], ["A2"])
        Wo = S.sb("Wo", [128, 8, D], BF16)
        Wg = S.sb("Wg", [128, 8, D_FF], BF16)
        Wu = S.sb("Wu", [128, 8, D_FF], BF16)
        Wd = S.sb("Wd", [128, NFF, D], BF16)
        S.dma("gpsimd", Wo[:], wo_d.rearrange("(kc p) n -> p kc n", p=128), writes=["Wo"])
        wgv = wg_d.rearrange("(kc p) n -> p kc n", p=128)
        wuv = wu_d.rearrange("(kc p) n -> p kc n", p=128)
        wdv = wd_d.rearrange("(m p) n -> p m n", p=128)
        WG_KEYS = []; WU_KEYS = []; WD_KEYS = []
        for h in range(2):
            S.dma("gpsimd", Wg[:, :, h * 1408:(h + 1) * 1408], wgv[:, :, h * 1408:(h + 1) * 1408], writes=[f"Wg{h}"]); WG_KEYS.append(f"Wg{h}")
            S.dma("gpsimd", Wu[:, :, h * 1408:(h + 1) * 1408], wuv[:, :, h * 1408:(h + 1) * 1408], writes=[f"Wu{h}"]); WU_KEYS.append(f"Wu{h}")
        for h in range(2):
            S.dma("gpsimd", Wd[:, h * 11:(h + 1) * 11, :], wdv[:, h * 11:(h + 1) * 11, :], writes=[f"Wd{h}"]); WD_KEYS.append(f"Wd{h}")
        TW = 256
        mv = mix_d.rearrange("(j p) t -> p j t", p=128)
        xt = [S.sb(f"xt{i}", [128, 8, TW]) for i in range(2)]
        mx = [S.sb(f"mx{i}", [128, 8, TW], BF16) for i in range(2)]
        sq = S.sb("sq", [128, 8, TW], BF16)
        tmp = S.sb("tmpn", [128, 2, TW])
        ev = S.sb("ev", [128, 2, TW])
        hT = S.sb("h2T", [128, 8, TW], BF16)
        aT = S.sb("aT", [128, NFF, TW], BF16)
        sg = S.sb("sg", [128, 2, TW], BF16); uc = S.sb("uc", [128, 2, TW], BF16)
        rs = S.sb("rs", [128, TW]); rstd = S.sb("rstd", [128, TW])
        ss_ps = S.ps("ss_ps", [128, 512])
        py = [S.ps(f"py{i}", [128, 512]) for i in range(2)]
        pg = [S.ps(f"pg{i}", [128, 512]) for i in range(2)]
        pu = [S.ps(f"pu{i}", [128, 512]) for i in range(2)]
        ny = 0; nev = 0
        tiles_all = [(u_, t0, n, which) for u_ in range(len(io["xT"])) for (t0, n, which) in token_tiles(TW)]
        for ti, (u_, t0, n, which) in enumerate(tiles_all):
            if t0 == 0:
                xv = io["xT"][u_].rearrange("(j p) t -> p j t", p=128)
                xov = io["xT_out"][u_].rearrange("(j p) t -> p j t", p=128)
                gcol = gcols[u_]
            xb = xt[ti % 2]; xk = f"xt{ti % 2}"; mb = mx[ti % 2]; mk = f"mx{ti % 2}"
            S.dma("sync", xb[:, :, :n], xv[:, :, t0:t0 + n], writes=[xk + f"_{j}" for j in range(8)])
            S.dma("gpsimd", mb[:, :, :n], mv[:, :, gcol(t0):gcol(t0) + n], writes=[mk])
            for j in range(8):
                ps = py[ny % 2]; pk = f"py{ny % 2}"; ny += 1
                for kc in range(8):
                    k.mm(ps[:, :n], Wo[:, kc, j * 128:(j + 1) * 128], mb[:, kc, :n], kc == 0, kc == 7, ["Wo", mk], [pk])
                e_ = ev[:, nev % 2, :n]; ek = f"ev{nev % 2}"; nev += 1
                k.act(e_, ps[:, :n], AF.Identity, [pk, "modT"], [ek], scale=modT[:, 16 + j, which:which + 1])
                k.tt(xb[:, j, :n], xb[:, j, :n], e_, ALU.add, [xk + f"_{j}", ek], [xk + f"_{j}"], eng="gpsimd")
            xkeys = [xk + f"_{j}" for j in range(8)]
            k.act(sq[:, :, :n], xb[:, :, :n], AF.Square, xkeys, ["sq"])
            for j in range(8):
                k.mm(ss_ps[:, :n], ones[:], sq[:, j, :n], j == 0, j == 7, ["ones", "sq"], ["ss_ps"])
            k.act(rs[:, :n], ss_ps[:, :n], AF.Sqrt, ["ss_ps"], ["rs"], bias=EPS, scale=1.0 / D)
            k.recip(rstd[:, :n], rs[:, :n], ["rs"], ["rstd"])
            for j in range(8):
                k.stt(tmp[:, j % 2, :n], xb[:, j, :n], A2[:, j, which:which + 1], rstd[:, :n], ALU.mult, ALU.mult,
                      [xk + f"_{j}", "A2", "rstd"], [f"tmpn{j % 2}"])
                k.act(hT[:, j, :n], tmp[:, j % 2, :n], AF.Identity, [f"tmpn{j % 2}", "modT"], [f"h2T_{j}"],
                      bias=modT[:, 24 + j, which:which + 1], scale=1.0)
            hkeys = [f"h2T_{j}" for j in range(8)]
            for m in range(NFF):
                g_ = pg[m % 2]; gk = f"pg{m % 2}"; u_ = pu[m % 2]; uk = f"pu{m % 2}"
                for kc in range(8):
                    k.mm(g_[:, :n], Wg[:, kc, m * 128:(m + 1) * 128], hT[:, kc, :n], kc == 0, kc == 7, [f"Wg{m // 11}"] + hkeys, [gk])
                for kc in range(8):
                    k.mm(u_[:, :n], Wu[:, kc, m * 128:(m + 1) * 128], hT[:, kc, :n], kc == 0, kc == 7, [f"Wu{m // 11}"] + hkeys, [uk])
                k.act(sg[:, m % 2, :n], g_[:, :n], AF.Silu, [gk], [f"sg{m % 2}"])
                k.act(uc[:, m % 2, :n], u_[:, :n], AF.Copy, [uk], [f"uc{m % 2}"])
                k.tt(aT[:, m, :n], sg[:, m % 2, :n], uc[:, m % 2, :n], ALU.mult, [f"sg{m % 2}", f"uc{m % 2}"], [f"aT{m}"])
            akeys = [f"aT{m}" for m in range(NFF)]
            for j in range(8):
                ps = py[ny % 2]; pk = f"py{ny % 2}"; ny += 1
                for m in range(NFF):
                    k.mm(ps[:, :n], Wd[:, m, j * 128:(j + 1) * 128], aT[:, m, :n], m == 0, m == NFF - 1, [f"Wd{m // 11}"] + akeys, [pk])
                e_ = ev[:, nev % 2, :n]; ek = f"ev{nev % 2}"; nev += 1
                k.act(e_, ps[:, :n], AF.Identity, [pk, "modT"], [ek], scale=modT[:, 40 + j, which:which + 1])
                k.tt(xb[:, j, :n], xb[:, j, :n], e_, ALU.add, [xk + f"_{j}", ek], [xk + f"_{j}"], eng="gpsimd")
            S.dma("sync", xov[:, :, t0:t0 + n], xb[:, :, :n], reads=xkeys)
        S.emit()


def e_inputs(l, core, x_cur, ctx_cur, mixT_core, modT_core, inp):
    b, hh = core_bh(core)
    xT = np.concatenate([x_cur[b, hh * NLAT:(hh + 1) * NLAT, :].T, ctx_cur[b, hh * NCTX:(hh + 1) * NCTX, :].T], axis=1)
    return {"xT": np.ascontiguousarray(xT, dtype=np.float32), "mixT": np.ascontiguousarray(mixT_core, dtype=np.float32),
            "modT": modT_core, "norm2_w": pcol(inp["norm2_w"][l]), "w_out": inp["w_out"][l],
            "w_gate": inp["ffn_w_gate"][l], "w_up": inp["ffn_w_up"][l], "w_down": inp["ffn_w_down"][l]}


def band_mask():
    s = np.arange(128)[:, None]
    t = np.arange(384)[None, :]
    return ((t - s >= 0) & (t - s <= 256)).astype(np.float32)


def emit_att(nc, bank, pfx, io, dbg=None):
    qT_d, kT_d, sink_d, mask_d, out_d = io["qT"], io["kT"], io["sink"], io["mask"], io["attT"]
    NB = SEQ // 128
    with ExitStack() as es:
        S = Sched(nc, es, bank, pfx)
        k = K(S)
        kT = S.sb("kT", [64, NTOK], BF16); S.dma("gpsimd", kT[:], kT_d, writes=["kT"])
        qT = S.sb("qT", [64, 3, NTOK], BF16)
        S.dma("gpsimd", qT[:], qT_d.rearrange("(h d) t -> d h t", d=64), writes=["qT"])
        V = S.sb("V", [128, 34, 64], BF16)
        S.dma("gpsimd", V[:, 0:NB, :], io["v_lat"].rearrange("(j p) d -> p j d", p=128), writes=["V"])
        S.dma("gpsimd", V[:, NB:NB + 2, :], io["v_ctx"].rearrange("(j p) d -> p j d", p=128), writes=["V"])
        ones64 = S.sb("ones64", [128, 64], BF16); k.memset("gpsimd", ones64[:], 1.0, ["ones64"])
        mask = S.sb("mask", [128, 384], BF16); S.dma("gpsimd", mask[:], mask_d, writes=["mask"])
        sink = S.sb("sink", [64, 3]); S.dma("sync", sink[:], sink_d, writes=["sink"])
        esink = S.sb("esink", [64, 3]); k.act(esink[:], sink[:], AF.Exp, ["sink"], ["esink"])
        P = [S.sb(f"P{r}", [128, 3, 384], BF16) for r in range(4)]
        Pe = [S.sb(f"Pe{r}", [128, 384], BF16) for r in range(2)]
        Pc = [S.sb(f"Pc{r}", [128, 384], BF16) for r in range(4)]
        psc = [S.ps(f"psc{i}", [128, 512]) for i in range(3)]
        po = [S.ps(f"po{i}", [128, 512]) for i in range(2)]
        pd = [S.ps(f"pd{i}", [128, 512]) for i in range(2)]
        den = S.sb("den", [64, 384]); rden = S.sb("rden", [64, 384])
        ost = [S.sb(f"ost{i}", [64, 384]) for i in range(2)]
        outv = out_d.rearrange("(h d) t -> d h t", d=64)
        cnt = {"sc": 0, "pe": 0, "pv": 0, "pc": 0}

        def local_scores(j):
            b0 = max(j - 1, 0); b1 = min(j + 1, NB - 1)
            q0 = b0 * 128; ln = (b1 - b0 + 1) * 128; mc0 = (b0 - (j - 1)) * 128
            Pj = P[j % 4]; pk_ = f"P{j % 4}"
            for h in range(3):
                ps = psc[cnt["sc"] % 3]; sk = f"psc{cnt['sc'] % 3}"; cnt["sc"] += 1
                k.mm(ps[:, :ln], kT[:, j * 128:(j + 1) * 128], qT[:, h, q0:q0 + ln], True, True, ["kT", "qT"], [sk])
                pe = Pe[cnt["pe"] % 2]; ek = f"Pe{cnt['pe'] % 2}"; cnt["pe"] += 1
                k.act(pe[:, :ln], ps[:, :ln], AF.Exp, [sk], [ek], scale=0.125)
                k.tt(Pj[:, h, mc0:mc0 + ln], pe[:, :ln], mask[:, mc0:mc0 + ln], ALU.mult, [ek, "mask"], [pk_ + f"_{h}"], eng="gpsimd")

        def ctx_scores(qtok0):
            res = []
            for c in range(2):
                ps = psc[cnt["sc"] % 3]; sk = f"psc{cnt['sc'] % 3}"; cnt["sc"] += 1
                k.mm(ps[:, :384].rearrange("p (h t) -> p h t", h=3), kT[:, SEQ + c * 128:SEQ + (c + 1) * 128],
                     qT[:, :, qtok0:qtok0 + 128], True, True, ["kT", "qT"], [sk])
                pc = Pc[cnt["pc"] % 4]; ck = f"Pc{cnt['pc'] % 4}"; cnt["pc"] += 1
                k.act(pc[:, :], ps[:, :384], AF.Exp, [sk], [ck], scale=0.125)
                res.append((pc[:, :].rearrange("p (h t) -> p h t", h=3), NB + c, [ck]))
            return res

        def pv(qtok0, terms):
            i_ = cnt["pv"]; cnt["pv"] += 1
            o_ = po[i_ % 2]; ok = f"po{i_ % 2}"; d_ = pd[i_ % 2]; dk = f"pd{i_ % 2}"
            ov = o_[:64, :384].rearrange("p (h t) -> p h t", h=3)
            dv = d_[:64, :384].rearrange("p (h t) -> p h t", h=3)
            nt = len(terms)
            for ti, (pap, vb, keys) in enumerate(terms):
                k.mm(ov, V[:, vb, :], pap, ti == 0, ti == nt - 1, ["V"] + keys, [ok])
            for ti, (pap, vb, keys) in enumerate(terms):
                k.mm(dv, ones64[:], pap, ti == 0, ti == nt - 1, ["ones64"] + keys, [dk])
            for h in range(3):
                k.ts(den[:, h * 128:(h + 1) * 128], d_[:64, h * 128:(h + 1) * 128], esink[:, h:h + 1], None, ALU.add, None,
                     [dk, "esink"], ["den"])
            k.recip(rden[:], den[:], ["den"], ["rden"])
            st = ost[i_ % 2]; sk = f"ost{i_ % 2}"
            k.tt(st[:], o_[:64, :384], rden[:], ALU.mult, [ok, "rden"], [sk])
            S.dma("sync", outv[:, :, qtok0:qtok0 + 128], st[:].rearrange("p (h t) -> p h t", h=3), reads=[sk])

        def local_terms(i):
            terms = []
            for j, c0 in ((i - 1, 256), (i, 128), (i + 1, 0)):
                if 0 <= j < NB:
                    terms.append((P[j % 4][:, :, c0:c0 + 128], j, [f"P{j % 4}_{h}" for h in range(3)]))
            return terms

        nblk = NB if dbg is None else 3
        for j in range(nblk):
            local_scores(j)
            if j >= 1:
                i = j - 1
                pv(i * 128, local_terms(i) + ctx_scores(i * 128))
        if dbg is None:
            i = NB - 1
            pv(i * 128, local_terms(i) + ctx_scores(i * 128))
            for cq in range(2):
                pv(SEQ + cq * 128, ctx_scores(SEQ + cq * 128))
        S.emit()


def att_inputs(l, core, fm_all, tm_all, inp, consts):
    b, g = core_bh(core)
    f0, f1 = fm_all[2 * b], fm_all[2 * b + 1]
    t0, t1 = tm_all[2 * b], tm_all[2 * b + 1]

    def seq_fm(rows):
        return np.concatenate([f0[rows, :NLAT], f1[rows, :NLAT], f0[rows, NLAT:], f1[rows, NLAT:]], axis=1)

    def seq_tm(cols):
        return np.concatenate([t0[:NLAT, cols], t1[:NLAT, cols], t0[NLAT:, cols], t1[NLAT:, cols]], axis=0)

    return {"qT": np.ascontiguousarray(seq_fm(slice(192 * g, 192 * g + 192))),
            "kT": np.ascontiguousarray(seq_fm(slice(384 + 64 * g, 384 + 64 * g + 64))),
            "v": np.ascontiguousarray(seq_tm(slice(64 * g, 64 * g + 64))),
            "sink": np.ascontiguousarray(np.tile(inp["attn_sink"][l][3 * g:3 * g + 3][None, :], (64, 1)).astype(np.float32)),
            "mask": consts["mask"]}


NCH = NTOK // 128


def ml_consts():
    sel = np.zeros((34, 4, 128), np.float32)
    for i, r in enumerate((0, 1, 32, 33)):
        sel[r, i, :] = 1.0
    ident = np.eye(128, dtype=np.float32)
    s = np.arange(128)[:, None]; t = np.arange(128)[None, :]
    return {"sel": sel.reshape(34, 512), "ident": ident, "maskf": (s <= t).astype(np.float32), "maskb": (s >= t).astype(np.float32)}


def proc_chunk(dr, c):
    if dr == 0:
        return 32 + c if c < 2 else c - 2
    return 33 - c if c < 2 else 31 - (c - 2)


def emit_ml(nc, bank, pfx, io, dbg=None):
    qT_d, kT_d, nw_d, sel_d, id_d, mf_d, mb_d, out_d, scr = (io["qT"], io["kT"], io["nw_bc"], io["sel"], io["ident"],
                                                            io["maskf"], io["maskb"], io["mloT"], io["scr"])
    g4 = io["g4"]
    with ExitStack() as es:
        S = Sched(nc, es, bank, pfx)
        k = K(S)
        qT = S.sb("qT", [128, NTOK], BF16); S.dma("gpsimd", qT[:], qT_d, writes=["qT"])
        kT = S.sb("kT", [128, NTOK], BF16); S.dma("gpsimd", kT[:], kT_d, writes=["kT"])
        ktok = S.sb("ktok", [128, NCH, 128], BF16)
        S.dma("gpsimd", ktok[:, 0:32, :], io["ktok"][0].rearrange("(c p) d -> p c d", p=128), writes=["ktok"])
        S.dma("gpsimd", ktok[:, 32:34, :], io["ktok"][1].rearrange("(c p) d -> p c d", p=128), writes=["ktok"])
        Vaug = S.sb("Vaug", [128, NCH, 2, 65], BF16)
        k.memset("gpsimd", Vaug[:], 1.0, ["Vaug"])
        for h_ in range(2):
            S.dma("gpsimd", Vaug[:, 0:32, h_, 0:64], io["vtok"][0][:, h_ * 64:(h_ + 1) * 64].rearrange("(c p) d -> p c d", p=128), writes=["Vaug"])
            S.dma("gpsimd", Vaug[:, 32:34, h_, 0:64], io["vtok"][1][:, h_ * 64:(h_ + 1) * 64].rearrange("(c p) d -> p c d", p=128), writes=["Vaug"])
        otok = S.sb("otok", [128, NCH, 128])
        S.dma("sync", otok[:, 0:32, :], io["otok"][0].rearrange("(c p) d -> p c d", p=128), writes=["otok"])
        S.dma("sync", otok[:, 32:34, :], io["otok"][1].rearrange("(c p) d -> p c d", p=128), writes=["otok"])
        nwb = S.sb("nwb", [128, 128]); S.dma("sync", nwb[:], nw_d, writes=["nwb"])
        sel = S.sb("sel", [34, 512]); S.dma("sync", sel[:], sel_d, writes=["sel"])
        ident = S.sb("ident", [128, 128]); S.dma("sync", ident[:], id_d, writes=["ident"])
        masks = []
        for nm, d_ in (("maskf", mf_d), ("maskb", mb_d)):
            m_ = S.sb(nm, [128, 128], BF16); S.dma("gpsimd", m_[:], d_, writes=[nm]); masks.append(m_)
        LI = S.sb("LI", [34, NTOK]); FF = S.sb("FF", [34, NTOK]); TM = S.sb("TM", [34, NTOK]); BN = S.sb("BN", [34, NTOK])
        for t_, nm in ((LI, "LI"), (FF, "FF"), (TM, "TM")):
            k.memset("gpsimd", t_[:], 0.0, [nm])
        S.dma("sync", LI[0:2, 0:CTX], g4[0][:, SEQ:NTOK], writes=["LI"]); S.dma("sync", LI[0:2, CTX:NTOK], g4[0][:, 0:SEQ], writes=["LI"], reads=["LI"])
        S.dma("sync", FF[0:2, 0:CTX], g4[1][:, SEQ:NTOK], writes=["FF"]); S.dma("sync", FF[0:2, CTX:NTOK], g4[1][:, 0:SEQ], writes=["FF"], reads=["FF"])

        def rev_rows(t_):
            return bass.AP(t_, 32 * NTOK + NTOK - 1, [[NTOK, 2], [-1, NTOK]])

        def revc_rows(t_):
            return bass.AP(t_, 32 * NTOK + 127, [[NTOK, 2], [128, NCH], [-1, 128]])

        S.dma("sync", TM[32:34, :], g4[2], writes=["TM"], reads=["TM"])
        k.copy("vector", LI[32:34, :], rev_rows(TM), ["TM", "LI"], ["LI"])
        S.dma("sync", TM[32:34, :], g4[3], writes=["TM"], reads=["TM"])
        k.copy("vector", FF[32:34, :], rev_rows(TM), ["TM", "FF"], ["FF"])
        k.act(FF[:], FF[:], AF.Exp, ["FF"], ["FF"], scale=-1.0)
        k.act(FF[:], FF[:], AF.Ln, ["FF"], ["FF"], bias=1.0, scale=1.0)
        S.op("vector", lambda e: e.tensor_tensor_scan(out=BN[:], data0=FF[:], data1=FF[:], initial=0.0, op0=ALU.add, op1=ALU.bypass),
             reads=["FF"], writes=["BN"])
        k.tt(LI[:], LI[:], BN[:], ALU.add, ["LI", "BN"], ["LI"])
        S.op("vector", lambda e: e.tensor_tensor_scan(out=TM[:], data0=LI[:], data1=LI[:], initial=0.0, op0=ALU.max, op1=ALU.bypass),
             reads=["LI", "TM"], writes=["TM"])
        MC = S.sb("MC", [34, NCH]); MP = S.sb("MP", [34, NCH]); DEC = S.sb("DEC", [34, NCH])
        k.copy("vector", MC[:], TM[:, 127::128], ["TM"], ["MC"])
        k.memset("vector", MP[:], 0.0, ["MP"])
        k.copy("vector", MP[:, 1:NCH], MC[:, 0:NCH - 1], ["MC", "MP"], ["MP"])
        k.tt(DEC[:], MP[:], MC[:], ALU.subtract, ["MP", "MC"], ["DEC"])
        k.act(DEC[:], DEC[:], AF.Exp, ["DEC"], ["DEC"])
        mcb = MC[:, :].unsqueeze(2).broadcast_to([34, NCH, 128])
        li3 = LI[:, :].rearrange("p (c s) -> p c s", s=128); bn3 = BN[:, :].rearrange("p (c s) -> p c s", s=128)
        k.tt(li3, li3, mcb, ALU.subtract, ["LI", "MC"], ["LI"])
        k.act(LI[:], LI[:], AF.Exp, ["LI"], ["LI"])
        k.tt(bn3, bn3, mcb, ALU.subtract, ["BN", "MC"], ["BN"])
        k.act(BN[:], BN[:], AF.Exp, ["BN"], ["BN"])
        COL = S.sb("COL", [128, 2, 4, NCH])
        ptr = S.ps("ptr", [128, 512])
        T68 = S.sb("T68", [68, 128])
        for qi, (src, nm) in enumerate(((LI, "LI"), (BN, "BN"))):
            k.copy("vector", FF[32:34, :].rearrange("p (c s) -> p c s", s=128), revc_rows(src), [nm, "FF"], ["FF"])
            S.dma("sync", scr[qi, 0:2, :], src[0:2, :], reads=[nm], writes=[f"scr{qi}"])
            S.dma("sync", scr[qi, 2:4, :], FF[32:34, :], reads=["FF", f"scr{qi}"], writes=[f"scr{qi}"])
            for dr in range(2):
                S.dma("sync", T68[:], scr[qi, 2 * dr:2 * dr + 2, :].rearrange("r (c s) -> (r c) s", s=128), reads=[f"scr{qi}"], writes=["T68"])
                S.op("tensor", lambda e: e.transpose(ptr[:, 0:68], T68[:], ident[0:68, 0:68]), reads=["T68", "ident"], writes=["ptr"])
                k.copy("vector", COL[:, qi, 2 * dr:2 * dr + 2, :], ptr[:, 0:68].rearrange("p (r c) -> p r c", r=2), ["ptr"], ["COL"])
        DECB = S.sb("DECB", [128, 4, NCH])
        for i in range(4):
            k.mm(ptr[:, 0:NCH], sel[:, i * 128:(i + 1) * 128], DEC[:], True, True, ["sel", "DEC", "ptr"], ["ptr"])
            k.copy("vector", DECB[:, i, :], ptr[:, 0:NCH], ["ptr"], ["DECB"])
        Cst = S.sb("Cst", [128, 4, 65]); Cd = S.sb("Cd", [128, 4, 65]); Cdb = S.sb("Cdb", [128, 4, 65], BF16)
        k.memset("vector", Cd[:], 0.0, ["Cd0", "Cd1", "Cd2", "Cd3"])
        k.memset("vector", Cdb[:], 0.0, ["Cdb0", "Cdb1", "Cdb2", "Cdb3"])
        HH = [S.sb("HF", [128, NCH, 128]), S.sb("HB", [128, NCH, 128])]
        pA = [S.ps(f"pA{i}", [128, 512]) for i in range(2)]
        pN = [S.ps(f"pN{i}", [128, 512]) for i in range(2)]
        pC = [S.ps(f"pC{i}", [128, 512]) for i in range(2)]
        pt1 = [S.sb(f"pt1_{i}", [128, 128], BF16) for i in range(2)]
        PT = [S.sb(f"PT{i}", [128, 128], BF16) for i in range(2)]
        VU = [S.sb(f"VU{i}", [128, 65], BF16) for i in range(2)]
        dcol = S.sb("dcol", [128, 2]); rcol = S.sb("rcol", [128, 2]); dabs = S.sb("dabs", [128, 2])
        n = 0
        nproc = NCH if dbg is None else 4
        for c in range(nproc):
            for i in range(4):
                dr, h = i // 2, i % 2
                ncn = proc_chunk(dr, c)
                tok0 = ncn * 128
                hs = slice(h * 64, (h + 1) * 64)
                r = n % 2; n += 1
                qc = qT[hs, tok0:tok0 + 128]; kc = kT[hs, tok0:tok0 + 128]
                k.mm(pA[r][:, 0:128], kc, qc, True, True, ["kT", "qT"], [f"pA{r}"])
                k.act(pt1[r][:], pA[r][:, 0:128], AF.Identity, [f"pA{r}", "COL"], [f"pt1_{r}"], scale=COL[:, 0, i, c:c + 1])
                k.tt(PT[r][:], pt1[r][:], masks[dr][:], ALU.mult, [f"pt1_{r}", "maskf", "maskb"], [f"PT{r}"], eng="gpsimd")
                k.act(VU[r][:], Vaug[:, ncn, h, :], AF.Identity, ["Vaug", "COL"], [f"VU{r}"], scale=COL[:, 0, i, c:c + 1])
                k.mm(pN[r][:, 0:65], PT[r][:], Vaug[:, ncn, h, :], True, False, [f"PT{r}", "Vaug"], [f"pN{r}"])
                k.mm(pN[r][:, 0:65], qc, Cdb[hs, i, :], False, True, ["qT", f"Cdb{i}"], [f"pN{r}"])
                k.act(dabs[:, r:r + 1], pN[r][:, 64:65], AF.Abs, [f"pN{r}"], [f"dabs{r}"])
                k.tt(dcol[:, r:r + 1], dabs[:, r:r + 1], COL[:, 1, i, c:c + 1], ALU.max, [f"dabs{r}", "COL"], [f"dcol{r}"])
                k.recip(rcol[:, r:r + 1], dcol[:, r:r + 1], [f"dcol{r}"], [f"rcol{r}"])
                k.act(HH[dr][:, ncn, hs], pN[r][:, 0:64], AF.Identity, [f"pN{r}", f"rcol{r}"], [f"H{dr}_{ncn}_{h}"], scale=rcol[:, r:r + 1])
                k.mm(pC[r][:, 0:65], ktok[:, ncn, :], VU[r][:], True, True, ["ktok", f"VU{r}"], [f"pC{r}"])
                k.tt(Cst[hs, i, :], Cd[hs, i, :], pC[r][hs, 0:65], ALU.add, [f"Cd{i}", f"pC{r}"], [f"Cst{i}"])
                if c + 1 < nproc:
                    k.ts(Cd[hs, i, :], Cst[hs, i, :], DECB[hs, i, c + 1:c + 2], None, ALU.mult, None, [f"Cst{i}", "DECB"], [f"Cd{i}"])
                    k.copy("scalar", Cdb[hs, i, :], Cd[hs, i, :], [f"Cd{i}"], [f"Cdb{i}"])
        hsum = [S.sb(f"hsum{i}", [128, 128]) for i in range(2)]
        hsq = S.sb("hsq", [128, 128]); ssq = S.sb("ssq", [128, 2]); rsq = S.sb("rsq", [128, 2]); rin = S.sb("rin", [128, 2])
        hn = S.sb("hn", [128, 128]); sgo = S.sb("sgo", [128, 128])
        ost = [S.sb(f"ost{i}", [128, 128]) for i in range(2)]
        ostT = [S.sb(f"ostT{i}", [128, 128]) for i in range(2)]
        chunks = range(NCH) if dbg is None else [32, 33, 0, 1]
        for j, ncn in enumerate(chunks):
            r = j % 2
            hk = [f"H{dr}_{ncn}_{h}" for dr in range(2) for h in range(2)]
            k.tt(hsum[r][:], HH[0][:, ncn, :], HH[1][:, ncn, :], ALU.add, hk, [f"hsum{r}"], eng="gpsimd")
            k.tt(hsq[:], hsum[r][:], hsum[r][:], ALU.mult, [f"hsum{r}"], ["hsq"], eng="gpsimd")
            S.op("vector", lambda e, a=hsq, b=ssq: e.tensor_reduce(out=b[:], in_=a[:].rearrange("p (h d) -> p h d", h=2), axis=AX.X, op=ALU.add),
                 reads=["hsq"], writes=["ssq"])
            k.act(rsq[:], ssq[:], AF.Sqrt, ["ssq"], ["rsq"], bias=EPS, scale=1.0 / 64)
            k.recip(rin[:], rsq[:], ["rsq"], ["rin"])
            for h in range(2):
                k.act(hn[:, h * 64:(h + 1) * 64], hsum[r][:, h * 64:(h + 1) * 64], AF.Identity, [f"hsum{r}", "rin"], [f"hn{h}"], scale=rin[:, h:h + 1])
            k.act(sgo[:], otok[:, ncn, :], AF.Sigmoid, ["otok"], ["sgo"])
            k.tt(hn[:], hn[:], nwb[:], ALU.mult, ["hn0", "hn1", "nwb"], ["hn0", "hn1"])
            k.tt(ost[r][:], hn[:], sgo[:], ALU.mult, ["hn0", "hn1", "sgo"], [f"ost{r}"])
            S.op("tensor", lambda e, r=r: e.transpose(ptr[:, 0:128], ost[r][:], ident[:]), reads=[f"ost{r}", "ident", "ptr"], writes=["ptr"])
            k.copy("scalar", ostT[r][:], ptr[:, 0:128], ["ptr"], [f"ostT{r}"])
            S.dma("sync", out_d[:, ncn * 128:(ncn + 1) * 128], ostT[r][:], reads=[f"ostT{r}"])
        S.emit()


def seq_fm(fm_all, b, rows):
    f0, f1 = fm_all[2 * b], fm_all[2 * b + 1]
    return np.ascontiguousarray(np.concatenate([f0[rows, :NLAT], f1[rows, :NLAT], f0[rows, NLAT:], f1[rows, NLAT:]], axis=1))


def seq_tm(tm_all, b, cols):
    t0, t1 = tm_all[2 * b], tm_all[2 * b + 1]
    return np.ascontiguousarray(np.concatenate([t0[:NLAT, cols], t1[:NLAT, cols], t0[NLAT:, cols], t1[NLAT:, cols]], axis=0))


def ml_inputs(l, core, fm_all, tm_all, inp, consts):
    b, hp = core_bh(core)
    grow = [1024 + 2 * hp, 1024 + 2 * hp + 1, 1028 + 2 * hp, 1028 + 2 * hp + 1, 1032 + 2 * hp, 1032 + 2 * hp + 1, 1036 + 2 * hp, 1036 + 2 * hp + 1]
    d = {"qT": seq_fm(fm_all, b, slice(512 + 128 * hp, 512 + 128 * hp + 128)),
         "kT": seq_fm(fm_all, b, slice(768 + 128 * hp, 768 + 128 * hp + 128)),
         "ktok": seq_tm(tm_all, b, slice(128 + 128 * hp, 128 + 128 * hp + 128)),
         "vtok": seq_tm(tm_all, b, slice(384 + 128 * hp, 384 + 128 * hp + 128)),
         "otok": seq_tm(tm_all, b, slice(640 + 128 * hp, 640 + 128 * hp + 128)),
         "gT": seq_fm(fm_all, b, grow),
         "nw_bc": np.ascontiguousarray(np.tile(inp["ml_norm_w"][l][128 * hp:128 * hp + 128][None, :], (128, 1)).astype(np.float32))}
    d.update(consts["ml"])
    return d


HY_CFG = {"lat": dict(L=SEQ, B=1024, nb=4), "ctx": dict(L=CTX, B=256, nb=1)}


def dft_tables(B):
    t = np.arange(B, dtype=np.float64)[:, None]
    om = np.pi * (2 * np.arange(B, dtype=np.float64)[None, :] + 1) / (2 * B)
    return np.cos(t * om).astype(np.float32), np.sin(t * om).astype(np.float32)


def pos_feats(L):
    t = np.linspace(0.0, 1.0, L, dtype=np.float32)[:, None]
    ang = (np.float32(2.0 * math.pi / L) * np.arange(L, dtype=np.float32))[:, None]
    bands = np.linspace(1e-4, 15, 16, dtype=np.float32)[None, :]
    feats = np.concatenate([t, np.cos(bands * ang), -np.sin(bands * ang)], axis=-1).astype(np.float32)
    return feats, t[:, 0]


def hy_consts():
    c = {}
    for nm, cfg in HY_CFG.items():
        L, B = cfg["L"], cfg["B"]
        TC, TS = dft_tables(B)
        feats, t = pos_feats(L)
        c[nm] = {"TC": TC, "TS": TS, "TCT": np.ascontiguousarray(TC.T), "TST": np.ascontiguousarray(TS.T),
                 "featsT": np.ascontiguousarray(feats.T), "featsTr": np.ascontiguousarray(feats[::-1].T),
                 "negt": np.ascontiguousarray((-t).reshape(L // 128, 128).T), "negtr": np.ascontiguousarray((-t[::-1]).reshape(L // 128, 128).T)}
    alt = np.where(np.arange(128) % 2 == 0, 1.0, -1.0).astype(np.float32).reshape(128, 1)
    c["alt"] = alt
    return c


def emit_F(nc, bank, pfx, io0):
    w1_d, w2_d, fb_d, alt_d = io0["w1"], io0["w2"], io0["fb"], io0["alt"]
    io = io0
    with ExitStack() as es:
        S = Sched(nc, es, bank, pfx)
        k = K(S)
        w1 = S.sb("w1", [33, 64]); S.dma("sync", w1[:], w1_d, writes=["w1"])
        w2 = S.sb("w2", [64, 64]); S.dma("sync", w2[:], w2_d, writes=["w2"])
        w3 = S.sb("w3", [64, 768])
        fb = S.sb("fb", [64, 4]); S.dma("sync", fb[:], fb_d, writes=["fb"])
        fbb = S.sb("fbb", [64, 2])
        k.ts(fbb[:], fb[:, 1:3], fb[:, 0:1], None, ALU.mult, None, ["fb"], ["fbb"])
        adec = S.sb("adec", [128, 768])
        alt = S.sb("alt", [128, 1]); S.dma("sync", alt[:], alt_d, writes=["alt"])
        pz = [S.ps(f"pz{i}", [128, 512]) for i in range(2)]
        pP = [S.ps(f"pP{i}", [128, 512]) for i in range(4)]
        TWO_PI = 2.0 * math.pi
        LM, BM, NBM = SEQ, 1024, 4
        altB = S.sb("altB", [128, 1])
        wm = S.sb("wm", [64, 512])
        negt_s = S.sb("negt", [128, LM // 128]); negtr_s = S.sb("negtr", [128, LM // 128])
        TC_s = S.sb("TC", [128, BM // 128, BM], BF16); TS_s = S.sb("TS", [128, BM // 128, BM], BF16)
        Gt_s = [S.sb(f"Gt{dr}", [128, LM // 128, 384], BF16) for dr in range(2)]
        feats_s = S.sb("feats", [33, LM]); z1_s = S.sb("z1", [64, LM]); z2_s = [S.sb(f"z2_{dr}", [64, LM]) for dr in range(2)]
        arg = S.sb("arg", [64, 512]); fsb = S.sb("fsb", [128, 384]); dct = S.sb("dct", [128, 384])
        XY_s = S.sb("XY", [128, 2 * NBM, 4, 384])
        gst = [S.sb(f"gst{i}", [128, 384]) for i in range(1)]
        for nm, cfg in HY_CFG.items():
            L, B, nb = cfg["L"], cfg["B"], cfg["nb"]
            d = io[nm]
            nt = L // 128; ntb = B // 128
            k.ts(altB[:], alt[:], 1.0 / B, None, ALU.mult, None, ["alt"], ["altB"])
            negt = negt_s[:, 0:nt]; negtr = negtr_s[:, 0:nt]
            S.dma("sync", negt, d["negt"], writes=["negt"]); S.dma("sync", negtr, d["negtr"], writes=["negtr"])
            TC = TC_s[:, 0:ntb, 0:B]; TS = TS_s[:, 0:ntb, 0:B]
            S.dma("gpsimd", TC, d["TC"].rearrange("(a p) k -> p a k", p=128), writes=["TC"])
            S.dma("gpsimd", TS, d["TS"].rearrange("(a p) k -> p a k", p=128), writes=["TS"])
            Gt = [Gt_s[dr][:, 0:nt, :] for dr in range(2)]
            feats = feats_s[:, 0:L]; z1 = z1_s[:, 0:L]
            XY = XY_s[:, 0:2 * nb]
            for dr in range(2):
                z2 = z2_s[dr][:, 0:L]
                S.dma("sync", feats, d["featsr" if dr else "feats"], writes=["feats"])
                for si, (src, w_, bcol, dst, K_) in enumerate(((feats, w1, 0, z1, 33), (z1, w2, 1, z2, 64))):
                    for c0 in range(0, L, 512):
                        n = min(512, L - c0)
                        ps = pz[(c0 // 512) % 2]; pk = f"pz{(c0 // 512) % 2}"
                        k.mm(ps[:64, :n], w_[:K_, :], src[:K_, c0:c0 + n], True, True, ["w1", "w2", "feats", "z1"], [pk])
                        k.act(arg[:, :n], ps[:64, :n], AF.Identity, [pk, "fb", "fbb"], ["arg"], scale=fb[:, 0:1], bias=fbb[:, bcol:bcol + 1])
                        for _ in range(2):
                            for (cmp_, val, sh) in ((ALU.is_gt, math.pi, -TWO_PI), (ALU.is_lt, -math.pi, TWO_PI)):
                                k.ts(wm[:, :n], arg[:, :n], val, None, cmp_, None, ["arg"], ["wm"])
                                k.stt(arg[:, :n], wm[:, :n], sh, arg[:, :n], ALU.mult, ALU.add, ["wm", "arg"], ["arg"])
                        k.act(dst[:, c0:c0 + n], arg[:, :n], AF.Sin, ["arg"], ["z1" if si == 0 else f"z2_{dr}"])
            for hh in range(len(io0["w3c"])):
                S.dma("sync", w3[:], io0["w3c"][hh], writes=["w3"])
                S.dma("sync", adec[:], io0["decay_bc"][hh], writes=["adec"])
                k.act(adec[:], adec[:], AF.Abs, ["adec"], ["adec"])
                for dr in range(2):
                    z2 = z2_s[dr][:, 0:L]
                    tcol = negtr if dr else negt
                    for mt in range(nt):
                        ps = pz[mt % 2]; pk = f"pz{mt % 2}"
                        k.mm(ps[:, :384], z2[:, mt * 128:(mt + 1) * 128], w3[:, dr * 384:(dr + 1) * 384], True, True, [f"z2_{dr}", "w3"], [pk])
                        k.act(dct[:], adec[:, dr * 384:(dr + 1) * 384], AF.Exp, ["adec", "negt", "negtr"], ["dct"], scale=tcol[:, mt:mt + 1])
                        k.copy("scalar", fsb[:], ps[:, :384], [pk], ["fsb"])
                        k.tt(Gt[dr][:, mt, :], fsb[:], dct[:], ALU.mult, ["fsb", "dct"], [f"Gt{dr}_{mt // ntb}"], eng="gpsimd")
                npp = 0; ng = 0
                for kt in range(ntb):
                    for ei in range(2 * nb):
                        src = Gt[1] if ei < nb else Gt[0]
                        base = (ei if ei < nb else ei - nb) * ntb
                        bk = f"Gt{1 if ei < nb else 0}_{ei if ei < nb else ei - nb}"
                        for ti, (T_, tk) in enumerate(((TC, "TC"), (TS, "TS"))):
                            ps = pP[npp % 4]; pk = f"pP{npp % 4}"; npp += 1
                            for tt in range(ntb):
                                k.mm(ps[:, :384], T_[:, tt, kt * 128:(kt + 1) * 128], src[:, base + tt, :], tt == 0, tt == ntb - 1, [tk, bk], [pk])
                            k.act(XY[:, ei, ti, :], ps[:, :384], AF.Identity, [pk], [f"XY{ei}_{ti}"], scale=1.0 / B)
                            k.act(XY[:, ei, 2 + ti, :], ps[:, :384], AF.Identity, [pk, "altB"], [f"XY{ei}_{2 + ti}"], scale=altB[:, 0:1])
                    for dd in range(-(nb - 1), nb):
                        ei = dd + nb; q = (nb - 1) - dd
                        for rj in range(2):
                            g_ = gst[0]; gk = "gst0"; ng += 1
                            if rj == 0:
                                k.tt(g_[:], XY[:, ei, 0, :], XY[:, ei - 1, 3, :], ALU.add, [f"XY{ei}_0", f"XY{ei - 1}_3"], [gk], eng="gpsimd")
                            else:
                                k.tt(g_[:], XY[:, ei, 1, :], XY[:, ei - 1, 2, :], ALU.subtract, [f"XY{ei}_1", f"XY{ei - 1}_2"], [gk], eng="gpsimd")
                            S.dma("sync", d["G"][hh][:, kt, rj, :, q, :].rearrange("o p c -> p o c"), g_[:].rearrange("p (o c) -> p o c", o=2), reads=[gk])
        S.emit()


def f_inputs(core, inp, consts):
    l, hh = core // 2, core % 2
    cols = []
    for dr in range(2):
        for o in range(2):
            c0 = dr * 768 + o * 384 + hh * 192
            cols.extend(range(c0, c0 + 192))
    cols = np.array(cols)
    hc = consts["hy"]
    d = {"w1": inp["hy_w1"][l], "w2": inp["hy_w2"][l], "w3c": np.ascontiguousarray(inp["hy_w3"][l][:, cols]),
         "fb": np.ascontiguousarray(np.stack([inp["hy_freq"][l], inp["hy_b1"][l], inp["hy_b2"][l], inp["hy_b2"][l]], axis=1)),
         "decay_bc": np.ascontiguousarray(np.tile(inp["hy_decay"][l][cols][None, :], (128, 1))), "alt": hc["alt"]}
    for nm in HY_CFG:
        d[f"featsT_{nm}"] = hc[nm]["featsT"]; d[f"featsTr_{nm}"] = hc[nm]["featsTr"]
        d[f"negt_{nm}"] = hc[nm]["negt"]; d[f"negtr_{nm}"] = hc[nm]["negtr"]
        d[f"TC_{nm}"] = hc[nm]["TC"]; d[f"TS_{nm}"] = hc[nm]["TS"]
    return d


def emit_hy(nc, bank, pfx, io, dbg=None):
    cw_d, sk_d, Gd, Td, out_d, id_d = io["cw_bc"], io["sk_bc"], io["G"], io["T"], io["hyoT"], io["ident"]
    with ExitStack() as es:
        S = Sched(nc, es, bank, pfx)
        k = K(S)
        cw = S.sb("cw", [128, 4, 576]); S.dma("sync", cw[:].rearrange("p a c -> p (a c)"), cw_d, writes=["cw"])
        sk = S.sb("sk", [128, 2, 192]); S.dma("sync", sk[:].rearrange("p a c -> p (a c)"), sk_d, writes=["sk"])
        VXX = S.sb("VXX", [128, NCH, 576], BF16)
        Z = S.sb("Z", [128, NCH, 192], BF16)
        tabs = {}
        for nm, cfg in HY_CFG.items():
            B = cfg["B"]; ntb = B // 128
            tabs[nm] = []
            for ti, tname in enumerate(("TC", "TS", "TCT", "TST")):
                t_ = S.sb(f"{tname}_{nm}", [128, ntb, B], BF16)
                S.dma("gpsimd", t_[:], Td[nm][ti].rearrange("(a p) k -> p a k", p=128), writes=[f"{tname}_{nm}"])
                tabs[nm].append((t_, f"{tname}_{nm}"))
        stg = [S.sb(f"stg{i}", [128, 3, 576]) for i in range(1)]
        pr = [S.sb(f"pr{i}", [128, 4, 192]) for i in range(2)]
        pb = [S.sb(f"pb{i}", [128, 4, 192], BF16) for i in range(4)]
        ct0 = pr[0][:, :, :].rearrange("p a c -> p (a c)")[:, 0:576]; ct1 = pr[1][:, :, :].rearrange("p a c -> p (a c)")[:, 0:576]
        tile_base = {"lat": 0, "ctx": SEQ // 128}
        for nm, cfg in HY_CFG.items():
            L = cfg["L"]
            for tt in range(L // 128):
                g = tile_base[nm] + tt
                s_ = stg[0]; skey = "stg0"
                tmt, pitch = io["tmT"]
                for part in range(3):
                    src = bass.AP(tmt, (io["row0"][nm] + tt * 128) * pitch + io["col0"] + part * 384, [[pitch, 128], [pitch, 3], [1, 192]])
                    S.dma("sync", s_[:, :, part * 192:(part + 1) * 192], src, writes=[skey])
                k.tt(ct0, s_[:, 0, :], cw[:, 0, :], ALU.mult, [skey, "cw"], ["pr0"])
                k.tt(ct1, s_[:, 1, :], cw[:, 1, :], ALU.mult, [skey, "cw"], ["pr1"], eng="gpsimd")
                k.tt(ct0, ct0, ct1, ALU.add, ["pr0", "pr1"], ["pr0"])
                k.tt(ct1, s_[:, 2, :], cw[:, 2, :], ALU.mult, [skey, "cw"], ["pr1"], eng="gpsimd")
                k.tt(ct0, ct0, ct1, ALU.add, ["pr0", "pr1"], ["pr0"])
                k.tt(VXX[:, g, :], ct0, cw[:, 3, :], ALU.add, ["pr0", "cw"], [f"VXX{g}"])
        psR = [S.ps(f"psR{i}", [128, 512]) for i in range(2)]
        psJ = [S.ps(f"psJ{i}", [128, 512]) for i in range(2)]
        pI = [S.ps(f"pI{i}", [128, 512]) for i in range(2)]
        NBM = 4
        RJ = [S.sb(f"RJ{i}", [128, 2, NBM, 192]) for i in range(2)]
        Gb = [S.sb(f"Gb{i}", [128, 2, 2 * NBM - 1, 192]) for i in range(1)]
        YAB = S.sb("YAB", [128, 8, 2, NBM, 192], BF16)
        et = [S.sb(f"et{i}", [128, 192]) for i in range(2)]
        ost = [S.sb(f"ost{i}", [128, 192]) for i in range(2)]
        oT = S.sb("oT", [128, 128]); ptp = S.ps("ptp", [128, 512])
        ident = S.sb("ident", [128, 128]); S.dma("sync", ident[:], id_d, writes=["ident"])
        identb = S.sb("identb", [128, 128], BF16); nidentb = S.sb("nidentb", [128, 128], BF16)
        k.act(identb[:], ident[:], AF.Identity, ["ident"], ["identb"], scale=1.0)
        k.act(nidentb[:], ident[:], AF.Identity, ["ident"], ["identb"], scale=-1.0)
        psY = S.ps("psY", [128, 512])
        cnt = {"f": 0, "g": 0, "i": 0, "e": 0}

        def conv(nm, o, src_fn, src_keys, epi):
            cfg = HY_CFG[nm]; B, nb = cfg["B"], cfg["nb"]; ntb = B // 128; nq = 2 * nb - 1
            base = tile_base[nm]
            (TC, kTC), (TS, kTS), (TCT, kTCT), (TST, kTST) = tabs[nm]
            for kt in range(ntb):
                rj = RJ[kt % 2]; rk = f"RJ{kt % 2}"
                for jp in range(0, nb, 2):
                    nj = min(2, nb - jp)
                    a_ = cnt["f"] % 2; cnt["f"] += 1
                    pR, pJ = psR[a_], psJ[a_]
                    for jj in range(nj):
                        for tt in range(ntb):
                            g = base + (jp + jj) * ntb + tt
                            k.mm(pR[:, jj * 192:(jj + 1) * 192], TC[:, tt, kt * 128:(kt + 1) * 128], src_fn(g), tt == 0, tt == ntb - 1,
                                 [kTC] + src_keys(g), [f"psR{a_}"], inc=(tt == ntb - 1 and jj == nj - 1))
                    for jj in range(nj):
                        for tt in range(ntb):
                            g = base + (jp + jj) * ntb + tt
                            k.mm(pJ[:, jj * 192:(jj + 1) * 192], TS[:, tt, kt * 128:(kt + 1) * 128], src_fn(g), tt == 0, tt == ntb - 1,
                                 [kTS] + src_keys(g), [f"psJ{a_}"], inc=(tt == ntb - 1 and jj == nj - 1))
                    k.copy("scalar", rj[:, 0, jp:jp + nj, :], pR[:, 0:nj * 192].rearrange("p (j c) -> p j c", j=nj), [f"psR{a_}"], [rk + f"_0_{jp}"])
                    k.copy("scalar", rj[:, 1, jp:jp + nj, :], pJ[:, 0:nj * 192].rearrange("p (j c) -> p j c", j=nj), [f"psJ{a_}"], [rk + f"_1_{jp}"])
                rkeys = [rk + f"_{a}_{jp}" for a in range(2) for jp in range(0, nb, 2)]
                gb = Gb[0]; gk = "Gb0"; cnt["g"] += 1
                for a in range(2):
                    S.dma("sync", gb[:, a, 0:nq, :], Gd[nm][o, kt, a], writes=[gk + f"_{a}"])
                gkeys = [gk + "_0", gk + "_1"]
                for i in range(nb):
                    qs = nb - 1 - i
                    Rv = rj[:, 0, 0:nb, :]; Jv = rj[:, 1, 0:nb, :]; GRs = gb[:, 0, qs:qs + nb, :]; GJs = gb[:, 1, qs:qs + nb, :]
                    b0, b1, b2, b3 = [p_[:, 0:nb, :] for p_ in pb]
                    k.tt(b0, Rv, GRs, ALU.mult, rkeys + gkeys, ["pb0"])
                    k.tt(b1, Jv, GJs, ALU.mult, rkeys + gkeys, ["pb1"], eng="gpsimd")
                    k.tt(b2, Rv, GJs, ALU.mult, rkeys + gkeys, ["pb2"], eng="gpsimd")
                    k.tt(b3, Jv, GRs, ALU.mult, rkeys + gkeys, ["pb3"])
                    terms = [(0, identb, "pb0"), (1, nidentb, "pb1")]
                    for half, tl in ((0, [(pb[0], identb, "pb0"), (pb[1], nidentb, "pb1")]), (1, [(pb[2], identb, "pb2"), (pb[3], identb, "pb3")])):
                        nmm = 2 * nb; cmm = 0
                        for (pp, idm, pk_) in tl:
                            for j in range(nb):
                                k.mm(psY[:, half * 192:(half + 1) * 192], idm[:], pp[:, j, :], cmm == 0, cmm == nmm - 1, [pk_, "identb"], ["psY"],
                                     inc=(cmm == nmm - 1))
                                cmm += 1
                    for a in range(2):
                        k.copy("scalar", YAB[:, kt, a, i, :], psY[:, a * 192:(a + 1) * 192], ["psY"], [f"YAB{kt}_{a}_{i}"])
            for pt in range(ntb):
                for ip in range(0, nb, 2):
                    ni = min(2, nb - ip)
                    a_ = cnt["i"] % 2; cnt["i"] += 1
                    ps = pI[a_]; pk = f"pI{a_}"
                    for ii in range(ni):
                        for kt in range(ntb):
                            last = (kt == ntb - 1 and ii == ni - 1)
                            k.mm(ps[:, ii * 192:(ii + 1) * 192], TCT[:, kt, pt * 128:(pt + 1) * 128], YAB[:, kt, 0, ip + ii, :], kt == 0, False,
                                 [kTCT, f"YAB{kt}_0_{ip + ii}"], [pk], inc=False)
                            k.mm(ps[:, ii * 192:(ii + 1) * 192], TST[:, kt, pt * 128:(pt + 1) * 128], YAB[:, kt, 1, ip + ii, :], False, kt == ntb - 1,
                                 [kTST, f"YAB{kt}_1_{ip + ii}"], [pk], inc=last)
                    for ii in range(ni):
                        g = base + (ip + ii) * ntb + pt
                        epi(g, ps[:, ii * 192:(ii + 1) * 192], pk)

        def epi1(g, y, pk):
            e_ = et[cnt["e"] % 2]; ek = f"et{cnt['e'] % 2}"; cnt["e"] += 1
            k.tt(e_[:], VXX[:, g, 0:192], sk[:, 0, :], ALU.mult, [f"VXX{g}", "sk"], [ek], eng="gpsimd")
            k.tt(e_[:], y, e_[:], ALU.add, [pk, ek], [ek])
            k.tt(Z[:, g, :], e_[:], VXX[:, g, 192:384], ALU.mult, [ek, f"VXX{g}"], [f"Z{g}"], eng="gpsimd")

        def epi2(g, y, pk):
            e_ = et[cnt["e"] % 2]; ek = f"et{cnt['e'] % 2}"; cnt["e"] += 1
            o_ = ost[cnt["e"] % 2]; ok = f"ost{cnt['e'] % 2}"
            k.tt(e_[:], Z[:, g, :], sk[:, 1, :], ALU.mult, [f"Z{g}", "sk"], [ek], eng="gpsimd")
            k.tt(e_[:], y, e_[:], ALU.add, [pk, ek], [ek])
            k.tt(o_[:], e_[:], VXX[:, g, 384:576], ALU.mult, [ek, f"VXX{g}"], [ok], eng="gpsimd")
            for (c0_, cn) in ((0, 128), (128, 64)):
                S.op("tensor", lambda e, o_=o_, c0_=c0_, cn=cn: e.transpose(ptp[:cn, 0:128], o_[:, c0_:c0_ + cn], ident[:]),
                     reads=[ok, "ident", "ptp"], writes=["ptp"])
                k.copy("scalar", oT[:cn, :], ptp[:cn, 0:128], ["ptp"], ["oT"])
                S.dma("sync", out_d[c0_:c0_ + cn, g * 128:(g + 1) * 128], oT[:cn, :], reads=["oT"])

        for nm in (("ctx",) if dbg == "ctx" else ("lat", "ctx")):
            conv(nm, 0, lambda g: VXX[:, g, 0:192], lambda g: [f"VXX{g}"], epi1)
            conv(nm, 1, lambda g: Z[:, g, :], lambda g: [f"Z{g}"], epi2)
        S.emit()


def hy_inputs(l, core, tm_all, G_core, inp, consts):
    b, hh = core_bh(core)
    cols = np.concatenate([896 + part * 384 + hh * 192 + np.arange(192) for part in range(3)])
    hy = seq_tm(tm_all, b, cols)
    z = np.zeros((1, 576), np.float32)
    ccols = np.concatenate([part * 384 + hh * 192 + np.arange(192) for part in range(3)])
    cwb = np.concatenate([inp["hy_conv_w"][l][:, ccols].reshape(-1), inp["hy_conv_b"][l][ccols]])
    d = {"hyp_lat": np.ascontiguousarray(np.concatenate([z, hy[:SEQ], z], 0)),
         "hyp_ctx": np.ascontiguousarray(np.concatenate([z, hy[SEQ:], z], 0)),
         "cw_bc": np.ascontiguousarray(np.tile(cwb[None, :], (128, 1)).astype(np.float32)),
         "sk_bc": np.ascontiguousarray(np.tile(inp["hy_skip"][l][:, hh * 192:(hh + 1) * 192].reshape(1, -1), (128, 1)).astype(np.float32))}
    for nm in HY_CFG:
        d[f"G_{nm}"] = G_core[nm]
        for t in ("TC", "TS", "TCT", "TST"):
            d[f"{t}_{nm}"] = consts["hy"][nm][t]
    return d


NROW = SEQ + CTX + 4
LAT0, CTX0 = 1, SEQ + 3


def ext_specs():
    sp = {"xT_in": [2, D, NT], "sc": [128, 16], "w_mod": [DEPTH, D, 6 * D], "b_mod_p": [DEPTH, 128, 48],
          "n1_p": [DEPTH, 128, 8], "n2_p": [DEPTH, 128, 8], "w_in": [DEPTH, D, P_IN], "qkn": [DEPTH, 128, 2],
          "gate_b": [DEPTH, 16, 1], "cosT": [2, 128, NT], "sinT": [2, 128, NT], "rm2": [128, 128], "blk1": [128, 128],
          "w_out": [DEPTH, D, D], "w_gate": [DEPTH, D, D_FF], "w_up": [DEPTH, D, D_FF], "w_down": [DEPTH, D_FF, D],
          "sink": [DEPTH, 2, 64, 3], "mask": [128, 384], "nw_bc": [DEPTH, 2, 128, 128],
          "sel": [34, 512], "ident": [128, 128], "maskf": [128, 128], "maskb": [128, 128],
          "cw_bc": [DEPTH, 2, 128, 4 * 576], "sk_bc": [DEPTH, 2, 128, 384],
          "f_w1": [DEPTH, 33, 64], "f_w2": [DEPTH, 64, 64], "f_w3c": [DEPTH, 2, 64, 768], "f_fb": [DEPTH, 64, 4],
          "f_dec": [DEPTH, 2, 128, 768], "alt": [128, 1]}
    for nm, cfg in HY_CFG.items():
        L, B = cfg["L"], cfg["B"]
        for t in ("TC", "TS", "TCT", "TST"):
            sp[f"{t}_{nm}"] = [B, B]
        sp[f"featsT_{nm}"] = [33, L]; sp[f"featsTr_{nm}"] = [33, L]
        sp[f"negt_{nm}"] = [128, L // 128]; sp[f"negtr_{nm}"] = [128, L // 128]
    return sp


def build_fused(depth=DEPTH):
    nc = bass.Bass("TRN2", target_bir_lowering=False)
    X = {n: dram_in(nc, n, shp) for n, shp in ext_specs().items()}
    OUT = dram_out(nc, "xT_out", [2, D, NT])
    FMS = nc.dram_tensor("FMS", [1424, NTOK], F32).ap()
    TMT = nc.dram_tensor("TMSP", [NROW, 2048], F32)
    TMS = TMT.ap()
    MIXS = nc.dram_tensor("MIXS", [D, NTOK], F32).ap()
    MODT = nc.dram_tensor("MODT", [128, 96], F32).ap()
    XTS = nc.dram_tensor("XTS", [2, D, NT], F32).ap()
    GS = {nm: nc.dram_tensor(f"GS_{nm}", [DEPTH, 2, 2, cfg["B"] // 128, 2, 128, 2 * cfg["nb"] - 1, 192], F32).ap()
          for nm, cfg in HY_CFG.items()}
    MLSCR = nc.dram_tensor("ml_scr", [2, 4, NTOK], F32).ap()
    import os
    PH = os.environ.get("FPH", "F,A,att,ml,hy,E").split(",")
    with ExitStack() as es0:
        bank = SemBank(nc, es0, nsets=1)
        with ExitStack() as es:
            S = Sched(nc, es, bank, "init_")
            z = S.sb("z", [4, 2048])
            S.op("vector", lambda e: e.memset(z[:], 0.0), writes=["z"])
            for i_, row in enumerate((0, SEQ + 1, SEQ + 2, NROW - 1)):
                S.dma("sync", TMS[row:row + 1, :], z[i_:i_ + 1, :], reads=["z"])
            S.emit()
        for l in range(depth):
            io = {"w1": X["f_w1"][l], "w2": X["f_w2"][l], "w3c": [X["f_w3c"][l, hh] for hh in range(2)], "fb": X["f_fb"][l],
                  "decay_bc": [X["f_dec"][l, hh] for hh in range(2)], "alt": X["alt"]}
            for nm in HY_CFG:
                io[nm] = dict(feats=X[f"featsT_{nm}"], featsr=X[f"featsTr_{nm}"], negt=X[f"negt_{nm}"], negtr=X[f"negtr_{nm}"],
                              TC=X[f"TC_{nm}"], TS=X[f"TS_{nm}"], G=[GS[nm][l, hh] for hh in range(2)])
            if "F" in PH:
                emit_F(nc, bank, f"F{l}_", io)
        for l in range(depth):
            xsrc = X["xT_in"] if l == 0 else XTS
            xdst = OUT if l == depth - 1 else XTS
            gcols = [(lambda t, u=u: u * NLAT + t if t < NLAT else SEQ + u * NCTX + (t - NLAT)) for u in range(2)]
            grows = [(lambda t, u=u: LAT0 + u * NLAT + t if t < NLAT else CTX0 + u * NCTX + (t - NLAT)) for u in range(2)]
            io = {"xT": [xsrc[0], xsrc[1]], "sc": X["sc"], "w_mod": X["w_mod"][l], "b_mod": X["b_mod_p"][l], "norm1_w": X["n1_p"][l],
                  "w_in": X["w_in"][l], "qkn": X["qkn"][l], "gate_b": X["gate_b"][l], "cosT": X["cosT"], "sinT": X["sinT"],
                  "rm2": X["rm2"], "blk1": X["blk1"], "modT": MODT, "fm": FMS, "tm": TMS}
            if "A" in PH:
                emit_A(nc, bank, f"A{l}_", io, gcols, grows)
            for g in range(2):
                io = {"qT": FMS[192 * g:192 * g + 192, :], "kT": FMS[384 + 64 * g:384 + 64 * g + 64, :],
                      "v_lat": TMS[LAT0:LAT0 + SEQ, 64 * g:64 * g + 64], "v_ctx": TMS[CTX0:CTX0 + CTX, 64 * g:64 * g + 64],
                      "sink": X["sink"][l, g], "mask": X["mask"], "attT": MIXS[192 * g:192 * g + 192, :]}
                if "att" in PH:
                    emit_att(nc, bank, f"T{l}{g}_", io)
            for hp in range(2):
                def tmp_(c0):
                    return (TMS[LAT0:LAT0 + SEQ, c0:c0 + 128], TMS[CTX0:CTX0 + CTX, c0:c0 + 128])
                io = {"qT": FMS[512 + 128 * hp:512 + 128 * hp + 128, :], "kT": FMS[768 + 128 * hp:768 + 128 * hp + 128, :],
                      "ktok": tmp_(128 + 128 * hp), "vtok": tmp_(384 + 128 * hp), "otok": tmp_(640 + 128 * hp),
                      "g4": [FMS[1024 + 4 * q + 2 * hp:1024 + 4 * q + 2 * hp + 2, :] for q in range(4)],
                      "nw_bc": X["nw_bc"][l, hp], "sel": X["sel"], "ident": X["ident"], "maskf": X["maskf"], "maskb": X["maskb"],
                      "mloT": MIXS[768 + 128 * hp:768 + 128 * hp + 128, :], "scr": MLSCR}
                if "ml" in PH:
                    emit_ml(nc, bank, f"L{l}{hp}_", io)
            for hh in range(2):
                io = {"tmT": (TMT, 2048), "row0": {"lat": LAT0 - 1, "ctx": CTX0 - 1}, "col0": 896 + 192 * hh,
                      "cw_bc": X["cw_bc"][l, hh], "sk_bc": X["sk_bc"][l, hh], "ident": X["ident"],
                      "G": {nm: GS[nm][l, hh] for nm in HY_CFG},
                      "T": {nm: [X[f"{t}_{nm}"] for t in ("TC", "TS", "TCT", "TST")] for nm in HY_CFG},
                      "hyoT": MIXS[384 + 192 * hh:384 + 192 * hh + 192, :]}
                if "hy" in PH:
                    emit_hy(nc, bank, f"H{l}{hh}_", io)
            io = {"xT": [xsrc[0], xsrc[1]], "mix": MIXS, "modT": MODT, "norm2_w": X["n2_p"][l], "w_out": X["w_out"][l],
                  "w_gate": X["w_gate"][l], "w_up": X["w_up"][l], "w_down": X["w_down"][l], "xT_out": [xdst[0], xdst[1]]}
            if "E" in PH:
                emit_E(nc, bank, f"E{l}_", io, gcols)
    return nc


def fused_inputs(b, inp, consts):
    cos, sin = consts["rope"]
    d = {}
    xT = np.empty((2, D, NT), np.float32)
    cosT = np.ones((2, 128, NT), np.float32)
    sinT = np.zeros((2, 128, NT), np.float32)
    for u in range(2):
        xT[u, :, :NLAT] = inp["x"][b, u * NLAT:(u + 1) * NLAT, :].T
        xT[u, :, NLAT:] = inp["ctx"][b, u * NCTX:(u + 1) * NCTX, :].T
        for hd in range(2):
            cosT[u, 64 * hd:64 * hd + 64, :NLAT] = cos[:, u * NLAT:(u + 1) * NLAT]
            sinT[u, 64 * hd:64 * hd + 64, :NLAT] = sin[:, u * NLAT:(u + 1) * NLAT]
    d["xT_in"] = xT; d["cosT"] = cosT; d["sinT"] = sinT
    d["sc"] = pcol(np.stack([inp["c"][b], inp["c_ctx"]], axis=1)).reshape(128, 16)
    for k_ in ("w_mod", "w_in", "w_out"):
        d[k_] = inp[k_]
    d["w_gate"], d["w_up"], d["w_down"] = inp["ffn_w_gate"], inp["ffn_w_up"], inp["ffn_w_down"]
    d["b_mod_p"] = np.stack([pcol(inp["b_mod"][l]) for l in range(DEPTH)])
    d["n1_p"] = np.stack([pcol(inp["norm1_w"][l]) for l in range(DEPTH)])
    d["n2_p"] = np.stack([pcol(inp["norm2_w"][l]) for l in range(DEPTH)])
    d["qkn"] = np.stack([np.stack([np.tile(inp["q_norm_w"][l], 2), np.tile(inp["k_norm_w"][l], 2)], axis=1) for l in range(DEPTH)])
    d["gate_b"] = inp["ml_gate_b"].reshape(DEPTH, 16, 1)
    d["rm2"], d["blk1"], d["mask"] = consts["rm2"], consts["blk1"], consts["mask"]
    d["sink"] = np.stack([np.stack([np.tile(inp["attn_sink"][l][3 * g:3 * g + 3][None, :], (64, 1)) for g in range(2)]) for l in range(DEPTH)])
    d["nw_bc"] = np.stack([np.stack([np.tile(inp["ml_norm_w"][l][128 * hp:128 * hp + 128][None, :], (128, 1)) for hp in range(2)]) for l in range(DEPTH)])
    d.update(consts["ml"])
    cw = np.empty((DEPTH, 2, 128, 4 * 576), np.float32); sk = np.empty((DEPTH, 2, 128, 384), np.float32)
    w3c = np.empty((DEPTH, 2, 64, 768), np.float32); dec = np.empty((DEPTH, 2, 128, 768), np.float32)
    for l in range(DEPTH):
        for hh in range(2):
            ccols = np.concatenate([part * 384 + hh * 192 + np.arange(192) for part in range(3)])
            cw[l, hh] = np.concatenate([inp["hy_conv_w"][l][:, ccols].reshape(-1), inp["hy_conv_b"][l][ccols]])[None, :]
            sk[l, hh] = inp["hy_skip"][l][:, hh * 192:(hh + 1) * 192].reshape(1, -1)
            cols = np.concatenate([dr * 768 + o * 384 + hh * 192 + np.arange(192) for dr in range(2) for o in range(2)])
            w3c[l, hh] = inp["hy_w3"][l][:, cols]
            dec[l, hh] = inp["hy_decay"][l][cols][None, :]
    d["cw_bc"], d["sk_bc"], d["f_w3c"], d["f_dec"] = cw, sk, w3c, dec
    d["f_w1"], d["f_w2"] = inp["hy_w1"], inp["hy_w2"]
    d["f_fb"] = np.stack([np.stack([inp["hy_freq"][l], inp["hy_b1"][l], inp["hy_b2"][l], inp["hy_b2"][l]], axis=1) for l in range(DEPTH)])
    hc = consts["hy"]
    d["alt"] = hc["alt"]
    for nm in HY_CFG:
        for t in ("TC", "TS", "TCT", "TST", "featsT", "featsTr", "negt", "negtr"):
            d[f"{t}_{nm}"] = hc[nm][t]
    sp = ext_specs()
    return {k_: np.ascontiguousarray(np.asarray(v, np.float32).reshape(sp[k_])) for k_, v in d.items()}


def kernel(**inputs):
    inp = {k_: np.asarray(v, dtype=np.float32) for k_, v in inputs.items()}
    consts = {"rope": rope_tables(), "mask": band_mask(), "ml": ml_consts(), "hy": hy_consts()}
    consts["rm2"], consts["blk1"] = rope_consts()
    nc = build_fused()
    maps = [fused_inputs(c // 2, inp, consts) for c in range(8)]
    res = run_bass_kernel_spmd(nc, maps, core_ids=list(range(8))).results
    out = np.empty((4, SEQ, D), np.float32)
    for b in range(4):
        xo = res[2 * b]["xT_out"]
        for u in range(2):
            out[b, u * NLAT:(u + 1) * NLAT, :] = xo[u][:, :NLAT].T
    return out
```

```python
from contextlib import ExitStack
import math
import numpy as np
import ml_dtypes
import concourse.bass as bass
import concourse.mybir as mybir
from concourse.bass_utils import run_bass_kernel_spmd

F32 = mybir.dt.float32
BF16 = mybir.dt.bfloat16
ALU = mybir.AluOpType
AF = mybir.ActivationFunctionType
AX = mybir.AxisListType

D = 1024
SEQ = 4096
CTX = 256
DEPTH = 4
NLAT = SEQ // 2
NCTX = CTX // 2
NT = NLAT + NCTX
NTOK = SEQ + CTX
P_IN = 2832
D_FF = 2816
NFF = D_FF // 128
EPS = 1e-6
O_AQ, O_AK, O_AV, O_HY, O_MQ, O_MK, O_MV, O_MO, O_MG = 0, 384, 512, 640, 1792, 2048, 2304, 2560, 2816

ENGS = ["sync", "scalar", "vector", "gpsimd", "tensor"]
NDS = 8
SES_SKIP = ()


class SemBank:
    def __init__(self, nc, es, nsets=2):
        self.sets = []
        for si in range(nsets):
            self.sets.append({
                "s": {e: es.enter_context(nc.semaphore(f"s{si}_{e}")) for e in ENGS},
                "d": {e: [es.enter_context(nc.semaphore(f"d{si}_{e}{i}")) for i in range(NDS)] for e in ENGS}})
        self.phase = 0


class Sched:
    def __init__(self, nc, es, bank=None, pfx="", same_engine_sync=True):
        self.nc = nc
        self.es = es
        self.pfx = pfx
        self.q = {e: [] for e in ENGS}
        if bank is None:
            bank = SemBank(nc, es, nsets=1)
        cur = bank.sets[bank.phase % len(bank.sets)]
        self.other = bank.sets[(bank.phase + 1) % len(bank.sets)] if len(bank.sets) > 1 else None
        bank.phase += 1
        self.sem = cur["s"]
        self.dsem = cur["d"]
        if len(bank.sets) == 1:
            if not hasattr(bank, "state"):
                bank.state = ({e: 0 for e in ENGS}, {e: [0] * NDS for e in ENGS}, {e: 0 for e in ENGS}, {e: {} for e in ENGS})
            self.cnt, self.dcnt, self.dnext, self.seen = bank.state
        else:
            self.cnt = {e: 0 for e in ENGS}
            self.dcnt = {e: [0] * NDS for e in ENGS}
            self.dnext = {e: 0 for e in ENGS}
            self.seen = {e: {} for e in ENGS}
        self.res = {}
        self.ses = same_engine_sync

    def sb(self, name, shape, dt=F32):
        return self.es.enter_context(self.nc.sbuf_tensor("sb_" + self.pfx + name, list(shape), dt))

    def ps(self, name, shape, dt=F32):
        return self.es.enter_context(self.nc.psum_tensor("ps_" + self.pfx + name, list(shape), dt))

    def _deps(self, eng, reads, writes):
        deps = {}

        def add(tok):
            sem, val, name = tok
            if name not in deps or deps[name][1] < val:
                deps[name] = tok

        for k in reads:
            r = self.res.get(k)
            if r and r[0] is not None:
                add(r[0])
        for k in writes:
            r = self.res.get(k)
            if r:
                if r[0] is not None:
                    add(r[0])
                for t in r[1]:
                    add(t)
        waits = []
        for name, (sem, val, _) in deps.items():
            if name == eng:
                if eng == "tensor" or not self.ses or val > self.cnt[eng] or eng in SES_SKIP:
                    continue
            if self.seen[eng].get(name, 0) < val:
                self.seen[eng][name] = val
                waits.append((sem, val))
        return waits

    def _record(self, tok, reads, writes):
        for k in reads:
            r = self.res.setdefault(k, [None, []])
            r[1].append(tok)
        for k in writes:
            self.res[k] = [tok, []]

    def op(self, eng, fn, reads=(), writes=(), inc=True):
        waits = self._deps(eng, reads, writes)
        if inc:
            self.cnt[eng] += 1
            tok = (self.sem[eng], self.cnt[eng], eng)
        else:
            tok = (self.sem[eng], self.cnt[eng] + 1, eng)
        self.q[eng].append((waits, fn, (self.sem[eng], 1) if inc else None))
        self._record(tok, reads, writes)

    def dma(self, eng, out, in_, reads=(), writes=(), **kw):
        waits = self._deps(eng, reads, writes)
        slot = self.dnext[eng]
        self.dnext[eng] = (slot + 1) % NDS
        name = f"d_{eng}{slot}"
        prev = self.dcnt[eng][slot]
        if prev > 0 and self.seen[eng].get(name, 0) < prev:
            self.seen[eng][name] = prev
            waits.append((self.dsem[eng][slot], prev))
        self.dcnt[eng][slot] = prev + 16
        tok = (self.dsem[eng][slot], prev + 16, name)
        self.q[eng].append((waits, lambda e: e.dma_start(out=out, in_=in_, **kw), (self.dsem[eng][slot], 16)))
        self._record(tok, reads, writes)

    def emit(self):
        for e in ENGS:
            for i in range(NDS):
                v = self.dcnt[e][i]
                if v > 0:
                    self.q["sync"].append(([(self.dsem[e][i], v)], None, None))
        for e in ENGS:
            if e != "sync" and self.cnt[e] > 0:
                self.q["sync"].append(([(self.sem[e], self.cnt[e])], None, None))
        if self.other is not None:
            clr = list(self.other["s"].values()) + [x for l_ in self.other["d"].values() for x in l_]
            self.q["gpsimd"] = [([], (lambda e, sm=sm: e.sem_clear(sm)), None) for sm in clr] + self.q["gpsimd"]
        with self.nc.Block() as block:
            for eng in ENGS:
                if not self.q[eng]:
                    continue

                def body(e, eng=eng):
                    for waits, fn, inc in self.q[eng]:
                        for sem, val in waits:
                            e.wait_ge(sem, val)
                        if fn is not None:
                            ins = fn(e)
                            if inc is not None:
                                ins.then_inc(inc[0], inc[1])

                getattr(block, eng)(body)


class K:
    def __init__(self, S):
        self.S = S

    def act(self, out, in_, func, r, w, bias=None, scale=None, eng="scalar"):
        kw = {}
        if bias is not None:
            kw["bias"] = bias
        if scale is not None:
            kw["scale"] = scale
        self.S.op("scalar", lambda e: e.activation(out=out, in_=in_, func=func, **kw), reads=r, writes=w)

    def tt(self, out, a, b, op, r, w, eng="vector"):
        self.S.op(eng, lambda e: e.tensor_tensor(out=out, in0=a, in1=b, op=op), reads=r, writes=w)

    def ts(self, out, a, s1, s2, op0, op1, r, w, eng="vector"):
        if op1 is None:
            self.S.op(eng, lambda e: e.tensor_scalar(out=out, in0=a, scalar1=s1, scalar2=None, op0=op0), reads=r, writes=w)
        else:
            self.S.op(eng, lambda e: e.tensor_scalar(out=out, in0=a, scalar1=s1, scalar2=s2, op0=op0, op1=op1), reads=r, writes=w)

    def stt(self, out, a, s, b, op0, op1, r, w):
        self.S.op("vector", lambda e: e.scalar_tensor_tensor(out=out, in0=a, scalar=s, in1=b, op0=op0, op1=op1), reads=r, writes=w)

    def copy(self, eng, out, in_, r, w):
        if eng == "scalar":
            self.S.op("scalar", lambda e: e.copy(out=out, in_=in_), reads=r, writes=w)
        else:
            self.S.op(eng, lambda e: e.tensor_copy(out=out, in_=in_), reads=r, writes=w)

    def recip(self, out, in_, r, w):
        self.S.op("vector", lambda e: e.reciprocal(out=out, in_=in_), reads=r, writes=w)

    def mm(self, out, lhsT, rhs, start, stop, r, w, inc=None):
        self.S.op("tensor", lambda e: e.matmul(out, lhsT=lhsT, rhs=rhs, start=start, stop=stop), reads=r, writes=w,
                  inc=stop if inc is None else inc)

    def memset(self, eng, ap, val, w):
        self.S.op(eng, lambda e: e.memset(ap, val), writes=w)


def dram_in(nc, name, shape, dt=F32):
    return nc.dram_tensor(name, list(shape), dt, kind="ExternalInput").ap()


def dram_out(nc, name, shape, dt=F32):
    return nc.dram_tensor(name, list(shape), dt, kind="ExternalOutput").ap()


def bcast_rows(ap1d, nparts):
    return ap1d.partition_broadcast(nparts)


def token_tiles(width):
    tiles = []
    t = 0
    while t < NLAT:
        tiles.append((t, width, 0))
        t += width
    tiles.append((NLAT, NCTX, 1))
    return tiles


def emit_mod_vectors(S, k, sc_d, wmod_d, bmod_d):
    sraw = S.sb("sraw", [128, 8, 2])
    sbf = S.sb("sbf", [128, 8, 2], BF16)
    bm = S.sb("bm", [128, 48])
    modT = S.sb("modT", [128, 48, 2])
    S.dma("sync", sraw[:].rearrange("p j c -> p (j c)"), sc_d, writes=["sraw"])
    S.dma("sync", bm[:], bmod_d, writes=["bm"])
    k.act(sbf[:], sraw[:], AF.Silu, ["sraw"], ["sbf"])
    wv = wmod_d.rearrange("(kc p) n -> p kc n", p=128)
    pm = S.ps("pmod", [128, 512])
    bufs = [S.sb(f"wmodb{i}", [128, 8, 1024], BF16) for i in range(2)]
    for g in range(6):
        wt = bufs[g % 2]
        key = f"wmodb{g % 2}"
        S.dma("gpsimd", wt[:], wv[:, :, g * 1024:(g + 1) * 1024], writes=[key])
        for j in range(8):
            jj = g * 8 + j
            for kc in range(8):
                k.mm(pm[:, 2 * jj:2 * jj + 2], wt[:, kc, j * 128:(j + 1) * 128], sbf[:, kc, :], kc == 0, kc == 7,
                     [key, "sbf"], ["pmod"])
    pv = pm[:, 0:96].rearrange("p (j c) -> p j c", c=2)
    for c in range(2):
        k.tt(modT[:, :, c], pv[:, :, c], bm[:], ALU.add, ["pmod", "bm"], ["modT"])
    return modT


def emit_A(nc, bank, pfx, io, gcols, grows, dbg=None):
    sc_d, wmod_d, bmod_d, n1_d, win_d = io["sc"], io["w_mod"], io["b_mod"], io["norm1_w"], io["w_in"]
    qkn_d, gb_d, cos_d, sin_d, rm_d, bo_d = io["qkn"], io["gate_b"], io["cosT"], io["sinT"], io["rm2"], io["blk1"]
    modT_o, fmT_o, tm_o = io["modT"], io["fm"], io["tm"]
    with ExitStack() as es:
        S = Sched(nc, es, bank, pfx)
        k = K(S)
        modT = emit_mod_vectors(S, k, sc_d, wmod_d, bmod_d)
        S.dma("sync", modT_o, modT[:].rearrange("p j c -> p (j c)"), reads=["modT"])
        n1 = S.sb("n1", [128, 8])
        S.dma("sync", n1[:], n1_d, writes=["n1"])
        qkn = S.sb("qkn", [128, 2]); S.dma("sync", qkn[:], qkn_d, writes=["qkn"])
        gb = S.sb("gb", [16, 1]); S.dma("sync", gb[:], gb_d, writes=["gb"])
        cosT = S.sb("cosT", [128, NT]); sinT = S.sb("sinT", [128, NT])
        rm2 = S.sb("rm2", [128, 128], BF16); S.dma("gpsimd", rm2[:], rm_d, writes=["rm2"])
        blk1 = S.sb("blk1", [128, 128], BF16); S.dma("gpsimd", blk1[:], bo_d, writes=["blk1"])
        ones = S.sb("ones", [128, 128], BF16); k.memset("gpsimd", ones[:], 1.0, ["ones"])
        A1 = S.sb("A1", [128, 8, 2])
        for c in range(2):
            k.stt(A1[:, :, c], modT[:, 8:16, c], 1.0, n1[:], ALU.add, ALU.mult, ["modT", "n1"], ["A1"])
        wv = win_d.rearrange("(kc p) n -> p kc n", p=128)
        fm_cols = [(O_AQ, 384), (O_AK, 128), (O_MQ, 256), (O_MK, 256), (O_MG, 16), (O_HY + 768, 384)]
        Wfm = S.sb("Wfm", [128, 8, 1424], BF16)
        off = 0
        for (c0, n) in fm_cols:
            S.dma("gpsimd", Wfm[:, :, off:off + n], wv[:, :, c0:c0 + n], writes=[f"Wfm{off}"])
            off += n
        tm_cols = [(O_AV, 128), (O_MK, 256), (O_MV, 256), (O_MO, 256), (O_HY, 1152)]
        Wtm = S.sb("Wtm", [128, 8, 2048], BF16)
        off = 0
        for (c0, n) in tm_cols:
            S.dma("gpsimd", Wtm[:, :, off:off + n], wv[:, :, c0:c0 + n], writes=[f"Wtm{off}"])
            off += n
        WFM_KEYS = ["Wfm0", "Wfm384", "Wfm512", "Wfm768", "Wfm1024", "Wfm1040"]
        WTM_KEYS = ["Wtm0", "Wtm128", "Wtm384", "Wtm640", "Wtm896"]
        k.ts(Wfm[:, :, 768:1024], Wfm[:, :, 768:1024], 0.125, None, ALU.mult, None, ["Wfm768"], ["Wfm768"], eng="gpsimd")
        k.ts(Wtm[:, :, 128:384], Wtm[:, :, 128:384], 0.125, None, ALU.mult, None, ["Wtm128"], ["Wtm128"], eng="gpsimd")
        fm_tiles = [(0, 128, "q"), (128, 128, "q"), (256, 128, "q"), (384, 128, "k"),
                    (512, 128, "c"), (640, 128, "c"), (768, 128, "c"), (896, 128, "c"),
                    (1024, 16, "g"), (1040, 128, "c"), (1168, 128, "c"), (1296, 128, "c")]
        xt = [S.sb(f"xt{i}", [128, 8, 512]) for i in range(2)]
        sq = S.sb("sq", [128, 8, 512], BF16)
        tmp = S.sb("tmpn", [128, 2, 512])
        hT = [S.sb(f"hT{i}", [128, 8, 512], BF16) for i in range(2)]
        rs = S.sb("rs", [128, 512]); rstd = S.sb("rstd", [128, 512])
        ss_ps = S.ps("ss_ps", [128, 512])
        pf = [S.ps(f"pf{i}", [128, 512]) for i in range(2)]
        pt = [S.ps(f"pt{i}", [128, 512]) for i in range(2)]
        pq = [S.ps(f"pq{i}", [128, 512]) for i in range(2)]
        stg_f = [S.sb(f"stgf{i}", [128, 512]) for i in range(3)]
        stg_t = [S.sb(f"stgt{i}", [128, 2048]) for i in range(1)]
        qsq = S.sb("qsq", [128, 512], BF16); qw = S.sb("qw", [128, 512], BF16)
        qrs = S.sb("qrs", [128, 512]); qri = S.sb("qri", [128, 512]); qt1 = S.sb("qt1", [128, 512]); qt2 = S.sb("qt2", [128, 512])
        nf = 0; ntm = 0; nst = 0
        tiles_all = [(u_, t0, n, which) for u_ in range(len(io["xT"])) for (t0, n, which) in token_tiles(512)]
        def load_x(tj):
            uj, tj0, nj, _ = tiles_all[tj]
            S.dma("sync", xt[tj % 2][:, :, :nj], io["xT"][uj].rearrange("(j p) t -> p j t", p=128)[:, :, tj0:tj0 + nj], writes=[f"xt{tj % 2}"])

        load_x(0)
        for ti, (u_, t0, n, which) in enumerate(tiles_all):
            if t0 == 0:
                gcol, grow = gcols[u_], grows[u_]
                S.dma("sync", cosT[:], cos_d[u_], writes=["cosT"]); S.dma("sync", sinT[:], sin_d[u_], writes=["sinT"])
            xb = xt[ti % 2]; xk = f"xt{ti % 2}"; hb = hT[ti % 2]; hk = f"hT{ti % 2}"
            if ti + 1 < len(tiles_all):
                load_x(ti + 1)
            k.act(sq[:, :, :n], xb[:, :, :n], AF.Square, [xk], ["sq"])
            for j in range(8):
                k.mm(ss_ps[:, :n], ones[:], sq[:, j, :n], j == 0, j == 7, ["ones", "sq"], ["ss_ps"])
            k.act(rs[:, :n], ss_ps[:, :n], AF.Sqrt, ["ss_ps"], ["rs"], bias=EPS, scale=1.0 / D)
            k.recip(rstd[:, :n], rs[:, :n], ["rs"], ["rstd"])
            for j in range(8):
                k.stt(tmp[:, j % 2, :n], xb[:, j, :n], A1[:, j, which:which + 1], rstd[:, :n], ALU.mult, ALU.mult,
                      [xk, "A1", "rstd"], [f"tmpn{j % 2}"])
                k.act(hb[:, j, :n], tmp[:, j % 2, :n], AF.Identity, [f"tmpn{j % 2}", "modT"], [hk + f"_{j}"],
                      bias=modT[:, j, which:which + 1], scale=1.0)
            hkeys = [hk + f"_{j}" for j in range(8)]
            if dbg == "norm":
                break
            for (c0, M, kind) in fm_tiles:
                if dbg == "fmc" and kind != "c":
                    continue
                if dbg == "fmq" and kind != "q":
                    continue
                if dbg == "fmg" and kind != "g":
                    continue
                ps = pf[nf % 2]; pk = f"pf{nf % 2}"; nf += 1
                for kc in range(8):
                    k.mm(ps[:, :n], Wfm[:, kc, c0:c0 + 128], hb[:, kc, :n], kc == 0, kc == 7, WFM_KEYS + hkeys, [pk])
                st = stg_f[nst % 3]; sk = f"stgf{nst % 3}"; nst += 1
                if kind == "c":
                    k.copy("scalar", st[:M, :n], ps[:M, :n], [pk], [sk])
                elif kind == "g":
                    k.ts(st[:M, :n], ps[:M, :n], gb[:, 0:1], None, ALU.add, None, [pk, "gb"], [sk])
                else:
                    nw = qkn[:, 0:1] if kind == "q" else qkn[:, 1:2]
                    p2 = pq[0]; p3 = pq[1]
                    import os
                    QS = int(os.environ.get("QSTEPS", "99"))
                    steps = [
                        lambda: k.act(qsq[:, :n], ps[:, :n], AF.Square, [pk], ["qsq"]),
                        lambda: k.act(qw[:, :n], ps[:, :n], AF.Identity, [pk, "qkn"], ["qw"], scale=nw),
                        lambda: k.mm(p2[:, :n], blk1[:], qsq[:, :n], True, True, ["blk1", "qsq"], ["pq0"]),
                        lambda: k.mm(p3[:, :n], rm2[:], qw[:, :n], True, True, ["rm2", "qw"], ["pq1"]),
                        lambda: k.act(qrs[:, :n], p2[:, :n], AF.Sqrt, ["pq0"], ["qrs"], bias=EPS, scale=1.0 / 64),
                        lambda: k.recip(qri[:, :n], qrs[:, :n], ["qrs"], ["qri"]),
                        lambda: k.tt(qt1[:, :n], qw[:, :n], cosT[:, t0:t0 + n], ALU.mult, ["qw", "cosT"], ["qt1"]),
                        lambda: k.tt(qt2[:, :n], p3[:, :n], sinT[:, t0:t0 + n], ALU.mult, ["pq1", "sinT"], ["qt2"]),
                        lambda: k.tt(qt1[:, :n], qt1[:, :n], qt2[:, :n], ALU.add, ["qt1", "qt2"], ["qt1"]),
                        lambda: k.tt(st[:, :n], qt1[:, :n], qri[:, :n], ALU.mult, ["qt1", "qri"], [sk]),
                    ]
                    for f_ in steps[:QS]:
                        f_()
                S.dma("sync", fmT_o[c0:c0 + M, gcol(t0):gcol(t0) + n], st[:M, :n], reads=[sk])
            if dbg in ("fm", "fmc", "fmq", "fmg"):
                break
            for s0 in range(0, n, 128):
                st = stg_t[0]; sk = "stgt0"; ntm += 1
                for g in range(4):
                    ps = pt[(ntm * 4 + g) % 2]; pk = f"pt{(ntm * 4 + g) % 2}"
                    for kc in range(8):
                        k.mm(ps[:, :], hb[:, kc, s0:s0 + 128], Wtm[:, kc, g * 512:(g + 1) * 512], kc == 0, kc == 7,
                             WTM_KEYS + hkeys, [pk])
                    k.copy("scalar" if g % 2 == 0 else "vector", st[:, g * 512:(g + 1) * 512], ps[:, :], [pk], [sk + f"_{g}"])
                S.dma("sync", tm_o[grow(t0 + s0):grow(t0 + s0) + 128, :], st[:], reads=[sk + f"_{g}" for g in range(4)])
        S.emit()


def rope_tables():
    t = np.arange(SEQ)
    row = (t // 64).astype(np.float64)
    col = (t % 64).astype(np.float64)
    nf = 16
    inv = 10000.0 ** (-np.arange(nf, dtype=np.float64) / nf)
    cos = np.zeros((64, SEQ), np.float32)
    sin = np.zeros((64, SEQ), np.float32)
    for a, pos in enumerate((row, col)):
        ang = (pos[None, :].astype(np.float32) * inv[:, None].astype(np.float32)).astype(np.float32)
        for half in range(2):
            cos[a * 32 + half * 16:a * 32 + half * 16 + 16] = np.cos(ang)
            sin[a * 32 + half * 16:a * 32 + half * 16 + 16] = np.sin(ang)
    return cos, sin


def rope_consts():
    rm = np.zeros((64, 64), np.float32)
    for d in range(64):
        if (d % 32) < 16:
            rm[d + 16, d] = -1.0
        else:
            rm[d - 16, d] = 1.0
    rm2 = np.zeros((128, 128), np.float32)
    rm2[:64, :64] = rm
    rm2[64:, 64:] = rm
    blk = np.zeros((128, 128), np.float32)
    blk[:64, :64] = 1.0
    blk[64:, 64:] = 1.0
    return rm2, blk


def pcol(v):
    v = np.asarray(v, np.float32)
    j = v.shape[0] // 128
    return np.ascontiguousarray(np.moveaxis(v.reshape(j, 128, *v.shape[1:]), 0, 1))


def core_bh(core):
    return core // 2, core % 2


def a_inputs(l, core, x_cur, ctx_cur, inp, consts):
    b, hh = core_bh(core)
    cos, sin = consts["rope"]
    xT = np.concatenate([x_cur[b, hh * NLAT:(hh + 1) * NLAT, :].T, ctx_cur[b, hh * NCTX:(hh + 1) * NCTX, :].T], axis=1)
    cosT = np.ones((128, NT), np.float32)
    sinT = np.zeros((128, NT), np.float32)
    cosT[:64, :NLAT] = cos[:, hh * NLAT:(hh + 1) * NLAT]; cosT[64:, :NLAT] = cosT[:64, :NLAT]
    sinT[:64, :NLAT] = sin[:, hh * NLAT:(hh + 1) * NLAT]; sinT[64:, :NLAT] = sinT[:64, :NLAT]
    return {
        "xT": np.ascontiguousarray(xT, dtype=np.float32),
        "sc": pcol(np.stack([inp["c"][b], inp["c_ctx"]], axis=1)).reshape(128, 16),
        "w_mod": inp["w_mod"][l], "b_mod": pcol(inp["b_mod"][l]), "norm1_w": pcol(inp["norm1_w"][l]), "w_in": inp["w_in"][l],
        "qkn": np.ascontiguousarray(np.stack([np.tile(inp["q_norm_w"][l], 2), np.tile(inp["k_norm_w"][l], 2)], axis=1)),
        "gate_b": np.ascontiguousarray(inp["ml_gate_b"][l].reshape(16, 1)),
        "cosT": cosT, "sinT": sinT, "rm2": consts["rm2"], "blk1": consts["blk1"],
    }


def emit_E(nc, bank, pfx, io, gcols, dbg=None):
    mix_d, mod_d, n2_d = io["mix"], io["modT"], io["norm2_w"]
    wo_d, wg_d, wu_d, wd_d = io["w_out"], io["w_gate"], io["w_up"], io["w_down"]
    with ExitStack() as es:
        S = Sched(nc, es, bank, pfx)
        k = K(S)
        modT = S.sb("modT", [128, 48, 2]); S.dma("sync", modT[:].rearrange("p j c -> p (j c)"), mod_d, writes=["modT"])
        n2 = S.sb("n2", [128, 8]); S.dma("sync", n2[:], n2_d, writes=["n2"])
        ones = S.sb("ones", [128, 128], BF16); k.memset("gpsimd", ones[:], 1.0, ["ones"])
        A2 = S.sb("A2", [128, 8, 2])
        for c in range(2):
            k.stt(A2[:, :, c], modT[:, 32:40, c], 1.0, n2[:], ALU.add, ALU.mult, ["modT", "n2"], ["A2"])
        Wo = S.sb("Wo", [128, 8, D], BF16)
        Wg = S.sb("Wg", [128, 8, D_FF], BF16)
        Wu = S.sb("Wu", [128, 8, D_FF], BF16)
        Wd = S.sb("Wd", [128, NFF, D], BF16)
        S.dma("gpsimd", Wo[:], wo_d.rearrange("(kc p) n -> p kc n", p=128), writes=["Wo"])
        wgv = wg_d.rearrange("(kc p) n -> p kc n", p=128)
        wuv = wu_d.rearrange("(kc p) n -> p kc n", p=128)
        wdv = wd_d.rearrange("(m p) n -> p m n", p=128)
        WG_KEYS = []; WU_KEYS = []; WD_KEYS = []
        for h in range(2):
            S.dma("gpsimd", Wg[:, :, h * 1408:(h + 1) * 1408], wgv[:, :, h * 1408:(h + 1) * 1408], writes=[f"Wg{h}"]); WG_KEYS.append(f"Wg{h}")
            S.dma("gpsimd", Wu[:, :, h * 1408:(h + 1) * 1408], wuv[:, :, h * 1408:(h + 1) * 1408], writes=[f"Wu{h}"]); WU_KEYS.append(f"Wu{h}")
            S.dma("gpsimd", Wd[:, h * 11:(h + 1) * 11, :], wdv[:, h * 11:(h + 1) * 11, :], writes=[f"Wd{h}"]); WD_KEYS.append(f"Wd{h}")
        TW = 256
        mv = mix_d.rearrange("(j p) t -> p j t", p=128)
        xt = [S.sb(f"xt{i}", [128, 8, TW]) for i in range(2)]
        mx = [S.sb(f"mx{i}", [128, 8, TW], BF16) for i in range(2)]
        sq = S.sb("sq", [128, 8, TW], BF16)
        tmp = S.sb("tmpn", [128, 2, TW])
        ev = S.sb("ev", [128, 2, TW])
        hT = S.sb("h2T", [128, 8, TW], BF16)
        aT = S.sb("aT", [128, NFF, TW], BF16)
        sg = S.sb("sg", [128, 2, TW], BF16); uc = S.sb("uc", [128, 2, TW], BF16)
        rs = S.sb("rs", [128, TW]); rstd = S.sb("rstd", [128, TW])
        ss_ps = S.ps("ss_ps", [128, 512])
        py = [S.ps(f"py{i}", [128, 512]) for i in range(2)]
        pg = [S.ps(f"pg{i}", [128, 512]) for i in range(2)]
        pu = [S.ps(f"pu{i}", [128, 512]) for i in range(2)]
        ny = 0; nev = 0
        tiles_all = [(u_, t0, n, which) for u_ in range(len(io["xT"])) for (t0, n, which) in token_tiles(TW)]
        def load_xm(tj):
            uj, tj0, nj, _ = tiles_all[tj]
            S.dma("sync", xt[tj % 2][:, :, :nj], io["xT"][uj].rearrange("(j p) t -> p j t", p=128)[:, :, tj0:tj0 + nj],
                  writes=[f"xt{tj % 2}_{j}" for j in range(8)])
            S.dma("gpsimd", mx[tj % 2][:, :, :nj], mv[:, :, gcols[uj](tj0):gcols[uj](tj0) + nj], writes=[f"mx{tj % 2}"])

        load_xm(0)
        for ti, (u_, t0, n, which) in enumerate(tiles_all):
            if t0 == 0:
                xov = io["xT_out"][u_].rearrange("(j p) t -> p j t", p=128)
            xb = xt[ti % 2]; xk = f"xt{ti % 2}"; mb = mx[ti % 2]; mk = f"mx{ti % 2}"
            if ti + 1 < len(tiles_all):
                load_xm(ti + 1)
            for j in range(8):
                ps = py[ny % 2]; pk = f"py{ny % 2}"; ny += 1
                for kc in range(8):
                    k.mm(ps[:, :n], Wo[:, kc, j * 128:(j + 1) * 128], mb[:, kc, :n], kc == 0, kc == 7, ["Wo", mk], [pk])
                e_ = ev[:, nev % 2, :n]; ek = f"ev{nev % 2}"; nev += 1
                k.act(e_, ps[:, :n], AF.Identity, [pk, "modT"], [ek], scale=modT[:, 16 + j, which:which + 1])
                k.tt(xb[:, j, :n], xb[:, j, :n], e_, ALU.add, [xk + f"_{j}", ek], [xk + f"_{j}"], eng="gpsimd")
            xkeys = [xk + f"_{j}" for j in range(8)]
            k.act(sq[:, :, :n], xb[:, :, :n], AF.Square, xkeys, ["sq"])
            for j in range(8):
                k.mm(ss_ps[:, :n], ones[:], sq[:, j, :n], j == 0, j == 7, ["ones", "sq"], ["ss_ps"])
            k.act(rs[:, :n], ss_ps[:, :n], AF.Sqrt, ["ss_ps"], ["rs"], bias=EPS, scale=1.0 / D)
            k.recip(rstd[:, :n], rs[:, :n], ["rs"], ["rstd"])
            for j in range(8):
                k.stt(tmp[:, j % 2, :n], xb[:, j, :n], A2[:, j, which:which + 1], rstd[:, :n], ALU.mult, ALU.mult,
                      [xk + f"_{j}", "A2", "rstd"], [f"tmpn{j % 2}"])
                k.act(hT[:, j, :n], tmp[:, j % 2, :n], AF.Identity, [f"tmpn{j % 2}", "modT"], [f"h2T_{j}"],
                      bias=modT[:, 24 + j, which:which + 1], scale=1.0)
            hkeys = [f"h2T_{j}" for j in range(8)]
            for m in range(NFF):
                g_ = pg[m % 2]; gk = f"pg{m % 2}"; u_ = pu[m % 2]; uk = f"pu{m % 2}"
                for kc in range(8):
                    k.mm(g_[:, :n], Wg[:, kc, m * 128:(m + 1) * 128], hT[:, kc, :n], kc == 0, kc == 7, WG_KEYS + hkeys, [gk])
                for kc in range(8):
                    k.mm(u_[:, :n], Wu[:, kc, m * 128:(m + 1) * 128], hT[:, kc, :n], kc == 0, kc == 7, WU_KEYS + hkeys, [uk])
                k.act(sg[:, m % 2, :n], g_[:, :n], AF.Silu, [gk], [f"sg{m % 2}"])
                k.act(uc[:, m % 2, :n], u_[:, :n], AF.Copy, [uk], [f"uc{m % 2}"])
                k.tt(aT[:, m, :n], sg[:, m % 2, :n], uc[:, m % 2, :n], ALU.mult, [f"sg{m % 2}", f"uc{m % 2}"], [f"aT{m}"])
            akeys = [f"aT{m}" for m in range(NFF)]
            for j in range(8):
                ps = py[ny % 2]; pk = f"py{ny % 2}"; ny += 1
                for m in range(NFF):
                    k.mm(ps[:, :n], Wd[:, m, j * 128:(j + 1) * 128], aT[:, m, :n], m == 0, m == NFF - 1, WD_KEYS + akeys, [pk])
                e_ = ev[:, nev % 2, :n]; ek = f"ev{nev % 2}"; nev += 1
                k.act(e_, ps[:, :n], AF.Identity, [pk, "modT"], [ek], scale=modT[:, 40 + j, which:which + 1])
                k.tt(xb[:, j, :n], xb[:, j, :n], e_, ALU.add, [xk + f"_{j}", ek], [xk + f"_{j}"], eng="gpsimd")
            S.dma("sync", xov[:, :, t0:t0 + n], xb[:, :, :n], reads=xkeys)
        S.emit()


def e_inputs(l, core, x_cur, ctx_cur, mixT_core, modT_core, inp):
    b, hh = core_bh(core)
    xT = np.concatenate([x_cur[b, hh * NLAT:(hh + 1) * NLAT, :].T, ctx_cur[b, hh * NCTX:(hh + 1) * NCTX, :].T], axis=1)
    return {"xT": np.ascontiguousarray(xT, dtype=np.float32), "mixT": np.ascontiguousarray(mixT_core, dtype=np.float32),
            "modT": modT_core, "norm2_w": pcol(inp["norm2_w"][l]), "w_out": inp["w_out"][l],
            "w_gate": inp["ffn_w_gate"][l], "w_up": inp["ffn_w_up"][l], "w_down": inp["ffn_w_down"][l]}


def band_mask():
    s = np.arange(128)[:, None]
    t = np.arange(384)[None, :]
    return ((t - s >= 0) & (t - s <= 256)).astype(np.float32)


def emit_att(nc, bank, pfx, io, dbg=None):
    qT_d, kT_d, sink_d, mask_d, out_d = io["qT"], io["kT"], io["sink"], io["mask"], io["attT"]
    NB = SEQ // 128
    with ExitStack() as es:
        S = Sched(nc, es, bank, pfx)
        k = K(S)
        kT = S.sb("kT", [64, NTOK], BF16); S.dma("gpsimd", kT[:], kT_d, writes=["kT"])
        qT = S.sb("qT", [64, 3, NTOK], BF16)
        S.dma("gpsimd", qT[:], qT_d.rearrange("(h d) t -> d h t", d=64), writes=["qT"])
        V = S.sb("V", [128, 34, 64], BF16)
        S.dma("gpsimd", V[:, 0:NB, :], io["v_lat"].rearrange("(j p) d -> p j d", p=128), writes=["V"])
        S.dma("gpsimd", V[:, NB:NB + 2, :], io["v_ctx"].rearrange("(j p) d -> p j d", p=128), writes=["V"])
        ones64 = S.sb("ones64", [128, 64], BF16); k.memset("gpsimd", ones64[:], 1.0, ["ones64"])
        mask = S.sb("mask", [128, 384], BF16); S.dma("gpsimd", mask[:], mask_d, writes=["mask"])
        sink = S.sb("sink", [64, 3]); S.dma("sync", sink[:], sink_d, writes=["sink"])
        esink = S.sb("esink", [64, 3]); k.act(esink[:], sink[:], AF.Exp, ["sink"], ["esink"])
        P = [S.sb(f"P{r}", [128, 3, 384], BF16) for r in range(4)]
        Pe = [S.sb(f"Pe{r}", [128, 384], BF16) for r in range(2)]
        Pc = [S.sb(f"Pc{r}", [128, 384], BF16) for r in range(4)]
        psc = [S.ps(f"psc{i}", [128, 512]) for i in range(3)]
        po = [S.ps(f"po{i}", [128, 512]) for i in range(2)]
        pd = [S.ps(f"pd{i}", [128, 512]) for i in range(2)]
        den = S.sb("den", [64, 384]); rden = S.sb("rden", [64, 384])
        ost = [S.sb(f"ost{i}", [64, 384]) for i in range(2)]
        outv = out_d.rearrange("(h d) t -> d h t", d=64)
        cnt = {"sc": 0, "pe": 0, "pv": 0, "pc": 0}

        def local_scores(j):
            b0 = max(j - 1, 0); b1 = min(j + 1, NB - 1)
            q0 = b0 * 128; ln = (b1 - b0 + 1) * 128; mc0 = (b0 - (j - 1)) * 128
            Pj = P[j % 4]; pk_ = f"P{j % 4}"
            for h in range(3):
                ps = psc[cnt["sc"] % 3]; sk = f"psc{cnt['sc'] % 3}"; cnt["sc"] += 1
                k.mm(ps[:, :ln], kT[:, j * 128:(j + 1) * 128], qT[:, h, q0:q0 + ln], True, True, ["kT", "qT"], [sk])
                pe = Pe[cnt["pe"] % 2]; ek = f"Pe{cnt['pe'] % 2}"; cnt["pe"] += 1
                k.act(pe[:, :ln], ps[:, :ln], AF.Exp, [sk], [ek], scale=0.125)
                k.tt(Pj[:, h, mc0:mc0 + ln], pe[:, :ln], mask[:, mc0:mc0 + ln], ALU.mult, [ek, "mask"], [pk_ + f"_{h}"], eng="gpsimd")

        def ctx_scores(qtok0):
            res = []
            for c in range(2):
                ps = psc[cnt["sc"] % 3]; sk = f"psc{cnt['sc'] % 3}"; cnt["sc"] += 1
                k.mm(ps[:, :384].rearrange("p (h t) -> p h t", h=3), kT[:, SEQ + c * 128:SEQ + (c + 1) * 128],
                     qT[:, :, qtok0:qtok0 + 128], True, True, ["kT", "qT"], [sk])
                pc = Pc[cnt["pc"] % 4]; ck = f"Pc{cnt['pc'] % 4}"; cnt["pc"] += 1
                k.act(pc[:, :], ps[:, :384], AF.Exp, [sk], [ck], scale=0.125)
                res.append((pc[:, :].rearrange("p (h t) -> p h t", h=3), NB + c, [ck]))
            return res

        def pv(qtok0, terms):
            i_ = cnt["pv"]; cnt["pv"] += 1
            o_ = po[i_ % 2]; ok = f"po{i_ % 2}"; d_ = pd[i_ % 2]; dk = f"pd{i_ % 2}"
            ov = o_[:64, :384].rearrange("p (h t) -> p h t", h=3)
            dv = d_[:64, :384].rearrange("p (h t) -> p h t", h=3)
            nt = len(terms)
            for ti, (pap, vb, keys) in enumerate(terms):
                k.mm(ov, V[:, vb, :], pap, ti == 0, ti == nt - 1, ["V"] + keys, [ok])
            for ti, (pap, vb, keys) in enumerate(terms):
                k.mm(dv, ones64[:], pap, ti == 0, ti == nt - 1, ["ones64"] + keys, [dk])
            for h in range(3):
                k.ts(den[:, h * 128:(h + 1) * 128], d_[:64, h * 128:(h + 1) * 128], esink[:, h:h + 1], None, ALU.add, None,
                     [dk, "esink"], ["den"])
            k.recip(rden[:], den[:], ["den"], ["rden"])
            st = ost[i_ % 2]; sk = f"ost{i_ % 2}"
            k.tt(st[:], o_[:64, :384], rden[:], ALU.mult, [ok, "rden"], [sk])
            S.dma("sync", outv[:, :, qtok0:qtok0 + 128], st[:].rearrange("p (h t) -> p h t", h=3), reads=[sk])

        def local_terms(i):
            terms = []
            for j, c0 in ((i - 1, 256), (i, 128), (i + 1, 0)):
                if 0 <= j < NB:
                    terms.append((P[j % 4][:, :, c0:c0 + 128], j, [f"P{j % 4}_{h}" for h in range(3)]))
            return terms

        nblk = NB if dbg is None else 3
        for j in range(nblk):
            local_scores(j)
            if j >= 1:
                i = j - 1
                pv(i * 128, local_terms(i) + ctx_scores(i * 128))
        if dbg is None:
            i = NB - 1
            pv(i * 128, local_terms(i) + ctx_scores(i * 128))
            for cq in range(2):
                pv(SEQ + cq * 128, ctx_scores(SEQ + cq * 128))
        S.emit()


def att_inputs(l, core, fm_all, tm_all, inp, consts):
    b, g = core_bh(core)
    f0, f1 = fm_all[2 * b], fm_all[2 * b + 1]
    t0, t1 = tm_all[2 * b], tm_all[2 * b + 1]

    def seq_fm(rows):
        return np.concatenate([f0[rows, :NLAT], f1[rows, :NLAT], f0[rows, NLAT:], f1[rows, NLAT:]], axis=1)

    def seq_tm(cols):
        return np.concatenate([t0[:NLAT, cols], t1[:NLAT, cols], t0[NLAT:, cols], t1[NLAT:, cols]], axis=0)

    return {"qT": np.ascontiguousarray(seq_fm(slice(192 * g, 192 * g + 192))),
            "kT": np.ascontiguousarray(seq_fm(slice(384 + 64 * g, 384 + 64 * g + 64))),
            "v": np.ascontiguousarray(seq_tm(slice(64 * g, 64 * g + 64))),
            "sink": np.ascontiguousarray(np.tile(inp["attn_sink"][l][3 * g:3 * g + 3][None, :], (64, 1)).astype(np.float32)),
            "mask": consts["mask"]}


NCH = NTOK // 128


def ml_consts():
    sel = np.zeros((34, 4, 128), np.float32)
    for i, r in enumerate((0, 1, 32, 33)):
        sel[r, i, :] = 1.0
    ident = np.eye(128, dtype=np.float32)
    s = np.arange(128)[:, None]; t = np.arange(128)[None, :]
    return {"sel": sel.reshape(34, 512), "ident": ident, "maskf": (s <= t).astype(np.float32), "maskb": (s >= t).astype(np.float32)}


def proc_chunk(dr, c):
    if dr == 0:
        return 32 + c if c < 2 else c - 2
    return 33 - c if c < 2 else 31 - (c - 2)


def emit_ml(nc, bank, pfx, io, dbg=None):
    qT_d, kT_d, nw_d, sel_d, id_d, mf_d, mb_d, out_d, scr = (io["qT"], io["kT"], io["nw_bc"], io["sel"], io["ident"],
                                                            io["maskf"], io["maskb"], io["mloT"], io["scr"])
    g4 = io["g4"]
    with ExitStack() as es:
        S = Sched(nc, es, bank, pfx)
        k = K(S)
        qT = S.sb("qT", [128, NTOK], BF16); S.dma("gpsimd", qT[:], qT_d, writes=["qT"])
        kT = S.sb("kT", [128, NTOK], BF16); S.dma("gpsimd", kT[:], kT_d, writes=["kT"])
        ktok = S.sb("ktok", [128, NCH, 128], BF16)
        S.dma("gpsimd", ktok[:, 0:32, :], io["ktok"][0].rearrange("(c p) d -> p c d", p=128), writes=["ktok"])
        S.dma("gpsimd", ktok[:, 32:34, :], io["ktok"][1].rearrange("(c p) d -> p c d", p=128), writes=["ktok"])
        Vaug = S.sb("Vaug", [128, NCH, 2, 65], BF16)
        k.memset("gpsimd", Vaug[:], 1.0, ["Vaug"])
        for h_ in range(2):
            S.dma("gpsimd", Vaug[:, 0:32, h_, 0:64], io["vtok"][0][:, h_ * 64:(h_ + 1) * 64].rearrange("(c p) d -> p c d", p=128), writes=["Vaug"])
            S.dma("gpsimd", Vaug[:, 32:34, h_, 0:64], io["vtok"][1][:, h_ * 64:(h_ + 1) * 64].rearrange("(c p) d -> p c d", p=128), writes=["Vaug"])
        otok = S.sb("otok", [128, NCH, 128])
        S.dma("sync", otok[:, 0:32, :], io["otok"][0].rearrange("(c p) d -> p c d", p=128), writes=["otok"])
        S.dma("sync", otok[:, 32:34, :], io["otok"][1].rearrange("(c p) d -> p c d", p=128), writes=["otok"])
        nwb = S.sb("nwb", [128, 128]); S.dma("sync", nwb[:], nw_d, writes=["nwb"])
        sel = S.sb("sel", [34, 512]); S.dma("sync", sel[:], sel_d, writes=["sel"])
        ident = S.sb("ident", [128, 128]); S.dma("sync", ident[:], id_d, writes=["ident"])
        masks = []
        for nm, d_ in (("maskf", mf_d), ("maskb", mb_d)):
            m_ = S.sb(nm, [128, 128], BF16); S.dma("gpsimd", m_[:], d_, writes=[nm]); masks.append(m_)
        LI = S.sb("LI", [34, NTOK]); FF = S.sb("FF", [34, NTOK]); TM = S.sb("TM", [34, NTOK]); BN = S.sb("BN", [34, NTOK])
        for t_, nm in ((LI, "LI"), (FF, "FF"), (TM, "TM")):
            k.memset("gpsimd", t_[:], 0.0, [nm])
        S.dma("sync", LI[0:2, 0:CTX], g4[0][:, SEQ:NTOK], writes=["LI"]); S.dma("sync", LI[0:2, CTX:NTOK], g4[0][:, 0:SEQ], writes=["LI"], reads=["LI"])
        S.dma("sync", FF[0:2, 0:CTX], g4[1][:, SEQ:NTOK], writes=["FF"]); S.dma("sync", FF[0:2, CTX:NTOK], g4[1][:, 0:SEQ], writes=["FF"], reads=["FF"])

        def rev_rows(t_):
            return bass.AP(t_, 32 * NTOK + NTOK - 1, [[NTOK, 2], [-1, NTOK]])

        def revc_rows(t_):
            return bass.AP(t_, 32 * NTOK + 127, [[NTOK, 2], [128, NCH], [-1, 128]])

        S.dma("sync", TM[32:34, :], g4[2], writes=["TM"], reads=["TM"])
        k.copy("vector", LI[32:34, :], rev_rows(TM), ["TM", "LI"], ["LI"])
        S.dma("sync", TM[32:34, :], g4[3], writes=["TM"], reads=["TM"])
        k.copy("vector", FF[32:34, :], rev_rows(TM), ["TM", "FF"], ["FF"])
        k.act(FF[:], FF[:], AF.Exp, ["FF"], ["FF"], scale=-1.0)
        k.act(FF[:], FF[:], AF.Ln, ["FF"], ["FF"], bias=1.0, scale=1.0)
        S.op("vector", lambda e: e.tensor_tensor_scan(out=BN[:], data0=FF[:], data1=FF[:], initial=0.0, op0=ALU.add, op1=ALU.bypass),
             reads=["FF"], writes=["BN"])
        k.tt(LI[:], LI[:], BN[:], ALU.add, ["LI", "BN"], ["LI"])
        S.op("vector", lambda e: e.tensor_tensor_scan(out=TM[:], data0=LI[:], data1=LI[:], initial=0.0, op0=ALU.max, op1=ALU.bypass),
             reads=["LI", "TM"], writes=["TM"])
        MC = S.sb("MC", [34, NCH]); MP = S.sb("MP", [34, NCH]); DEC = S.sb("DEC", [34, NCH])
        k.copy("vector", MC[:], TM[:, 127::128], ["TM"], ["MC"])
        k.memset("vector", MP[:], 0.0, ["MP"])
        k.copy("vector", MP[:, 1:NCH], MC[:, 0:NCH - 1], ["MC", "MP"], ["MP"])
        k.tt(DEC[:], MP[:], MC[:], ALU.subtract, ["MP", "MC"], ["DEC"])
        k.act(DEC[:], DEC[:], AF.Exp, ["DEC"], ["DEC"])
        mcb = MC[:, :].unsqueeze(2).broadcast_to([34, NCH, 128])
        li3 = LI[:, :].rearrange("p (c s) -> p c s", s=128); bn3 = BN[:, :].rearrange("p (c s) -> p c s", s=128)
        k.tt(li3, li3, mcb, ALU.subtract, ["LI", "MC"], ["LI"])
        k.act(LI[:], LI[:], AF.Exp, ["LI"], ["LI"])
        k.tt(bn3, bn3, mcb, ALU.subtract, ["BN", "MC"], ["BN"])
        k.act(BN[:], BN[:], AF.Exp, ["BN"], ["BN"])
        COL = S.sb("COL", [128, 2, 4, NCH])
        ptr = S.ps("ptr", [128, 512])
        T68 = S.sb("T68", [68, 128])
        for qi, (src, nm) in enumerate(((LI, "LI"), (BN, "BN"))):
            k.copy("vector", FF[32:34, :].rearrange("p (c s) -> p c s", s=128), revc_rows(src), [nm, "FF"], ["FF"])
            S.dma("sync", scr[qi, 0:2, :], src[0:2, :], reads=[nm], writes=[f"scr{qi}"])
            S.dma("sync", scr[qi, 2:4, :], FF[32:34, :], reads=["FF", f"scr{qi}"], writes=[f"scr{qi}"])
            for dr in range(2):
                S.dma("sync", T68[:], scr[qi, 2 * dr:2 * dr + 2, :].rearrange("r (c s) -> (r c) s", s=128), reads=[f"scr{qi}"], writes=["T68"])
                S.op("tensor", lambda e: e.transpose(ptr[:, 0:68], T68[:], ident[0:68, 0:68]), reads=["T68", "ident"], writes=["ptr"])
                k.copy("vector", COL[:, qi, 2 * dr:2 * dr + 2, :], ptr[:, 0:68].rearrange("p (r c) -> p r c", r=2), ["ptr"], ["COL"])
        DECB = S.sb("DECB", [128, 4, NCH])
        for i in range(4):
            k.mm(ptr[:, 0:NCH], sel[:, i * 128:(i + 1) * 128], DEC[:], True, True, ["sel", "DEC", "ptr"], ["ptr"])
            k.copy("vector", DECB[:, i, :], ptr[:, 0:NCH], ["ptr"], ["DECB"])
        Cst = S.sb("Cst", [128, 4, 65]); Cd = S.sb("Cd", [128, 4, 65]); Cdb = S.sb("Cdb", [128, 4, 65], BF16)
        k.memset("vector", Cd[:], 0.0, ["Cd0", "Cd1", "Cd2", "Cd3"])
        k.memset("vector", Cdb[:], 0.0, ["Cdb0", "Cdb1", "Cdb2", "Cdb3"])
        HH = [S.sb("HF", [128, NCH, 128]), S.sb("HB", [128, NCH, 128])]
        pA = [S.ps(f"pA{i}", [128, 512]) for i in range(2)]
        pN = [S.ps(f"pN{i}", [128, 512]) for i in range(2)]
        pC = [S.ps(f"pC{i}", [128, 512]) for i in range(2)]
        pt1 = [S.sb(f"pt1_{i}", [128, 128], BF16) for i in range(2)]
        PT = [S.sb(f"PT{i}", [128, 128], BF16) for i in range(2)]
        VU = [S.sb(f"VU{i}", [128, 65], BF16) for i in range(2)]
        dcol = S.sb("dcol", [128, 2]); rcol = S.sb("rcol", [128, 2]); dabs = S.sb("dabs", [128, 2])
        n = 0
        nproc = NCH if dbg is None else 4
        for c in range(nproc):
            for i in range(4):
                dr, h = i // 2, i % 2
                ncn = proc_chunk(dr, c)
                tok0 = ncn * 128
                hs = slice(h * 64, (h + 1) * 64)
                r = n % 2; n += 1
                qc = qT[hs, tok0:tok0 + 128]; kc = kT[hs, tok0:tok0 + 128]
                k.mm(pA[r][:, 0:128], kc, qc, True, True, ["kT", "qT"], [f"pA{r}"])
                k.act(pt1[r][:], pA[r][:, 0:128], AF.Identity, [f"pA{r}", "COL"], [f"pt1_{r}"], scale=COL[:, 0, i, c:c + 1])
                k.tt(PT[r][:], pt1[r][:], masks[dr][:], ALU.mult, [f"pt1_{r}", "maskf", "maskb"], [f"PT{r}"], eng="gpsimd")
                k.act(VU[r][:], Vaug[:, ncn, h, :], AF.Identity, ["Vaug", "COL"], [f"VU{r}"], scale=COL[:, 0, i, c:c + 1])
                k.mm(pN[r][:, 0:65], PT[r][:], Vaug[:, ncn, h, :], True, False, [f"PT{r}", "Vaug"], [f"pN{r}"])
                k.mm(pN[r][:, 0:65], qc, Cdb[hs, i, :], False, True, ["qT", f"Cdb{i}"], [f"pN{r}"])
                k.act(dabs[:, r:r + 1], pN[r][:, 64:65], AF.Abs, [f"pN{r}"], [f"dabs{r}"])
                k.tt(dcol[:, r:r + 1], dabs[:, r:r + 1], COL[:, 1, i, c:c + 1], ALU.max, [f"dabs{r}", "COL"], [f"dcol{r}"])
                k.recip(rcol[:, r:r + 1], dcol[:, r:r + 1], [f"dcol{r}"], [f"rcol{r}"])
                k.act(HH[dr][:, ncn, hs], pN[r][:, 0:64], AF.Identity, [f"pN{r}", f"rcol{r}"], [f"H{dr}_{ncn}_{h}"], scale=rcol[:, r:r + 1])
                k.mm(pC[r][:, 0:65], ktok[:, ncn, :], VU[r][:], True, True, ["ktok", f"VU{r}"], [f"pC{r}"])
                k.tt(Cst[hs, i, :], Cd[hs, i, :], pC[r][hs, 0:65], ALU.add, [f"Cd{i}", f"pC{r}"], [f"Cst{i}"])
                if c + 1 < nproc:
                    k.ts(Cd[hs, i, :], Cst[hs, i, :], DECB[hs, i, c + 1:c + 2], None, ALU.mult, None, [f"Cst{i}", "DECB"], [f"Cd{i}"])
                    k.copy("scalar", Cdb[hs, i, :], Cd[hs, i, :], [f"Cd{i}"], [f"Cdb{i}"])
        hsum = [S.sb(f"hsum{i}", [128, 128]) for i in range(2)]
        hsq = S.sb("hsq", [128, 128]); ssq = S.sb("ssq", [128, 2]); rsq = S.sb("rsq", [128, 2]); rin = S.sb("rin", [128, 2])
        hn = S.sb("hn", [128, 128]); sgo = S.sb("sgo", [128, 128])
        ost = [S.sb(f"ost{i}", [128, 128]) for i in range(2)]
        ostT = [S.sb(f"ostT{i}", [128, 128]) for i in range(2)]
        chunks = range(NCH) if dbg is None else [32, 33, 0, 1]
        for j, ncn in enumerate(chunks):
            r = j % 2
            hk = [f"H{dr}_{ncn}_{h}" for dr in range(2) for h in range(2)]
            k.tt(hsum[r][:], HH[0][:, ncn, :], HH[1][:, ncn, :], ALU.add, hk, [f"hsum{r}"], eng="gpsimd")
            k.tt(hsq[:], hsum[r][:], hsum[r][:], ALU.mult, [f"hsum{r}"], ["hsq"], eng="gpsimd")
            S.op("vector", lambda e, a=hsq, b=ssq: e.tensor_reduce(out=b[:], in_=a[:].rearrange("p (h d) -> p h d", h=2), axis=AX.X, op=ALU.add),
                 reads=["hsq"], writes=["ssq"])
            k.act(rsq[:], ssq[:], AF.Sqrt, ["ssq"], ["rsq"], bias=EPS, scale=1.0 / 64)
            k.recip(rin[:], rsq[:], ["rsq"], ["rin"])
            for h in range(2):
                k.act(hn[:, h * 64:(h + 1) * 64], hsum[r][:, h * 64:(h + 1) * 64], AF.Identity, [f"hsum{r}", "rin"], [f"hn{h}"], scale=rin[:, h:h + 1])
            k.act(sgo[:], otok[:, ncn, :], AF.Sigmoid, ["otok"], ["sgo"])
            k.tt(hn[:], hn[:], nwb[:], ALU.mult, ["hn0", "hn1", "nwb"], ["hn0", "hn1"])
            k.tt(ost[r][:], hn[:], sgo[:], ALU.mult, ["hn0", "hn1", "sgo"], [f"ost{r}"])
            S.op("tensor", lambda e, r=r: e.transpose(ptr[:, 0:128], ost[r][:], ident[:]), reads=[f"ost{r}", "ident", "ptr"], writes=["ptr"])
            k.copy("scalar", ostT[r][:], ptr[:, 0:128], ["ptr"], [f"ostT{r}"])
            S.dma("sync", out_d[:, ncn * 128:(ncn + 1) * 128], ostT[r][:], reads=[f"ostT{r}"])
        S.emit()


def seq_fm(fm_all, b, rows):
    f0, f1 = fm_all[2 * b], fm_all[2 * b + 1]
    return np.ascontiguousarray(np.concatenate([f0[rows, :NLAT], f1[rows, :NLAT], f0[rows, NLAT:], f1[rows, NLAT:]], axis=1))


def seq_tm(tm_all, b, cols):
    t0, t1 = tm_all[2 * b], tm_all[2 * b + 1]
    return np.ascontiguousarray(np.concatenate([t0[:NLAT, cols], t1[:NLAT, cols], t0[NLAT:, cols], t1[NLAT:, cols]], axis=0))


def ml_inputs(l, core, fm_all, tm_all, inp, consts):
    b, hp = core_bh(core)
    grow = [1024 + 2 * hp, 1024 + 2 * hp + 1, 1028 + 2 * hp, 1028 + 2 * hp + 1, 1032 + 2 * hp, 1032 + 2 * hp + 1, 1036 + 2 * hp, 1036 + 2 * hp + 1]
    d = {"qT": seq_fm(fm_all, b, slice(512 + 128 * hp, 512 + 128 * hp + 128)),
         "kT": seq_fm(fm_all, b, slice(768 + 128 * hp, 768 + 128 * hp + 128)),
         "ktok": seq_tm(tm_all, b, slice(128 + 128 * hp, 128 + 128 * hp + 128)),
         "vtok": seq_tm(tm_all, b, slice(384 + 128 * hp, 384 + 128 * hp + 128)),
         "otok": seq_tm(tm_all, b, slice(640 + 128 * hp, 640 + 128 * hp + 128)),
         "gT": seq_fm(fm_all, b, grow),
         "nw_bc": np.ascontiguousarray(np.tile(inp["ml_norm_w"][l][128 * hp:128 * hp + 128][None, :], (128, 1)).astype(np.float32))}
    d.update(consts["ml"])
    return d


HY_CFG = {"lat": dict(L=SEQ, B=1024, nb=4), "ctx": dict(L=CTX, B=256, nb=1)}


def dft_tables(B):
    t = np.arange(B, dtype=np.float64)[:, None]
    om = np.pi * (2 * np.arange(B, dtype=np.float64)[None, :] + 1) / (2 * B)
    return np.cos(t * om).astype(np.float32), np.sin(t * om).astype(np.float32)


def pos_feats(L):
    t = np.linspace(0.0, 1.0, L, dtype=np.float32)[:, None]
    ang = (np.float32(2.0 * math.pi / L) * np.arange(L, dtype=np.float32))[:, None]
    bands = np.linspace(1e-4, 15, 16, dtype=np.float32)[None, :]
    feats = np.concatenate([t, np.cos(bands * ang), -np.sin(bands * ang)], axis=-1).astype(np.float32)
    return feats, t[:, 0]


def hy_consts():
    c = {}
    for nm, cfg in HY_CFG.items():
        L, B = cfg["L"], cfg["B"]
        TC, TS = dft_tables(B)
        feats, t = pos_feats(L)
        c[nm] = {"TC": TC, "TS": TS, "TCT": np.ascontiguousarray(TC.T), "TST": np.ascontiguousarray(TS.T),
                 "featsT": np.ascontiguousarray(feats.T), "featsTr": np.ascontiguousarray(feats[::-1].T),
                 "negt": np.ascontiguousarray((-t).reshape(L // 128, 128).T), "negtr": np.ascontiguousarray((-t[::-1]).reshape(L // 128, 128).T)}
    alt = np.where(np.arange(128) % 2 == 0, 1.0, -1.0).astype(np.float32).reshape(128, 1)
    c["alt"] = alt
    return c


def emit_F(nc, bank, pfx, io0):
    w1_d, w2_d, fb_d, alt_d = io0["w1"], io0["w2"], io0["fb"], io0["alt"]
    io = io0
    with ExitStack() as es:
        S = Sched(nc, es, bank, pfx)
        k = K(S)
        w1 = S.sb("w1", [33, 64]); S.dma("sync", w1[:], w1_d, writes=["w1"])
        w2 = S.sb("w2", [64, 64]); S.dma("sync", w2[:], w2_d, writes=["w2"])
        w3 = S.sb("w3", [64, 768])
        fb = S.sb("fb", [64, 4]); S.dma("sync", fb[:], fb_d, writes=["fb"])
        fbb = S.sb("fbb", [64, 2])
        k.ts(fbb[:], fb[:, 1:3], fb[:, 0:1], None, ALU.mult, None, ["fb"], ["fbb"])
        adec = S.sb("adec", [128, 768])
        alt = S.sb("alt", [128, 1]); S.dma("sync", alt[:], alt_d, writes=["alt"])
        pz = [S.ps(f"pz{i}", [128, 512]) for i in range(2)]
        pP = [S.ps(f"pP{i}", [128, 512]) for i in range(4)]
        TWO_PI = 2.0 * math.pi
        LM, BM, NBM = SEQ, 1024, 4
        altB = S.sb("altB", [128, 1])
        wm = S.sb("wm", [64, 512])
        negt_s = S.sb("negt", [128, LM // 128]); negtr_s = S.sb("negtr", [128, LM // 128])
        TC_s = S.sb("TC", [128, BM // 128, BM], BF16); TS_s = S.sb("TS", [128, BM // 128, BM], BF16)
        Gt_s = [S.sb(f"Gt{dr}", [128, LM // 128, 384], BF16) for dr in range(2)]
        feats_s = S.sb("feats", [33, LM]); z1_s = S.sb("z1", [64, LM]); z2_s = [S.sb(f"z2_{dr}", [64, LM]) for dr in range(2)]
        arg = S.sb("arg", [64, 512]); fsb = S.sb("fsb", [128, 384]); dct = S.sb("dct", [128, 384])
        XY_s = S.sb("XY", [128, 2 * NBM, 4, 384])
        gst = [S.sb(f"gst{i}", [128, 384]) for i in range(1)]
        for nm, cfg in HY_CFG.items():
            L, B, nb = cfg["L"], cfg["B"], cfg["nb"]
            d = io[nm]
            nt = L // 128; ntb = B // 128
            k.ts(altB[:], alt[:], 1.0 / B, None, ALU.mult, None, ["alt"], ["altB"])
            negt = negt_s[:, 0:nt]; negtr = negtr_s[:, 0:nt]
            S.dma("sync", negt, d["negt"], writes=["negt"]); S.dma("sync", negtr, d["negtr"], writes=["negtr"])
            TC = TC_s[:, 0:ntb, 0:B]; TS = TS_s[:, 0:ntb, 0:B]
            S.dma("gpsimd", TC, d["TC"].rearrange("(a p) k -> p a k", p=128), writes=["TC"])
            S.dma("gpsimd", TS, d["TS"].rearrange("(a p) k -> p a k", p=128), writes=["TS"])
            Gt = [Gt_s[dr][:, 0:nt, :] for dr in range(2)]
            feats = feats_s[:, 0:L]; z1 = z1_s[:, 0:L]
            XY = XY_s[:, 0:2 * nb]
            for dr in range(2):
                z2 = z2_s[dr][:, 0:L]
                S.dma("sync", feats, d["featsr" if dr else "feats"], writes=["feats"])
                for si, (src, w_, bcol, dst, K_) in enumerate(((feats, w1, 0, z1, 33), (z1, w2, 1, z2, 64))):
                    for c0 in range(0, L, 512):
                        n = min(512, L - c0)
                        ps = pz[(c0 // 512) % 2]; pk = f"pz{(c0 // 512) % 2}"
                        k.mm(ps[:64, :n], w_[:K_, :], src[:K_, c0:c0 + n], True, True, ["w1", "w2", "feats", "z1"], [pk])
                        k.act(arg[:, :n], ps[:64, :n], AF.Identity, [pk, "fb", "fbb"], ["arg"], scale=fb[:, 0:1], bias=fbb[:, bcol:bcol + 1])
                        for _ in range(2):
                            for (cmp_, val, sh) in ((ALU.is_gt, math.pi, -TWO_PI), (ALU.is_lt, -math.pi, TWO_PI)):
                                k.ts(wm[:, :n], arg[:, :n], val, None, cmp_, None, ["arg"], ["wm"])
                                k.stt(arg[:, :n], wm[:, :n], sh, arg[:, :n], ALU.mult, ALU.add, ["wm", "arg"], ["arg"])
                        k.act(dst[:, c0:c0 + n], arg[:, :n], AF.Sin, ["arg"], ["z1" if si == 0 else f"z2_{dr}"])
            for hh in range(len(io0["w3c"])):
                S.dma("sync", w3[:], io0["w3c"][hh], writes=["w3"])
                S.dma("sync", adec[:], io0["decay_bc"][hh], writes=["adec"])
                k.act(adec[:], adec[:], AF.Abs, ["adec"], ["adec"])
                for dr in range(2):
                    z2 = z2_s[dr][:, 0:L]
                    tcol = negtr if dr else negt
                    for mt in range(nt):
                        ps = pz[mt % 2]; pk = f"pz{mt % 2}"
                        k.mm(ps[:, :384], z2[:, mt * 128:(mt + 1) * 128], w3[:, dr * 384:(dr + 1) * 384], True, True, [f"z2_{dr}", "w3"], [pk])
                        k.act(dct[:], adec[:, dr * 384:(dr + 1) * 384], AF.Exp, ["adec", "negt", "negtr"], ["dct"], scale=tcol[:, mt:mt + 1])
                        k.copy("scalar", fsb[:], ps[:, :384], [pk], ["fsb"])
                        k.tt(Gt[dr][:, mt, :], fsb[:], dct[:], ALU.mult, ["fsb", "dct"], [f"Gt{dr}_{mt // ntb}"], eng="gpsimd")
                npp = 0; ng = 0
                for kt in range(ntb):
                    for ei in range(2 * nb):
                        src = Gt[1] if ei < nb else Gt[0]
                        base = (ei if ei < nb else ei - nb) * ntb
                        bk = f"Gt{1 if ei < nb else 0}_{ei if ei < nb else ei - nb}"
                        for ti, (T_, tk) in enumerate(((TC, "TC"), (TS, "TS"))):
                            ps = pP[npp % 4]; pk = f"pP{npp % 4}"; npp += 1
                            for tt in range(ntb):
                                k.mm(ps[:, :384], T_[:, tt, kt * 128:(kt + 1) * 128], src[:, base + tt, :], tt == 0, tt == ntb - 1, [tk, bk], [pk])
                            k.act(XY[:, ei, ti, :], ps[:, :384], AF.Identity, [pk], [f"XY{ei}_{ti}"], scale=1.0 / B)
                            k.act(XY[:, ei, 2 + ti, :], ps[:, :384], AF.Identity, [pk, "altB"], [f"XY{ei}_{2 + ti}"], scale=altB[:, 0:1])
                    for dd in range(-(nb - 1), nb):
                        ei = dd + nb; q = (nb - 1) - dd
                        for rj in range(2):
                            g_ = gst[0]; gk = "gst0"; ng += 1
                            if rj == 0:
                                k.tt(g_[:], XY[:, ei, 0, :], XY[:, ei - 1, 3, :], ALU.add, [f"XY{ei}_0", f"XY{ei - 1}_3"], [gk], eng="gpsimd")
                            else:
                                k.tt(g_[:], XY[:, ei, 1, :], XY[:, ei - 1, 2, :], ALU.subtract, [f"XY{ei}_1", f"XY{ei - 1}_2"], [gk], eng="gpsimd")
                            S.dma("sync", d["G"][hh][:, kt, rj, :, q, :].rearrange("o p c -> p o c"), g_[:].rearrange("p (o c) -> p o c", o=2), reads=[gk])
        S.emit()


def f_inputs(core, inp, consts):
    l, hh = core // 2, core % 2
    cols = []
    for dr in range(2):
        for o in range(2):
            c0 = dr * 768 + o * 384 + hh * 192
            cols.extend(range(c0, c0 + 192))
    cols = np.array(cols)
    hc = consts["hy"]
    d = {"w1": inp["hy_w1"][l], "w2": inp["hy_w2"][l], "w3c": np.ascontiguousarray(inp["hy_w3"][l][:, cols]),
         "fb": np.ascontiguousarray(np.stack([inp["hy_freq"][l], inp["hy_b1"][l], inp["hy_b2"][l], inp["hy_b2"][l]], axis=1)),
         "decay_bc": np.ascontiguousarray(np.tile(inp["hy_decay"][l][cols][None, :], (128, 1))), "alt": hc["alt"]}
    for nm in HY_CFG:
        d[f"featsT_{nm}"] = hc[nm]["featsT"]; d[f"featsTr_{nm}"] = hc[nm]["featsTr"]
        d[f"negt_{nm}"] = hc[nm]["negt"]; d[f"negtr_{nm}"] = hc[nm]["negtr"]
        d[f"TC_{nm}"] = hc[nm]["TC"]; d[f"TS_{nm}"] = hc[nm]["TS"]
    return d


def emit_hy(nc, bank, pfx, io, dbg=None):
    cw_d, sk_d, Gd, Td, out_d, id_d = io["cw_bc"], io["sk_bc"], io["G"], io["T"], io["hyoT"], io["ident"]
    with ExitStack() as es:
        S = Sched(nc, es, bank, pfx)
        k = K(S)
        cw = S.sb("cw", [128, 4, 576]); S.dma("sync", cw[:].rearrange("p a c -> p (a c)"), cw_d, writes=["cw"])
        sk = S.sb("sk", [128, 2, 192]); S.dma("sync", sk[:].rearrange("p a c -> p (a c)"), sk_d, writes=["sk"])
        VXX = S.sb("VXX", [128, NCH, 576], BF16)
        Z = S.sb("Z", [128, NCH, 192], BF16)
        tabs = {}
        for nm, cfg in HY_CFG.items():
            B = cfg["B"]; ntb = B // 128
            tabs[nm] = []
            for ti, tname in enumerate(("TC", "TS", "TCT", "TST")):
                t_ = S.sb(f"{tname}_{nm}", [128, ntb, B], BF16)
                S.dma("gpsimd", t_[:], Td[nm][ti].rearrange("(a p) k -> p a k", p=128), writes=[f"{tname}_{nm}"])
                tabs[nm].append((t_, f"{tname}_{nm}"))
        stg = [S.sb(f"stg{i}", [128, 3, 576]) for i in range(1)]
        pr = [S.sb(f"pr{i}", [128, 4, 192]) for i in range(2)]
        pb = [S.sb(f"pb{i}", [128, 4, 192], BF16) for i in range(4)]
        ct0 = pr[0][:, :, :].rearrange("p a c -> p (a c)")[:, 0:576]; ct1 = pr[1][:, :, :].rearrange("p a c -> p (a c)")[:, 0:576]
        tile_base = {"lat": 0, "ctx": SEQ // 128}
        for nm, cfg in HY_CFG.items():
            L = cfg["L"]
            for tt in range(L // 128):
                g = tile_base[nm] + tt
                s_ = stg[0]; skey = "stg0"
                tmt, pitch = io["tmT"]
                for part in range(3):
                    src = bass.AP(tmt, (io["row0"][nm] + tt * 128) * pitch + io["col0"] + part * 384, [[pitch, 128], [pitch, 3], [1, 192]])
                    S.dma("sync", s_[:, :, part * 192:(part + 1) * 192], src, writes=[skey])
                k.tt(ct0, s_[:, 0, :], cw[:, 0, :], ALU.mult, [skey, "cw"], ["pr0"])
                k.tt(ct1, s_[:, 1, :], cw[:, 1, :], ALU.mult, [skey, "cw"], ["pr1"], eng="gpsimd")
                k.tt(ct0, ct0, ct1, ALU.add, ["pr0", "pr1"], ["pr0"])
                k.tt(ct1, s_[:, 2, :], cw[:, 2, :], ALU.mult, [skey, "cw"], ["pr1"], eng="gpsimd")
                k.tt(ct0, ct0, ct1, ALU.add, ["pr0", "pr1"], ["pr0"])
                k.tt(VXX[:, g, :], ct0, cw[:, 3, :], ALU.add, ["pr0", "cw"], [f"VXX{g}"])
        psR = [S.ps(f"psR{i}", [128, 512]) for i in range(2)]
        psJ = [S.ps(f"psJ{i}", [128, 512]) for i in range(2)]
        pI = [S.ps(f"pI{i}", [128, 512]) for i in range(2)]
        NBM = 4
        RJ = [S.sb(f"RJ{i}", [128, 2, NBM, 192]) for i in range(2)]
        Gb = [S.sb(f"Gb{i}", [128, 2, 2 * NBM - 1, 192]) for i in range(1)]
        YAB = S.sb("YAB", [128, 8, 2, NBM, 192], BF16)
        et = [S.sb(f"et{i}", [128, 192]) for i in range(2)]
        ost = [S.sb(f"ost{i}", [128, 192]) for i in range(2)]
        oT = S.sb("oT", [128, 128]); ptp = S.ps("ptp", [128, 512])
        ident = S.sb("ident", [128, 128]); S.dma("sync", ident[:], id_d, writes=["ident"])
        identb = S.sb("identb", [128, 128], BF16); nidentb = S.sb("nidentb", [128, 128], BF16)
        k.act(identb[:], ident[:], AF.Identity, ["ident"], ["identb"], scale=1.0)
        k.act(nidentb[:], ident[:], AF.Identity, ["ident"], ["identb"], scale=-1.0)
        psY = S.ps("psY", [128, 512])
        cnt = {"f": 0, "g": 0, "i": 0, "e": 0}

        def conv(nm, o, src_fn, src_keys, epi):
            cfg = HY_CFG[nm]; B, nb = cfg["B"], cfg["nb"]; ntb = B // 128; nq = 2 * nb - 1
            base = tile_base[nm]
            (TC, kTC), (TS, kTS), (TCT, kTCT), (TST, kTST) = tabs[nm]
            for kt in range(ntb):
                rj = RJ[kt % 2]; rk = f"RJ{kt % 2}"
                for jp in range(0, nb, 2):
                    nj = min(2, nb - jp)
                    a_ = cnt["f"] % 2; cnt["f"] += 1
                    pR, pJ = psR[a_], psJ[a_]
                    for jj in range(nj):
                        for tt in range(ntb):
                            g = base + (jp + jj) * ntb + tt
                            k.mm(pR[:, jj * 192:(jj + 1) * 192], TC[:, tt, kt * 128:(kt + 1) * 128], src_fn(g), tt == 0, tt == ntb - 1,
                                 [kTC] + src_keys(g), [f"psR{a_}"], inc=(tt == ntb - 1 and jj == nj - 1))
                    for jj in range(nj):
                        for tt in range(ntb):
                            g = base + (jp + jj) * ntb + tt
                            k.mm(pJ[:, jj * 192:(jj + 1) * 192], TS[:, tt, kt * 128:(kt + 1) * 128], src_fn(g), tt == 0, tt == ntb - 1,
                                 [kTS] + src_keys(g), [f"psJ{a_}"], inc=(tt == ntb - 1 and jj == nj - 1))
                    k.copy("scalar", rj[:, 0, jp:jp + nj, :], pR[:, 0:nj * 192].rearrange("p (j c) -> p j c", j=nj), [f"psR{a_}"], [rk + f"_0_{jp}"])
                    k.copy("scalar", rj[:, 1, jp:jp + nj, :], pJ[:, 0:nj * 192].rearrange("p (j c) -> p j c", j=nj), [f"psJ{a_}"], [rk + f"_1_{jp}"])
                rkeys = [rk + f"_{a}_{jp}" for a in range(2) for jp in range(0, nb, 2)]
                gb = Gb[0]; gk = "Gb0"; cnt["g"] += 1
                for a in range(2):
                    S.dma("sync", gb[:, a, 0:nq, :], Gd[nm][o, kt, a], writes=[gk + f"_{a}"])
                gkeys = [gk + "_0", gk + "_1"]
                for i in range(nb):
                    qs = nb - 1 - i
                    Rv = rj[:, 0, 0:nb, :]; Jv = rj[:, 1, 0:nb, :]; GRs = gb[:, 0, qs:qs + nb, :]; GJs = gb[:, 1, qs:qs + nb, :]
                    b0, b1, b2, b3 = [p_[:, 0:nb, :] for p_ in pb]
                    k.tt(b0, Rv, GRs, ALU.mult, rkeys + gkeys, ["pb0"])
                    k.tt(b1, Jv, GJs, ALU.mult, rkeys + gkeys, ["pb1"], eng="gpsimd")
                    k.tt(b2, Rv, GJs, ALU.mult, rkeys + gkeys, ["pb2"], eng="gpsimd")
                    k.tt(b3, Jv, GRs, ALU.mult, rkeys + gkeys, ["pb3"])
                    terms = [(0, identb, "pb0"), (1, nidentb, "pb1")]
                    for half, tl in ((0, [(pb[0], identb, "pb0"), (pb[1], nidentb, "pb1")]), (1, [(pb[2], identb, "pb2"), (pb[3], identb, "pb3")])):
                        nmm = 2 * nb; cmm = 0
                        for (pp, idm, pk_) in tl:
                            for j in range(nb):
                                k.mm(psY[:, half * 192:(half + 1) * 192], idm[:], pp[:, j, :], cmm == 0, cmm == nmm - 1, [pk_, "identb"], ["psY"],
                                     inc=(cmm == nmm - 1))
                                cmm += 1
                    for a in range(2):
                        k.copy("scalar", YAB[:, kt, a, i, :], psY[:, a * 192:(a + 1) * 192], ["psY"], [f"YAB{kt}_{a}_{i}"])
            for pt in range(ntb):
                for ip in range(0, nb, 2):
                    ni = min(2, nb - ip)
                    a_ = cnt["i"] % 2; cnt["i"] += 1
                    ps = pI[a_]; pk = f"pI{a_}"
                    for ii in range(ni):
                        for kt in range(ntb):
                            last = (kt == ntb - 1 and ii == ni - 1)
                            k.mm(ps[:, ii * 192:(ii + 1) * 192], TCT[:, kt, pt * 128:(pt + 1) * 128], YAB[:, kt, 0, ip + ii, :], kt == 0, False,
                                 [kTCT, f"YAB{kt}_0_{ip + ii}"], [pk], inc=False)
                            k.mm(ps[:, ii * 192:(ii + 1) * 192], TST[:, kt, pt * 128:(pt + 1) * 128], YAB[:, kt, 1, ip + ii, :], False, kt == ntb - 1,
                                 [kTST, f"YAB{kt}_1_{ip + ii}"], [pk], inc=last)
                    for ii in range(ni):
                        g = base + (ip + ii) * ntb + pt
                        epi(g, ps[:, ii * 192:(ii + 1) * 192], pk)

        def epi1(g, y, pk):
            e_ = et[cnt["e"] % 2]; ek = f"et{cnt['e'] % 2}"; cnt["e"] += 1
            k.tt(e_[:], VXX[:, g, 0:192], sk[:, 0, :], ALU.mult, [f"VXX{g}", "sk"], [ek], eng="gpsimd")
            k.tt(e_[:], y, e_[:], ALU.add, [pk, ek], [ek])
            k.tt(Z[:, g, :], e_[:], VXX[:, g, 192:384], ALU.mult, [ek, f"VXX{g}"], [f"Z{g}"], eng="gpsimd")

        def epi2(g, y, pk):
            e_ = et[cnt["e"] % 2]; ek = f"et{cnt['e'] % 2}"; cnt["e"] += 1
            o_ = ost[cnt["e"] % 2]; ok = f"ost{cnt['e'] % 2}"
            k.tt(e_[:], Z[:, g, :], sk[:, 1, :], ALU.mult, [f"Z{g}", "sk"], [ek], eng="gpsimd")
            k.tt(e_[:], y, e_[:], ALU.add, [pk, ek], [ek])
            k.tt(o_[:], e_[:], VXX[:, g, 384:576], ALU.mult, [ek, f"VXX{g}"], [ok], eng="gpsimd")
            for (c0_, cn) in ((0, 128), (128, 64)):
                S.op("tensor", lambda e, o_=o_, c0_=c0_, cn=cn: e.transpose(ptp[:cn, 0:128], o_[:, c0_:c0_ + cn], ident[:]),
                     reads=[ok, "ident", "ptp"], writes=["ptp"])
                k.copy("scalar", oT[:cn, :], ptp[:cn, 0:128], ["ptp"], ["oT"])
                S.dma("sync", out_d[c0_:c0_ + cn, g * 128:(g + 1) * 128], oT[:cn, :], reads=["oT"])

        for nm in (("ctx",) if dbg == "ctx" else ("lat", "ctx")):
            conv(nm, 0, lambda g: VXX[:, g, 0:192], lambda g: [f"VXX{g}"], epi1)
            conv(nm, 1, lambda g: Z[:, g, :], lambda g: [f"Z{g}"], epi2)
        S.emit()


def hy_inputs(l, core, tm_all, G_core, inp, consts):
    b, hh = core_bh(core)
    cols = np.concatenate([896 + part * 384 + hh * 192 + np.arange(192) for part in range(3)])
    hy = seq_tm(tm_all, b, cols)
    z = np.zeros((1, 576), np.float32)
    ccols = np.concatenate([part * 384 + hh * 192 + np.arange(192) for part in range(3)])
    cwb = np.concatenate([inp["hy_conv_w"][l][:, ccols].reshape(-1), inp["hy_conv_b"][l][ccols]])
    d = {"hyp_lat": np.ascontiguousarray(np.concatenate([z, hy[:SEQ], z], 0)),
         "hyp_ctx": np.ascontiguousarray(np.concatenate([z, hy[SEQ:], z], 0)),
         "cw_bc": np.ascontiguousarray(np.tile(cwb[None, :], (128, 1)).astype(np.float32)),
         "sk_bc": np.ascontiguousarray(np.tile(inp["hy_skip"][l][:, hh * 192:(hh + 1) * 192].reshape(1, -1), (128, 1)).astype(np.float32))}
    for nm in HY_CFG:
        d[f"G_{nm}"] = G_core[nm]
        for t in ("TC", "TS", "TCT", "TST"):
            d[f"{t}_{nm}"] = consts["hy"][nm][t]
    return d


NROW = SEQ + CTX + 4
LAT0, CTX0 = 1, SEQ + 3


def ext_specs():
    sp = {"xT_in": [2, D, NT], "sc": [128, 16], "w_mod": [DEPTH, D, 6 * D], "b_mod_p": [DEPTH, 128, 48],
          "n1_p": [DEPTH, 128, 8], "n2_p": [DEPTH, 128, 8], "w_in": [DEPTH, D, P_IN], "qkn": [DEPTH, 128, 2],
          "gate_b": [DEPTH, 16, 1], "cosT": [2, 128, NT], "sinT": [2, 128, NT], "rm2": [128, 128], "blk1": [128, 128],
          "w_out": [DEPTH, D, D], "w_gate": [DEPTH, D, D_FF], "w_up": [DEPTH, D, D_FF], "w_down": [DEPTH, D_FF, D],
          "sink": [DEPTH, 2, 64, 3], "mask": [128, 384], "nw_bc": [DEPTH, 2, 128, 128],
          "sel": [34, 512], "ident": [128, 128], "maskf": [128, 128], "maskb": [128, 128],
          "cw_bc": [DEPTH, 2, 128, 4 * 576], "sk_bc": [DEPTH, 2, 128, 384],
          "f_w1": [DEPTH, 33, 64], "f_w2": [DEPTH, 64, 64], "f_w3c": [DEPTH, 2, 64, 768], "f_fb": [DEPTH, 64, 4],
          "f_dec": [DEPTH, 2, 128, 768], "alt": [128, 1]}
    for nm, cfg in HY_CFG.items():
        L, B = cfg["L"], cfg["B"]
        for t in ("TC", "TS", "TCT", "TST"):
            sp[f"{t}_{nm}"] = [B, B]
        sp[f"featsT_{nm}"] = [33, L]; sp[f"featsTr_{nm}"] = [33, L]
        sp[f"negt_{nm}"] = [128, L // 128]; sp[f"negtr_{nm}"] = [128, L // 128]
    return sp


def build_fused(depth=DEPTH):
    nc = bass.Bass("TRN2", target_bir_lowering=False)
    X = {n: dram_in(nc, n, shp) for n, shp in ext_specs().items()}
    OUT = dram_out(nc, "xT_out", [2, D, NT])
    FMS = nc.dram_tensor("FMS", [1424, NTOK], F32).ap()
    TMT = nc.dram_tensor("TMSP", [NROW, 2048], F32)
    TMS = TMT.ap()
    MIXS = nc.dram_tensor("MIXS", [D, NTOK], F32).ap()
    MODT = nc.dram_tensor("MODT", [128, 96], F32).ap()
    XTS = nc.dram_tensor("XTS", [2, D, NT], F32).ap()
    GS = {nm: nc.dram_tensor(f"GS_{nm}", [DEPTH, 2, 2, cfg["B"] // 128, 2, 128, 2 * cfg["nb"] - 1, 192], F32).ap()
          for nm, cfg in HY_CFG.items()}
    MLSCR = nc.dram_tensor("ml_scr", [2, 4, NTOK], F32).ap()
    import os
    PH = os.environ.get("FPH", "F,A,att,ml,hy,E").split(",")
    with ExitStack() as es0:
        bank = SemBank(nc, es0, nsets=1)
        with ExitStack() as es:
            S = Sched(nc, es, bank, "init_")
            z = S.sb("z", [4, 2048])
            S.op("vector", lambda e: e.memset(z[:], 0.0), writes=["z"])
            for i_, row in enumerate((0, SEQ + 1, SEQ + 2, NROW - 1)):
                S.dma("sync", TMS[row:row + 1, :], z[i_:i_ + 1, :], reads=["z"])
            S.emit()
        for l in range(depth):
            io = {"w1": X["f_w1"][l], "w2": X["f_w2"][l], "w3c": [X["f_w3c"][l, hh] for hh in range(2)], "fb": X["f_fb"][l],
                  "decay_bc": [X["f_dec"][l, hh] for hh in range(2)], "alt": X["alt"]}
            for nm in HY_CFG:
                io[nm] = dict(feats=X[f"featsT_{nm}"], featsr=X[f"featsTr_{nm}"], negt=X[f"negt_{nm}"], negtr=X[f"negtr_{nm}"],
                              TC=X[f"TC_{nm}"], TS=X[f"TS_{nm}"], G=[GS[nm][l, hh] for hh in range(2)])
            if "F" in PH:
                emit_F(nc, bank, f"F{l}_", io)
        for l in range(depth):
            xsrc = X["xT_in"] if l == 0 else XTS
            xdst = OUT if l == depth - 1 else XTS
            gcols = [(lambda t, u=u: u * NLAT + t if t < NLAT else SEQ + u * NCTX + (t - NLAT)) for u in range(2)]
            grows = [(lambda t, u=u: LAT0 + u * NLAT + t if t < NLAT else CTX0 + u * NCTX + (t - NLAT)) for u in range(2)]
            io = {"xT": [xsrc[0], xsrc[1]], "sc": X["sc"], "w_mod": X["w_mod"][l], "b_mod": X["b_mod_p"][l], "norm1_w": X["n1_p"][l],
                  "w_in": X["w_in"][l], "qkn": X["qkn"][l], "gate_b": X["gate_b"][l], "cosT": X["cosT"], "sinT": X["sinT"],
                  "rm2": X["rm2"], "blk1": X["blk1"], "modT": MODT, "fm": FMS, "tm": TMS}
            if "A" in PH:
                emit_A(nc, bank, f"A{l}_", io, gcols, grows)
            for g in range(2):
                io = {"qT": FMS[192 * g:192 * g + 192, :], "kT": FMS[384 + 64 * g:384 + 64 * g + 64, :],
                      "v_lat": TMS[LAT0:LAT0 + SEQ, 64 * g:64 * g + 64], "v_ctx": TMS[CTX0:CTX0 + CTX, 64 * g:64 * g + 64],
                      "sink": X["sink"][l, g], "mask": X["mask"], "attT": MIXS[192 * g:192 * g + 192, :]}
                if "att" in PH:
                    emit_att(nc, bank, f"T{l}{g}_", io)
            for hp in range(2):
                def tmp_(c0):
                    return (TMS[LAT0:LAT0 + SEQ, c0:c0 + 128], TMS[CTX0:CTX0 + CTX, c0:c0 + 128])
                io = {"qT": FMS[512 + 128 * hp:512 + 128 * hp + 128, :], "kT": FMS[768 + 128 * hp:768 + 128 * hp + 128, :],
                      "ktok": tmp_(128 + 128 * hp), "vtok": tmp_(384 + 128 * hp), "otok": tmp_(640 + 128 * hp),
                      "g4": [FMS[1024 + 4 * q + 2 * hp:1024 + 4 * q + 2 * hp + 2, :] for q in range(4)],
                      "nw_bc": X["nw_bc"][l, hp], "sel": X["sel"], "ident": X["ident"], "maskf": X["maskf"], "maskb": X["maskb"],
                      "mloT": MIXS[768 + 128 * hp:768 + 128 * hp + 128, :], "scr": MLSCR}
                if "ml" in PH:
                    emit_ml(nc, bank, f"L{l}{hp}_", io)
            for hh in range(2):
                io = {"tmT": (TMT, 2048), "row0": {"lat": LAT0 - 1, "ctx": CTX0 - 1}, "col0": 896 + 192 * hh,
                      "cw_bc": X["cw_bc"][l, hh], "sk_bc": X["sk_bc"][l, hh], "ident": X["ident"],
                      "G": {nm: GS[nm][l, hh] for nm in HY_CFG},
                      "T": {nm: [X[f"{t}_{nm}"] for t in ("TC", "TS", "TCT", "TST")] for nm in HY_CFG},
                      "hyoT": MIXS[384 + 192 * hh:384 + 192 * hh + 192, :]}
                if "hy" in PH:
                    emit_hy(nc, bank, f"H{l}{hh}_", io)
            io = {"xT": [xsrc[0], xsrc[1]], "mix": MIXS, "modT": MODT, "norm2_w": X["n2_p"][l], "w_out": X["w_out"][l],
                  "w_gate": X["w_gate"][l], "w_up": X["w_up"][l], "w_down": X["w_down"][l], "xT_out": [xdst[0], xdst[1]]}
            if "E" in PH:
                emit_E(nc, bank, f"E{l}_", io, gcols)
    return nc


def fused_inputs(b, inp, consts):
    cos, sin = consts["rope"]
    d = {}
    xT = np.empty((2, D, NT), np.float32)
    cosT = np.ones((2, 128, NT), np.float32)
    sinT = np.zeros((2, 128, NT), np.float32)
    for u in range(2):
        xT[u, :, :NLAT] = inp["x"][b, u * NLAT:(u + 1) * NLAT, :].T
        xT[u, :, NLAT:] = inp["ctx"][b, u * NCTX:(u + 1) * NCTX, :].T
        for hd in range(2):
            cosT[u, 64 * hd:64 * hd + 64, :NLAT] = cos[:, u * NLAT:(u + 1) * NLAT]
            sinT[u, 64 * hd:64 * hd + 64, :NLAT] = sin[:, u * NLAT:(u + 1) * NLAT]
    d["xT_in"] = xT; d["cosT"] = cosT; d["sinT"] = sinT
    d["sc"] = pcol(np.stack([inp["c"][b], inp["c_ctx"]], axis=1)).reshape(128, 16)
    for k_ in ("w_mod", "w_in", "w_out"):
        d[k_] = inp[k_]
    d["w_gate"], d["w_up"], d["w_down"] = inp["ffn_w_gate"], inp["ffn_w_up"], inp["ffn_w_down"]
    d["b_mod_p"] = np.stack([pcol(inp["b_mod"][l]) for l in range(DEPTH)])
    d["n1_p"] = np.stack([pcol(inp["norm1_w"][l]) for l in range(DEPTH)])
    d["n2_p"] = np.stack([pcol(inp["norm2_w"][l]) for l in range(DEPTH)])
    d["qkn"] = np.stack([np.stack([np.tile(inp["q_norm_w"][l], 2), np.tile(inp["k_norm_w"][l], 2)], axis=1) for l in range(DEPTH)])
    d["gate_b"] = inp["ml_gate_b"].reshape(DEPTH, 16, 1)
    d["rm2"], d["blk1"], d["mask"] = consts["rm2"], consts["blk1"], consts["mask"]
    d["sink"] = np.stack([np.stack([np.tile(inp["attn_sink"][l][3 * g:3 * g + 3][None, :], (64, 1)) for g in range(2)]) for l in range(DEPTH)])
    d["nw_bc"] = np.stack([np.stack([np.tile(inp["ml_norm_w"][l][128 * hp:128 * hp + 128][None, :], (128, 1)) for hp in range(2)]) for l in range(DEPTH)])
    d.update(consts["ml"])
    cw = np.empty((DEPTH, 2, 128, 4 * 576), np.float32); sk = np.empty((DEPTH, 2, 128, 384), np.float32)
    w3c = np.empty((DEPTH, 2, 64, 768), np.float32); dec = np.empty((DEPTH, 2, 128, 768), np.float32)
    for l in range(DEPTH):
        for hh in range(2):
            ccols = np.concatenate([part * 384 + hh * 192 + np.arange(192) for part in range(3)])
            cw[l, hh] = np.concatenate([inp["hy_conv_w"][l][:, ccols].reshape(-1), inp["hy_conv_b"][l][ccols]])[None, :]
            sk[l, hh] = inp["hy_skip"][l][:, hh * 192:(hh + 1) * 192].reshape(1, -1)
            cols = np.concatenate([dr * 768 + o * 384 + hh * 192 + np.arange(192) for dr in range(2) for o in range(2)])
            w3c[l, hh] = inp["hy_w3"][l][:, cols]
            dec[l, hh] = inp["hy_decay"][l][cols][None, :]
    d["cw_bc"], d["sk_bc"], d["f_w3c"], d["f_dec"] = cw, sk, w3c, dec
    d["f_w1"], d["f_w2"] = inp["hy_w1"], inp["hy_w2"]
    d["f_fb"] = np.stack([np.stack([inp["hy_freq"][l], inp["hy_b1"][l], inp["hy_b2"][l], inp["hy_b2"][l]], axis=1) for l in range(DEPTH)])
    hc = consts["hy"]
    d["alt"] = hc["alt"]
    for nm in HY_CFG:
        for t in ("TC", "TS", "TCT", "TST", "featsT", "featsTr", "negt", "negtr"):
            d[f"{t}_{nm}"] = hc[nm][t]
    sp = ext_specs()
    return {k_: np.ascontiguousarray(np.asarray(v, np.float32).reshape(sp[k_])) for k_, v in d.items()}


def kernel(**inputs):
    inp = {k_: np.asarray(v, dtype=np.float32) for k_, v in inputs.items()}
    consts = {"rope": rope_tables(), "mask": band_mask(), "ml": ml_consts(), "hy": hy_consts()}
    consts["rm2"], consts["blk1"] = rope_consts()
    nc = build_fused()
    maps = [fused_inputs(c // 2, inp, consts) for c in range(8)]
    res = run_bass_kernel_spmd(nc, maps, core_ids=list(range(8))).results
    out = np.empty((4, SEQ, D), np.float32)
    for b in range(4):
        xo = res[2 * b]["xT_out"]
        for u in range(2):
            out[b, u * NLAT:(u + 1) * NLAT, :] = xo[u][:, :NLAT].T
    return out
```

```python
from contextlib import ExitStack
import math
import numpy as np
import ml_dtypes
import concourse.bass as bass
import concourse.mybir as mybir
from concourse.bass_utils import run_bass_kernel_spmd

F32 = mybir.dt.float32
BF16 = mybir.dt.bfloat16
ALU = mybir.AluOpType
AF = mybir.ActivationFunctionType
AX = mybir.AxisListType

D = 1024
SEQ = 4096
CTX = 256
DEPTH = 4
NLAT = SEQ // 2
NCTX = CTX // 2
NT = NLAT + NCTX
NTOK = SEQ + CTX
P_IN = 2832
D_FF = 2816
NFF = D_FF // 128
EPS = 1e-6
O_AQ, O_AK, O_AV, O_HY, O_MQ, O_MK, O_MV, O_MO, O_MG = 0, 384, 512, 640, 1792, 2048, 2304, 2560, 2816

ENGS = ["sync", "scalar", "vector", "gpsimd", "tensor"]
NDS = 8
SES_SKIP = ()


class SemBank:
    def __init__(self, nc, es, nsets=2):
        self.sets = []
        for si in range(nsets):
            self.sets.append({
                "s": {e: es.enter_context(nc.semaphore(f"s{si}_{e}")) for e in ENGS},
                "d": {e: [es.enter_context(nc.semaphore(f"d{si}_{e}{i}")) for i in range(NDS)] for e in ENGS}})
        self.phase = 0


class Sched:
    def __init__(self, nc, es, bank=None, pfx="", same_engine_sync=True):
        self.nc = nc
        self.es = es
        self.pfx = pfx
        self.q = {e: [] for e in ENGS}
        if bank is None:
            bank = SemBank(nc, es, nsets=1)
        cur = bank.sets[bank.phase % len(bank.sets)]
        self.other = bank.sets[(bank.phase + 1) % len(bank.sets)] if len(bank.sets) > 1 else None
        bank.phase += 1
        self.sem = cur["s"]
        self.dsem = cur["d"]
        if len(bank.sets) == 1:
            if not hasattr(bank, "state"):
                bank.state = ({e: 0 for e in ENGS}, {e: [0] * NDS for e in ENGS}, {e: 0 for e in ENGS}, {e: {} for e in ENGS})
            self.cnt, self.dcnt, self.dnext, self.seen = bank.state
        else:
            self.cnt = {e: 0 for e in ENGS}
            self.dcnt = {e: [0] * NDS for e in ENGS}
            self.dnext = {e: 0 for e in ENGS}
            self.seen = {e: {} for e in ENGS}
        self.res = {}
        self.ses = same_engine_sync

    def sb(self, name, shape, dt=F32):
        return self.es.enter_context(self.nc.sbuf_tensor("sb_" + self.pfx + name, list(shape), dt))

    def ps(self, name, shape, dt=F32):
        return self.es.enter_context(self.nc.psum_tensor("ps_" + self.pfx + name, list(shape), dt))

    def _deps(self, eng, reads, writes):
        deps = {}

        def add(tok):
            sem, val, name = tok
            if name not in deps or deps[name][1] < val:
                deps[name] = tok

        for k in reads:
            r = self.res.get(k)
            if r and r[0] is not None:
                add(r[0])
        for k in writes:
            r = self.res.get(k)
            if r:
                if r[0] is not None:
                    add(r[0])
                for t in r[1]:
                    add(t)
        waits = []
        for name, (sem, val, _) in deps.items():
            if name == eng:
                if eng == "tensor" or not self.ses or val > self.cnt[eng] or eng in SES_SKIP:
                    continue
            if self.seen[eng].get(name, 0) < val:
                self.seen[eng][name] = val
                waits.append((sem, val))
        return waits

    def _record(self, tok, reads, writes):
        for k in reads:
            r = self.res.setdefault(k, [None, []])
            r[1].append(tok)
        for k in writes:
            self.res[k] = [tok, []]

    def op(self, eng, fn, reads=(), writes=(), inc=True):
        waits = self._deps(eng, reads, writes)
        if inc:
            self.cnt[eng] += 1
            tok = (self.sem[eng], self.cnt[eng], eng)
        else:
            tok = (self.sem[eng], self.cnt[eng] + 1, eng)
        self.q[eng].append((waits, fn, (self.sem[eng], 1) if inc else None))
        self._record(tok, reads, writes)

    def dma(self, eng, out, in_, reads=(), writes=(), **kw):
        waits = self._deps(eng, reads, writes)
        slot = self.dnext[eng]
        self.dnext[eng] = (slot + 1) % NDS
        name = f"d_{eng}{slot}"
        prev = self.dcnt[eng][slot]
        if prev > 0 and self.seen[eng].get(name, 0) < prev:
            self.seen[eng][name] = prev
            waits.append((self.dsem[eng][slot], prev))
        self.dcnt[eng][slot] = prev + 16
        tok = (self.dsem[eng][slot], prev + 16, name)
        self.q[eng].append((waits, lambda e: e.dma_start(out=out, in_=in_, **kw), (self.dsem[eng][slot], 16)))
        self._record(tok, reads, writes)

    def emit(self):
        for e in ENGS:
            for i in range(NDS):
                v = self.dcnt[e][i]
                if v > 0:
                    self.q["sync"].append(([(self.dsem[e][i], v)], None, None))
        for e in ENGS:
            if e != "sync" and self.cnt[e] > 0:
                self.q["sync"].append(([(self.sem[e], self.cnt[e])], None, None))
        if self.other is not None:
            clr = list(self.other["s"].values()) + [x for l_ in self.other["d"].values() for x in l_]
            self.q["gpsimd"] = [([], (lambda e, sm=sm: e.sem_clear(sm)), None) for sm in clr] + self.q["gpsimd"]
        with self.nc.Block() as block:
            for eng in ENGS:
                if not self.q[eng]:
                    continue

                def body(e, eng=eng):
                    for waits, fn, inc in self.q[eng]:
                        for sem, val in waits:
                            e.wait_ge(sem, val)
                        if fn is not None:
                            ins = fn(e)
                            if inc is not None:
                                ins.then_inc(inc[0], inc[1])

                getattr(block, eng)(body)


class K:
    def __init__(self, S):
        self.S = S

    def act(self, out, in_, func, r, w, bias=None, scale=None, eng="scalar"):
        kw = {}
        if bias is not None:
            kw["bias"] = bias
        if scale is not None:
            kw["scale"] = scale
        self.S.op("scalar", lambda e: e.activation(out=out, in_=in_, func=func, **kw), reads=r, writes=w)

    def tt(self, out, a, b, op, r, w, eng="vector"):
        self.S.op(eng, lambda e: e.tensor_tensor(out=out, in0=a, in1=b, op=op), reads=r, writes=w)

    def ts(self, out, a, s1, s2, op0, op1, r, w, eng="vector"):
        if op1 is None:
            self.S.op(eng, lambda e: e.tensor_scalar(out=out, in0=a, scalar1=s1, scalar2=None, op0=op0), reads=r, writes=w)
        else:
            self.S.op(eng, lambda e: e.tensor_scalar(out=out, in0=a, scalar1=s1, scalar2=s2, op0=op0, op1=op1), reads=r, writes=w)

    def stt(self, out, a, s, b, op0, op1, r, w):
        self.S.op("vector", lambda e: e.scalar_tensor_tensor(out=out, in0=a, scalar=s, in1=b, op0=op0, op1=op1), reads=r, writes=w)

    def copy(self, eng, out, in_, r, w):
        if eng == "scalar":
            self.S.op("scalar", lambda e: e.copy(out=out, in_=in_), reads=r, writes=w)
        else:
            self.S.op(eng, lambda e: e.tensor_copy(out=out, in_=in_), reads=r, writes=w)

    def recip(self, out, in_, r, w):
        self.S.op("vector", lambda e: e.reciprocal(out=out, in_=in_), reads=r, writes=w)

    def mm(self, out, lhsT, rhs, start, stop, r, w, inc=None):
        self.S.op("tensor", lambda e: e.matmul(out, lhsT=lhsT, rhs=rhs, start=start, stop=stop), reads=r, writes=w,
                  inc=stop if inc is None else inc)

    def memset(self, eng, ap, val, w):
        self.S.op(eng, lambda e: e.memset(ap, val), writes=w)


def dram_in(nc, name, shape, dt=F32):
    return nc.dram_tensor(name, list(shape), dt, kind="ExternalInput").ap()


def dram_out(nc, name, shape, dt=F32):
    return nc.dram_tensor(name, list(shape), dt, kind="ExternalOutput").ap()


def bcast_rows(ap1d, nparts):
    return ap1d.partition_broadcast(nparts)


def token_tiles(width):
    tiles = []
    t = 0
    while t < NLAT:
        tiles.append((t, width, 0))
        t += width
    tiles.append((NLAT, NCTX, 1))
    return tiles


def emit_mod_vectors(S, k, sc_d, wmod_d, bmod_d):
    sraw = S.sb("sraw", [128, 8, 2])
    sbf = S.sb("sbf", [128, 8, 2], BF16)
    bm = S.sb("bm", [128, 48])
    modT = S.sb("modT", [128, 48, 2])
    S.dma("sync", sraw[:].rearrange("p j c -> p (j c)"), sc_d, writes=["sraw"])
    S.dma("sync", bm[:], bmod_d, writes=["bm"])
    k.act(sbf[:], sraw[:], AF.Silu, ["sraw"], ["sbf"])
    wv = wmod_d.rearrange("(kc p) n -> p kc n", p=128)
    pm = S.ps("pmod", [128, 512])
    bufs = [S.sb(f"wmodb{i}", [128, 8, 1024], BF16) for i in range(2)]
    for g in range(6):
        wt = bufs[g % 2]
        key = f"wmodb{g % 2}"
        S.dma("gpsimd", wt[:], wv[:, :, g * 1024:(g + 1) * 1024], writes=[key])
        for j in range(8):
            jj = g * 8 + j
            for kc in range(8):
                k.mm(pm[:, 2 * jj:2 * jj + 2], wt[:, kc, j * 128:(j + 1) * 128], sbf[:, kc, :], kc == 0, kc == 7,
                     [key, "sbf"], ["pmod"])
    pv = pm[:, 0:96].rearrange("p (j c) -> p j c", c=2)
    for c in range(2):
        k.tt(modT[:, :, c], pv[:, :, c], bm[:], ALU.add, ["pmod", "bm"], ["modT"])
    return modT


def emit_A(nc, bank, pfx, io, gcols, grows, dbg=None):
    sc_d, wmod_d, bmod_d, n1_d, win_d = io["sc"], io["w_mod"], io["b_mod"], io["norm1_w"], io["w_in"]
    qkn_d, gb_d, cos_d, sin_d, rm_d, bo_d = io["qkn"], io["gate_b"], io["cosT"], io["sinT"], io["rm2"], io["blk1"]
    modT_o, fmT_o, tm_o = io["modT"], io["fm"], io["tm"]
    with ExitStack() as es:
        S = Sched(nc, es, bank, pfx)
        k = K(S)
        modT = emit_mod_vectors(S, k, sc_d, wmod_d, bmod_d)
        S.dma("sync", modT_o, modT[:].rearrange("p j c -> p (j c)"), reads=["modT"])
        n1 = S.sb("n1", [128, 8])
        S.dma("sync", n1[:], n1_d, writes=["n1"])
        qkn = S.sb("qkn", [128, 2]); S.dma("sync", qkn[:], qkn_d, writes=["qkn"])
        gb = S.sb("gb", [16, 1]); S.dma("sync", gb[:], gb_d, writes=["gb"])
        cosT = S.sb("cosT", [128, NT]); sinT = S.sb("sinT", [128, NT])
        rm2 = S.sb("rm2", [128, 128], BF16); S.dma("gpsimd", rm2[:], rm_d, writes=["rm2"])
        blk1 = S.sb("blk1", [128, 128], BF16); S.dma("gpsimd", blk1[:], bo_d, writes=["blk1"])
        ones = S.sb("ones", [128, 128], BF16); k.memset("gpsimd", ones[:], 1.0, ["ones"])
        A1 = S.sb("A1", [128, 8, 2])
        for c in range(2):
            k.stt(A1[:, :, c], modT[:, 8:16, c], 1.0, n1[:], ALU.add, ALU.mult, ["modT", "n1"], ["A1"])
        wv = win_d.rearrange("(kc p) n -> p kc n", p=128)
        fm_cols = [(O_AQ, 384), (O_AK, 128), (O_MQ, 256), (O_MK, 256), (O_MG, 16), (O_HY + 768, 384)]
        Wfm = S.sb("Wfm", [128, 8, 1424], BF16)
        off = 0
        for (c0, n) in fm_cols:
            S.dma("gpsimd", Wfm[:, :, off:off + n], wv[:, :, c0:c0 + n], writes=[f"Wfm{off}"])
            off += n
        tm_cols = [(O_AV, 128), (O_MK, 256), (O_MV, 256), (O_MO, 256), (O_HY, 1152)]
        Wtm = S.sb("Wtm", [128, 8, 2048], BF16)
        off = 0
        for (c0, n) in tm_cols:
            S.dma("gpsimd", Wtm[:, :, off:off + n], wv[:, :, c0:c0 + n], writes=[f"Wtm{off}"])
            off += n
        WFM_KEYS = ["Wfm0", "Wfm384", "Wfm512", "Wfm768", "Wfm1024", "Wfm1040"]
        WTM_KEYS = ["Wtm0", "Wtm128", "Wtm384", "Wtm640", "Wtm896"]
        k.ts(Wfm[:, :, 768:1024], Wfm[:, :, 768:1024], 0.125, None, ALU.mult, None, ["Wfm768"], ["Wfm768"], eng="gpsimd")
        k.ts(Wtm[:, :, 128:384], Wtm[:, :, 128:384], 0.125, None, ALU.mult, None, ["Wtm128"], ["Wtm128"], eng="gpsimd")
        fm_tiles = [(0, 128, "q"), (128, 128, "q"), (256, 128, "q"), (384, 128, "k"),
                    (512, 128, "c"), (640, 128, "c"), (768, 128, "c"), (896, 128, "c"),
                    (1024, 16, "g"), (1040, 128, "c"), (1168, 128, "c"), (1296, 128, "c")]
        xt = [S.sb(f"xt{i}", [128, 8, 512]) for i in range(2)]
        sq = S.sb("sq", [128, 8, 512], BF16)
        tmp = S.sb("tmpn", [128, 2, 512])
        hT = [S.sb(f"hT{i}", [128, 8, 512], BF16) for i in range(2)]
        rs = S.sb("rs", [128, 512]); rstd = S.sb("rstd", [128, 512])
        ss_ps = S.ps("ss_ps", [128, 512])
        pf = [S.ps(f"pf{i}", [128, 512]) for i in range(2)]
        pt = [S.ps(f"pt{i}", [128, 512]) for i in range(2)]
        pq = [S.ps(f"pq{i}", [128, 512]) for i in range(2)]
        stg_f = [S.sb(f"stgf{i}", [128, 512]) for i in range(3)]
        stg_t = [S.sb(f"stgt{i}", [128, 2048]) for i in range(2)]
        qsq = S.sb("qsq", [128, 512], BF16); qw = S.sb("qw", [128, 512], BF16)
        qrs = S.sb("qrs", [128, 512]); qri = S.sb("qri", [128, 512]); qt1 = S.sb("qt1", [128, 512]); qt2 = S.sb("qt2", [128, 512])
        nf = 0; ntm = 0; nst = 0
        tiles_all = [(u_, t0, n, which) for u_ in range(len(io["xT"])) for (t0, n, which) in token_tiles(512)]
        def load_x(tj):
            uj, tj0, nj, _ = tiles_all[tj]
            S.dma("sync", xt[tj % 2][:, :, :nj], io["xT"][uj].rearrange("(j p) t -> p j t", p=128)[:, :, tj0:tj0 + nj], writes=[f"xt{tj % 2}"])

        load_x(0)
        for ti, (u_, t0, n, which) in enumerate(tiles_all):
            if t0 == 0:
                gcol, grow = gcols[u_], grows[u_]
                S.dma("sync", cosT[:], cos_d[u_], writes=["cosT"]); S.dma("sync", sinT[:], sin_d[u_], writes=["sinT"])
            xb = xt[ti % 2]; xk = f"xt{ti % 2}"; hb = hT[ti % 2]; hk = f"hT{ti % 2}"
            if ti + 1 < len(tiles_all):
                load_x(ti + 1)
            k.act(sq[:, :, :n], xb[:, :, :n], AF.Square, [xk], ["sq"])
            for j in range(8):
                k.mm(ss_ps[:, :n], ones[:], sq[:, j, :n], j == 0, j == 7, ["ones", "sq"], ["ss_ps"])
            k.act(rs[:, :n], ss_ps[:, :n], AF.Sqrt, ["ss_ps"], ["rs"], bias=EPS, scale=1.0 / D)
            k.recip(rstd[:, :n], rs[:, :n], ["rs"], ["rstd"])
            for j in range(8):
                k.stt(tmp[:, j % 2, :n], xb[:, j, :n], A1[:, j, which:which + 1], rstd[:, :n], ALU.mult, ALU.mult,
                      [xk, "A1", "rstd"], [f"tmpn{j % 2}"])
                k.act(hb[:, j, :n], tmp[:, j % 2, :n], AF.Identity, [f"tmpn{j % 2}", "modT"], [hk + f"_{j}"],
                      bias=modT[:, j, which:which + 1], scale=1.0)
            hkeys = [hk + f"_{j}" for j in range(8)]
            if dbg == "norm":
                break
            for (c0, M, kind) in fm_tiles:
                if dbg == "fmc" and kind != "c":
                    continue
                if dbg == "fmq" and kind != "q":
                    continue
                if dbg == "fmg" and kind != "g":
                    continue
                ps = pf[nf % 2]; pk = f"pf{nf % 2}"; nf += 1
                for kc in range(8):
                    k.mm(ps[:, :n], Wfm[:, kc, c0:c0 + 128], hb[:, kc, :n], kc == 0, kc == 7, WFM_KEYS + hkeys, [pk])
                st = stg_f[nst % 3]; sk = f"stgf{nst % 3}"; nst += 1
                if kind == "c":
                    k.copy("scalar", st[:M, :n], ps[:M, :n], [pk], [sk])
                elif kind == "g":
                    k.ts(st[:M, :n], ps[:M, :n], gb[:, 0:1], None, ALU.add, None, [pk, "gb"], [sk])
                else:
                    nw = qkn[:, 0:1] if kind == "q" else qkn[:, 1:2]
                    p2 = pq[0]; p3 = pq[1]
                    import os
                    QS = int(os.environ.get("QSTEPS", "99"))
                    steps = [
                        lambda: k.act(qsq[:, :n], ps[:, :n], AF.Square, [pk], ["qsq"]),
                        lambda: k.act(qw[:, :n], ps[:, :n], AF.Identity, [pk, "qkn"], ["qw"], scale=nw),
                        lambda: k.mm(p2[:, :n], blk1[:], qsq[:, :n], True, True, ["blk1", "qsq"], ["pq0"]),
                        lambda: k.mm(p3[:, :n], rm2[:], qw[:, :n], True, True, ["rm2", "qw"], ["pq1"]),
                        lambda: k.act(qrs[:, :n], p2[:, :n], AF.Sqrt, ["pq0"], ["qrs"], bias=EPS, scale=1.0 / 64),
                        lambda: k.recip(qri[:, :n], qrs[:, :n], ["qrs"], ["qri"]),
                        lambda: k.tt(qt1[:, :n], qw[:, :n], cosT[:, t0:t0 + n], ALU.mult, ["qw", "cosT"], ["qt1"]),
                        lambda: k.tt(qt2[:, :n], p3[:, :n], sinT[:, t0:t0 + n], ALU.mult, ["pq1", "sinT"], ["qt2"]),
                        lambda: k.tt(qt1[:, :n], qt1[:, :n], qt2[:, :n], ALU.add, ["qt1", "qt2"], ["qt1"]),
                        lambda: k.tt(st[:, :n], qt1[:, :n], qri[:, :n], ALU.mult, ["qt1", "qri"], [sk]),
                    ]
                    for f_ in steps[:QS]:
                        f_()
                S.dma("sync", fmT_o[c0:c0 + M, gcol(t0):gcol(t0) + n], st[:M, :n], reads=[sk])
            if dbg in ("fm", "fmc", "fmq", "fmg"):
                break
            for s0 in range(0, n, 128):
                st = stg_t[ntm % 2]; sk = f"stgt{ntm % 2}"; ntm += 1
                for g in range(4):
                    ps = pt[(ntm * 4 + g) % 2]; pk = f"pt{(ntm * 4 + g) % 2}"
                    for kc in range(8):
                        k.mm(ps[:, :], hb[:, kc, s0:s0 + 128], Wtm[:, kc, g * 512:(g + 1) * 512], kc == 0, kc == 7,
                             WTM_KEYS + hkeys, [pk])
                    k.copy("scalar" if g % 2 == 0 else "vector", st[:, g * 512:(g + 1) * 512], ps[:, :], [pk], [sk + f"_{g}"])
                S.dma("sync", tm_o[grow(t0 + s0):grow(t0 + s0) + 128, :], st[:], reads=[sk + f"_{g}" for g in range(4)])
        S.emit()


def rope_tables():
    t = np.arange(SEQ)
    row = (t // 64).astype(np.float64)
    col = (t % 64).astype(np.float64)
    nf = 16
    inv = 10000.0 ** (-np.arange(nf, dtype=np.float64) / nf)
    cos = np.zeros((64, SEQ), np.float32)
    sin = np.zeros((64, SEQ), np.float32)
    for a, pos in enumerate((row, col)):
        ang = (pos[None, :].astype(np.float32) * inv[:, None].astype(np.float32)).astype(np.float32)
        for half in range(2):
            cos[a * 32 + half * 16:a * 32 + half * 16 + 16] = np.cos(ang)
            sin[a * 32 + half * 16:a * 32 + half * 16 + 16] = np.sin(ang)
    return cos, sin


def rope_consts():
    rm = np.zeros((64, 64), np.float32)
    for d in range(64):
        if (d % 32) < 16:
            rm[d + 16, d] = -1.0
        else:
            rm[d - 16, d] = 1.0
    rm2 = np.zeros((128, 128), np.float32)
    rm2[:64, :64] = rm
    rm2[64:, 64:] = rm
    blk = np.zeros((128, 128), np.float32)
    blk[:64, :64] = 1.0
    blk[64:, 64:] = 1.0
    return rm2, blk


def pcol(v):
    v = np.asarray(v, np.float32)
    j = v.shape[0] // 128
    return np.ascontiguousarray(np.moveaxis(v.reshape(j, 128, *v.shape[1:]), 0, 1))


def core_bh(core):
    return core // 2, core % 2


def a_inputs(l, core, x_cur, ctx_cur, inp, consts):
    b, hh = core_bh(core)
    cos, sin = consts["rope"]
    xT = np.concatenate([x_cur[b, hh * NLAT:(hh + 1) * NLAT, :].T, ctx_cur[b, hh * NCTX:(hh + 1) * NCTX, :].T], axis=1)
    cosT = np.ones((128, NT), np.float32)
    sinT = np.zeros((128, NT), np.float32)
    cosT[:64, :NLAT] = cos[:, hh * NLAT:(hh + 1) * NLAT]; cosT[64:, :NLAT] = cosT[:64, :NLAT]
    sinT[:64, :NLAT] = sin[:, hh * NLAT:(hh + 1) * NLAT]; sinT[64:, :NLAT] = sinT[:64, :NLAT]
    return {
        "xT": np.ascontiguousarray(xT, dtype=np.float32),
        "sc": pcol(np.stack([inp["c"][b], inp["c_ctx"]], axis=1)).reshape(128, 16),
        "w_mod": inp["w_mod"][l], "b_mod": pcol(inp["b_mod"][l]), "norm1_w": pcol(inp["norm1_w"][l]), "w_in": inp["w_in"][l],
        "qkn": np.ascontiguousarray(np.stack([np.tile(inp["q_norm_w"][l], 2), np.tile(inp["k_norm_w"][l], 2)], axis=1)),
        "gate_b": np.ascontiguousarray(inp["ml_gate_b"][l].reshape(16, 1)),
        "cosT": cosT, "sinT": sinT, "rm2": consts["rm2"], "blk1": consts["blk1"],
    }


def emit_E(nc, bank, pfx, io, gcols, dbg=None):
    mix_d, mod_d, n2_d = io["mix"], io["modT"], io["norm2_w"]
    wo_d, wg_d, wu_d, wd_d = io["w_out"], io["w_gate"], io["w_up"], io["w_down"]
    with ExitStack() as es:
        S = Sched(nc, es, bank, pfx)
        k = K(S)
        modT = S.sb("modT", [128, 48, 2]); S.dma("sync", modT[:].rearrange("p j c -> p (j c)"), mod_d, writes=["modT"])
        n2 = S.sb("n2", [128, 8]); S.dma("sync", n2[:], n2_d, writes=["n2"])
        ones = S.sb("ones", [128, 128], BF16); k.memset("gpsimd", ones[:], 1.0, ["ones"])
        A2 = S.sb("A2", [128, 8, 2])
        for c in range(2):
            k.stt(A2[:, :, c], modT[:, 32:40, c], 1.0, n2[:], ALU.add, ALU.mult, ["modT", "n2"], ["A2"])
        Wo = S.sb("Wo", [128, 8, D], BF16)
        Wg = S.sb("Wg", [128, 8, D_FF], BF16)
        Wu = S.sb("Wu", [128, 8, D_FF], BF16)
        Wd = S.sb("Wd", [128, NFF, D], BF16)
        S.dma("gpsimd", Wo[:], wo_d.rearrange("(kc p) n -> p kc n", p=128), writes=["Wo"])
        wgv = wg_d.rearrange("(kc p) n -> p kc n", p=128)
        wuv = wu_d.rearrange("(kc p) n -> p kc n", p=128)
        wdv = wd_d.rearrange("(m p) n -> p m n", p=128)
        WG_KEYS = []; WU_KEYS = []; WD_KEYS = []
        for h in range(2):
            S.dma("gpsimd", Wg[:, :, h * 1408:(h + 1) * 1408], wgv[:, :, h * 1408:(h + 1) * 1408], writes=[f"Wg{h}"]); WG_KEYS.append(f"Wg{h}")
            S.dma("gpsimd", Wu[:, :, h * 1408:(h + 1) * 1408], wuv[:, :, h * 1408:(h + 1) * 1408], writes=[f"Wu{h}"]); WU_KEYS.append(f"Wu{h}")
            S.dma("gpsimd", Wd[:, h * 11:(h + 1) * 11, :], wdv[:, h * 11:(h + 1) * 11, :], writes=[f"Wd{h}"]); WD_KEYS.append(f"Wd{h}")
        TW = 256
        mv = mix_d.rearrange("(j p) t -> p j t", p=128)
        xt = [S.sb(f"xt{i}", [128, 8, TW]) for i in range(2)]
        mx = [S.sb(f"mx{i}", [128, 8, TW], BF16) for i in range(2)]
        sq = S.sb("sq", [128, 8, TW], BF16)
        tmp = S.sb("tmpn", [128, 2, TW])
        ev = S.sb("ev", [128, 2, TW])
        hT = S.sb("h2T", [128, 8, TW], BF16)
        aT = S.sb("aT", [128, NFF, TW], BF16)
        sg = S.sb("sg", [128, 2, TW], BF16); uc = S.sb("uc", [128, 2, TW], BF16)
        rs = S.sb("rs", [128, TW]); rstd = S.sb("rstd", [128, TW])
        ss_ps = S.ps("ss_ps", [128, 512])
        py = [S.ps(f"py{i}", [128, 512]) for i in range(2)]
        pg = [S.ps(f"pg{i}", [128, 512]) for i in range(2)]
        pu = [S.ps(f"pu{i}", [128, 512]) for i in range(2)]
        ny = 0; nev = 0
        tiles_all = [(u_, t0, n, which) for u_ in range(len(io["xT"])) for (t0, n, which) in token_tiles(TW)]
        def load_xm(tj):
            uj, tj0, nj, _ = tiles_all[tj]
            S.dma("sync", xt[tj % 2][:, :, :nj], io["xT"][uj].rearrange("(j p) t -> p j t", p=128)[:, :, tj0:tj0 + nj],
                  writes=[f"xt{tj % 2}_{j}" for j in range(8)])
            S.dma("gpsimd", mx[tj % 2][:, :, :nj], mv[:, :, gcols[uj](tj0):gcols[uj](tj0) + nj], writes=[f"mx{tj % 2}"])

        load_xm(0)
        for ti, (u_, t0, n, which) in enumerate(tiles_all):
            if t0 == 0:
                xov = io["xT_out"][u_].rearrange("(j p) t -> p j t", p=128)
            xb = xt[ti % 2]; xk = f"xt{ti % 2}"; mb = mx[ti % 2]; mk = f"mx{ti % 2}"
            if ti + 1 < len(tiles_all):
                load_xm(ti + 1)
            for j in range(8):
                ps = py[ny % 2]; pk = f"py{ny % 2}"; ny += 1
                for kc in range(8):
                    k.mm(ps[:, :n], Wo[:, kc, j * 128:(j + 1) * 128], mb[:, kc, :n], kc == 0, kc == 7, ["Wo", mk], [pk])
                e_ = ev[:, nev % 2, :n]; ek = f"ev{nev % 2}"; nev += 1
                k.act(e_, ps[:, :n], AF.Identity, [pk, "modT"], [ek], scale=modT[:, 16 + j, which:which + 1])
                k.tt(xb[:, j, :n], xb[:, j, :n], e_, ALU.add, [xk + f"_{j}", ek], [xk + f"_{j}"], eng="gpsimd")
            xkeys = [xk + f"_{j}" for j in range(8)]
            k.act(sq[:, :, :n], xb[:, :, :n], AF.Square, xkeys, ["sq"])
            for j in range(8):
                k.mm(ss_ps[:, :n], ones[:], sq[:, j, :n], j == 0, j == 7, ["ones", "sq"], ["ss_ps"])
            k.act(rs[:, :n], ss_ps[:, :n], AF.Sqrt, ["ss_ps"], ["rs"], bias=EPS, scale=1.0 / D)
            k.recip(rstd[:, :n], rs[:, :n], ["rs"], ["rstd"])
            for j in range(8):
                k.stt(tmp[:, j % 2, :n], xb[:, j, :n], A2[:, j, which:which + 1], rstd[:, :n], ALU.mult, ALU.mult,
                      [xk + f"_{j}", "A2", "rstd"], [f"tmpn{j % 2}"])
                k.act(hT[:, j, :n], tmp[:, j % 2, :n], AF.Identity, [f"tmpn{j % 2}", "modT"], [f"h2T_{j}"],
                      bias=modT[:, 24 + j, which:which + 1], scale=1.0)
            hkeys = [f"h2T_{j}" for j in range(8)]
            for m in range(NFF):
                g_ = pg[m % 2]; gk = f"pg{m % 2}"; u_ = pu[m % 2]; uk = f"pu{m % 2}"
                for kc in range(8):
                    k.mm(g_[:, :n], Wg[:, kc, m * 128:(m + 1) * 128], hT[:, kc, :n], kc == 0, kc == 7, WG_KEYS + hkeys, [gk])
                for kc in range(8):
                    k.mm(u_[:, :n], Wu[:, kc, m * 128:(m + 1) * 128], hT[:, kc, :n], kc == 0, kc == 7, WU_KEYS + hkeys, [uk])
                k.act(sg[:, m % 2, :n], g_[:, :n], AF.Silu, [gk], [f"sg{m % 2}"])
                k.act(uc[:, m % 2, :n], u_[:, :n], AF.Copy, [uk], [f"uc{m % 2}"])
                k.tt(aT[:, m, :n], sg[:, m % 2, :n], uc[:, m % 2, :n], ALU.mult, [f"sg{m % 2}", f"uc{m % 2}"], [f"aT{m}"])
            akeys = [f"aT{m}" for m in range(NFF)]
            for j in range(8):
                ps = py[ny % 2]; pk = f"py{ny % 2}"; ny += 1
                for m in range(NFF):
                    k.mm(ps[:, :n], Wd[:, m, j * 128:(j + 1) * 128], aT[:, m, :n], m == 0, m == NFF - 1, WD_KEYS + akeys, [pk])
                e_ = ev[:, nev % 2, :n]; ek = f"ev{nev % 2}"; nev += 1
                k.act(e_, ps[:, :n], AF.Identity, [pk, "modT"], [ek], scale=modT[:, 40 + j, which:which + 1])
                k.tt(xb[:, j, :n], xb[:, j, :n], e_, ALU.add, [xk + f"_{j}", ek], [xk + f"_{j}"], eng="gpsimd")
            S.dma("sync", xov[:, :, t0:t0 + n], xb[:, :, :n], reads=xkeys)
        S.emit()


def e_inputs(l, core, x_cur, ctx_cur, mixT_core, modT_core, inp):
    b, hh = core_bh(core)
    xT = np.concatenate([x_cur[b, hh * NLAT:(hh + 1) * NLAT, :].T, ctx_cur[b, hh * NCTX:(hh + 1) * NCTX, :].T], axis=1)
    return {"xT": np.ascontiguousarray(xT, dtype=np.float32), "mixT": np.ascontiguousarray(mixT_core, dtype=np.float32),
            "modT": modT_core, "norm2_w": pcol(inp["norm2_w"][l]), "w_out": inp["w_out"][l],
            "w_gate": inp["ffn_w_gate"][l], "w_up": inp["ffn_w_up"][l], "w_down": inp["ffn_w_down"][l]}


def band_mask():
    s = np.arange(128)[:, None]
    t = np.arange(384)[None, :]
    return ((t - s >= 0) & (t - s <= 256)).astype(np.float32)


def emit_att(nc, bank, pfx, io, dbg=None):
    qT_d, kT_d, sink_d, mask_d, out_d = io["qT"], io["kT"], io["sink"], io["mask"], io["attT"]
    NB = SEQ // 128
    with ExitStack() as es:
        S = Sched(nc, es, bank, pfx)
        k = K(S)
        kT = S.sb("kT", [64, NTOK], BF16); S.dma("gpsimd", kT[:], kT_d, writes=["kT"])
        qT = S.sb("qT", [64, 3, NTOK], BF16)
        S.dma("gpsimd", qT[:], qT_d.rearrange("(h d) t -> d h t", d=64), writes=["qT"])
        V = S.sb("V", [128, 34, 64], BF16)
        S.dma("gpsimd", V[:, 0:NB, :], io["v_lat"].rearrange("(j p) d -> p j d", p=128), writes=["V"])
        S.dma("gpsimd", V[:, NB:NB + 2, :], io["v_ctx"].rearrange("(j p) d -> p j d", p=128), writes=["V"])
        ones64 = S.sb("ones64", [128, 64], BF16); k.memset("gpsimd", ones64[:], 1.0, ["ones64"])
        mask = S.sb("mask", [128, 384], BF16); S.dma("gpsimd", mask[:], mask_d, writes=["mask"])
        sink = S.sb("sink", [64, 3]); S.dma("sync", sink[:], sink_d, writes=["sink"])
        esink = S.sb("esink", [64, 3]); k.act(esink[:], sink[:], AF.Exp, ["sink"], ["esink"])
        P = [S.sb(f"P{r}", [128, 3, 384], BF16) for r in range(4)]
        Pe = [S.sb(f"Pe{r}", [128, 384], BF16) for r in range(2)]
        Pc = [S.sb(f"Pc{r}", [128, 384], BF16) for r in range(4)]
        psc = [S.ps(f"psc{i}", [128, 512]) for i in range(3)]
        po = [S.ps(f"po{i}", [128, 512]) for i in range(2)]
        pd = [S.ps(f"pd{i}", [128, 512]) for i in range(2)]
        den = S.sb("den", [64, 384]); rden = S.sb("rden", [64, 384])
        ost = [S.sb(f"ost{i}", [64, 384]) for i in range(2)]
        outv = out_d.rearrange("(h d) t -> d h t", d=64)
        cnt = {"sc": 0, "pe": 0, "pv": 0, "pc": 0}

        def local_scores(j):
            b0 = max(j - 1, 0); b1 = min(j + 1, NB - 1)
            q0 = b0 * 128; ln = (b1 - b0 + 1) * 128; mc0 = (b0 - (j - 1)) * 128
            Pj = P[j % 4]; pk_ = f"P{j % 4}"
            for h in range(3):
                ps = psc[cnt["sc"] % 3]; sk = f"psc{cnt['sc'] % 3}"; cnt["sc"] += 1
                k.mm(ps[:, :ln], kT[:, j * 128:(j + 1) * 128], qT[:, h, q0:q0 + ln], True, True, ["kT", "qT"], [sk])
                pe = Pe[cnt["pe"] % 2]; ek = f"Pe{cnt['pe'] % 2}"; cnt["pe"] += 1
                k.act(pe[:, :ln], ps[:, :ln], AF.Exp, [sk], [ek], scale=0.125)
                k.tt(Pj[:, h, mc0:mc0 + ln], pe[:, :ln], mask[:, mc0:mc0 + ln], ALU.mult, [ek, "mask"], [pk_ + f"_{h}"], eng="gpsimd")

        def ctx_scores(qtok0):
            res = []
            for c in range(2):
                ps = psc[cnt["sc"] % 3]; sk = f"psc{cnt['sc'] % 3}"; cnt["sc"] += 1
                k.mm(ps[:, :384].rearrange("p (h t) -> p h t", h=3), kT[:, SEQ + c * 128:SEQ + (c + 1) * 128],
                     qT[:, :, qtok0:qtok0 + 128], True, True, ["kT", "qT"], [sk])
                pc = Pc[cnt["pc"] % 4]; ck = f"Pc{cnt['pc'] % 4}"; cnt["pc"] += 1
                k.act(pc[:, :], ps[:, :384], AF.Exp, [sk], [ck], scale=0.125)
                res.append((pc[:, :].rearrange("p (h t) -> p h t", h=3), NB + c, [ck]))
            return res

        def pv(qtok0, terms):
            i_ = cnt["pv"]; cnt["pv"] += 1
            o_ = po[i_ % 2]; ok = f"po{i_ % 2}"; d_ = pd[i_ % 2]; dk = f"pd{i_ % 2}"
            ov = o_[:64, :384].rearrange("p (h t) -> p h t", h=3)
            dv = d_[:64, :384].rearrange("p (h t) -> p h t", h=3)
            nt = len(terms)
            for ti, (pap, vb, keys) in enumerate(terms):
                k.mm(ov, V[:, vb, :], pap, ti == 0, ti == nt - 1, ["V"] + keys, [ok])
            for ti, (pap, vb, keys) in enumerate(terms):
                k.mm(dv, ones64[:], pap, ti == 0, ti == nt - 1, ["ones64"] + keys, [dk])
            for h in range(3):
                k.ts(den[:, h * 128:(h + 1) * 128], d_[:64, h * 128:(h + 1) * 128], esink[:, h:h + 1], None, ALU.add, None,
                     [dk, "esink"], ["den"])
            k.recip(rden[:], den[:], ["den"], ["rden"])
            st = ost[i_ % 2]; sk = f"ost{i_ % 2}"
            k.tt(st[:], o_[:64, :384], rden[:], ALU.mult, [ok, "rden"], [sk])
            S.dma("sync", outv[:, :, qtok0:qtok0 + 128], st[:].rearrange("p (h t) -> p h t", h=3), reads=[sk])

        def local_terms(i):
            terms = []
            for j, c0 in ((i - 1, 256), (i, 128), (i + 1, 0)):
                if 0 <= j < NB:
                    terms.append((P[j % 4][:, :, c0:c0 + 128], j, [f"P{j % 4}_{h}" for h in range(3)]))
            return terms

        nblk = NB if dbg is None else 3
        for j in range(nblk):
            local_scores(j)
            if j >= 1:
                i = j - 1
                pv(i * 128, local_terms(i) + ctx_scores(i * 128))
        if dbg is None:
            i = NB - 1
            pv(i * 128, local_terms(i) + ctx_scores(i * 128))
            for cq in range(2):
                pv(SEQ + cq * 128, ctx_scores(SEQ + cq * 128))
        S.emit()


def att_inputs(l, core, fm_all, tm_all, inp, consts):
    b, g = core_bh(core)
    f0, f1 = fm_all[2 * b], fm_all[2 * b + 1]
    t0, t1 = tm_all[2 * b], tm_all[2 * b + 1]

    def seq_fm(rows):
        return np.concatenate([f0[rows, :NLAT], f1[rows, :NLAT], f0[rows, NLAT:], f1[rows, NLAT:]], axis=1)

    def seq_tm(cols):
        return np.concatenate([t0[:NLAT, cols], t1[:NLAT, cols], t0[NLAT:, cols], t1[NLAT:, cols]], axis=0)

    return {"qT": np.ascontiguousarray(seq_fm(slice(192 * g, 192 * g + 192))),
            "kT": np.ascontiguousarray(seq_fm(slice(384 + 64 * g, 384 + 64 * g + 64))),
            "v": np.ascontiguousarray(seq_tm(slice(64 * g, 64 * g + 64))),
            "sink": np.ascontiguousarray(np.tile(inp["attn_sink"][l][3 * g:3 * g + 3][None, :], (64, 1)).astype(np.float32)),
            "mask": consts["mask"]}


NCH = NTOK // 128


def ml_consts():
    sel = np.zeros((34, 4, 128), np.float32)
    for i, r in enumerate((0, 1, 32, 33)):
        sel[r, i, :] = 1.0
    ident = np.eye(128, dtype=np.float32)
    s = np.arange(128)[:, None]; t = np.arange(128)[None, :]
    return {"sel": sel.reshape(34, 512), "ident": ident, "maskf": (s <= t).astype(np.float32), "maskb": (s >= t).astype(np.float32)}


def proc_chunk(dr, c):
    if dr == 0:
        return 32 + c if c < 2 else c - 2
    return 33 - c if c < 2 else 31 - (c - 2)


def emit_ml(nc, bank, pfx, io, dbg=None):
    qT_d, kT_d, nw_d, sel_d, id_d, mf_d, mb_d, out_d, scr = (io["qT"], io["kT"], io["nw_bc"], io["sel"], io["ident"],
                                                            io["maskf"], io["maskb"], io["mloT"], io["scr"])
    g4 = io["g4"]
    with ExitStack() as es:
        S = Sched(nc, es, bank, pfx)
        k = K(S)
        qT = S.sb("qT", [128, NTOK], BF16); S.dma("gpsimd", qT[:], qT_d, writes=["qT"])
        kT = S.sb("kT", [128, NTOK], BF16); S.dma("gpsimd", kT[:], kT_d, writes=["kT"])
        ktok = S.sb("ktok", [128, NCH, 128], BF16)
        S.dma("gpsimd", ktok[:, 0:32, :], io["ktok"][0].rearrange("(c p) d -> p c d", p=128), writes=["ktok"])
        S.dma("gpsimd", ktok[:, 32:34, :], io["ktok"][1].rearrange("(c p) d -> p c d", p=128), writes=["ktok"])
        Vaug = S.sb("Vaug", [128, NCH, 2, 65], BF16)
        k.memset("gpsimd", Vaug[:], 1.0, ["Vaug"])
        for h_ in range(2):
            S.dma("gpsimd", Vaug[:, 0:32, h_, 0:64], io["vtok"][0][:, h_ * 64:(h_ + 1) * 64].rearrange("(c p) d -> p c d", p=128), writes=["Vaug"])
            S.dma("gpsimd", Vaug[:, 32:34, h_, 0:64], io["vtok"][1][:, h_ * 64:(h_ + 1) * 64].rearrange("(c p) d -> p c d", p=128), writes=["Vaug"])
        otok = S.sb("otok", [128, NCH, 128])
        S.dma("sync", otok[:, 0:32, :], io["otok"][0].rearrange("(c p) d -> p c d", p=128), writes=["otok"])
        S.dma("sync", otok[:, 32:34, :], io["otok"][1].rearrange("(c p) d -> p c d", p=128), writes=["otok"])
        nwb = S.sb("nwb", [128, 128]); S.dma("sync", nwb[:], nw_d, writes=["nwb"])
        sel = S.sb("sel", [34, 512]); S.dma("sync", sel[:], sel_d, writes=["sel"])
        ident = S.sb("ident", [128, 128]); S.dma("sync", ident[:], id_d, writes=["ident"])
        masks = []
        for nm, d_ in (("maskf", mf_d), ("maskb", mb_d)):
            m_ = S.sb(nm, [128, 128], BF16); S.dma("gpsimd", m_[:], d_, writes=[nm]); masks.append(m_)
        LI = S.sb("LI", [34, NTOK]); FF = S.sb("FF", [34, NTOK]); TM = S.sb("TM", [34, NTOK]); BN = S.sb("BN", [34, NTOK])
        for t_, nm in ((LI, "LI"), (FF, "FF"), (TM, "TM")):
            k.memset("gpsimd", t_[:], 0.0, [nm])
        S.dma("sync", LI[0:2, 0:CTX], g4[0][:, SEQ:NTOK], writes=["LI"]); S.dma("sync", LI[0:2, CTX:NTOK], g4[0][:, 0:SEQ], writes=["LI"], reads=["LI"])
        S.dma("sync", FF[0:2, 0:CTX], g4[1][:, SEQ:NTOK], writes=["FF"]); S.dma("sync", FF[0:2, CTX:NTOK], g4[1][:, 0:SEQ], writes=["FF"], reads=["FF"])

        def rev_rows(t_):
            return bass.AP(t_, 32 * NTOK + NTOK - 1, [[NTOK, 2], [-1, NTOK]])

        def revc_rows(t_):
            return bass.AP(t_, 32 * NTOK + 127, [[NTOK, 2], [128, NCH], [-1, 128]])

        S.dma("sync", TM[32:34, :], g4[2], writes=["TM"], reads=["TM"])
        k.copy("vector", LI[32:34, :], rev_rows(TM), ["TM", "LI"], ["LI"])
        S.dma("sync", TM[32:34, :], g4[3], writes=["TM"], reads=["TM"])
        k.copy("vector", FF[32:34, :], rev_rows(TM), ["TM", "FF"], ["FF"])
        k.act(FF[:], FF[:], AF.Exp, ["FF"], ["FF"], scale=-1.0)
        k.act(FF[:], FF[:], AF.Ln, ["FF"], ["FF"], bias=1.0, scale=1.0)
        S.op("vector", lambda e: e.tensor_tensor_scan(out=BN[:], data0=FF[:], data1=FF[:], initial=0.0, op0=ALU.add, op1=ALU.bypass),
             reads=["FF"], writes=["BN"])
        k.tt(LI[:], LI[:], BN[:], ALU.add, ["LI", "BN"], ["LI"])
        S.op("vector", lambda e: e.tensor_tensor_scan(out=TM[:], data0=LI[:], data1=LI[:], initial=0.0, op0=ALU.max, op1=ALU.bypass),
             reads=["LI", "TM"], writes=["TM"])
        MC = S.sb("MC", [34, NCH]); MP = S.sb("MP", [34, NCH]); DEC = S.sb("DEC", [34, NCH])
        k.copy("vector", MC[:], TM[:, 127::128], ["TM"], ["MC"])
        k.memset("vector", MP[:], 0.0, ["MP"])
        k.copy("vector", MP[:, 1:NCH], MC[:, 0:NCH - 1], ["MC", "MP"], ["MP"])
        k.tt(DEC[:], MP[:], MC[:], ALU.subtract, ["MP", "MC"], ["DEC"])
        k.act(DEC[:], DEC[:], AF.Exp, ["DEC"], ["DEC"])
        mcb = MC[:, :].unsqueeze(2).broadcast_to([34, NCH, 128])
        li3 = LI[:, :].rearrange("p (c s) -> p c s", s=128); bn3 = BN[:, :].rearrange("p (c s) -> p c s", s=128)
        k.tt(li3, li3, mcb, ALU.subtract, ["LI", "MC"], ["LI"])
        k.act(LI[:], LI[:], AF.Exp, ["LI"], ["LI"])
        k.tt(bn3, bn3, mcb, ALU.subtract, ["BN", "MC"], ["BN"])
        k.act(BN[:], BN[:], AF.Exp, ["BN"], ["BN"])
        COL = S.sb("COL", [128, 2, 4, NCH])
        ptr = S.ps("ptr", [128, 512])
        T68 = S.sb("T68", [68, 128])
        for qi, (src, nm) in enumerate(((LI, "LI"), (BN, "BN"))):
            k.copy("vector", FF[32:34, :].rearrange("p (c s) -> p c s", s=128), revc_rows(src), [nm, "FF"], ["FF"])
            S.dma("sync", scr[qi, 0:2, :], src[0:2, :], reads=[nm], writes=[f"scr{qi}"])
            S.dma("sync", scr[qi, 2:4, :], FF[32:34, :], reads=["FF", f"scr{qi}"], writes=[f"scr{qi}"])
            for dr in range(2):
                S.dma("sync", T68[:], scr[qi, 2 * dr:2 * dr + 2, :].rearrange("r (c s) -> (r c) s", s=128), reads=[f"scr{qi}"], writes=["T68"])
                S.op("tensor", lambda e: e.transpose(ptr[:, 0:68], T68[:], ident[0:68, 0:68]), reads=["T68", "ident"], writes=["ptr"])
                k.copy("vector", COL[:, qi, 2 * dr:2 * dr + 2, :], ptr[:, 0:68].rearrange("p (r c) -> p r c", r=2), ["ptr"], ["COL"])
        DECB = S.sb("DECB", [128, 4, NCH])
        for i in range(4):
            k.mm(ptr[:, 0:NCH], sel[:, i * 128:(i + 1) * 128], DEC[:], True, True, ["sel", "DEC", "ptr"], ["ptr"])
            k.copy("vector", DECB[:, i, :], ptr[:, 0:NCH], ["ptr"], ["DECB"])
        Cst = S.sb("Cst", [128, 4, 65]); Cd = S.sb("Cd", [128, 4, 65]); Cdb = S.sb("Cdb", [128, 4, 65], BF16)
        k.memset("vector", Cd[:], 0.0, ["Cd0", "Cd1", "Cd2", "Cd3"])
        k.memset("vector", Cdb[:], 0.0, ["Cdb0", "Cdb1", "Cdb2", "Cdb3"])
        HH = [S.sb("HF", [128, NCH, 128]), S.sb("HB", [128, NCH, 128])]
        pA = [S.ps(f"pA{i}", [128, 512]) for i in range(2)]
        pN = [S.ps(f"pN{i}", [128, 512]) for i in range(2)]
        pC = [S.ps(f"pC{i}", [128, 512]) for i in range(2)]
        pt1 = [S.sb(f"pt1_{i}", [128, 128], BF16) for i in range(2)]
        PT = [S.sb(f"PT{i}", [128, 128], BF16) for i in range(2)]
        VU = [S.sb(f"VU{i}", [128, 65], BF16) for i in range(2)]
        dcol = S.sb("dcol", [128, 2]); rcol = S.sb("rcol", [128, 2]); dabs = S.sb("dabs", [128, 2])
        n = 0
        nproc = NCH if dbg is None else 4
        for c in range(nproc):
            for i in range(4):
                dr, h = i // 2, i % 2
                ncn = proc_chunk(dr, c)
                tok0 = ncn * 128
                hs = slice(h * 64, (h + 1) * 64)
                r = n % 2; n += 1
                qc = qT[hs, tok0:tok0 + 128]; kc = kT[hs, tok0:tok0 + 128]
                k.mm(pA[r][:, 0:128], kc, qc, True, True, ["kT", "qT"], [f"pA{r}"])
                k.act(pt1[r][:], pA[r][:, 0:128], AF.Identity, [f"pA{r}", "COL"], [f"pt1_{r}"], scale=COL[:, 0, i, c:c + 1])
                k.tt(PT[r][:], pt1[r][:], masks[dr][:], ALU.mult, [f"pt1_{r}", "maskf", "maskb"], [f"PT{r}"], eng="gpsimd")
                k.act(VU[r][:], Vaug[:, ncn, h, :], AF.Identity, ["Vaug", "COL"], [f"VU{r}"], scale=COL[:, 0, i, c:c + 1])
                k.mm(pN[r][:, 0:65], PT[r][:], Vaug[:, ncn, h, :], True, False, [f"PT{r}", "Vaug"], [f"pN{r}"])
                k.mm(pN[r][:, 0:65], qc, Cdb[hs, i, :], False, True, ["qT", f"Cdb{i}"], [f"pN{r}"])
                k.act(dabs[:, r:r + 1], pN[r][:, 64:65], AF.Abs, [f"pN{r}"], [f"dabs{r}"])
                k.tt(dcol[:, r:r + 1], dabs[:, r:r + 1], COL[:, 1, i, c:c + 1], ALU.max, [f"dabs{r}", "COL"], [f"dcol{r}"])
                k.recip(rcol[:, r:r + 1], dcol[:, r:r + 1], [f"dcol{r}"], [f"rcol{r}"])
                k.act(HH[dr][:, ncn, hs], pN[r][:, 0:64], AF.Identity, [f"pN{r}", f"rcol{r}"], [f"H{dr}_{ncn}_{h}"], scale=rcol[:, r:r + 1])
                k.mm(pC[r][:, 0:65], ktok[:, ncn, :], VU[r][:], True, True, ["ktok", f"VU{r}"], [f"pC{r}"])
                k.tt(Cst[hs, i, :], Cd[hs, i, :], pC[r][hs, 0:65], ALU.add, [f"Cd{i}", f"pC{r}"], [f"Cst{i}"])
                if c + 1 < nproc:
                    k.ts(Cd[hs, i, :], Cst[hs, i, :], DECB[hs, i, c + 1:c + 2], None, ALU.mult, None, [f"Cst{i}", "DECB"], [f"Cd{i}"])
                    k.copy("scalar", Cdb[hs, i, :], Cd[hs, i, :], [f"Cd{i}"], [f"Cdb{i}"])
        hsum = [S.sb(f"hsum{i}", [128, 128]) for i in range(2)]
        hsq = S.sb("hsq", [128, 128]); ssq = S.sb("ssq", [128, 2]); rsq = S.sb("rsq", [128, 2]); rin = S.sb("rin", [128, 2])
        hn = S.sb("hn", [128, 128]); sgo = S.sb("sgo", [128, 128])
        ost = [S.sb(f"ost{i}", [128, 128]) for i in range(2)]
        ostT = [S.sb(f"ostT{i}", [128, 128]) for i in range(2)]
        chunks = range(NCH) if dbg is None else [32, 33, 0, 1]
        for j, ncn in enumerate(chunks):
            r = j % 2
            hk = [f"H{dr}_{ncn}_{h}" for dr in range(2) for h in range(2)]
            k.tt(hsum[r][:], HH[0][:, ncn, :], HH[1][:, ncn, :], ALU.add, hk, [f"hsum{r}"], eng="gpsimd")
            k.tt(hsq[:], hsum[r][:], hsum[r][:], ALU.mult, [f"hsum{r}"], ["hsq"], eng="gpsimd")
            S.op("vector", lambda e, a=hsq, b=ssq: e.tensor_reduce(out=b[:], in_=a[:].rearrange("p (h d) -> p h d", h=2), axis=AX.X, op=ALU.add),
                 reads=["hsq"], writes=["ssq"])
            k.act(rsq[:], ssq[:], AF.Sqrt, ["ssq"], ["rsq"], bias=EPS, scale=1.0 / 64)
            k.recip(rin[:], rsq[:], ["rsq"], ["rin"])
            for h in range(2):
                k.act(hn[:, h * 64:(h + 1) * 64], hsum[r][:, h * 64:(h + 1) * 64], AF.Identity, [f"hsum{r}", "rin"], [f"hn{h}"], scale=rin[:, h:h + 1])
            k.act(sgo[:], otok[:, ncn, :], AF.Sigmoid, ["otok"], ["sgo"])
            k.tt(hn[:], hn[:], nwb[:], ALU.mult, ["hn0", "hn1", "nwb"], ["hn0", "hn1"])
            k.tt(ost[r][:], hn[:], sgo[:], ALU.mult, ["hn0", "hn1", "sgo"], [f"ost{r}"])
            S.op("tensor", lambda e, r=r: e.transpose(ptr[:, 0:128], ost[r][:], ident[:]), reads=[f"ost{r}", "ident", "ptr"], writes=["ptr"])
            k.copy("scalar", ostT[r][:], ptr[:, 0:128], ["ptr"], [f"ostT{r}"])
            S.dma("sync", out_d[:, ncn * 128:(ncn + 1) * 128], ostT[r][:], reads=[f"ostT{r}"])
        S.emit()


def seq_fm(fm_all, b, rows):
    f0, f1 = fm_all[2 * b], fm_all[2 * b + 1]
    return np.ascontiguousarray(np.concatenate([f0[rows, :NLAT], f1[rows, :NLAT], f0[rows, NLAT:], f1[rows, NLAT:]], axis=1))


def seq_tm(tm_all, b, cols):
    t0, t1 = tm_all[2 * b], tm_all[2 * b + 1]
    return np.ascontiguousarray(np.concatenate([t0[:NLAT, cols], t1[:NLAT, cols], t0[NLAT:, cols], t1[NLAT:, cols]], axis=0))


def ml_inputs(l, core, fm_all, tm_all, inp, consts):
    b, hp = core_bh(core)
    grow = [1024 + 2 * hp, 1024 + 2 * hp + 1, 1028 + 2 * hp, 1028 + 2 * hp + 1, 1032 + 2 * hp, 1032 + 2 * hp + 1, 1036 + 2 * hp, 1036 + 2 * hp + 1]
    d = {"qT": seq_fm(fm_all, b, slice(512 + 128 * hp, 512 + 128 * hp + 128)),
         "kT": seq_fm(fm_all, b, slice(768 + 128 * hp, 768 + 128 * hp + 128)),
         "ktok": seq_tm(tm_all, b, slice(128 + 128 * hp, 128 + 128 * hp + 128)),
         "vtok": seq_tm(tm_all, b, slice(384 + 128 * hp, 384 + 128 * hp + 128)),
         "otok": seq_tm(tm_all, b, slice(640 + 128 * hp, 640 + 128 * hp + 128)),
         "gT": seq_fm(fm_all, b, grow),
         "nw_bc": np.ascontiguousarray(np.tile(inp["ml_norm_w"][l][128 * hp:128 * hp + 128][None, :], (128, 1)).astype(np.float32))}
    d.update(consts["ml"])
    return d


HY_CFG = {"lat": dict(L=SEQ, B=1024, nb=4), "ctx": dict(L=CTX, B=256, nb=1)}


def dft_tables(B):
    t = np.arange(B, dtype=np.float64)[:, None]
    om = np.pi * (2 * np.arange(B, dtype=np.float64)[None, :] + 1) / (2 * B)
    return np.cos(t * om).astype(np.float32), np.sin(t * om).astype(np.float32)


def pos_feats(L):
    t = np.linspace(0.0, 1.0, L, dtype=np.float32)[:, None]
    ang = (np.float32(2.0 * math.pi / L) * np.arange(L, dtype=np.float32))[:, None]
    bands = np.linspace(1e-4, 15, 16, dtype=np.float32)[None, :]
    feats = np.concatenate([t, np.cos(bands * ang), -np.sin(bands * ang)], axis=-1).astype(np.float32)
    return feats, t[:, 0]


def hy_consts():
    c = {}
    for nm, cfg in HY_CFG.items():
        L, B = cfg["L"], cfg["B"]
        TC, TS = dft_tables(B)
        feats, t = pos_feats(L)
        c[nm] = {"TC": TC, "TS": TS, "TCT": np.ascontiguousarray(TC.T), "TST": np.ascontiguousarray(TS.T),
                 "featsT": np.ascontiguousarray(feats.T), "featsTr": np.ascontiguousarray(feats[::-1].T),
                 "negt": np.ascontiguousarray((-t).reshape(L // 128, 128).T), "negtr": np.ascontiguousarray((-t[::-1]).reshape(L // 128, 128).T)}
    alt = np.where(np.arange(128) % 2 == 0, 1.0, -1.0).astype(np.float32).reshape(128, 1)
    c["alt"] = alt
    return c


def emit_F(nc, bank, pfx, io0):
    w1_d, w2_d, fb_d, alt_d = io0["w1"], io0["w2"], io0["fb"], io0["alt"]
    io = io0
    with ExitStack() as es:
        S = Sched(nc, es, bank, pfx)
        k = K(S)
        w1 = S.sb("w1", [33, 64]); S.dma("sync", w1[:], w1_d, writes=["w1"])
        w2 = S.sb("w2", [64, 64]); S.dma("sync", w2[:], w2_d, writes=["w2"])
        w3 = S.sb("w3", [64, 768])
        fb = S.sb("fb", [64, 4]); S.dma("sync", fb[:], fb_d, writes=["fb"])
        fbb = S.sb("fbb", [64, 2])
        k.ts(fbb[:], fb[:, 1:3], fb[:, 0:1], None, ALU.mult, None, ["fb"], ["fbb"])
        adec = S.sb("adec", [128, 768])
        alt = S.sb("alt", [128, 1]); S.dma("sync", alt[:], alt_d, writes=["alt"])
        pz = [S.ps(f"pz{i}", [128, 512]) for i in range(2)]
        pP = [S.ps(f"pP{i}", [128, 512]) for i in range(4)]
        TWO_PI = 2.0 * math.pi
        LM, BM, NBM = SEQ, 1024, 4
        altB = S.sb("altB", [128, 1])
        wm = S.sb("wm", [64, 512])
        negt_s = S.sb("negt", [128, LM // 128]); negtr_s = S.sb("negtr", [128, LM // 128])
        TC_s = S.sb("TC", [128, BM // 128, BM], BF16); TS_s = S.sb("TS", [128, BM // 128, BM], BF16)
        Gt_s = [S.sb(f"Gt{dr}", [128, LM // 128, 384], BF16) for dr in range(2)]
        feats_s = S.sb("feats", [33, LM]); z1_s = S.sb("z1", [64, LM]); z2_s = [S.sb(f"z2_{dr}", [64, LM]) for dr in range(2)]
        arg = S.sb("arg", [64, 512]); fsb = S.sb("fsb", [128, 384]); dct = S.sb("dct", [128, 384])
        XY_s = S.sb("XY", [128, 2 * NBM, 4, 384])
        gst = [S.sb(f"gst{i}", [128, 384]) for i in range(1)]
        for nm, cfg in HY_CFG.items():
            L, B, nb = cfg["L"], cfg["B"], cfg["nb"]
            d = io[nm]
            nt = L // 128; ntb = B // 128
            k.ts(altB[:], alt[:], 1.0 / B, None, ALU.mult, None, ["alt"], ["altB"])
            negt = negt_s[:, 0:nt]; negtr = negtr_s[:, 0:nt]
            S.dma("sync", negt, d["negt"], writes=["negt"]); S.dma("sync", negtr, d["negtr"], writes=["negtr"])
            TC = TC_s[:, 0:ntb, 0:B]; TS = TS_s[:, 0:ntb, 0:B]
            S.dma("gpsimd", TC, d["TC"].rearrange("(a p) k -> p a k", p=128), writes=["TC"])
            S.dma("gpsimd", TS, d["TS"].rearrange("(a p) k -> p a k", p=128), writes=["TS"])
            Gt = [Gt_s[dr][:, 0:nt, :] for dr in range(2)]
            feats = feats_s[:, 0:L]; z1 = z1_s[:, 0:L]
            XY = XY_s[:, 0:2 * nb]
            for dr in range(2):
                z2 = z2_s[dr][:, 0:L]
                S.dma("sync", feats, d["featsr" if dr else "feats"], writes=["feats"])
                for si, (src, w_, bcol, dst, K_) in enumerate(((feats, w1, 0, z1, 33), (z1, w2, 1, z2, 64))):
                    for c0 in range(0, L, 512):
                        n = min(512, L - c0)
                        ps = pz[(c0 // 512) % 2]; pk = f"pz{(c0 // 512) % 2}"
                        k.mm(ps[:64, :n], w_[:K_, :], src[:K_, c0:c0 + n], True, True, ["w1", "w2", "feats", "z1"], [pk])
                        k.act(arg[:, :n], ps[:64, :n], AF.Identity, [pk, "fb", "fbb"], ["arg"], scale=fb[:, 0:1], bias=fbb[:, bcol:bcol + 1])
                        for _ in range(2):
                            for (cmp_, val, sh) in ((ALU.is_gt, math.pi, -TWO_PI), (ALU.is_lt, -math.pi, TWO_PI)):
                                k.ts(wm[:, :n], arg[:, :n], val, None, cmp_, None, ["arg"], ["wm"])
                                k.stt(arg[:, :n], wm[:, :n], sh, arg[:, :n], ALU.mult, ALU.add, ["wm", "arg"], ["arg"])
                        k.act(dst[:, c0:c0 + n], arg[:, :n], AF.Sin, ["arg"], ["z1" if si == 0 else f"z2_{dr}"])
            for hh in range(len(io0["w3c"])):
                S.dma("sync", w3[:], io0["w3c"][hh], writes=["w3"])
                S.dma("sync", adec[:], io0["decay_bc"][hh], writes=["adec"])
                k.act(adec[:], adec[:], AF.Abs, ["adec"], ["adec"])
                for dr in range(2):
                    z2 = z2_s[dr][:, 0:L]
                    tcol = negtr if dr else negt
                    for mt in range(nt):
                        ps = pz[mt % 2]; pk = f"pz{mt % 2}"
                        k.mm(ps[:, :384], z2[:, mt * 128:(mt + 1) * 128], w3[:, dr * 384:(dr + 1) * 384], True, True, [f"z2_{dr}", "w3"], [pk])
                        k.act(dct[:], adec[:, dr * 384:(dr + 1) * 384], AF.Exp, ["adec", "negt", "negtr"], ["dct"], scale=tcol[:, mt:mt + 1])
                        k.copy("scalar", fsb[:], ps[:, :384], [pk], ["fsb"])
                        k.tt(Gt[dr][:, mt, :], fsb[:], dct[:], ALU.mult, ["fsb", "dct"], [f"Gt{dr}_{mt // ntb}"], eng="gpsimd")
                npp = 0; ng = 0
                for kt in range(ntb):
                    for ei in range(2 * nb):
                        src = Gt[1] if ei < nb else Gt[0]
                        base = (ei if ei < nb else ei - nb) * ntb
                        bk = f"Gt{1 if ei < nb else 0}_{ei if ei < nb else ei - nb}"
                        for ti, (T_, tk) in enumerate(((TC, "TC"), (TS, "TS"))):
                            ps = pP[npp % 4]; pk = f"pP{npp % 4}"; npp += 1
                            for tt in range(ntb):
                                k.mm(ps[:, :384], T_[:, tt, kt * 128:(kt + 1) * 128], src[:, base + tt, :], tt == 0, tt == ntb - 1, [tk, bk], [pk])
                            k.act(XY[:, ei, ti, :], ps[:, :384], AF.Identity, [pk], [f"XY{ei}_{ti}"], scale=1.0 / B)
                            k.act(XY[:, ei, 2 + ti, :], ps[:, :384], AF.Identity, [pk, "altB"], [f"XY{ei}_{2 + ti}"], scale=altB[:, 0:1])
                    for dd in range(-(nb - 1), nb):
                        ei = dd + nb; q = (nb - 1) - dd
                        for rj in range(2):
                            g_ = gst[0]; gk = "gst0"; ng += 1
                            if rj == 0:
                                k.tt(g_[:], XY[:, ei, 0, :], XY[:, ei - 1, 3, :], ALU.add, [f"XY{ei}_0", f"XY{ei - 1}_3"], [gk], eng="gpsimd")
                            else:
                                k.tt(g_[:], XY[:, ei, 1, :], XY[:, ei - 1, 2, :], ALU.subtract, [f"XY{ei}_1", f"XY{ei - 1}_2"], [gk], eng="gpsimd")
                            S.dma("sync", d["G"][hh][:, kt, rj, :, q, :].rearrange("o p c -> p o c"), g_[:].rearrange("p (o c) -> p o c", o=2), reads=[gk])
        S.emit()


def f_inputs(core, inp, consts):
    l, hh = core // 2, core % 2
    cols = []
    for dr in range(2):
        for o in range(2):
            c0 = dr * 768 + o * 384 + hh * 192
            cols.extend(range(c0, c0 + 192))
    cols = np.array(cols)
    hc = consts["hy"]
    d = {"w1": inp["hy_w1"][l], "w2": inp["hy_w2"][l], "w3c": np.ascontiguousarray(inp["hy_w3"][l][:, cols]),
         "fb": np.ascontiguousarray(np.stack([inp["hy_freq"][l], inp["hy_b1"][l], inp["hy_b2"][l], inp["hy_b2"][l]], axis=1)),
         "decay_bc": np.ascontiguousarray(np.tile(inp["hy_decay"][l][cols][None, :], (128, 1))), "alt": hc["alt"]}
    for nm in HY_CFG:
        d[f"featsT_{nm}"] = hc[nm]["featsT"]; d[f"featsTr_{nm}"] = hc[nm]["featsTr"]
        d[f"negt_{nm}"] = hc[nm]["negt"]; d[f"negtr_{nm}"] = hc[nm]["negtr"]
        d[f"TC_{nm}"] = hc[nm]["TC"]; d[f"TS_{nm}"] = hc[nm]["TS"]
    return d


def emit_hy(nc, bank, pfx, io, dbg=None):
    cw_d, sk_d, Gd, Td, out_d, id_d = io["cw_bc"], io["sk_bc"], io["G"], io["T"], io["hyoT"], io["ident"]
    with ExitStack() as es:
        S = Sched(nc, es, bank, pfx)
        k = K(S)
        cw = S.sb("cw", [128, 4, 576]); S.dma("sync", cw[:].rearrange("p a c -> p (a c)"), cw_d, writes=["cw"])
        sk = S.sb("sk", [128, 2, 192]); S.dma("sync", sk[:].rearrange("p a c -> p (a c)"), sk_d, writes=["sk"])
        VXX = S.sb("VXX", [128, NCH, 576], BF16)
        Z = S.sb("Z", [128, NCH, 192], BF16)
        tabs = {}
        for nm, cfg in HY_CFG.items():
            B = cfg["B"]; ntb = B // 128
            tabs[nm] = []
            for ti, tname in enumerate(("TC", "TS", "TCT", "TST")):
                t_ = S.sb(f"{tname}_{nm}", [128, ntb, B], BF16)
                S.dma("gpsimd", t_[:], Td[nm][ti].rearrange("(a p) k -> p a k", p=128), writes=[f"{tname}_{nm}"])
                tabs[nm].append((t_, f"{tname}_{nm}"))
        stg = [S.sb(f"stg{i}", [128, 3, 576]) for i in range(1)]
        pr = [S.sb(f"pr{i}", [128, 4, 192]) for i in range(2)]
        pb = [S.sb(f"pb{i}", [128, 4, 192], BF16) for i in range(4)]
        ct0 = pr[0][:, :, :].rearrange("p a c -> p (a c)")[:, 0:576]; ct1 = pr[1][:, :, :].rearrange("p a c -> p (a c)")[:, 0:576]
        tile_base = {"lat": 0, "ctx": SEQ // 128}
        for nm, cfg in HY_CFG.items():
            L = cfg["L"]
            for tt in range(L // 128):
                g = tile_base[nm] + tt
                s_ = stg[0]; skey = "stg0"
                tmt, pitch = io["tmT"]
                for part in range(3):
                    src = bass.AP(tmt, (io["row0"][nm] + tt * 128) * pitch + io["col0"] + part * 384, [[pitch, 128], [pitch, 3], [1, 192]])
                    S.dma("sync", s_[:, :, part * 192:(part + 1) * 192], src, writes=[skey])
                k.tt(ct0, s_[:, 0, :], cw[:, 0, :], ALU.mult, [skey, "cw"], ["pr0"])
                k.tt(ct1, s_[:, 1, :], cw[:, 1, :], ALU.mult, [skey, "cw"], ["pr1"], eng="gpsimd")
                k.tt(ct0, ct0, ct1, ALU.add, ["pr0", "pr1"], ["pr0"])
                k.tt(ct1, s_[:, 2, :], cw[:, 2, :], ALU.mult, [skey, "cw"], ["pr1"], eng="gpsimd")
                k.tt(ct0, ct0, ct1, ALU.add, ["pr0", "pr1"], ["pr0"])
                k.tt(VXX[:, g, :], ct0, cw[:, 3, :], ALU.add, ["pr0", "cw"], [f"VXX{g}"])
        psR = [S.ps(f"psR{i}", [128, 512]) for i in range(2)]
        psJ = [S.ps(f"psJ{i}", [128, 512]) for i in range(2)]
        pI = [S.ps(f"pI{i}", [128, 512]) for i in range(2)]
        NBM = 4
        RJ = [S.sb(f"RJ{i}", [128, 2, NBM, 192]) for i in range(2)]
        Gb = [S.sb(f"Gb{i}", [128, 2, 2 * NBM - 1, 192]) for i in range(1)]
        YAB = S.sb("YAB", [128, 8, 2, NBM, 192], BF16)
        et = [S.sb(f"et{i}", [128, 192]) for i in range(2)]
        ost = [S.sb(f"ost{i}", [128, 192]) for i in range(2)]
        oT = S.sb("oT", [128, 128]); ptp = S.ps("ptp", [128, 512])
        ident = S.sb("ident", [128, 128]); S.dma("sync", ident[:], id_d, writes=["ident"])
        identb = S.sb("identb", [128, 128], BF16); nidentb = S.sb("nidentb", [128, 128], BF16)
        k.act(identb[:], ident[:], AF.Identity, ["ident"], ["identb"], scale=1.0)
        k.act(nidentb[:], ident[:], AF.Identity, ["ident"], ["identb"], scale=-1.0)
        psY = S.ps("psY", [128, 512])
        cnt = {"f": 0, "g": 0, "i": 0, "e": 0}

        def conv(nm, o, src_fn, src_keys, epi):
            cfg = HY_CFG[nm]; B, nb = cfg["B"], cfg["nb"]; ntb = B // 128; nq = 2 * nb - 1
            base = tile_base[nm]
            (TC, kTC), (TS, kTS), (TCT, kTCT), (TST, kTST) = tabs[nm]
            for kt in range(ntb):
                rj = RJ[kt % 2]; rk = f"RJ{kt % 2}"
                for jp in range(0, nb, 2):
                    nj = min(2, nb - jp)
                    a_ = cnt["f"] % 2; cnt["f"] += 1
                    pR, pJ = psR[a_], psJ[a_]
                    for jj in range(nj):
                        for tt in range(ntb):
                            g = base + (jp + jj) * ntb + tt
                            k.mm(pR[:, jj * 192:(jj + 1) * 192], TC[:, tt, kt * 128:(kt + 1) * 128], src_fn(g), tt == 0, tt == ntb - 1,
                                 [kTC] + src_keys(g), [f"psR{a_}"], inc=(tt == ntb - 1 and jj == nj - 1))
                    for jj in range(nj):
                        for tt in range(ntb):
                            g = base + (jp + jj) * ntb + tt
                            k.mm(pJ[:, jj * 192:(jj + 1) * 192], TS[:, tt, kt * 128:(kt + 1) * 128], src_fn(g), tt == 0, tt == ntb - 1,
                                 [kTS] + src_keys(g), [f"psJ{a_}"], inc=(tt == ntb - 1 and jj == nj - 1))
                    k.copy("scalar", rj[:, 0, jp:jp + nj, :], pR[:, 0:nj * 192].rearrange("p (j c) -> p j c", j=nj), [f"psR{a_}"], [rk + f"_0_{jp}"])
                    k.copy("scalar", rj[:, 1, jp:jp + nj, :], pJ[:, 0:nj * 192].rearrange("p (j c) -> p j c", j=nj), [f"psJ{a_}"], [rk + f"_1_{jp}"])
                rkeys = [rk + f"_{a}_{jp}" for a in range(2) for jp in range(0, nb, 2)]
                gb = Gb[0]; gk = "Gb0"; cnt["g"] += 1
                for a in range(2):
                    S.dma("sync", gb[:, a, 0:nq, :], Gd[nm][o, kt, a], writes=[gk + f"_{a}"])
                gkeys = [gk + "_0", gk + "_1"]
                for i in range(nb):
                    qs = nb - 1 - i
                    Rv = rj[:, 0, 0:nb, :]; Jv = rj[:, 1, 0:nb, :]; GRs = gb[:, 0, qs:qs + nb, :]; GJs = gb[:, 1, qs:qs + nb, :]
                    b0, b1, b2, b3 = [p_[:, 0:nb, :] for p_ in pb]
                    k.tt(b0, Rv, GRs, ALU.mult, rkeys + gkeys, ["pb0"])
                    k.tt(b1, Jv, GJs, ALU.mult, rkeys + gkeys, ["pb1"], eng="gpsimd")
                    k.tt(b2, Rv, GJs, ALU.mult, rkeys + gkeys, ["pb2"], eng="gpsimd")
                    k.tt(b3, Jv, GRs, ALU.mult, rkeys + gkeys, ["pb3"])
                    terms = [(0, identb, "pb0"), (1, nidentb, "pb1")]
                    for half, tl in ((0, [(pb[0], identb, "pb0"), (pb[1], nidentb, "pb1")]), (1, [(pb[2], identb, "pb2"), (pb[3], identb, "pb3")])):
                        nmm = 2 * nb; cmm = 0
                        for (pp, idm, pk_) in tl:
                            for j in range(nb):
                                k.mm(psY[:, half * 192:(half + 1) * 192], idm[:], pp[:, j, :], cmm == 0, cmm == nmm - 1, [pk_, "identb"], ["psY"],
                                     inc=(cmm == nmm - 1))
                                cmm += 1
                    for a in range(2):
                        k.copy("scalar", YAB[:, kt, a, i, :], psY[:, a * 192:(a + 1) * 192], ["psY"], [f"YAB{kt}_{a}_{i}"])
            for pt in range(ntb):
                for ip in range(0, nb, 2):
                    ni = min(2, nb - ip)
                    a_ = cnt["i"] % 2; cnt["i"] += 1
                    ps = pI[a_]; pk = f"pI{a_}"
                    for ii in range(ni):
                        for kt in range(ntb):
                            last = (kt == ntb - 1 and ii == ni - 1)
                            k.mm(ps[:, ii * 192:(ii + 1) * 192], TCT[:, kt, pt * 128:(pt + 1) * 128], YAB[:, kt, 0, ip + ii, :], kt == 0, False,
                                 [kTCT, f"YAB{kt}_0_{ip + ii}"], [pk], inc=False)
                            k.mm(ps[:, ii * 192:(ii + 1) * 192], TST[:, kt, pt * 128:(pt + 1) * 128], YAB[:, kt, 1, ip + ii, :], False, kt == ntb - 1,
                                 [kTST, f"YAB{kt}_1_{ip + ii}"], [pk], inc=last)
                    for ii in range(ni):
                        g = base + (ip + ii) * ntb + pt
                        epi(g, ps[:, ii * 192:(ii + 1) * 192], pk)

        def epi1(g, y, pk):
            e_ = et[cnt["e"] % 2]; ek = f"et{cnt['e'] % 2}"; cnt["e"] += 1
            k.tt(e_[:], VXX[:, g, 0:192], sk[:, 0, :], ALU.mult, [f"VXX{g}", "sk"], [ek], eng="gpsimd")
            k.tt(e_[:], y, e_[:], ALU.add, [pk, ek], [ek])
            k.tt(Z[:, g, :], e_[:], VXX[:, g, 192:384], ALU.mult, [ek, f"VXX{g}"], [f"Z{g}"], eng="gpsimd")

        def epi2(g, y, pk):
            e_ = et[cnt["e"] % 2]; ek = f"et{cnt['e'] % 2}"; cnt["e"] += 1
            o_ = ost[cnt["e"] % 2]; ok = f"ost{cnt['e'] % 2}"
            k.tt(e_[:], Z[:, g, :], sk[:, 1, :], ALU.mult, [f"Z{g}", "sk"], [ek], eng="gpsimd")
            k.tt(e_[:], y, e_[:], ALU.add, [pk, ek], [ek])
            k.tt(o_[:], e_[:], VXX[:, g, 384:576], ALU.mult, [ek, f"VXX{g}"], [ok], eng="gpsimd")
            for (c0_, cn) in ((0, 128), (128, 64)):
                S.op("tensor", lambda e, o_=o_, c0_=c0_, cn=cn: e.transpose(ptp[:cn, 0:128], o_[:, c0_:c0_ + cn], ident[:]),
                     reads=[ok, "ident", "ptp"], writes=["ptp"])
                k.copy("scalar", oT[:cn, :], ptp[:cn, 0:128], ["ptp"], ["oT"])
                S.dma("sync", out_d[c0_:c0_ + cn, g * 128:(g + 1) * 128], oT[:cn, :], reads=["oT"])

        for nm in (("ctx",) if dbg == "ctx" else ("lat", "ctx")):
            conv(nm, 0, lambda g: VXX[:, g, 0:192], lambda g: [f"VXX{g}"], epi1)
            conv(nm, 1, lambda g: Z[:, g, :], lambda g: [f"Z{g}"], epi2)
        S.emit()


def hy_inputs(l, core, tm_all, G_core, inp, consts):
    b, hh = core_bh(core)
    cols = np.concatenate([896 + part * 384 + hh * 192 + np.arange(192) for part in range(3)])
    hy = seq_tm(tm_all, b, cols)
    z = np.zeros((1, 576), np.float32)
    ccols = np.concatenate([part * 384 + hh * 192 + np.arange(192) for part in range(3)])
    cwb = np.concatenate([inp["hy_conv_w"][l][:, ccols].reshape(-1), inp["hy_conv_b"][l][ccols]])
    d = {"hyp_lat": np.ascontiguousarray(np.concatenate([z, hy[:SEQ], z], 0)),
         "hyp_ctx": np.ascontiguousarray(np.concatenate([z, hy[SEQ:], z], 0)),
         "cw_bc": np.ascontiguousarray(np.tile(cwb[None, :], (128, 1)).astype(np.float32)),
         "sk_bc": np.ascontiguousarray(np.tile(inp["hy_skip"][l][:, hh * 192:(hh + 1) * 192].reshape(1, -1), (128, 1)).astype(np.float32))}
    for nm in HY_CFG:
        d[f"G_{nm}"] = G_core[nm]
        for t in ("TC", "TS", "TCT", "TST"):
            d[f"{t}_{nm}"] = consts["hy"][nm][t]
    return d


NROW = SEQ + CTX + 4
LAT0, CTX0 = 1, SEQ + 3


def ext_specs():
    sp = {"xT_in": [2, D, NT], "sc": [128, 16], "w_mod": [DEPTH, D, 6 * D], "b_mod_p": [DEPTH, 128, 48],
          "n1_p": [DEPTH, 128, 8], "n2_p": [DEPTH, 128, 8], "w_in": [DEPTH, D, P_IN], "qkn": [DEPTH, 128, 2],
          "gate_b": [DEPTH, 16, 1], "cosT": [2, 128, NT], "sinT": [2, 128, NT], "rm2": [128, 128], "blk1": [128, 128],
          "w_out": [DEPTH, D, D], "w_gate": [DEPTH, D, D_FF], "w_up": [DEPTH, D, D_FF], "w_down": [DEPTH, D_FF, D],
          "sink": [DEPTH, 2, 64, 3], "mask": [128, 384], "nw_bc": [DEPTH, 2, 128, 128],
          "sel": [34, 512], "ident": [128, 128], "maskf": [128, 128], "maskb": [128, 128],
          "cw_bc": [DEPTH, 2, 128, 4 * 576], "sk_bc": [DEPTH, 2, 128, 384],
          "f_w1": [DEPTH, 33, 64], "f_w2": [DEPTH, 64, 64], "f_w3c": [DEPTH, 2, 64, 768], "f_fb": [DEPTH, 64, 4],
          "f_dec": [DEPTH, 2, 128, 768], "alt": [128, 1]}
    for nm, cfg in HY_CFG.items():
        L, B = cfg["L"], cfg["B"]
        for t in ("TC", "TS", "TCT", "TST"):
            sp[f"{t}_{nm}"] = [B, B]
        sp[f"featsT_{nm}"] = [33, L]; sp[f"featsTr_{nm}"] = [33, L]
        sp[f"negt_{nm}"] = [128, L // 128]; sp[f"negtr_{nm}"] = [128, L // 128]
    return sp


def build_fused(depth=DEPTH):
    nc = bass.Bass("TRN2", target_bir_lowering=False)
    X = {n: dram_in(nc, n, shp) for n, shp in ext_specs().items()}
    OUT = dram_out(nc, "xT_out", [2, D, NT])
    FMS = nc.dram_tensor("FMS", [1424, NTOK], F32).ap()
    TMT = nc.dram_tensor("TMSP", [NROW, 2048], F32)
    TMS = TMT.ap()
    MIXS = nc.dram_tensor("MIXS", [D, NTOK], F32).ap()
    MODT = nc.dram_tensor("MODT", [128, 96], F32).ap()
    XTS = nc.dram_tensor("XTS", [2, D, NT], F32).ap()
    GS = {nm: nc.dram_tensor(f"GS_{nm}", [DEPTH, 2, 2, cfg["B"] // 128, 2, 128, 2 * cfg["nb"] - 1, 192], F32).ap()
          for nm, cfg in HY_CFG.items()}
    MLSCR = nc.dram_tensor("ml_scr", [2, 4, NTOK], F32).ap()
    import os
    PH = os.environ.get("FPH", "F,A,att,ml,hy,E").split(",")
    with ExitStack() as es0:
        bank = SemBank(nc, es0, nsets=1)
        with ExitStack() as es:
            S = Sched(nc, es, bank, "init_")
            z = S.sb("z", [4, 2048])
            S.op("vector", lambda e: e.memset(z[:], 0.0), writes=["z"])
            for i_, row in enumerate((0, SEQ + 1, SEQ + 2, NROW - 1)):
                S.dma("sync", TMS[row:row + 1, :], z[i_:i_ + 1, :], reads=["z"])
            S.emit()
        for l in range(depth):
            io = {"w1": X["f_w1"][l], "w2": X["f_w2"][l], "w3c": [X["f_w3c"][l, hh] for hh in range(2)], "fb": X["f_fb"][l],
                  "decay_bc": [X["f_dec"][l, hh] for hh in range(2)], "alt": X["alt"]}
            for nm in HY_CFG:
                io[nm] = dict(feats=X[f"featsT_{nm}"], featsr=X[f"featsTr_{nm}"], negt=X[f"negt_{nm}"], negtr=X[f"negtr_{nm}"],
                              TC=X[f"TC_{nm}"], TS=X[f"TS_{nm}"], G=[GS[nm][l, hh] for hh in range(2)])
            if "F" in PH:
                emit_F(nc, bank, f"F{l}_", io)
        for l in range(depth):
            xsrc = X["xT_in"] if l == 0 else XTS
            xdst = OUT if l == depth - 1 else XTS
            gcols = [(lambda t, u=u: u * NLAT + t if t < NLAT else SEQ + u * NCTX + (t - NLAT)) for u in range(2)]
            grows = [(lambda t, u=u: LAT0 + u * NLAT + t if t < NLAT else CTX0 + u * NCTX + (t - NLAT)) for u in range(2)]
            io = {"xT": [xsrc[0], xsrc[1]], "sc": X["sc"], "w_mod": X["w_mod"][l], "b_mod": X["b_mod_p"][l], "norm1_w": X["n1_p"][l],
                  "w_in": X["w_in"][l], "qkn": X["qkn"][l], "gate_b": X["gate_b"][l], "cosT": X["cosT"], "sinT": X["sinT"],
                  "rm2": X["rm2"], "blk1": X["blk1"], "modT": MODT, "fm": FMS, "tm": TMS}
            if "A" in PH:
                emit_A(nc, bank, f"A{l}_", io, gcols, grows)
            for g in range(2):
                io = {"qT": FMS[192 * g:192 * g + 192, :], "kT": FMS[384 + 64 * g:384 + 64 * g + 64, :],
                      "v_lat": TMS[LAT0:LAT0 + SEQ, 64 * g:64 * g + 64], "v_ctx": TMS[CTX0:CTX0 + CTX, 64 * g:64 * g + 64],
                      "sink": X["sink"][l, g], "mask": X["mask"], "attT": MIXS[192 * g:192 * g + 192, :]}
                if "att" in PH:
                    emit_att(nc, bank, f"T{l}{g}_", io)
            for hp in range(2):
                def tmp_(c0):
                    return (TMS[LAT0:LAT0 + SEQ, c0:c0 + 128], TMS[CTX0:CTX0 + CTX, c0:c0 + 128])
                io = {"qT": FMS[512 + 128 * hp:512 + 128 * hp + 128, :], "kT": FMS[768 + 128 * hp:768 + 128 * hp + 128, :],
                      "ktok": tmp_(128 + 128 * hp), "vtok": tmp_(384 + 128 * hp), "otok": tmp_(640 + 128 * hp),
                      "g4": [FMS[1024 + 4 * q + 2 * hp:1024 + 4 * q + 2 * hp + 2, :] for q in range(4)],
                      "nw_bc": X["nw_bc"][l, hp], "sel": X["sel"], "ident": X["ident"], "maskf": X["maskf"], "maskb": X["maskb"],
                      "mloT": MIXS[768 + 128 * hp:768 + 128 * hp + 128, :], "scr": MLSCR}
                if "ml" in PH:
                    emit_ml(nc, bank, f"L{l}{hp}_", io)
            for hh in range(2):
                io = {"tmT": (TMT, 2048), "row0": {"lat": LAT0 - 1, "ctx": CTX0 - 1}, "col0": 896 + 192 * hh,
                      "cw_bc": X["cw_bc"][l, hh], "sk_bc": X["sk_bc"][l, hh], "ident": X["ident"],
                      "G": {nm: GS[nm][l, hh] for nm in HY_CFG},
                      "T": {nm: [X[f"{t}_{nm}"] for t in ("TC", "TS", "TCT", "TST")] for nm in HY_CFG},
                      "hyoT": MIXS[384 + 192 * hh:384 + 192 * hh + 192, :]}
                if "hy" in PH:
                    emit_hy(nc, bank, f"H{l}{hh}_", io)
            io = {"xT": [xsrc[0], xsrc[1]], "mix": MIXS, "modT": MODT, "norm2_w": X["n2_p"][l], "w_out": X["w_out"][l],
                  "w_gate": X["w_gate"][l], "w_up": X["w_up"][l], "w_down": X["w_down"][l], "xT_out": [xdst[0], xdst[1]]}
            if "E" in PH:
                emit_E(nc, bank, f"E{l}_", io, gcols)
    return nc


def fused_inputs(b, inp, consts):
    cos, sin = consts["rope"]
    d = {}
    xT = np.empty((2, D, NT), np.float32)
    cosT = np.ones((2, 128, NT), np.float32)
    sinT = np.zeros((2, 128, NT), np.float32)
    for u in range(2):
        xT[u, :, :NLAT] = inp["x"][b, u * NLAT:(u + 1) * NLAT, :].T
        xT[u, :, NLAT:] = inp["ctx"][b, u * NCTX:(u + 1) * NCTX, :].T
        for hd in range(2):
            cosT[u, 64 * hd:64 * hd + 64, :NLAT] = cos[:, u * NLAT:(u + 1) * NLAT]
            sinT[u, 64 * hd:64 * hd + 64, :NLAT] = sin[:, u * NLAT:(u + 1) * NLAT]
    d["xT_in"] = xT; d["cosT"] = cosT; d["sinT"] = sinT
    d["sc"] = pcol(np.stack([inp["c"][b], inp["c_ctx"]], axis=1)).reshape(128, 16)
    for k_ in ("w_mod", "w_in", "w_out"):
        d[k_] = inp[k_]
    d["w_gate"], d["w_up"], d["w_down"] = inp["ffn_w_gate"], inp["ffn_w_up"], inp["ffn_w_down"]
    d["b_mod_p"] = np.stack([pcol(inp["b_mod"][l]) for l in range(DEPTH)])
    d["n1_p"] = np.stack([pcol(inp["norm1_w"][l]) for l in range(DEPTH)])
    d["n2_p"] = np.stack([pcol(inp["norm2_w"][l]) for l in range(DEPTH)])
    d["qkn"] = np.stack([np.stack([np.tile(inp["q_norm_w"][l], 2), np.tile(inp["k_norm_w"][l], 2)], axis=1) for l in range(DEPTH)])
    d["gate_b"] = inp["ml_gate_b"].reshape(DEPTH, 16, 1)
    d["rm2"], d["blk1"], d["mask"] = consts["rm2"], consts["blk1"], consts["mask"]
    d["sink"] = np.stack([np.stack([np.tile(inp["attn_sink"][l][3 * g:3 * g + 3][None, :], (64, 1)) for g in range(2)]) for l in range(DEPTH)])
    d["nw_bc"] = np.stack([np.stack([np.tile(inp["ml_norm_w"][l][128 * hp:128 * hp + 128][None, :], (128, 1)) for hp in range(2)]) for l in range(DEPTH)])
    d.update(consts["ml"])
    cw = np.empty((DEPTH, 2, 128, 4 * 576), np.float32); sk = np.empty((DEPTH, 2, 128, 384), np.float32)
    w3c = np.empty((DEPTH, 2, 64, 768), np.float32); dec = np.empty((DEPTH, 2, 128, 768), np.float32)
    for l in range(DEPTH):
        for hh in range(2):
            ccols = np.concatenate([part * 384 + hh * 192 + np.arange(192) for part in range(3)])
            cw[l, hh] = np.concatenate([inp["hy_conv_w"][l][:, ccols].reshape(-1), inp["hy_conv_b"][l][ccols]])[None, :]
            sk[l, hh] = inp["hy_skip"][l][:, hh * 192:(hh + 1) * 192].reshape(1, -1)
            cols = np.concatenate([dr * 768 + o * 384 + hh * 192 + np.arange(192) for dr in range(2) for o in range(2)])
            w3c[l, hh] = inp["hy_w3"][l][:, cols]
            dec[l, hh] = inp["hy_decay"][l][cols][None, :]
    d["cw_bc"], d["sk_bc"], d["f_w3c"], d["f_dec"] = cw, sk, w3c, dec
    d["f_w1"], d["f_w2"] = inp["hy_w1"], inp["hy_w2"]
    d["f_fb"] = np.stack([np.stack([inp["hy_freq"][l], inp["hy_b1"][l], inp["hy_b2"][l], inp["hy_b2"][l]], axis=1) for l in range(DEPTH)])
    hc = consts["hy"]
    d["alt"] = hc["alt"]
    for nm in HY_CFG:
        for t in ("TC", "TS", "TCT", "TST", "featsT", "featsTr", "negt", "negtr"):
            d[f"{t}_{nm}"] = hc[nm][t]
    sp = ext_specs()
    return {k_: np.ascontiguousarray(np.asarray(v, np.float32).reshape(sp[k_])) for k_, v in d.items()}


def kernel(**inputs):
    inp = {k_: np.asarray(v, dtype=np.float32) for k_, v in inputs.items()}
    consts = {"rope": rope_tables(), "mask": band_mask(), "ml": ml_consts(), "hy": hy_consts()}
    consts["rm2"], consts["blk1"] = rope_consts()
    nc = build_fused()
    maps = [fused_inputs(c // 2, inp, consts) for c in range(8)]
    res = run_bass_kernel_spmd(nc, maps, core_ids=list(range(8))).results
    out = np.empty((4, SEQ, D), np.float32)
    for b in range(4):
        xo = res[2 * b]["xT_out"]
        for u in range(2):
            out[b, u * NLAT:(u + 1) * NLAT, :] = xo[u][:, :NLAT].T
    return out
```

```python
from contextlib import ExitStack
import math
import numpy as np
import ml_dtypes
import concourse.bass as bass
import concourse.mybir as mybir
from concourse.bass_utils import run_bass_kernel_spmd

F32 = mybir.dt.float32
BF16 = mybir.dt.bfloat16
ALU = mybir.AluOpType
AF = mybir.ActivationFunctionType
AX = mybir.AxisListType

D = 1024
SEQ = 4096
CTX = 256
DEPTH = 4
NLAT = SEQ // 2
NCTX = CTX // 2
NT = NLAT + NCTX
NTOK = SEQ + CTX
P_IN = 2832
D_FF = 2816
NFF = D_FF // 128
EPS = 1e-6
O_AQ, O_AK, O_AV, O_HY, O_MQ, O_MK, O_MV, O_MO, O_MG = 0, 384, 512, 640, 1792, 2048, 2304, 2560, 2816

ENGS = ["sync", "scalar", "vector", "gpsimd", "tensor"]
NDS = 8
SES_SKIP = ()


class SemBank:
    def __init__(self, nc, es, nsets=2):
        self.sets = []
        for si in range(nsets):
            self.sets.append({
                "s": {e: es.enter_context(nc.semaphore(f"s{si}_{e}")) for e in ENGS},
                "d": {e: [es.enter_context(nc.semaphore(f"d{si}_{e}{i}")) for i in range(NDS)] for e in ENGS}})
        self.phase = 0


class Sched:
    def __init__(self, nc, es, bank=None, pfx="", same_engine_sync=True):
        self.nc = nc
        self.es = es
        self.pfx = pfx
        self.q = {e: [] for e in ENGS}
        if bank is None:
            bank = SemBank(nc, es, nsets=1)
        cur = bank.sets[bank.phase % len(bank.sets)]
        self.other = bank.sets[(bank.phase + 1) % len(bank.sets)] if len(bank.sets) > 1 else None
        bank.phase += 1
        self.sem = cur["s"]
        self.dsem = cur["d"]
        if len(bank.sets) == 1:
            if not hasattr(bank, "state"):
                bank.state = ({e: 0 for e in ENGS}, {e: [0] * NDS for e in ENGS}, {e: 0 for e in ENGS}, {e: {} for e in ENGS})
            self.cnt, self.dcnt, self.dnext, self.seen = bank.state
        else:
            self.cnt = {e: 0 for e in ENGS}
            self.dcnt = {e: [0] * NDS for e in ENGS}
            self.dnext = {e: 0 for e in ENGS}
            self.seen = {e: {} for e in ENGS}
        self.res = {}
        self.ses = same_engine_sync

    def sb(self, name, shape, dt=F32):
        return self.es.enter_context(self.nc.sbuf_tensor("sb_" + self.pfx + name, list(shape), dt))

    def ps(self, name, shape, dt=F32):
        return self.es.enter_context(self.nc.psum_tensor("ps_" + self.pfx + name, list(shape), dt))

    def _deps(self, eng, reads, writes):
        deps = {}

        def add(tok):
            sem, val, name = tok
            if name not in deps or deps[name][1] < val:
                deps[name] = tok

        for k in reads:
            r = self.res.get(k)
            if r and r[0] is not None:
                add(r[0])
        for k in writes:
            r = self.res.get(k)
            if r:
                if r[0] is not None:
                    add(r[0])
                for t in r[1]:
                    add(t)
        waits = []
        for name, (sem, val, _) in deps.items():
            if name == eng:
                if eng == "tensor" or not self.ses or val > self.cnt[eng] or eng in SES_SKIP:
                    continue
            if self.seen[eng].get(name, 0) < val:
                self.seen[eng][name] = val
                waits.append((sem, val))
        return waits

    def _record(self, tok, reads, writes):
        for k in reads:
            r = self.res.setdefault(k, [None, []])
            r[1].append(tok)
        for k in writes:
            self.res[k] = [tok, []]

    def op(self, eng, fn, reads=(), writes=(), inc=True):
        waits = self._deps(eng, reads, writes)
        if inc:
            self.cnt[eng] += 1
            tok = (self.sem[eng], self.cnt[eng], eng)
        else:
            tok = (self.sem[eng], self.cnt[eng] + 1, eng)
        self.q[eng].append((waits, fn, (self.sem[eng], 1) if inc else None))
        self._record(tok, reads, writes)

    def dma(self, eng, out, in_, reads=(), writes=(), **kw):
        waits = self._deps(eng, reads, writes)
        slot = self.dnext[eng]
        self.dnext[eng] = (slot + 1) % NDS
        name = f"d_{eng}{slot}"
        prev = self.dcnt[eng][slot]
        if prev > 0 and self.seen[eng].get(name, 0) < prev:
            self.seen[eng][name] = prev
            waits.append((self.dsem[eng][slot], prev))
        self.dcnt[eng][slot] = prev + 16
        tok = (self.dsem[eng][slot], prev + 16, name)
        self.q[eng].append((waits, lambda e: e.dma_start(out=out, in_=in_, **kw), (self.dsem[eng][slot], 16)))
        self._record(tok, reads, writes)

    def emit(self):
        for e in ENGS:
            for i in range(NDS):
                v = self.dcnt[e][i]
                if v > 0:
                    self.q["sync"].append(([(self.dsem[e][i], v)], None, None))
        for e in ENGS:
            if e != "sync" and self.cnt[e] > 0:
                self.q["sync"].append(([(self.sem[e], self.cnt[e])], None, None))
        if self.other is not None:
            clr = list(self.other["s"].values()) + [x for l_ in self.other["d"].values() for x in l_]
            self.q["gpsimd"] = [([], (lambda e, sm=sm: e.sem_clear(sm)), None) for sm in clr] + self.q["gpsimd"]
        with self.nc.Block() as block:
            for eng in ENGS:
                if not self.q[eng]:
                    continue

                def body(e, eng=eng):
                    for waits, fn, inc in self.q[eng]:
                        for sem, val in waits:
                            e.wait_ge(sem, val)
                        if fn is not None:
                            ins = fn(e)
                            if inc is not None:
                                ins.then_inc(inc[0], inc[1])

                getattr(block, eng)(body)


class K:
    def __init__(self, S):
        self.S = S

    def act(self, out, in_, func, r, w, bias=None, scale=None, eng="scalar"):
        kw = {}
        if bias is not None:
            kw["bias"] = bias
        if scale is not None:
            kw["scale"] = scale
        self.S.op("scalar", lambda e: e.activation(out=out, in_=in_, func=func, **kw), reads=r, writes=w)

    def tt(self, out, a, b, op, r, w, eng="vector"):
        self.S.op(eng, lambda e: e.tensor_tensor(out=out, in0=a, in1=b, op=op), reads=r, writes=w)

    def ts(self, out, a, s1, s2, op0, op1, r, w, eng="vector"):
        if op1 is None:
            self.S.op(eng, lambda e: e.tensor_scalar(out=out, in0=a, scalar1=s1, scalar2=None, op0=op0), reads=r, writes=w)
        else:
            self.S.op(eng, lambda e: e.tensor_scalar(out=out, in0=a, scalar1=s1, scalar2=s2, op0=op0, op1=op1), reads=r, writes=w)

    def stt(self, out, a, s, b, op0, op1, r, w):
        self.S.op("vector", lambda e: e.scalar_tensor_tensor(out=out, in0=a, scalar=s, in1=b, op0=op0, op1=op1), reads=r, writes=w)

    def copy(self, eng, out, in_, r, w):
        if eng == "scalar":
            self.S.op("scalar", lambda e: e.copy(out=out, in_=in_), reads=r, writes=w)
        else:
            self.S.op(eng, lambda e: e.tensor_copy(out=out, in_=in_), reads=r, writes=w)

    def recip(self, out, in_, r, w):
        self.S.op("vector", lambda e: e.reciprocal(out=out, in_=in_), reads=r, writes=w)

    def mm(self, out, lhsT, rhs, start, stop, r, w, inc=None):
        self.S.op("tensor", lambda e: e.matmul(out, lhsT=lhsT, rhs=rhs, start=start, stop=stop), reads=r, writes=w,
                  inc=stop if inc is None else inc)

    def memset(self, eng, ap, val, w):
        self.S.op(eng, lambda e: e.memset(ap, val), writes=w)


def dram_in(nc, name, shape, dt=F32):
    return nc.dram_tensor(name, list(shape), dt, kind="ExternalInput").ap()


def dram_out(nc, name, shape, dt=F32):
    return nc.dram_tensor(name, list(shape), dt, kind="ExternalOutput").ap()


def bcast_rows(ap1d, nparts):
    return ap1d.partition_broadcast(nparts)


def token_tiles(width):
    tiles = []
    t = 0
    while t < NLAT:
        tiles.append((t, width, 0))
        t += width
    tiles.append((NLAT, NCTX, 1))
    return tiles


def emit_mod_vectors(S, k, sc_d, wmod_d, bmod_d):
    sraw = S.sb("sraw", [128, 8, 2])
    sbf = S.sb("sbf", [128, 8, 2], BF16)
    bm = S.sb("bm", [128, 48])
    modT = S.sb("modT", [128, 48, 2])
    S.dma("sync", sraw[:].rearrange("p j c -> p (j c)"), sc_d, writes=["sraw"])
    S.dma("sync", bm[:], bmod_d, writes=["bm"])
    k.act(sbf[:], sraw[:], AF.Silu, ["sraw"], ["sbf"])
    wv = wmod_d.rearrange("(kc p) n -> p kc n", p=128)
    pm = S.ps("pmod", [128, 512])
    bufs = [S.sb(f"wmodb{i}", [128, 8, 1024], BF16) for i in range(2)]
    for g in range(6):
        wt = bufs[g % 2]
        key = f"wmodb{g % 2}"
        S.dma("gpsimd", wt[:], wv[:, :, g * 1024:(g + 1) * 1024], writes=[key])
        for j in range(8):
            jj = g * 8 + j
            for kc in range(8):
                k.mm(pm[:, 2 * jj:2 * jj + 2], wt[:, kc, j * 128:(j + 1) * 128], sbf[:, kc, :], kc == 0, kc == 7,
                     [key, "sbf"], ["pmod"])
    pv = pm[:, 0:96].rearrange("p (j c) -> p j c", c=2)
    for c in range(2):
        k.tt(modT[:, :, c], pv[:, :, c], bm[:], ALU.add, ["pmod", "bm"], ["modT"])
    return modT


def emit_A(nc, bank, pfx, io, gcols, grows, dbg=None):
    sc_d, wmod_d, bmod_d, n1_d, win_d = io["sc"], io["w_mod"], io["b_mod"], io["norm1_w"], io["w_in"]
    qkn_d, gb_d, cos_d, sin_d, rm_d, bo_d = io["qkn"], io["gate_b"], io["cosT"], io["sinT"], io["rm2"], io["blk1"]
    modT_o, fmT_o, tm_o = io["modT"], io["fm"], io["tm"]
    with ExitStack() as es:
        S = Sched(nc, es, bank, pfx)
        k = K(S)
        modT = emit_mod_vectors(S, k, sc_d, wmod_d, bmod_d)
        S.dma("sync", modT_o, modT[:].rearrange("p j c -> p (j c)"), reads=["modT"])
        n1 = S.sb("n1", [128, 8])
        S.dma("sync", n1[:], n1_d, writes=["n1"])
        qkn = S.sb("qkn", [128, 2]); S.dma("sync", qkn[:], qkn_d, writes=["qkn"])
        gb = S.sb("gb", [16, 1]); S.dma("sync", gb[:], gb_d, writes=["gb"])
        cosT = S.sb("cosT", [128, NT]); sinT = S.sb("sinT", [128, NT])
        rm2 = S.sb("rm2", [128, 128], BF16); S.dma("gpsimd", rm2[:], rm_d, writes=["rm2"])
        blk1 = S.sb("blk1", [128, 128], BF16); S.dma("gpsimd", blk1[:], bo_d, writes=["blk1"])
        ones = S.sb("ones", [128, 128], BF16); k.memset("gpsimd", ones[:], 1.0, ["ones"])
        A1 = S.sb("A1", [128, 8, 2])
        for c in range(2):
            k.stt(A1[:, :, c], modT[:, 8:16, c], 1.0, n1[:], ALU.add, ALU.mult, ["modT", "n1"], ["A1"])
        wv = win_d.rearrange("(kc p) n -> p kc n", p=128)
        fm_cols = [(O_AQ, 384), (O_AK, 128), (O_MQ, 256), (O_MK, 256), (O_MG, 16), (O_HY + 768, 384)]
        Wfm = S.sb("Wfm", [128, 8, 1424], BF16)
        off = 0
        for (c0, n) in fm_cols:
            S.dma("gpsimd", Wfm[:, :, off:off + n], wv[:, :, c0:c0 + n], writes=[f"Wfm{off}"])
            off += n
        tm_cols = [(O_AV, 128), (O_MK, 256), (O_MV, 256), (O_MO, 256), (O_HY, 1152)]
        Wtm = S.sb("Wtm", [128, 8, 2048], BF16)
        off = 0
        for (c0, n) in tm_cols:
            S.dma("gpsimd", Wtm[:, :, off:off + n], wv[:, :, c0:c0 + n], writes=[f"Wtm{off}"])
            off += n
        WFM_KEYS = ["Wfm0", "Wfm384", "Wfm512", "Wfm768", "Wfm1024", "Wfm1040"]
        WTM_KEYS = ["Wtm0", "Wtm128", "Wtm384", "Wtm640", "Wtm896"]
        k.ts(Wfm[:, :, 768:1024], Wfm[:, :, 768:1024], 0.125, None, ALU.mult, None, ["Wfm768"], ["Wfm768"], eng="gpsimd")
        k.ts(Wtm[:, :, 128:384], Wtm[:, :, 128:384], 0.125, None, ALU.mult, None, ["Wtm128"], ["Wtm128"], eng="gpsimd")
        fm_tiles = [(0, 128, "q"), (128, 128, "q"), (256, 128, "q"), (384, 128, "k"),
                    (512, 128, "c"), (640, 128, "c"), (768, 128, "c"), (896, 128, "c"),
                    (1024, 16, "g"), (1040, 128, "c"), (1168, 128, "c"), (1296, 128, "c")]
        xt = [S.sb(f"xt{i}", [128, 8, 512]) for i in range(2)]
        sq = S.sb("sq", [128, 8, 512], BF16)
        tmp = S.sb("tmpn", [128, 2, 512])
        hT = [S.sb(f"hT{i}", [128, 8, 512], BF16) for i in range(2)]
        rs = S.sb("rs", [128, 512]); rstd = S.sb("rstd", [128, 512])
        ss_ps = S.ps("ss_ps", [128, 512])
        pf = [S.ps(f"pf{i}", [128, 512]) for i in range(2)]
        pt = [S.ps(f"pt{i}", [128, 512]) for i in range(2)]
        pq = [S.ps(f"pq{i}", [128, 512]) for i in range(2)]
        stg_f = [S.sb(f"stgf{i}", [128, 512]) for i in range(3)]
        stg_t = [S.sb(f"stgt{i}", [128, 2048]) for i in range(2)]
        qsq = S.sb("qsq", [128, 512], BF16); qw = S.sb("qw", [128, 512], BF16)
        qrs = S.sb("qrs", [128, 512]); qri = S.sb("qri", [128, 512]); qt1 = S.sb("qt1", [128, 512]); qt2 = S.sb("qt2", [128, 512])
        nf = 0; ntm = 0; nst = 0
        tiles_all = [(u_, t0, n, which) for u_ in range(len(io["xT"])) for (t0, n, which) in token_tiles(512)]
        def load_x(tj):
            uj, tj0, nj, _ = tiles_all[tj]
            S.dma("sync", xt[tj % 2][:, :, :nj], io["xT"][uj].rearrange("(j p) t -> p j t", p=128)[:, :, tj0:tj0 + nj], writes=[f"xt{tj % 2}"])

        load_x(0)
        for ti, (u_, t0, n, which) in enumerate(tiles_all):
            if t0 == 0:
                gcol, grow = gcols[u_], grows[u_]
                S.dma("sync", cosT[:], cos_d[u_], writes=["cosT"]); S.dma("sync", sinT[:], sin_d[u_], writes=["sinT"])
            xb = xt[ti % 2]; xk = f"xt{ti % 2}"; hb = hT[ti % 2]; hk = f"hT{ti % 2}"
            if ti + 1 < len(tiles_all):
                load_x(ti + 1)
            k.act(sq[:, :, :n], xb[:, :, :n], AF.Square, [xk], ["sq"])
            for j in range(8):
                k.mm(ss_ps[:, :n], ones[:], sq[:, j, :n], j == 0, j == 7, ["ones", "sq"], ["ss_ps"])
            k.act(rs[:, :n], ss_ps[:, :n], AF.Sqrt, ["ss_ps"], ["rs"], bias=EPS, scale=1.0 / D)
            k.recip(rstd[:, :n], rs[:, :n], ["rs"], ["rstd"])
            for j in range(8):
                k.stt(tmp[:, j % 2, :n], xb[:, j, :n], A1[:, j, which:which + 1], rstd[:, :n], ALU.mult, ALU.mult,
                      [xk, "A1", "rstd"], [f"tmpn{j % 2}"])
                k.act(hb[:, j, :n], tmp[:, j % 2, :n], AF.Identity, [f"tmpn{j % 2}", "modT"], [hk + f"_{j}"],
                      bias=modT[:, j, which:which + 1], scale=1.0)
            hkeys = [hk + f"_{j}" for j in range(8)]
            if dbg == "norm":
                break
            for (c0, M, kind) in fm_tiles:
                if dbg == "fmc" and kind != "c":
                    continue
                if dbg == "fmq" and kind != "q":
                    continue
                if dbg == "fmg" and kind != "g":
                    continue
                ps = pf[nf % 2]; pk = f"pf{nf % 2}"; nf += 1
                for kc in range(8):
                    k.mm(ps[:, :n], Wfm[:, kc, c0:c0 + 128], hb[:, kc, :n], kc == 0, kc == 7, WFM_KEYS + hkeys, [pk])
                st = stg_f[nst % 3]; sk = f"stgf{nst % 3}"; nst += 1
                if kind == "c":
                    k.copy("scalar", st[:M, :n], ps[:M, :n], [pk], [sk])
                elif kind == "g":
                    k.ts(st[:M, :n], ps[:M, :n], gb[:, 0:1], None, ALU.add, None, [pk, "gb"], [sk])
                else:
                    nw = qkn[:, 0:1] if kind == "q" else qkn[:, 1:2]
                    p2 = pq[0]; p3 = pq[1]
                    import os
                    QS = int(os.environ.get("QSTEPS", "99"))
                    steps = [
                        lambda: k.act(qsq[:, :n], ps[:, :n], AF.Square, [pk], ["qsq"]),
                        lambda: k.act(qw[:, :n], ps[:, :n], AF.Identity, [pk, "qkn"], ["qw"], scale=nw),
                        lambda: k.mm(p2[:, :n], blk1[:], qsq[:, :n], True, True, ["blk1", "qsq"], ["pq0"]),
                        lambda: k.mm(p3[:, :n], rm2[:], qw[:, :n], True, True, ["rm2", "qw"], ["pq1"]),
                        lambda: k.act(qrs[:, :n], p2[:, :n], AF.Sqrt, ["pq0"], ["qrs"], bias=EPS, scale=1.0 / 64),
                        lambda: k.recip(qri[:, :n], qrs[:, :n], ["qrs"], ["qri"]),
                        lambda: k.tt(qt1[:, :n], qw[:, :n], cosT[:, t0:t0 + n], ALU.mult, ["qw", "cosT"], ["qt1"]),
                        lambda: k.tt(qt2[:, :n], p3[:, :n], sinT[:, t0:t0 + n], ALU.mult, ["pq1", "sinT"], ["qt2"]),
                        lambda: k.tt(qt1[:, :n], qt1[:, :n], qt2[:, :n], ALU.add, ["qt1", "qt2"], ["qt1"]),
                        lambda: k.tt(st[:, :n], qt1[:, :n], qri[:, :n], ALU.mult, ["qt1", "qri"], [sk]),
                    ]
                    for f_ in steps[:QS]:
                        f_()
                S.dma("sync", fmT_o[c0:c0 + M, gcol(t0):gcol(t0) + n], st[:M, :n], reads=[sk])
            if dbg in ("fm", "fmc", "fmq", "fmg"):
                break
            for s0 in range(0, n, 128):
                st = stg_t[ntm % 2]; sk = f"stgt{ntm % 2}"; ntm += 1
                for g in range(4):
                    ps = pt[(ntm * 4 + g) % 2]; pk = f"pt{(ntm * 4 + g) % 2}"
                    for kc in range(8):
                        k.mm(ps[:, :], hb[:, kc, s0:s0 + 128], Wtm[:, kc, g * 512:(g + 1) * 512], kc == 0, kc == 7,
                             WTM_KEYS + hkeys, [pk])
                    k.copy("scalar" if g % 2 == 0 else "vector", st[:, g * 512:(g + 1) * 512], ps[:, :], [pk], [sk + f"_{g}"])
                S.dma("sync", tm_o[grow(t0 + s0):grow(t0 + s0) + 128, :], st[:], reads=[sk + f"_{g}" for g in range(4)])
        S.emit()


def rope_tables():
    t = np.arange(SEQ)
    row = (t // 64).astype(np.float64)
    col = (t % 64).astype(np.float64)
    nf = 16
    inv = 10000.0 ** (-np.arange(nf, dtype=np.float64) / nf)
    cos = np.zeros((64, SEQ), np.float32)
    sin = np.zeros((64, SEQ), np.float32)
    for a, pos in enumerate((row, col)):
        ang = (pos[None, :].astype(np.float32) * inv[:, None].astype(np.float32)).astype(np.float32)
        for half in range(2):
            cos[a * 32 + half * 16:a * 32 + half * 16 + 16] = np.cos(ang)
            sin[a * 32 + half * 16:a * 32 + half * 16 + 16] = np.sin(ang)
    return cos, sin


def rope_consts():
    rm = np.zeros((64, 64), np.float32)
    for d in range(64):
        if (d % 32) < 16:
            rm[d + 16, d] = -1.0
        else:
            rm[d - 16, d] = 1.0
    rm2 = np.zeros((128, 128), np.float32)
    rm2[:64, :64] = rm
    rm2[64:, 64:] = rm
    blk = np.zeros((128, 128), np.float32)
    blk[:64, :64] = 1.0
    blk[64:, 64:] = 1.0
    return rm2, blk


def pcol(v):
    v = np.asarray(v, np.float32)
    j = v.shape[0] // 128
    return np.ascontiguousarray(np.moveaxis(v.reshape(j, 128, *v.shape[1:]), 0, 1))


def core_bh(core):
    return core // 2, core % 2


def a_inputs(l, core, x_cur, ctx_cur, inp, consts):
    b, hh = core_bh(core)
    cos, sin = consts["rope"]
    xT = np.concatenate([x_cur[b, hh * NLAT:(hh + 1) * NLAT, :].T, ctx_cur[b, hh * NCTX:(hh + 1) * NCTX, :].T], axis=1)
    cosT = np.ones((128, NT), np.float32)
    sinT = np.zeros((128, NT), np.float32)
    cosT[:64, :NLAT] = cos[:, hh * NLAT:(hh + 1) * NLAT]; cosT[64:, :NLAT] = cosT[:64, :NLAT]
    sinT[:64, :NLAT] = sin[:, hh * NLAT:(hh + 1) * NLAT]; sinT[64:, :NLAT] = sinT[:64, :NLAT]
    return {
        "xT": np.ascontiguousarray(xT, dtype=np.float32),
        "sc": pcol(np.stack([inp["c"][b], inp["c_ctx"]], axis=1)).reshape(128, 16),
        "w_mod": inp["w_mod"][l], "b_mod": pcol(inp["b_mod"][l]), "norm1_w": pcol(inp["norm1_w"][l]), "w_in": inp["w_in"][l],
        "qkn": np.ascontiguousarray(np.stack([np.tile(inp["q_norm_w"][l], 2), np.tile(inp["k_norm_w"][l], 2)], axis=1)),
        "gate_b": np.ascontiguousarray(inp["ml_gate_b"][l].reshape(16, 1)),
        "cosT": cosT, "sinT": sinT, "rm2": consts["rm2"], "blk1": consts["blk1"],
    }


def emit_E(nc, bank, pfx, io, gcols, dbg=None):
    mix_d, mod_d, n2_d = io["mix"], io["modT"], io["norm2_w"]
    wo_d, wg_d, wu_d, wd_d = io["w_out"], io["w_gate"], io["w_up"], io["w_down"]
    with ExitStack() as es:
        S = Sched(nc, es, bank, pfx)
        k = K(S)
        modT = S.sb("modT", [128, 48, 2]); S.dma("sync", modT[:].rearrange("p j c -> p (j c)"), mod_d, writes=["modT"])
        n2 = S.sb("n2", [128, 8]); S.dma("sync", n2[:], n2_d, writes=["n2"])
        ones = S.sb("ones", [128, 128], BF16); k.memset("gpsimd", ones[:], 1.0, ["ones"])
        A2 = S.sb("A2", [128, 8, 2])
        for c in range(2):
            k.stt(A2[:, :, c], modT[:, 32:40, c], 1.0, n2[:], ALU.add, ALU.mult, ["modT", "n2"], ["A2"])
        Wo = S.sb("Wo", [128, 8, D], BF16)
        Wg = S.sb("Wg", [128, 8, D_FF], BF16)
        Wu = S.sb("Wu", [128, 8, D_FF], BF16)
        Wd = S.sb("Wd", [128, NFF, D], BF16)
        S.dma("gpsimd", Wo[:], wo_d.rearrange("(kc p) n -> p kc n", p=128), writes=["Wo"])
        wgv = wg_d.rearrange("(kc p) n -> p kc n", p=128)
        wuv = wu_d.rearrange("(kc p) n -> p kc n", p=128)
        wdv = wd_d.rearrange("(m p) n -> p m n", p=128)
        WG_KEYS = []; WU_KEYS = []; WD_KEYS = []
        for h in range(2):
            S.dma("gpsimd", Wg[:, :, h * 1408:(h + 1) * 1408], wgv[:, :, h * 1408:(h + 1) * 1408], writes=[f"Wg{h}"]); WG_KEYS.append(f"Wg{h}")
            S.dma("gpsimd", Wu[:, :, h * 1408:(h + 1) * 1408], wuv[:, :, h * 1408:(h + 1) * 1408], writes=[f"Wu{h}"]); WU_KEYS.append(f"Wu{h}")
            S.dma("gpsimd", Wd[:, h * 11:(h + 1) * 11, :], wdv[:, h * 11:(h + 1) * 11, :], writes=[f"Wd{h}"]); WD_KEYS.append(f"Wd{h}")
        TW = 256
        mv = mix_d.rearrange("(j p) t -> p j t", p=128)
        xt = [S.sb(f"xt{i}", [128, 8, TW]) for i in range(2)]
        mx = [S.sb(f"mx{i}", [128, 8, TW], BF16) for i in range(2)]
        sq = S.sb("sq", [128, 8, TW], BF16)
        tmp = S.sb("tmpn", [128, 2, TW])
        ev = S.sb("ev", [128, 2, TW])
        hT = S.sb("h2T", [128, 8, TW], BF16)
        aT = S.sb("aT", [128, NFF, TW], BF16)
        sg = S.sb("sg", [128, 2, TW], BF16); uc = S.sb("uc", [128, 2, TW], BF16)
        rs = S.sb("rs", [128, TW]); rstd = S.sb("rstd", [128, TW])
        ss_ps = S.ps("ss_ps", [128, 512])
        py = [S.ps(f"py{i}", [128, 512]) for i in range(2)]
        pg = [S.ps(f"pg{i}", [128, 512]) for i in range(2)]
        pu = [S.ps(f"pu{i}", [128, 512]) for i in range(2)]
        ny = 0; nev = 0
        tiles_all = [(u_, t0, n, which) for u_ in range(len(io["xT"])) for (t0, n, which) in token_tiles(TW)]
        def load_xm(tj):
            uj, tj0, nj, _ = tiles_all[tj]
            S.dma("sync", xt[tj % 2][:, :, :nj], io["xT"][uj].rearrange("(j p) t -> p j t", p=128)[:, :, tj0:tj0 + nj],
                  writes=[f"xt{tj % 2}_{j}" for j in range(8)])
            S.dma("gpsimd", mx[tj % 2][:, :, :nj], mv[:, :, gcols[uj](tj0):gcols[uj](tj0) + nj], writes=[f"mx{tj % 2}"])

        load_xm(0)
        for ti, (u_, t0, n, which) in enumerate(tiles_all):
            if t0 == 0:
                xov = io["xT_out"][u_].rearrange("(j p) t -> p j t", p=128)
            xb = xt[ti % 2]; xk = f"xt{ti % 2}"; mb = mx[ti % 2]; mk = f"mx{ti % 2}"
            if ti + 1 < len(tiles_all):
                load_xm(ti + 1)
            for j in range(8):
                ps = py[ny % 2]; pk = f"py{ny % 2}"; ny += 1
                for kc in range(8):
                    k.mm(ps[:, :n], Wo[:, kc, j * 128:(j + 1) * 128], mb[:, kc, :n], kc == 0, kc == 7, ["Wo", mk], [pk])
                e_ = ev[:, nev % 2, :n]; ek = f"ev{nev % 2}"; nev += 1
                k.act(e_, ps[:, :n], AF.Identity, [pk, "modT"], [ek], scale=modT[:, 16 + j, which:which + 1])
                k.tt(xb[:, j, :n], xb[:, j, :n], e_, ALU.add, [xk + f"_{j}", ek], [xk + f"_{j}"], eng="gpsimd")
            xkeys = [xk + f"_{j}" for j in range(8)]
            k.act(sq[:, :, :n], xb[:, :, :n], AF.Square, xkeys, ["sq"])
            for j in range(8):
                k.mm(ss_ps[:, :n], ones[:], sq[:, j, :n], j == 0, j == 7, ["ones", "sq"], ["ss_ps"])
            k.act(rs[:, :n], ss_ps[:, :n], AF.Sqrt, ["ss_ps"], ["rs"], bias=EPS, scale=1.0 / D)
            k.recip(rstd[:, :n], rs[:, :n], ["rs"], ["rstd"])
            for j in range(8):
                k.stt(tmp[:, j % 2, :n], xb[:, j, :n], A2[:, j, which:which + 1], rstd[:, :n], ALU.mult, ALU.mult,
                      [xk + f"_{j}", "A2", "rstd"], [f"tmpn{j % 2}"])
                k.act(hT[:, j, :n], tmp[:, j % 2, :n], AF.Identity, [f"tmpn{j % 2}", "modT"], [f"h2T_{j}"],
                      bias=modT[:, 24 + j, which:which + 1], scale=1.0)
            hkeys = [f"h2T_{j}" for j in range(8)]
            for m in range(NFF):
                g_ = pg[m % 2]; gk = f"pg{m % 2}"; u_ = pu[m % 2]; uk = f"pu{m % 2}"
                for kc in range(8):
                    k.mm(g_[:, :n], Wg[:, kc, m * 128:(m + 1) * 128], hT[:, kc, :n], kc == 0, kc == 7, WG_KEYS + hkeys, [gk])
                for kc in range(8):
                    k.mm(u_[:, :n], Wu[:, kc, m * 128:(m + 1) * 128], hT[:, kc, :n], kc == 0, kc == 7, WU_KEYS + hkeys, [uk])
                k.act(sg[:, m % 2, :n], g_[:, :n], AF.Silu, [gk], [f"sg{m % 2}"])
                k.act(uc[:, m % 2, :n], u_[:, :n], AF.Copy, [uk], [f"uc{m % 2}"])
                k.tt(aT[:, m, :n], sg[:, m % 2, :n], uc[:, m % 2, :n], ALU.mult, [f"sg{m % 2}", f"uc{m % 2}"], [f"aT{m}"])
            akeys = [f"aT{m}" for m in range(NFF)]
            for j in range(8):
                ps = py[ny % 2]; pk = f"py{ny % 2}"; ny += 1
                for m in range(NFF):
                    k.mm(ps[:, :n], Wd[:, m, j * 128:(j + 1) * 128], aT[:, m, :n], m == 0, m == NFF - 1, WD_KEYS + akeys, [pk])
                e_ = ev[:, nev % 2, :n]; ek = f"ev{nev % 2}"; nev += 1
                k.act(e_, ps[:, :n], AF.Identity, [pk, "modT"], [ek], scale=modT[:, 40 + j, which:which + 1])
                k.tt(xb[:, j, :n], xb[:, j, :n], e_, ALU.add, [xk + f"_{j}", ek], [xk + f"_{j}"], eng="gpsimd")
            S.dma("sync", xov[:, :, t0:t0 + n], xb[:, :, :n], reads=xkeys)
        S.emit()


def e_inputs(l, core, x_cur, ctx_cur, mixT_core, modT_core, inp):
    b, hh = core_bh(core)
    xT = np.concatenate([x_cur[b, hh * NLAT:(hh + 1) * NLAT, :].T, ctx_cur[b, hh * NCTX:(hh + 1) * NCTX, :].T], axis=1)
    return {"xT": np.ascontiguousarray(xT, dtype=np.float32), "mixT": np.ascontiguousarray(mixT_core, dtype=np.float32),
            "modT": modT_core, "norm2_w": pcol(inp["norm2_w"][l]), "w_out": inp["w_out"][l],
            "w_gate": inp["ffn_w_gate"][l], "w_up": inp["ffn_w_up"][l], "w_down": inp["ffn_w_down"][l]}


def band_mask():
    s = np.arange(128)[:, None]
    t = np.arange(384)[None, :]
    return ((t - s >= 0) & (t - s <= 256)).astype(np.float32)


def emit_att(nc, bank, pfx, io, dbg=None):
    qT_d, kT_d, sink_d, mask_d, out_d = io["qT"], io["kT"], io["sink"], io["mask"], io["attT"]
    NB = SEQ // 128
    with ExitStack() as es:
        S = Sched(nc, es, bank, pfx)
        k = K(S)
        kT = S.sb("kT", [64, NTOK], BF16); S.dma("gpsimd", kT[:], kT_d, writes=["kT"])
        qT = S.sb("qT", [64, 3, NTOK], BF16)
        S.dma("gpsimd", qT[:], qT_d.rearrange("(h d) t -> d h t", d=64), writes=["qT"])
        V = S.sb("V", [128, 34, 64], BF16)
        S.dma("gpsimd", V[:, 0:NB, :], io["v_lat"].rearrange("(j p) d -> p j d", p=128), writes=["V"])
        S.dma("gpsimd", V[:, NB:NB + 2, :], io["v_ctx"].rearrange("(j p) d -> p j d", p=128), writes=["V"])
        ones64 = S.sb("ones64", [128, 64], BF16); k.memset("gpsimd", ones64[:], 1.0, ["ones64"])
        mask = S.sb("mask", [128, 384], BF16); S.dma("gpsimd", mask[:], mask_d, writes=["mask"])
        sink = S.sb("sink", [64, 3]); S.dma("sync", sink[:], sink_d, writes=["sink"])
        esink = S.sb("esink", [64, 3]); k.act(esink[:], sink[:], AF.Exp, ["sink"], ["esink"])
        P = [S.sb(f"P{r}", [128, 3, 384], BF16) for r in range(4)]
        Pe = [S.sb(f"Pe{r}", [128, 384], BF16) for r in range(2)]
        Pc = [S.sb(f"Pc{r}", [128, 384], BF16) for r in range(4)]
        psc = [S.ps(f"psc{i}", [128, 512]) for i in range(3)]
        po = [S.ps(f"po{i}", [128, 512]) for i in range(2)]
        pd = [S.ps(f"pd{i}", [128, 512]) for i in range(2)]
        den = S.sb("den", [64, 384]); rden = S.sb("rden", [64, 384])
        ost = [S.sb(f"ost{i}", [64, 384]) for i in range(2)]
        outv = out_d.rearrange("(h d) t -> d h t", d=64)
        cnt = {"sc": 0, "pe": 0, "pv": 0, "pc": 0}

        def local_scores(j):
            b0 = max(j - 1, 0); b1 = min(j + 1, NB - 1)
            q0 = b0 * 128; ln = (b1 - b0 + 1) * 128; mc0 = (b0 - (j - 1)) * 128
            Pj = P[j % 4]; pk_ = f"P{j % 4}"
            for h in range(3):
                ps = psc[cnt["sc"] % 3]; sk = f"psc{cnt['sc'] % 3}"; cnt["sc"] += 1
                k.mm(ps[:, :ln], kT[:, j * 128:(j + 1) * 128], qT[:, h, q0:q0 + ln], True, True, ["kT", "qT"], [sk])
                pe = Pe[cnt["pe"] % 2]; ek = f"Pe{cnt['pe'] % 2}"; cnt["pe"] += 1
                k.act(pe[:, :ln], ps[:, :ln], AF.Exp, [sk], [ek], scale=0.125)
                k.tt(Pj[:, h, mc0:mc0 + ln], pe[:, :ln], mask[:, mc0:mc0 + ln], ALU.mult, [ek, "mask"], [pk_ + f"_{h}"], eng="gpsimd")

        def ctx_scores(qtok0):
            res = []
            for c in range(2):
                ps = psc[cnt["sc"] % 3]; sk = f"psc{cnt['sc'] % 3}"; cnt["sc"] += 1
                k.mm(ps[:, :384].rearrange("p (h t) -> p h t", h=3), kT[:, SEQ + c * 128:SEQ + (c + 1) * 128],
                     qT[:, :, qtok0:qtok0 + 128], True, True, ["kT", "qT"], [sk])
                pc = Pc[cnt["pc"] % 4]; ck = f"Pc{cnt['pc'] % 4}"; cnt["pc"] += 1
                k.act(pc[:, :], ps[:, :384], AF.Exp, [sk], [ck], scale=0.125)
                res.append((pc[:, :].rearrange("p (h t) -> p h t", h=3), NB + c, [ck]))
            return res

        def pv(qtok0, terms):
            i_ = cnt["pv"]; cnt["pv"] += 1
            o_ = po[i_ % 2]; ok = f"po{i_ % 2}"; d_ = pd[i_ % 2]; dk = f"pd{i_ % 2}"
            ov = o_[:64, :384].rearrange("p (h t) -> p h t", h=3)
            dv = d_[:64, :384].rearrange("p (h t) -> p h t", h=3)
            nt = len(terms)
            for ti, (pap, vb, keys) in enumerate(terms):
                k.mm(ov, V[:, vb, :], pap, ti == 0, ti == nt - 1, ["V"] + keys, [ok])
            for ti, (pap, vb, keys) in enumerate(terms):
                k.mm(dv, ones64[:], pap, ti == 0, ti == nt - 1, ["ones64"] + keys, [dk])
            for h in range(3):
                k.ts(den[:, h * 128:(h + 1) * 128], d_[:64, h * 128:(h + 1) * 128], esink[:, h:h + 1], None, ALU.add, None,
                     [dk, "esink"], ["den"])
            k.recip(rden[:], den[:], ["den"], ["rden"])
            st = ost[i_ % 2]; sk = f"ost{i_ % 2}"
            k.tt(st[:], o_[:64, :384], rden[:], ALU.mult, [ok, "rden"], [sk])
            S.dma("sync", outv[:, :, qtok0:qtok0 + 128], st[:].rearrange("p (h t) -> p h t", h=3), reads=[sk])

        def local_terms(i):
            terms = []
            for j, c0 in ((i - 1, 256), (i, 128), (i + 1, 0)):
                if 0 <= j < NB:
                    terms.append((P[j % 4][:, :, c0:c0 + 128], j, [f"P{j % 4}_{h}" for h in range(3)]))
            return terms

        nblk = NB if dbg is None else 3
        for j in range(nblk):
            local_scores(j)
            if j >= 1:
                i = j - 1
                pv(i * 128, local_terms(i) + ctx_scores(i * 128))
        if dbg is None:
            i = NB - 1
            pv(i * 128, local_terms(i) + ctx_scores(i * 128))
            for cq in range(2):
                pv(SEQ + cq * 128, ctx_scores(SEQ + cq * 128))
        S.emit()


def att_inputs(l, core, fm_all, tm_all, inp, consts):
    b, g = core_bh(core)
    f0, f1 = fm_all[2 * b], fm_all[2 * b + 1]
    t0, t1 = tm_all[2 * b], tm_all[2 * b + 1]

    def seq_fm(rows):
        return np.concatenate([f0[rows, :NLAT], f1[rows, :NLAT], f0[rows, NLAT:], f1[rows, NLAT:]], axis=1)

    def seq_tm(cols):
        return np.concatenate([t0[:NLAT, cols], t1[:NLAT, cols], t0[NLAT:, cols], t1[NLAT:, cols]], axis=0)

    return {"qT": np.ascontiguousarray(seq_fm(slice(192 * g, 192 * g + 192))),
            "kT": np.ascontiguousarray(seq_fm(slice(384 + 64 * g, 384 + 64 * g + 64))),
            "v": np.ascontiguousarray(seq_tm(slice(64 * g, 64 * g + 64))),
            "sink": np.ascontiguousarray(np.tile(inp["attn_sink"][l][3 * g:3 * g + 3][None, :], (64, 1)).astype(np.float32)),
            "mask": consts["mask"]}


NCH = NTOK // 128


def ml_consts():
    sel = np.zeros((34, 4, 128), np.float32)
    for i, r in enumerate((0, 1, 32, 33)):
        sel[r, i, :] = 1.0
    ident = np.eye(128, dtype=np.float32)
    s = np.arange(128)[:, None]; t = np.arange(128)[None, :]
    return {"sel": sel.reshape(34, 512), "ident": ident, "maskf": (s <= t).astype(np.float32), "maskb": (s >= t).astype(np.float32)}


def proc_chunk(dr, c):
    if dr == 0:
        return 32 + c if c < 2 else c - 2
    return 33 - c if c < 2 else 31 - (c - 2)


def emit_ml(nc, bank, pfx, io, dbg=None):
    qT_d, kT_d, nw_d, sel_d, id_d, mf_d, mb_d, out_d, scr = (io["qT"], io["kT"], io["nw_bc"], io["sel"], io["ident"],
                                                            io["maskf"], io["maskb"], io["mloT"], io["scr"])
    g4 = io["g4"]
    with ExitStack() as es:
        S = Sched(nc, es, bank, pfx)
        k = K(S)
        qT = S.sb("qT", [128, NTOK], BF16); S.dma("gpsimd", qT[:], qT_d, writes=["qT"])
        kT = S.sb("kT", [128, NTOK], BF16); S.dma("gpsimd", kT[:], kT_d, writes=["kT"])
        ktok = S.sb("ktok", [128, NCH, 128], BF16)
        S.dma("gpsimd", ktok[:, 0:32, :], io["ktok"][0].rearrange("(c p) d -> p c d", p=128), writes=["ktok"])
        S.dma("gpsimd", ktok[:, 32:34, :], io["ktok"][1].rearrange("(c p) d -> p c d", p=128), writes=["ktok"])
        Vaug = S.sb("Vaug", [128, NCH, 2, 65], BF16)
        k.memset("gpsimd", Vaug[:], 1.0, ["Vaug"])
        for h_ in range(2):
            S.dma("gpsimd", Vaug[:, 0:32, h_, 0:64], io["vtok"][0][:, h_ * 64:(h_ + 1) * 64].rearrange("(c p) d -> p c d", p=128), writes=["Vaug"])
            S.dma("gpsimd", Vaug[:, 32:34, h_, 0:64], io["vtok"][1][:, h_ * 64:(h_ + 1) * 64].rearrange("(c p) d -> p c d", p=128), writes=["Vaug"])
        otok = S.sb("otok", [128, NCH, 128])
        S.dma("sync", otok[:, 0:32, :], io["otok"][0].rearrange("(c p) d -> p c d", p=128), writes=["otok"])
        S.dma("sync", otok[:, 32:34, :], io["otok"][1].rearrange("(c p) d -> p c d", p=128), writes=["otok"])
        nwb = S.sb("nwb", [128, 128]); S.dma("sync", nwb[:], nw_d, writes=["nwb"])
        sel = S.sb("sel", [34, 512]); S.dma("sync", sel[:], sel_d, writes=["sel"])
        ident = S.sb("ident", [128, 128]); S.dma("sync", ident[:], id_d, writes=["ident"])
        masks = []
        for nm, d_ in (("maskf", mf_d), ("maskb", mb_d)):
            m_ = S.sb(nm, [128, 128], BF16); S.dma("gpsimd", m_[:], d_, writes=[nm]); masks.append(m_)
        LI = S.sb("LI", [34, NTOK]); FF = S.sb("FF", [34, NTOK]); TM = S.sb("TM", [34, NTOK]); BN = S.sb("BN", [34, NTOK])
        for t_, nm in ((LI, "LI"), (FF, "FF"), (TM, "TM")):
            k.memset("gpsimd", t_[:], 0.0, [nm])
        S.dma("sync", LI[0:2, 0:CTX], g4[0][:, SEQ:NTOK], writes=["LI"]); S.dma("sync", LI[0:2, CTX:NTOK], g4[0][:, 0:SEQ], writes=["LI"], reads=["LI"])
        S.dma("sync", FF[0:2, 0:CTX], g4[1][:, SEQ:NTOK], writes=["FF"]); S.dma("sync", FF[0:2, CTX:NTOK], g4[1][:, 0:SEQ], writes=["FF"], reads=["FF"])

        def rev_rows(t_):
            return bass.AP(t_, 32 * NTOK + NTOK - 1, [[NTOK, 2], [-1, NTOK]])

        def revc_rows(t_):
            return bass.AP(t_, 32 * NTOK + 127, [[NTOK, 2], [128, NCH], [-1, 128]])

        S.dma("sync", TM[32:34, :], g4[2], writes=["TM"], reads=["TM"])
        k.copy("vector", LI[32:34, :], rev_rows(TM), ["TM", "LI"], ["LI"])
        S.dma("sync", TM[32:34, :], g4[3], writes=["TM"], reads=["TM"])
        k.copy("vector", FF[32:34, :], rev_rows(TM), ["TM", "FF"], ["FF"])
        k.act(FF[:], FF[:], AF.Exp, ["FF"], ["FF"], scale=-1.0)
        k.act(FF[:], FF[:], AF.Ln, ["FF"], ["FF"], bias=1.0, scale=1.0)
        S.op("vector", lambda e: e.tensor_tensor_scan(out=BN[:], data0=FF[:], data1=FF[:], initial=0.0, op0=ALU.add, op1=ALU.bypass),
             reads=["FF"], writes=["BN"])
        k.tt(LI[:], LI[:], BN[:], ALU.add, ["LI", "BN"], ["LI"])
        S.op("vector", lambda e: e.tensor_tensor_scan(out=TM[:], data0=LI[:], data1=LI[:], initial=0.0, op0=ALU.max, op1=ALU.bypass),
             reads=["LI", "TM"], writes=["TM"])
        MC = S.sb("MC", [34, NCH]); MP = S.sb("MP", [34, NCH]); DEC = S.sb("DEC", [34, NCH])
        k.copy("vector", MC[:], TM[:, 127::128], ["TM"], ["MC"])
        k.memset("vector", MP[:], 0.0, ["MP"])
        k.copy("vector", MP[:, 1:NCH], MC[:, 0:NCH - 1], ["MC", "MP"], ["MP"])
        k.tt(DEC[:], MP[:], MC[:], ALU.subtract, ["MP", "MC"], ["DEC"])
        k.act(DEC[:], DEC[:], AF.Exp, ["DEC"], ["DEC"])
        mcb = MC[:, :].unsqueeze(2).broadcast_to([34, NCH, 128])
        li3 = LI[:, :].rearrange("p (c s) -> p c s", s=128); bn3 = BN[:, :].rearrange("p (c s) -> p c s", s=128)
        k.tt(li3, li3, mcb, ALU.subtract, ["LI", "MC"], ["LI"])
        k.act(LI[:], LI[:], AF.Exp, ["LI"], ["LI"])
        k.tt(bn3, bn3, mcb, ALU.subtract, ["BN", "MC"], ["BN"])
        k.act(BN[:], BN[:], AF.Exp, ["BN"], ["BN"])
        COL = S.sb("COL", [128, 2, 4, NCH])
        ptr = S.ps("ptr", [128, 512])
        T68 = S.sb("T68", [68, 128])
        for qi, (src, nm) in enumerate(((LI, "LI"), (BN, "BN"))):
            k.copy("vector", FF[32:34, :].rearrange("p (c s) -> p c s", s=128), revc_rows(src), [nm, "FF"], ["FF"])
            S.dma("sync", scr[qi, 0:2, :], src[0:2, :], reads=[nm], writes=[f"scr{qi}"])
            S.dma("sync", scr[qi, 2:4, :], FF[32:34, :], reads=["FF", f"scr{qi}"], writes=[f"scr{qi}"])
            for dr in range(2):
                S.dma("sync", T68[:], scr[qi, 2 * dr:2 * dr + 2, :].rearrange("r (c s) -> (r c) s", s=128), reads=[f"scr{qi}"], writes=["T68"])
                S.op("tensor", lambda e: e.transpose(ptr[:, 0:68], T68[:], ident[0:68, 0:68]), reads=["T68", "ident"], writes=["ptr"])
                k.copy("vector", COL[:, qi, 2 * dr:2 * dr + 2, :], ptr[:, 0:68].rearrange("p (r c) -> p r c", r=2), ["ptr"], ["COL"])
        DECB = S.sb("DECB", [128, 4, NCH])
        for i in range(4):
            k.mm(ptr[:, 0:NCH], sel[:, i * 128:(i + 1) * 128], DEC[:], True, True, ["sel", "DEC", "ptr"], ["ptr"])
            k.copy("vector", DECB[:, i, :], ptr[:, 0:NCH], ["ptr"], ["DECB"])
        Cst = S.sb("Cst", [128, 4, 65]); Cd = S.sb("Cd", [128, 4, 65]); Cdb = S.sb("Cdb", [128, 4, 65], BF16)
        k.memset("vector", Cd[:], 0.0, ["Cd0", "Cd1", "Cd2", "Cd3"])
        k.memset("vector", Cdb[:], 0.0, ["Cdb0", "Cdb1", "Cdb2", "Cdb3"])
        HH = [S.sb("HF", [128, NCH, 128]), S.sb("HB", [128, NCH, 128])]
        pA = [S.ps(f"pA{i}", [128, 512]) for i in range(2)]
        pN = [S.ps(f"pN{i}", [128, 512]) for i in range(2)]
        pC = [S.ps(f"pC{i}", [128, 512]) for i in range(2)]
        pt1 = [S.sb(f"pt1_{i}", [128, 128], BF16) for i in range(2)]
        PT = [S.sb(f"PT{i}", [128, 128], BF16) for i in range(2)]
        VU = [S.sb(f"VU{i}", [128, 65], BF16) for i in range(2)]
        dcol = S.sb("dcol", [128, 2]); rcol = S.sb("rcol", [128, 2]); dabs = S.sb("dabs", [128, 2])
        n = 0
        nproc = NCH if dbg is None else 4
        for c in range(nproc):
            for i in range(4):
                dr, h = i // 2, i % 2
                ncn = proc_chunk(dr, c)
                tok0 = ncn * 128
                hs = slice(h * 64, (h + 1) * 64)
                r = n % 2; n += 1
                qc = qT[hs, tok0:tok0 + 128]; kc = kT[hs, tok0:tok0 + 128]
                k.mm(pA[r][:, 0:128], kc, qc, True, True, ["kT", "qT"], [f"pA{r}"])
                k.act(pt1[r][:], pA[r][:, 0:128], AF.Identity, [f"pA{r}", "COL"], [f"pt1_{r}"], scale=COL[:, 0, i, c:c + 1])
                k.tt(PT[r][:], pt1[r][:], masks[dr][:], ALU.mult, [f"pt1_{r}", "maskf", "maskb"], [f"PT{r}"], eng="gpsimd")
                k.act(VU[r][:], Vaug[:, ncn, h, :], AF.Identity, ["Vaug", "COL"], [f"VU{r}"], scale=COL[:, 0, i, c:c + 1])
                k.mm(pN[r][:, 0:65], PT[r][:], Vaug[:, ncn, h, :], True, False, [f"PT{r}", "Vaug"], [f"pN{r}"])
                k.mm(pN[r][:, 0:65], qc, Cdb[hs, i, :], False, True, ["qT", f"Cdb{i}"], [f"pN{r}"])
                k.act(dabs[:, r:r + 1], pN[r][:, 64:65], AF.Abs, [f"pN{r}"], [f"dabs{r}"])
                k.tt(dcol[:, r:r + 1], dabs[:, r:r + 1], COL[:, 1, i, c:c + 1], ALU.max, [f"dabs{r}", "COL"], [f"dcol{r}"])
                k.recip(rcol[:, r:r + 1], dcol[:, r:r + 1], [f"dcol{r}"], [f"rcol{r}"])
                k.act(HH[dr][:, ncn, hs], pN[r][:, 0:64], AF.Identity, [f"pN{r}", f"rcol{r}"], [f"H{dr}_{ncn}_{h}"], scale=rcol[:, r:r + 1])
                k.mm(pC[r][:, 0:65], ktok[:, ncn, :], VU[r][:], True, True, ["ktok", f"VU{r}"], [f"pC{r}"])
                k.tt(Cst[hs, i, :], Cd[hs, i, :], pC[r][hs, 0:65], ALU.add, [f"Cd{i}", f"pC{r}"], [f"Cst{i}"])
                if c + 1 < nproc:
                    k.ts(Cd[hs, i, :], Cst[hs, i, :], DECB[hs, i, c + 1:c + 2], None, ALU.mult, None, [f"Cst{i}", "DECB"], [f"Cd{i}"])
                    k.copy("scalar", Cdb[hs, i, :], Cd[hs, i, :], [f"Cd{i}"], [f"Cdb{i}"])
        hsum = [S.sb(f"hsum{i}", [128, 128]) for i in range(2)]
        hsq = S.sb("hsq", [128, 128]); ssq = S.sb("ssq", [128, 2]); rsq = S.sb("rsq", [128, 2]); rin = S.sb("rin", [128, 2])
        hn = S.sb("hn", [128, 128]); sgo = S.sb("sgo", [128, 128])
        ost = [S.sb(f"ost{i}", [128, 128]) for i in range(2)]
        ostT = [S.sb(f"ostT{i}", [128, 128]) for i in range(2)]
        chunks = range(NCH) if dbg is None else [32, 33, 0, 1]
        for j, ncn in enumerate(chunks):
            r = j % 2
            hk = [f"H{dr}_{ncn}_{h}" for dr in range(2) for h in range(2)]
            k.tt(hsum[r][:], HH[0][:, ncn, :], HH[1][:, ncn, :], ALU.add, hk, [f"hsum{r}"], eng="gpsimd")
            k.tt(hsq[:], hsum[r][:], hsum[r][:], ALU.mult, [f"hsum{r}"], ["hsq"], eng="gpsimd")
            S.op("vector", lambda e, a=hsq, b=ssq: e.tensor_reduce(out=b[:], in_=a[:].rearrange("p (h d) -> p h d", h=2), axis=AX.X, op=ALU.add),
                 reads=["hsq"], writes=["ssq"])
            k.act(rsq[:], ssq[:], AF.Sqrt, ["ssq"], ["rsq"], bias=EPS, scale=1.0 / 64)
            k.recip(rin[:], rsq[:], ["rsq"], ["rin"])
            for h in range(2):
                k.act(hn[:, h * 64:(h + 1) * 64], hsum[r][:, h * 64:(h + 1) * 64], AF.Identity, [f"hsum{r}", "rin"], [f"hn{h}"], scale=rin[:, h:h + 1])
            k.act(sgo[:], otok[:, ncn, :], AF.Sigmoid, ["otok"], ["sgo"])
            k.tt(hn[:], hn[:], nwb[:], ALU.mult, ["hn0", "hn1", "nwb"], ["hn0", "hn1"])
            k.tt(ost[r][:], hn[:], sgo[:], ALU.mult, ["hn0", "hn1", "sgo"], [f"ost{r}"])
            S.op("tensor", lambda e, r=r: e.transpose(ptr[:, 0:128], ost[r][:], ident[:]), reads=[f"ost{r}", "ident", "ptr"], writes=["ptr"])
            k.copy("scalar", ostT[r][:], ptr[:, 0:128], ["ptr"], [f"ostT{r}"])
            S.dma("sync", out_d[:, ncn * 128:(ncn + 1) * 128], ostT[r][:], reads=[f"ostT{r}"])
        S.emit()


def seq_fm(fm_all, b, rows):
    f0, f1 = fm_all[2 * b], fm_all[2 * b + 1]
    return np.ascontiguousarray(np.concatenate([f0[rows, :NLAT], f1[rows, :NLAT], f0[rows, NLAT:], f1[rows, NLAT:]], axis=1))


def seq_tm(tm_all, b, cols):
    t0, t1 = tm_all[2 * b], tm_all[2 * b + 1]
    return np.ascontiguousarray(np.concatenate([t0[:NLAT, cols], t1[:NLAT, cols], t0[NLAT:, cols], t1[NLAT:, cols]], axis=0))


def ml_inputs(l, core, fm_all, tm_all, inp, consts):
    b, hp = core_bh(core)
    grow = [1024 + 2 * hp, 1024 + 2 * hp + 1, 1028 + 2 * hp, 1028 + 2 * hp + 1, 1032 + 2 * hp, 1032 + 2 * hp + 1, 1036 + 2 * hp, 1036 + 2 * hp + 1]
    d = {"qT": seq_fm(fm_all, b, slice(512 + 128 * hp, 512 + 128 * hp + 128)),
         "kT": seq_fm(fm_all, b, slice(768 + 128 * hp, 768 + 128 * hp + 128)),
         "ktok": seq_tm(tm_all, b, slice(128 + 128 * hp, 128 + 128 * hp + 128)),
         "vtok": seq_tm(tm_all, b, slice(384 + 128 * hp, 384 + 128 * hp + 128)),
         "otok": seq_tm(tm_all, b, slice(640 + 128 * hp, 640 + 128 * hp + 128)),
         "gT": seq_fm(fm_all, b, grow),
         "nw_bc": np.ascontiguousarray(np.tile(inp["ml_norm_w"][l][128 * hp:128 * hp + 128][None, :], (128, 1)).astype(np.float32))}
    d.update(consts["ml"])
    return d


HY_CFG = {"lat": dict(L=SEQ, B=1024, nb=4), "ctx": dict(L=CTX, B=256, nb=1)}


def dft_tables(B):
    t = np.arange(B, dtype=np.float64)[:, None]
    om = np.pi * (2 * np.arange(B, dtype=np.float64)[None, :] + 1) / (2 * B)
    return np.cos(t * om).astype(np.float32), np.sin(t * om).astype(np.float32)


def pos_feats(L):
    t = np.linspace(0.0, 1.0, L, dtype=np.float32)[:, None]
    ang = (np.float32(2.0 * math.pi / L) * np.arange(L, dtype=np.float32))[:, None]
    bands = np.linspace(1e-4, 15, 16, dtype=np.float32)[None, :]
    feats = np.concatenate([t, np.cos(bands * ang), -np.sin(bands * ang)], axis=-1).astype(np.float32)
    return feats, t[:, 0]


def hy_consts():
    c = {}
    for nm, cfg in HY_CFG.items():
        L, B = cfg["L"], cfg["B"]
        TC, TS = dft_tables(B)
        feats, t = pos_feats(L)
        c[nm] = {"TC": TC, "TS": TS, "TCT": np.ascontiguousarray(TC.T), "TST": np.ascontiguousarray(TS.T),
                 "featsT": np.ascontiguousarray(feats.T), "featsTr": np.ascontiguousarray(feats[::-1].T),
                 "negt": np.ascontiguousarray((-t).reshape(L // 128, 128).T), "negtr": np.ascontiguousarray((-t[::-1]).reshape(L // 128, 128).T)}
    alt = np.where(np.arange(128) % 2 == 0, 1.0, -1.0).astype(np.float32).reshape(128, 1)
    c["alt"] = alt
    return c


def emit_F(nc, bank, pfx, io0):
    w1_d, w2_d, fb_d, alt_d = io0["w1"], io0["w2"], io0["fb"], io0["alt"]
    io = io0
    with ExitStack() as es:
        S = Sched(nc, es, bank, pfx)
        k = K(S)
        w1 = S.sb("w1", [33, 64]); S.dma("sync", w1[:], w1_d, writes=["w1"])
        w2 = S.sb("w2", [64, 64]); S.dma("sync", w2[:], w2_d, writes=["w2"])
        w3 = S.sb("w3", [64, 768])
        fb = S.sb("fb", [64, 4]); S.dma("sync", fb[:], fb_d, writes=["fb"])
        fbb = S.sb("fbb", [64, 2])
        k.ts(fbb[:], fb[:, 1:3], fb[:, 0:1], None, ALU.mult, None, ["fb"], ["fbb"])
        adec = S.sb("adec", [128, 768])
        alt = S.sb("alt", [128, 1]); S.dma("sync", alt[:], alt_d, writes=["alt"])
        pz = [S.ps(f"pz{i}", [128, 512]) for i in range(2)]
        pP = [S.ps(f"pP{i}", [128, 512]) for i in range(4)]
        TWO_PI = 2.0 * math.pi
        LM, BM, NBM = SEQ, 1024, 4
        altB = S.sb("altB", [128, 1])
        wm = S.sb("wm", [64, 512])
        negt_s = S.sb("negt", [128, LM // 128]); negtr_s = S.sb("negtr", [128, LM // 128])
        TC_s = S.sb("TC", [128, BM // 128, BM], BF16); TS_s = S.sb("TS", [128, BM // 128, BM], BF16)
        Gt_s = [S.sb(f"Gt{dr}", [128, LM // 128, 384], BF16) for dr in range(2)]
        feats_s = S.sb("feats", [33, LM]); z1_s = S.sb("z1", [64, LM]); z2_s = [S.sb(f"z2_{dr}", [64, LM]) for dr in range(2)]
        arg = S.sb("arg", [64, 512]); fsb = S.sb("fsb", [128, 384]); dct = S.sb("dct", [128, 384])
        XY_s = S.sb("XY", [128, 2 * NBM, 4, 384])
        gst = [S.sb(f"gst{i}", [128, 384]) for i in range(1)]
        for nm, cfg in HY_CFG.items():
            L, B, nb = cfg["L"], cfg["B"], cfg["nb"]
            d = io[nm]
            nt = L // 128; ntb = B // 128
            k.ts(altB[:], alt[:], 1.0 / B, None, ALU.mult, None, ["alt"], ["altB"])
            negt = negt_s[:, 0:nt]; negtr = negtr_s[:, 0:nt]
            S.dma("sync", negt, d["negt"], writes=["negt"]); S.dma("sync", negtr, d["negtr"], writes=["negtr"])
            TC = TC_s[:, 0:ntb, 0:B]; TS = TS_s[:, 0:ntb, 0:B]
            S.dma("gpsimd", TC, d["TC"].rearrange("(a p) k -> p a k", p=128), writes=["TC"])
            S.dma("gpsimd", TS, d["TS"].rearrange("(a p) k -> p a k", p=128), writes=["TS"])
            Gt = [Gt_s[dr][:, 0:nt, :] for dr in range(2)]
            feats = feats_s[:, 0:L]; z1 = z1_s[:, 0:L]
            XY = XY_s[:, 0:2 * nb]
            for dr in range(2):
                z2 = z2_s[dr][:, 0:L]
                S.dma("sync", feats, d["featsr" if dr else "feats"], writes=["feats"])
                for si, (src, w_, bcol, dst, K_) in enumerate(((feats, w1, 0, z1, 33), (z1, w2, 1, z2, 64))):
                    for c0 in range(0, L, 512):
                        n = min(512, L - c0)
                        ps = pz[(c0 // 512) % 2]; pk = f"pz{(c0 // 512) % 2}"
                        k.mm(ps[:64, :n], w_[:K_, :], src[:K_, c0:c0 + n], True, True, ["w1", "w2", "feats", "z1"], [pk])
                        k.act(arg[:, :n], ps[:64, :n], AF.Identity, [pk, "fb", "fbb"], ["arg"], scale=fb[:, 0:1], bias=fbb[:, bcol:bcol + 1])
                        for _ in range(2):
                            for (cmp_, val, sh) in ((ALU.is_gt, math.pi, -TWO_PI), (ALU.is_lt, -math.pi, TWO_PI)):
                                k.ts(wm[:, :n], arg[:, :n], val, None, cmp_, None, ["arg"], ["wm"])
                                k.stt(arg[:, :n], wm[:, :n], sh, arg[:, :n], ALU.mult, ALU.add, ["wm", "arg"], ["arg"])
                        k.act(dst[:, c0:c0 + n], arg[:, :n], AF.Sin, ["arg"], ["z1" if si == 0 else f"z2_{dr}"])
            for hh in range(len(io0["w3c"])):
                S.dma("sync", w3[:], io0["w3c"][hh], writes=["w3"])
                S.dma("sync", adec[:], io0["decay_bc"][hh], writes=["adec"])
                k.act(adec[:], adec[:], AF.Abs, ["adec"], ["adec"])
                for dr in range(2):
                    z2 = z2_s[dr][:, 0:L]
                    tcol = negtr if dr else negt
                    for mt in range(nt):
                        ps = pz[mt % 2]; pk = f"pz{mt % 2}"
                        k.mm(ps[:, :384], z2[:, mt * 128:(mt + 1) * 128], w3[:, dr * 384:(dr + 1) * 384], True, True, [f"z2_{dr}", "w3"], [pk])
                        k.act(dct[:], adec[:, dr * 384:(dr + 1) * 384], AF.Exp, ["adec", "negt", "negtr"], ["dct"], scale=tcol[:, mt:mt + 1])
                        k.copy("scalar", fsb[:], ps[:, :384], [pk], ["fsb"])
                        k.tt(Gt[dr][:, mt, :], fsb[:], dct[:], ALU.mult, ["fsb", "dct"], [f"Gt{dr}_{mt // ntb}"], eng="gpsimd")
                npp = 0; ng = 0
                for kt in range(ntb):
                    for ei in range(2 * nb):
                        src = Gt[1] if ei < nb else Gt[0]
                        base = (ei if ei < nb else ei - nb) * ntb
                        bk = f"Gt{1 if ei < nb else 0}_{ei if ei < nb else ei - nb}"
                        for ti, (T_, tk) in enumerate(((TC, "TC"), (TS, "TS"))):
                            ps = pP[npp % 4]; pk = f"pP{npp % 4}"; npp += 1
                            for tt in range(ntb):
                                k.mm(ps[:, :384], T_[:, tt, kt * 128:(kt + 1) * 128], src[:, base + tt, :], tt == 0, tt == ntb - 1, [tk, bk], [pk])
                            k.act(XY[:, ei, ti, :], ps[:, :384], AF.Identity, [pk], [f"XY{ei}_{ti}"], scale=1.0 / B)
                            k.act(XY[:, ei, 2 + ti, :], ps[:, :384], AF.Identity, [pk, "altB"], [f"XY{ei}_{2 + ti}"], scale=altB[:, 0:1])
                    for dd in range(-(nb - 1), nb):
                        ei = dd + nb; q = (nb - 1) - dd
                        for rj in range(2):
                            g_, gk = ((gst[0], "gst0"), (fsb, "fsb"), (dct, "dct"))[ng % 3]; ng += 1
                            if rj == 0:
                                k.tt(g_[:], XY[:, ei, 0, :], XY[:, ei - 1, 3, :], ALU.add, [f"XY{ei}_0", f"XY{ei - 1}_3"], [gk], eng="gpsimd")
                            else:
                                k.tt(g_[:], XY[:, ei, 1, :], XY[:, ei - 1, 2, :], ALU.subtract, [f"XY{ei}_1", f"XY{ei - 1}_2"], [gk], eng="gpsimd")
                            S.dma("sync", d["G"][hh][:, kt, rj, :, q, :].rearrange("o p c -> p o c"), g_[:].rearrange("p (o c) -> p o c", o=2), reads=[gk])
        S.emit()


def f_inputs(core, inp, consts):
    l, hh = core // 2, core % 2
    cols = []
    for dr in range(2):
        for o in range(2):
            c0 = dr * 768 + o * 384 + hh * 192
            cols.extend(range(c0, c0 + 192))
    cols = np.array(cols)
    hc = consts["hy"]
    d = {"w1": inp["hy_w1"][l], "w2": inp["hy_w2"][l], "w3c": np.ascontiguousarray(inp["hy_w3"][l][:, cols]),
         "fb": np.ascontiguousarray(np.stack([inp["hy_freq"][l], inp["hy_b1"][l], inp["hy_b2"][l], inp["hy_b2"][l]], axis=1)),
         "decay_bc": np.ascontiguousarray(np.tile(inp["hy_decay"][l][cols][None, :], (128, 1))), "alt": hc["alt"]}
    for nm in HY_CFG:
        d[f"featsT_{nm}"] = hc[nm]["featsT"]; d[f"featsTr_{nm}"] = hc[nm]["featsTr"]
        d[f"negt_{nm}"] = hc[nm]["negt"]; d[f"negtr_{nm}"] = hc[nm]["negtr"]
        d[f"TC_{nm}"] = hc[nm]["TC"]; d[f"TS_{nm}"] = hc[nm]["TS"]
    return d


def emit_hy(nc, bank, pfx, io, dbg=None):
    cw_d, sk_d, Gd, Td, out_d, id_d = io["cw_bc"], io["sk_bc"], io["G"], io["T"], io["hyoT"], io["ident"]
    with ExitStack() as es:
        S = Sched(nc, es, bank, pfx)
        k = K(S)
        cw = S.sb("cw", [128, 4, 576]); S.dma("sync", cw[:].rearrange("p a c -> p (a c)"), cw_d, writes=["cw"])
        sk = S.sb("sk", [128, 2, 192]); S.dma("sync", sk[:].rearrange("p a c -> p (a c)"), sk_d, writes=["sk"])
        VXX = S.sb("VXX", [128, NCH, 576], BF16)
        Z = S.sb("Z", [128, NCH, 192], BF16)
        tabs = {}
        for nm, cfg in HY_CFG.items():
            B = cfg["B"]; ntb = B // 128
            tabs[nm] = []
            for ti, tname in enumerate(("TC", "TS", "TCT", "TST")):
                t_ = S.sb(f"{tname}_{nm}", [128, ntb, B], BF16)
                S.dma("gpsimd", t_[:], Td[nm][ti].rearrange("(a p) k -> p a k", p=128), writes=[f"{tname}_{nm}"])
                tabs[nm].append((t_, f"{tname}_{nm}"))
        stg = [S.sb(f"stg{i}", [128, 3, 576]) for i in range(1)]
        pr = [S.sb(f"pr{i}", [128, 4, 192]) for i in range(2)]
        pb = [S.sb(f"pb{i}", [128, 4, 192], BF16) for i in range(4)]
        ct0 = pr[0][:, :, :].rearrange("p a c -> p (a c)")[:, 0:576]; ct1 = pr[1][:, :, :].rearrange("p a c -> p (a c)")[:, 0:576]
        tile_base = {"lat": 0, "ctx": SEQ // 128}
        for nm, cfg in HY_CFG.items():
            L = cfg["L"]
            for tt in range(L // 128):
                g = tile_base[nm] + tt
                s_ = stg[0]; skey = "stg0"
                tmt, pitch = io["tmT"]
                for part in range(3):
                    src = bass.AP(tmt, (io["row0"][nm] + tt * 128) * pitch + io["col0"] + part * 384, [[pitch, 128], [pitch, 3], [1, 192]])
                    S.dma("sync", s_[:, :, part * 192:(part + 1) * 192], src, writes=[skey])
                k.tt(ct0, s_[:, 0, :], cw[:, 0, :], ALU.mult, [skey, "cw"], ["pr0"])
                k.tt(ct1, s_[:, 1, :], cw[:, 1, :], ALU.mult, [skey, "cw"], ["pr1"], eng="gpsimd")
                k.tt(ct0, ct0, ct1, ALU.add, ["pr0", "pr1"], ["pr0"])
                k.tt(ct1, s_[:, 2, :], cw[:, 2, :], ALU.mult, [skey, "cw"], ["pr1"], eng="gpsimd")
                k.tt(ct0, ct0, ct1, ALU.add, ["pr0", "pr1"], ["pr0"])
                k.tt(VXX[:, g, :], ct0, cw[:, 3, :], ALU.add, ["pr0", "cw"], [f"VXX{g}"])
        psR = [S.ps(f"psR{i}", [128, 512]) for i in range(2)]
        psJ = [S.ps(f"psJ{i}", [128, 512]) for i in range(2)]
        pI = [S.ps(f"pI{i}", [128, 512]) for i in range(2)]
        NBM = 4
        RJ = [S.sb(f"RJ{i}", [128, 2, NBM, 192]) for i in range(2)]
        Gb = [S.sb(f"Gb{i}", [128, 2, 2 * NBM - 1, 192]) for i in range(1)]
        YAB = S.sb("YAB", [128, 8, 2, NBM, 192], BF16)
        et = [S.sb(f"et{i}", [128, 192]) for i in range(2)]
        ost = [S.sb(f"ost{i}", [128, 192]) for i in range(2)]
        oT = S.sb("oT", [128, 128]); ptp = S.ps("ptp", [128, 512])
        ident = S.sb("ident", [128, 128]); S.dma("sync", ident[:], id_d, writes=["ident"])
        identb = S.sb("identb", [128, 128], BF16); nidentb = S.sb("nidentb", [128, 128], BF16)
        k.act(identb[:], ident[:], AF.Identity, ["ident"], ["identb"], scale=1.0)
        k.act(nidentb[:], ident[:], AF.Identity, ["ident"], ["identb"], scale=-1.0)
        psY = S.ps("psY", [128, 512])
        cnt = {"f": 0, "g": 0, "i": 0, "e": 0}

        def conv(nm, o, src_fn, src_keys, epi):
            cfg = HY_CFG[nm]; B, nb = cfg["B"], cfg["nb"]; ntb = B // 128; nq = 2 * nb - 1
            base = tile_base[nm]
            (TC, kTC), (TS, kTS), (TCT, kTCT), (TST, kTST) = tabs[nm]
            for kt in range(ntb):
                rj = RJ[kt % 2]; rk = f"RJ{kt % 2}"
                for jp in range(0, nb, 2):
                    nj = min(2, nb - jp)
                    a_ = cnt["f"] % 2; cnt["f"] += 1
                    pR, pJ = psR[a_], psJ[a_]
                    for jj in range(nj):
                        for tt in range(ntb):
                            g = base + (jp + jj) * ntb + tt
                            k.mm(pR[:, jj * 192:(jj + 1) * 192], TC[:, tt, kt * 128:(kt + 1) * 128], src_fn(g), tt == 0, tt == ntb - 1,
                                 [kTC] + src_keys(g), [f"psR{a_}"], inc=(tt == ntb - 1 and jj == nj - 1))
                    for jj in range(nj):
                        for tt in range(ntb):
                            g = base + (jp + jj) * ntb + tt
                            k.mm(pJ[:, jj * 192:(jj + 1) * 192], TS[:, tt, kt * 128:(kt + 1) * 128], src_fn(g), tt == 0, tt == ntb - 1,
                                 [kTS] + src_keys(g), [f"psJ{a_}"], inc=(tt == ntb - 1 and jj == nj - 1))
                    k.copy("scalar", rj[:, 0, jp:jp + nj, :], pR[:, 0:nj * 192].rearrange("p (j c) -> p j c", j=nj), [f"psR{a_}"], [rk + f"_0_{jp}"])
                    k.copy("scalar", rj[:, 1, jp:jp + nj, :], pJ[:, 0:nj * 192].rearrange("p (j c) -> p j c", j=nj), [f"psJ{a_}"], [rk + f"_1_{jp}"])
                rkeys = [rk + f"_{a}_{jp}" for a in range(2) for jp in range(0, nb, 2)]
                gb = Gb[0]; gk = "Gb0"; cnt["g"] += 1
                for a in range(2):
                    S.dma("sync", gb[:, a, 0:nq, :], Gd[nm][o, kt, a], writes=[gk + f"_{a}"])
                gkeys = [gk + "_0", gk + "_1"]
                for i in range(nb):
                    qs = nb - 1 - i
                    Rv = rj[:, 0, 0:nb, :]; Jv = rj[:, 1, 0:nb, :]; GRs = gb[:, 0, qs:qs + nb, :]; GJs = gb[:, 1, qs:qs + nb, :]
                    b0, b1, b2, b3 = [p_[:, 0:nb, :] for p_ in pb]
                    k.tt(b0, Rv, GRs, ALU.mult, rkeys + gkeys, ["pb0"])
                    k.tt(b1, Jv, GJs, ALU.mult, rkeys + gkeys, ["pb1"], eng="gpsimd")
                    k.tt(b2, Rv, GJs, ALU.mult, rkeys + gkeys, ["pb2"], eng="gpsimd")
                    k.tt(b3, Jv, GRs, ALU.mult, rkeys + gkeys, ["pb3"])
                    terms = [(0, identb, "pb0"), (1, nidentb, "pb1")]
                    for half, tl in ((0, [(pb[0], identb, "pb0"), (pb[1], nidentb, "pb1")]), (1, [(pb[2], identb, "pb2"), (pb[3], identb, "pb3")])):
                        nmm = 2 * nb; cmm = 0
                        for (pp, idm, pk_) in tl:
                            for j in range(nb):
                                k.mm(psY[:, half * 192:(half + 1) * 192], idm[:], pp[:, j, :], cmm == 0, cmm == nmm - 1, [pk_, "identb"], ["psY"],
                                     inc=(cmm == nmm - 1))
                                cmm += 1
                    for a in range(2):
                        k.copy("scalar", YAB[:, kt, a, i, :], psY[:, a * 192:(a + 1) * 192], ["psY"], [f"YAB{kt}_{a}_{i}"])
            for pt in range(ntb):
                for ip in range(0, nb, 2):
                    ni = min(2, nb - ip)
                    a_ = cnt["i"] % 2; cnt["i"] += 1
                    ps = pI[a_]; pk = f"pI{a_}"
                    for ii in range(ni):
                        for kt in range(ntb):
                            last = (kt == ntb - 1 and ii == ni - 1)
                            k.mm(ps[:, ii * 192:(ii + 1) * 192], TCT[:, kt, pt * 128:(pt + 1) * 128], YAB[:, kt, 0, ip + ii, :], kt == 0, False,
                                 [kTCT, f"YAB{kt}_0_{ip + ii}"], [pk], inc=False)
                            k.mm(ps[:, ii * 192:(ii + 1) * 192], TST[:, kt, pt * 128:(pt + 1) * 128], YAB[:, kt, 1, ip + ii, :], False, kt == ntb - 1,
                                 [kTST, f"YAB{kt}_1_{ip + ii}"], [pk], inc=last)
                    for ii in range(ni):
                        g = base + (ip + ii) * ntb + pt
                        epi(g, ps[:, ii * 192:(ii + 1) * 192], pk)

        def epi1(g, y, pk):
            e_ = et[cnt["e"] % 2]; ek = f"et{cnt['e'] % 2}"; cnt["e"] += 1
            k.tt(e_[:], VXX[:, g, 0:192], sk[:, 0, :], ALU.mult, [f"VXX{g}", "sk"], [ek], eng="gpsimd")
            k.tt(e_[:], y, e_[:], ALU.add, [pk, ek], [ek])
            k.tt(Z[:, g, :], e_[:], VXX[:, g, 192:384], ALU.mult, [ek, f"VXX{g}"], [f"Z{g}"], eng="gpsimd")

        def epi2(g, y, pk):
            e_ = et[cnt["e"] % 2]; ek = f"et{cnt['e'] % 2}"; cnt["e"] += 1
            o_ = ost[cnt["e"] % 2]; ok = f"ost{cnt['e'] % 2}"
            k.tt(e_[:], Z[:, g, :], sk[:, 1, :], ALU.mult, [f"Z{g}", "sk"], [ek], eng="gpsimd")
            k.tt(e_[:], y, e_[:], ALU.add, [pk, ek], [ek])
            k.tt(o_[:], e_[:], VXX[:, g, 384:576], ALU.mult, [ek, f"VXX{g}"], [ok], eng="gpsimd")
            for (c0_, cn) in ((0, 128), (128, 64)):
                S.op("tensor", lambda e, o_=o_, c0_=c0_, cn=cn: e.transpose(ptp[:cn, 0:128], o_[:, c0_:c0_ + cn], ident[:]),
                     reads=[ok, "ident", "ptp"], writes=["ptp"])
                k.copy("scalar", oT[:cn, :], ptp[:cn, 0:128], ["ptp"], ["oT"])
                S.dma("sync", out_d[c0_:c0_ + cn, g * 128:(g + 1) * 128], oT[:cn, :], reads=["oT"])

        for nm in (("ctx",) if dbg == "ctx" else ("lat", "ctx")):
            conv(nm, 0, lambda g: VXX[:, g, 0:192], lambda g: [f"VXX{g}"], epi1)
            conv(nm, 1, lambda g: Z[:, g, :], lambda g: [f"Z{g}"], epi2)
        S.emit()


def hy_inputs(l, core, tm_all, G_core, inp, consts):
    b, hh = core_bh(core)
    cols = np.concatenate([896 + part * 384 + hh * 192 + np.arange(192) for part in range(3)])
    hy = seq_tm(tm_all, b, cols)
    z = np.zeros((1, 576), np.float32)
    ccols = np.concatenate([part * 384 + hh * 192 + np.arange(192) for part in range(3)])
    cwb = np.concatenate([inp["hy_conv_w"][l][:, ccols].reshape(-1), inp["hy_conv_b"][l][ccols]])
    d = {"hyp_lat": np.ascontiguousarray(np.concatenate([z, hy[:SEQ], z], 0)),
         "hyp_ctx": np.ascontiguousarray(np.concatenate([z, hy[SEQ:], z], 0)),
         "cw_bc": np.ascontiguousarray(np.tile(cwb[None, :], (128, 1)).astype(np.float32)),
         "sk_bc": np.ascontiguousarray(np.tile(inp["hy_skip"][l][:, hh * 192:(hh + 1) * 192].reshape(1, -1), (128, 1)).astype(np.float32))}
    for nm in HY_CFG:
        d[f"G_{nm}"] = G_core[nm]
        for t in ("TC", "TS", "TCT", "TST"):
            d[f"{t}_{nm}"] = consts["hy"][nm][t]
    return d


NROW = SEQ + CTX + 4
LAT0, CTX0 = 1, SEQ + 3


def ext_specs():
    sp = {"xT_in": [2, D, NT], "sc": [128, 16], "w_mod": [DEPTH, D, 6 * D], "b_mod_p": [DEPTH, 128, 48],
          "n1_p": [DEPTH, 128, 8], "n2_p": [DEPTH, 128, 8], "w_in": [DEPTH, D, P_IN], "qkn": [DEPTH, 128, 2],
          "gate_b": [DEPTH, 16, 1], "cosT": [2, 128, NT], "sinT": [2, 128, NT], "rm2": [128, 128], "blk1": [128, 128],
          "w_out": [DEPTH, D, D], "w_gate": [DEPTH, D, D_FF], "w_up": [DEPTH, D, D_FF], "w_down": [DEPTH, D_FF, D],
          "sink": [DEPTH, 2, 64, 3], "mask": [128, 384], "nw_bc": [DEPTH, 2, 128, 128],
          "sel": [34, 512], "ident": [128, 128], "maskf": [128, 128], "maskb": [128, 128],
          "cw_bc": [DEPTH, 2, 128, 4 * 576], "sk_bc": [DEPTH, 2, 128, 384],
          "f_w1": [DEPTH, 33, 64], "f_w2": [DEPTH, 64, 64], "f_w3c": [DEPTH, 2, 64, 768], "f_fb": [DEPTH, 64, 4],
          "f_dec": [DEPTH, 2, 128, 768], "alt": [128, 1]}
    for nm, cfg in HY_CFG.items():
        L, B = cfg["L"], cfg["B"]
        for t in ("TC", "TS", "TCT", "TST"):
            sp[f"{t}_{nm}"] = [B, B]
        sp[f"featsT_{nm}"] = [33, L]; sp[f"featsTr_{nm}"] = [33, L]
        sp[f"negt_{nm}"] = [128, L // 128]; sp[f"negtr_{nm}"] = [128, L // 128]
    return sp


def build_fused(depth=DEPTH):
    nc = bass.Bass("TRN2", target_bir_lowering=False)
    X = {n: dram_in(nc, n, shp) for n, shp in ext_specs().items()}
    OUT = dram_out(nc, "xT_out", [2, D, NT])
    FMS = nc.dram_tensor("FMS", [1424, NTOK], F32).ap()
    TMT = nc.dram_tensor("TMSP", [NROW, 2048], F32)
    TMS = TMT.ap()
    MIXS = nc.dram_tensor("MIXS", [D, NTOK], F32).ap()
    MODT = nc.dram_tensor("MODT", [128, 96], F32).ap()
    XTS = nc.dram_tensor("XTS", [2, D, NT], F32).ap()
    GS = {nm: nc.dram_tensor(f"GS_{nm}", [DEPTH, 2, 2, cfg["B"] // 128, 2, 128, 2 * cfg["nb"] - 1, 192], F32).ap()
          for nm, cfg in HY_CFG.items()}
    MLSCR = nc.dram_tensor("ml_scr", [2, 4, NTOK], F32).ap()
    import os
    PH = os.environ.get("FPH", "F,A,att,ml,hy,E").split(",")
    with ExitStack() as es0:
        bank = SemBank(nc, es0, nsets=1)
        with ExitStack() as es:
            S = Sched(nc, es, bank, "init_")
            z = S.sb("z", [4, 2048])
            S.op("vector", lambda e: e.memset(z[:], 0.0), writes=["z"])
            for i_, row in enumerate((0, SEQ + 1, SEQ + 2, NROW - 1)):
                S.dma("sync", TMS[row:row + 1, :], z[i_:i_ + 1, :], reads=["z"])
            S.emit()
        for l in range(depth):
            io = {"w1": X["f_w1"][l], "w2": X["f_w2"][l], "w3c": [X["f_w3c"][l, hh] for hh in range(2)], "fb": X["f_fb"][l],
                  "decay_bc": [X["f_dec"][l, hh] for hh in range(2)], "alt": X["alt"]}
            for nm in HY_CFG:
                io[nm] = dict(feats=X[f"featsT_{nm}"], featsr=X[f"featsTr_{nm}"], negt=X[f"negt_{nm}"], negtr=X[f"negtr_{nm}"],
                              TC=X[f"TC_{nm}"], TS=X[f"TS_{nm}"], G=[GS[nm][l, hh] for hh in range(2)])
            if "F" in PH:
                emit_F(nc, bank, f"F{l}_", io)
        for l in range(depth):
            xsrc = X["xT_in"] if l == 0 else XTS
            xdst = OUT if l == depth - 1 else XTS
            gcols = [(lambda t, u=u: u * NLAT + t if t < NLAT else SEQ + u * NCTX + (t - NLAT)) for u in range(2)]
            grows = [(lambda t, u=u: LAT0 + u * NLAT + t if t < NLAT else CTX0 + u * NCTX + (t - NLAT)) for u in range(2)]
            io = {"xT": [xsrc[0], xsrc[1]], "sc": X["sc"], "w_mod": X["w_mod"][l], "b_mod": X["b_mod_p"][l], "norm1_w": X["n1_p"][l],
                  "w_in": X["w_in"][l], "qkn": X["qkn"][l], "gate_b": X["gate_b"][l], "cosT": X["cosT"], "sinT": X["sinT"],
                  "rm2": X["rm2"], "blk1": X["blk1"], "modT": MODT, "fm": FMS, "tm": TMS}
            if "A" in PH:
                emit_A(nc, bank, f"A{l}_", io, gcols, grows)
            for g in range(2):
                io = {"qT": FMS[192 * g:192 * g + 192, :], "kT": FMS[384 + 64 * g:384 + 64 * g + 64, :],
                      "v_lat": TMS[LAT0:LAT0 + SEQ, 64 * g:64 * g + 64], "v_ctx": TMS[CTX0:CTX0 + CTX, 64 * g:64 * g + 64],
                      "sink": X["sink"][l, g], "mask": X["mask"], "attT": MIXS[192 * g:192 * g + 192, :]}
                if "att" in PH:
                    emit_att(nc, bank, f"T{l}{g}_", io)
            for hp in range(2):
                def tmp_(c0):
                    return (TMS[LAT0:LAT0 + SEQ, c0:c0 + 128], TMS[CTX0:CTX0 + CTX, c0:c0 + 128])
                io = {"qT": FMS[512 + 128 * hp:512 + 128 * hp + 128, :], "kT": FMS[768 + 128 * hp:768 + 128 * hp + 128, :],
                      "ktok": tmp_(128 + 128 * hp), "vtok": tmp_(384 + 128 * hp), "otok": tmp_(640 + 128 * hp),
                      "g4": [FMS[1024 + 4 * q + 2 * hp:1024 + 4 * q + 2 * hp + 2, :] for q in range(4)],
                      "nw_bc": X["nw_bc"][l, hp], "sel": X["sel"], "ident": X["ident"], "maskf": X["maskf"], "maskb": X["maskb"],
                      "mloT": MIXS[768 + 128 * hp:768 + 128 * hp + 128, :], "scr": MLSCR}
                if "ml" in PH:
                    emit_ml(nc, bank, f"L{l}{hp}_", io)
            for hh in range(2):
                io = {"tmT": (TMT, 2048), "row0": {"lat": LAT0 - 1, "ctx": CTX0 - 1}, "col0": 896 + 192 * hh,
                      "cw_bc": X["cw_bc"][l, hh], "sk_bc": X["sk_bc"][l, hh], "ident": X["ident"],
                      "G": {nm: GS[nm][l, hh] for nm in HY_CFG},
                      "T": {nm: [X[f"{t}_{nm}"] for t in ("TC", "TS", "TCT", "TST")] for nm in HY_CFG},
                      "hyoT": MIXS[384 + 192 * hh:384 + 192 * hh + 192, :]}
                if "hy" in PH:
                    emit_hy(nc, bank, f"H{l}{hh}_", io)
            io = {"xT": [xsrc[0], xsrc[1]], "mix": MIXS, "modT": MODT, "norm2_w": X["n2_p"][l], "w_out": X["w_out"][l],
                  "w_gate": X["w_gate"][l], "w_up": X["w_up"][l], "w_down": X["w_down"][l], "xT_out": [xdst[0], xdst[1]]}
            if "E" in PH:
                emit_E(nc, bank, f"E{l}_", io, gcols)
    return nc


def fused_inputs(b, inp, consts):
    cos, sin = consts["rope"]
    d = {}
    xT = np.empty((2, D, NT), np.float32)
    cosT = np.ones((2, 128, NT), np.float32)
    sinT = np.zeros((2, 128, NT), np.float32)
    for u in range(2):
        xT[u, :, :NLAT] = inp["x"][b, u * NLAT:(u + 1) * NLAT, :].T
        xT[u, :, NLAT:] = inp["ctx"][b, u * NCTX:(u + 1) * NCTX, :].T
        for hd in range(2):
            cosT[u, 64 * hd:64 * hd + 64, :NLAT] = cos[:, u * NLAT:(u + 1) * NLAT]
            sinT[u, 64 * hd:64 * hd + 64, :NLAT] = sin[:, u * NLAT:(u + 1) * NLAT]
    d["xT_in"] = xT; d["cosT"] = cosT; d["sinT"] = sinT
    d["sc"] = pcol(np.stack([inp["c"][b], inp["c_ctx"]], axis=1)).reshape(128, 16)
    for k_ in ("w_mod", "w_in", "w_out"):
        d[k_] = inp[k_]
    d["w_gate"], d["w_up"], d["w_down"] = inp["ffn_w_gate"], inp["ffn_w_up"], inp["ffn_w_down"]
    d["b_mod_p"] = np.stack([pcol(inp["b_mod"][l]) for l in range(DEPTH)])
    d["n1_p"] = np.stack([pcol(inp["norm1_w"][l]) for l in range(DEPTH)])
    d["n2_p"] = np.stack([pcol(inp["norm2_w"][l]) for l in range(DEPTH)])
    d["qkn"] = np.stack([np.stack([np.tile(inp["q_norm_w"][l], 2), np.tile(inp["k_norm_w"][l], 2)], axis=1) for l in range(DEPTH)])
    d["gate_b"] = inp["ml_gate_b"].reshape(DEPTH, 16, 1)
    d["rm2"], d["blk1"], d["mask"] = consts["rm2"], consts["blk1"], consts["mask"]
    d["sink"] = np.stack([np.stack([np.tile(inp["attn_sink"][l][3 * g:3 * g + 3][None, :], (64, 1)) for g in range(2)]) for l in range(DEPTH)])
    d["nw_bc"] = np.stack([np.stack([np.tile(inp["ml_norm_w"][l][128 * hp:128 * hp + 128][None, :], (128, 1)) for hp in range(2)]) for l in range(DEPTH)])
    d.update(consts["ml"])
    cw = np.empty((DEPTH, 2, 128, 4 * 576), np.float32); sk = np.empty((DEPTH, 2, 128, 384), np.float32)
    w3c = np.empty((DEPTH, 2, 64, 768), np.float32); dec = np.empty((DEPTH, 2, 128, 768), np.float32)
    for l in range(DEPTH):
        for hh in range(2):
            ccols = np.concatenate([part * 384 + hh * 192 + np.arange(192) for part in range(3)])
            cw[l, hh] = np.concatenate([inp["hy_conv_w"][l][:, ccols].reshape(-1), inp["hy_conv_b"][l][ccols]])[None, :]
            sk[l, hh] = inp["hy_skip"][l][:, hh * 192:(hh + 1) * 192].reshape(1, -1)
            cols = np.concatenate([dr * 768 + o * 384 + hh * 192 + np.arange(192) for dr in range(2) for o in range(2)])
            w3c[l, hh] = inp["hy_w3"][l][:, cols]
            dec[l, hh] = inp["hy_decay"][l][cols][None, :]
    d["cw_bc"], d["sk_bc"], d["f_w3c"], d["f_dec"] = cw, sk, w3c, dec
    d["f_w1"], d["f_w2"] = inp["hy_w1"], inp["hy_w2"]
    d["f_fb"] = np.stack([np.stack([inp["hy_freq"][l], inp["hy_b1"][l], inp["hy_b2"][l], inp["hy_b2"][l]], axis=1) for l in range(DEPTH)])
    hc = consts["hy"]
    d["alt"] = hc["alt"]
    for nm in HY_CFG:
        for t in ("TC", "TS", "TCT", "TST", "featsT", "featsTr", "negt", "negtr"):
            d[f"{t}_{nm}"] = hc[nm][t]
    sp = ext_specs()
    return {k_: np.ascontiguousarray(np.asarray(v, np.float32).reshape(sp[k_])) for k_, v in d.items()}


def kernel(**inputs):
    inp = {k_: np.asarray(v, dtype=np.float32) for k_, v in inputs.items()}
    consts = {"rope": rope_tables(), "mask": band_mask(), "ml": ml_consts(), "hy": hy_consts()}
    consts["rm2"], consts["blk1"] = rope_consts()
    nc = build_fused()
    maps = [fused_inputs(c // 2, inp, consts) for c in range(8)]
    res = run_bass_kernel_spmd(nc, maps, core_ids=list(range(8))).results
    out = np.empty((4, SEQ, D), np.float32)
    for b in range(4):
        xo = res[2 * b]["xT_out"]
        for u in range(2):
            out[b, u * NLAT:(u + 1) * NLAT, :] = xo[u][:, :NLAT].T
    return out
```

```python
from contextlib import ExitStack
import math
import numpy as np
import ml_dtypes
import concourse.bass as bass
import concourse.mybir as mybir
from concourse.bass_utils import run_bass_kernel_spmd

F32 = mybir.dt.float32
BF16 = mybir.dt.bfloat16
ALU = mybir.AluOpType
AF = mybir.ActivationFunctionType
AX = mybir.AxisListType

D = 1024
SEQ = 4096
CTX = 256
DEPTH = 4
NLAT = SEQ // 2
NCTX = CTX // 2
NT = NLAT + NCTX
NTOK = SEQ + CTX
P_IN = 2832
D_FF = 2816
NFF = D_FF // 128
EPS = 1e-6
O_AQ, O_AK, O_AV, O_HY, O_MQ, O_MK, O_MV, O_MO, O_MG = 0, 384, 512, 640, 1792, 2048, 2304, 2560, 2816

ENGS = ["sync", "scalar", "vector", "gpsimd", "tensor"]
NDS = 8
SES_SKIP = ()


class SemBank:
    def __init__(self, nc, es, nsets=2):
        self.sets = []
        for si in range(nsets):
            self.sets.append({
                "s": {e: es.enter_context(nc.semaphore(f"s{si}_{e}")) for e in ENGS},
                "d": {e: [es.enter_context(nc.semaphore(f"d{si}_{e}{i}")) for i in range(NDS)] for e in ENGS}})
        self.phase = 0


class Sched:
    def __init__(self, nc, es, bank=None, pfx="", same_engine_sync=True):
        self.nc = nc
        self.es = es
        self.pfx = pfx
        self.q = {e: [] for e in ENGS}
        if bank is None:
            bank = SemBank(nc, es, nsets=1)
        cur = bank.sets[bank.phase % len(bank.sets)]
        self.other = bank.sets[(bank.phase + 1) % len(bank.sets)] if len(bank.sets) > 1 else None
        bank.phase += 1
        self.sem = cur["s"]
        self.dsem = cur["d"]
        if len(bank.sets) == 1:
            if not hasattr(bank, "state"):
                bank.state = ({e: 0 for e in ENGS}, {e: [0] * NDS for e in ENGS}, {e: 0 for e in ENGS}, {e: {} for e in ENGS})
            self.cnt, self.dcnt, self.dnext, self.seen = bank.state
        else:
            self.cnt = {e: 0 for e in ENGS}
            self.dcnt = {e: [0] * NDS for e in ENGS}
            self.dnext = {e: 0 for e in ENGS}
            self.seen = {e: {} for e in ENGS}
        self.res = {}
        self.ses = same_engine_sync

    def sb(self, name, shape, dt=F32):
        return self.es.enter_context(self.nc.sbuf_tensor("sb_" + self.pfx + name, list(shape), dt))

    def ps(self, name, shape, dt=F32):
        return self.es.enter_context(self.nc.psum_tensor("ps_" + self.pfx + name, list(shape), dt))

    def _deps(self, eng, reads, writes):
        deps = {}

        def add(tok):
            sem, val, name = tok
            if name not in deps or deps[name][1] < val:
                deps[name] = tok

        for k in reads:
            r = self.res.get(k)
            if r and r[0] is not None:
                add(r[0])
        for k in writes:
            r = self.res.get(k)
            if r:
                if r[0] is not None:
                    add(r[0])
                for t in r[1]:
                    add(t)
        waits = []
        for name, (sem, val, _) in deps.items():
            if name == eng:
                if eng == "tensor" or not self.ses or val > self.cnt[eng] or eng in SES_SKIP:
                    continue
            if self.seen[eng].get(name, 0) < val:
                self.seen[eng][name] = val
                waits.append((sem, val))
        return waits

    def _record(self, tok, reads, writes):
        for k in reads:
            r = self.res.setdefault(k, [None, []])
            r[1].append(tok)
        for k in writes:
            self.res[k] = [tok, []]

    def op(self, eng, fn, reads=(), writes=(), inc=True):
        waits = self._deps(eng, reads, writes)
        if inc:
            self.cnt[eng] += 1
            tok = (self.sem[eng], self.cnt[eng], eng)
        else:
            tok = (self.sem[eng], self.cnt[eng] + 1, eng)
        self.q[eng].append((waits, fn, (self.sem[eng], 1) if inc else None))
        self._record(tok, reads, writes)

    def dma(self, eng, out, in_, reads=(), writes=(), **kw):
        waits = self._deps(eng, reads, writes)
        slot = self.dnext[eng]
        self.dnext[eng] = (slot + 1) % NDS
        name = f"d_{eng}{slot}"
        prev = self.dcnt[eng][slot]
        if prev > 0 and self.seen[eng].get(name, 0) < prev:
            self.seen[eng][name] = prev
            waits.append((self.dsem[eng][slot], prev))
        self.dcnt[eng][slot] = prev + 16
        tok = (self.dsem[eng][slot], prev + 16, name)
        self.q[eng].append((waits, lambda e: e.dma_start(out=out, in_=in_, **kw), (self.dsem[eng][slot], 16)))
        self._record(tok, reads, writes)

    def emit(self):
        for e in ENGS:
            for i in range(NDS):
                v = self.dcnt[e][i]
                if v > 0:
                    self.q["sync"].append(([(self.dsem[e][i], v)], None, None))
        for e in ENGS:
            if e != "sync" and self.cnt[e] > 0:
                self.q["sync"].append(([(self.sem[e], self.cnt[e])], None, None))
        if self.other is not None:
            clr = list(self.other["s"].values()) + [x for l_ in self.other["d"].values() for x in l_]
            self.q["gpsimd"] = [([], (lambda e, sm=sm: e.sem_clear(sm)), None) for sm in clr] + self.q["gpsimd"]
        with self.nc.Block() as block:
            for eng in ENGS:
                if not self.q[eng]:
                    continue

                def body(e, eng=eng):
                    for waits, fn, inc in self.q[eng]:
                        for sem, val in waits:
                            e.wait_ge(sem, val)
                        if fn is not None:
                            ins = fn(e)
                            if inc is not None:
                                ins.then_inc(inc[0], inc[1])

                getattr(block, eng)(body)


class K:
    def __init__(self, S):
        self.S = S

    def act(self, out, in_, func, r, w, bias=None, scale=None, eng="scalar"):
        kw = {}
        if bias is not None:
            kw["bias"] = bias
        if scale is not None:
            kw["scale"] = scale
        self.S.op("scalar", lambda e: e.activation(out=out, in_=in_, func=func, **kw), reads=r, writes=w)

    def tt(self, out, a, b, op, r, w, eng="vector"):
        self.S.op(eng, lambda e: e.tensor_tensor(out=out, in0=a, in1=b, op=op), reads=r, writes=w)

    def ts(self, out, a, s1, s2, op0, op1, r, w, eng="vector"):
        if op1 is None:
            self.S.op(eng, lambda e: e.tensor_scalar(out=out, in0=a, scalar1=s1, scalar2=None, op0=op0), reads=r, writes=w)
        else:
            self.S.op(eng, lambda e: e.tensor_scalar(out=out, in0=a, scalar1=s1, scalar2=s2, op0=op0, op1=op1), reads=r, writes=w)

    def stt(self, out, a, s, b, op0, op1, r, w):
        self.S.op("vector", lambda e: e.scalar_tensor_tensor(out=out, in0=a, scalar=s, in1=b, op0=op0, op1=op1), reads=r, writes=w)

    def copy(self, eng, out, in_, r, w):
        if eng == "scalar":
            self.S.op("scalar", lambda e: e.copy(out=out, in_=in_), reads=r, writes=w)
        else:
            self.S.op(eng, lambda e: e.tensor_copy(out=out, in_=in_), reads=r, writes=w)

    def recip(self, out, in_, r, w):
        self.S.op("vector", lambda e: e.reciprocal(out=out, in_=in_), reads=r, writes=w)

    def mm(self, out, lhsT, rhs, start, stop, r, w, inc=None):
        self.S.op("tensor", lambda e: e.matmul(out, lhsT=lhsT, rhs=rhs, start=start, stop=stop), reads=r, writes=w,
                  inc=stop if inc is None else inc)

    def memset(self, eng, ap, val, w):
        self.S.op(eng, lambda e: e.memset(ap, val), writes=w)


def dram_in(nc, name, shape, dt=F32):
    return nc.dram_tensor(name, list(shape), dt, kind="ExternalInput").ap()


def dram_out(nc, name, shape, dt=F32):
    return nc.dram_tensor(name, list(shape), dt, kind="ExternalOutput").ap()


def bcast_rows(ap1d, nparts):
    return ap1d.partition_broadcast(nparts)


def token_tiles(width):
    tiles = []
    t = 0
    while t < NLAT:
        tiles.append((t, width, 0))
        t += width
    tiles.append((NLAT, NCTX, 1))
    return tiles


def emit_mod_vectors(S, k, sc_d, wmod_d, bmod_d):
    sraw = S.sb("sraw", [128, 8, 2])
    sbf = S.sb("sbf", [128, 8, 2], BF16)
    bm = S.sb("bm", [128, 48])
    modT = S.sb("modT", [128, 48, 2])
    S.dma("sync", sraw[:].rearrange("p j c -> p (j c)"), sc_d, writes=["sraw"])
    S.dma("sync", bm[:], bmod_d, writes=["bm"])
    k.act(sbf[:], sraw[:], AF.Silu, ["sraw"], ["sbf"])
    wv = wmod_d.rearrange("(kc p) n -> p kc n", p=128)
    pm = S.ps("pmod", [128, 512])
    bufs = [S.sb(f"wmodb{i}", [128, 8, 1024], BF16) for i in range(2)]
    for g in range(6):
        wt = bufs[g % 2]
        key = f"wmodb{g % 2}"
        S.dma("gpsimd", wt[:], wv[:, :, g * 1024:(g + 1) * 1024], writes=[key])
        for j in range(8):
            jj = g * 8 + j
            for kc in range(8):
                k.mm(pm[:, 2 * jj:2 * jj + 2], wt[:, kc, j * 128:(j + 1) * 128], sbf[:, kc, :], kc == 0, kc == 7,
                     [key, "sbf"], ["pmod"])
    pv = pm[:, 0:96].rearrange("p (j c) -> p j c", c=2)
    for c in range(2):
        k.tt(modT[:, :, c], pv[:, :, c], bm[:], ALU.add, ["pmod", "bm"], ["modT"])
    return modT


def emit_A(nc, bank, pfx, io, gcols, grows, dbg=None):
    sc_d, wmod_d, bmod_d, n1_d, win_d = io["sc"], io["w_mod"], io["b_mod"], io["norm1_w"], io["w_in"]
    qkn_d, gb_d, cos_d, sin_d, rm_d, bo_d = io["qkn"], io["gate_b"], io["cosT"], io["sinT"], io["rm2"], io["blk1"]
    modT_o, fmT_o, tm_o = io["modT"], io["fm"], io["tm"]
    with ExitStack() as es:
        S = Sched(nc, es, bank, pfx)
        k = K(S)
        modT = emit_mod_vectors(S, k, sc_d, wmod_d, bmod_d)
        S.dma("sync", modT_o, modT[:].rearrange("p j c -> p (j c)"), reads=["modT"])
        n1 = S.sb("n1", [128, 8])
        S.dma("sync", n1[:], n1_d, writes=["n1"])
        qkn = S.sb("qkn", [128, 2]); S.dma("sync", qkn[:], qkn_d, writes=["qkn"])
        gb = S.sb("gb", [16, 1]); S.dma("sync", gb[:], gb_d, writes=["gb"])
        cosT = S.sb("cosT", [128, NT]); sinT = S.sb("sinT", [128, NT])
        rm2 = S.sb("rm2", [128, 128], BF16); S.dma("gpsimd", rm2[:], rm_d, writes=["rm2"])
        blk1 = S.sb("blk1", [128, 128], BF16); S.dma("gpsimd", blk1[:], bo_d, writes=["blk1"])
        ones = S.sb("ones", [128, 128], BF16); k.memset("gpsimd", ones[:], 1.0, ["ones"])
        A1 = S.sb("A1", [128, 8, 2])
        for c in range(2):
            k.stt(A1[:, :, c], modT[:, 8:16, c], 1.0, n1[:], ALU.add, ALU.mult, ["modT", "n1"], ["A1"])
        wv = win_d.rearrange("(kc p) n -> p kc n", p=128)
        fm_cols = [(O_AQ, 384), (O_AK, 128), (O_MQ, 256), (O_MK, 256), (O_MG, 16), (O_HY + 768, 384)]
        Wfm = S.sb("Wfm", [128, 8, 1424], BF16)
        off = 0
        for (c0, n) in fm_cols:
            S.dma("gpsimd", Wfm[:, :, off:off + n], wv[:, :, c0:c0 + n], writes=[f"Wfm{off}"])
            off += n
        tm_cols = [(O_AV, 128), (O_MK, 256), (O_MV, 256), (O_MO, 256), (O_HY, 1152)]
        Wtm = S.sb("Wtm", [128, 8, 2048], BF16)
        off = 0
        for (c0, n) in tm_cols:
            S.dma("gpsimd", Wtm[:, :, off:off + n], wv[:, :, c0:c0 + n], writes=[f"Wtm{off}"])
            off += n
        WFM_KEYS = ["Wfm0", "Wfm384", "Wfm512", "Wfm768", "Wfm1024", "Wfm1040"]
        WTM_KEYS = ["Wtm0", "Wtm128", "Wtm384", "Wtm640", "Wtm896"]
        k.ts(Wfm[:, :, 768:1024], Wfm[:, :, 768:1024], 0.125, None, ALU.mult, None, ["Wfm768"], ["Wfm768"], eng="gpsimd")
        k.ts(Wtm[:, :, 128:384], Wtm[:, :, 128:384], 0.125, None, ALU.mult, None, ["Wtm128"], ["Wtm128"], eng="gpsimd")
        fm_tiles = [(0, 128, "q"), (128, 128, "q"), (256, 128, "q"), (384, 128, "k"),
                    (512, 128, "c"), (640, 128, "c"), (768, 128, "c"), (896, 128, "c"),
                    (1024, 16, "g"), (1040, 128, "c"), (1168, 128, "c"), (1296, 128, "c")]
        xt = [S.sb(f"xt{i}", [128, 8, 512]) for i in range(2)]
        sq = S.sb("sq", [128, 8, 512], BF16)
        tmp = S.sb("tmpn", [128, 2, 512])
        hT = [S.sb(f"hT{i}", [128, 8, 512], BF16) for i in range(2)]
        rs = S.sb("rs", [128, 512]); rstd = S.sb("rstd", [128, 512])
        ss_ps = S.ps("ss_ps", [128, 512])
        pf = [S.ps(f"pf{i}", [128, 512]) for i in range(2)]
        pt = [S.ps(f"pt{i}", [128, 512]) for i in range(2)]
        pq = [S.ps(f"pq{i}", [128, 512]) for i in range(2)]
        stg_f = [S.sb(f"stgf{i}", [128, 512]) for i in range(3)]
        stg_t = [S.sb(f"stgt{i}", [128, 2048]) for i in range(2)]
        qsq = S.sb("qsq", [128, 512], BF16); qw = S.sb("qw", [128, 512], BF16)
        qrs = S.sb("qrs", [128, 512]); qri = S.sb("qri", [128, 512]); qt1 = S.sb("qt1", [128, 512]); qt2 = S.sb("qt2", [128, 512])
        nf = 0; ntm = 0; nst = 0
        tiles_all = [(u_, t0, n, which) for u_ in range(len(io["xT"])) for (t0, n, which) in token_tiles(512)]
        def load_x(tj):
            uj, tj0, nj, _ = tiles_all[tj]
            S.dma("sync", xt[tj % 2][:, :, :nj], io["xT"][uj].rearrange("(j p) t -> p j t", p=128)[:, :, tj0:tj0 + nj], writes=[f"xt{tj % 2}"])

        load_x(0)
        for ti, (u_, t0, n, which) in enumerate(tiles_all):
            if t0 == 0:
                gcol, grow = gcols[u_], grows[u_]
                S.dma("sync", cosT[:], cos_d[u_], writes=["cosT"]); S.dma("sync", sinT[:], sin_d[u_], writes=["sinT"])
            xb = xt[ti % 2]; xk = f"xt{ti % 2}"; hb = hT[ti % 2]; hk = f"hT{ti % 2}"
            if ti + 1 < len(tiles_all):
                load_x(ti + 1)
            k.act(sq[:, :, :n], xb[:, :, :n], AF.Square, [xk], ["sq"])
            for j in range(8):
                k.mm(ss_ps[:, :n], ones[:], sq[:, j, :n], j == 0, j == 7, ["ones", "sq"], ["ss_ps"])
            k.act(rs[:, :n], ss_ps[:, :n], AF.Sqrt, ["ss_ps"], ["rs"], bias=EPS, scale=1.0 / D)
            k.recip(rstd[:, :n], rs[:, :n], ["rs"], ["rstd"])
            for j in range(8):
                k.stt(tmp[:, j % 2, :n], xb[:, j, :n], A1[:, j, which:which + 1], rstd[:, :n], ALU.mult, ALU.mult,
                      [xk, "A1", "rstd"], [f"tmpn{j % 2}"])
                k.act(hb[:, j, :n], tmp[:, j % 2, :n], AF.Identity, [f"tmpn{j % 2}", "modT"], [hk + f"_{j}"],
                      bias=modT[:, j, which:which + 1], scale=1.0)
            hkeys = [hk + f"_{j}" for j in range(8)]
            if dbg == "norm":
                break
            for (c0, M, kind) in fm_tiles:
                if dbg == "fmc" and kind != "c":
                    continue
                if dbg == "fmq" and kind != "q":
                    continue
                if dbg == "fmg" and kind != "g":
                    continue
                ps = pf[nf % 2]; pk = f"pf{nf % 2}"; nf += 1
                for kc in range(8):
                    k.mm(ps[:, :n], Wfm[:, kc, c0:c0 + 128], hb[:, kc, :n], kc == 0, kc == 7, WFM_KEYS + hkeys, [pk])
                st = stg_f[nst % 3]; sk = f"stgf{nst % 3}"; nst += 1
                if kind == "c":
                    k.copy("scalar", st[:M, :n], ps[:M, :n], [pk], [sk])
                elif kind == "g":
                    k.ts(st[:M, :n], ps[:M, :n], gb[:, 0:1], None, ALU.add, None, [pk, "gb"], [sk])
                else:
                    nw = qkn[:, 0:1] if kind == "q" else qkn[:, 1:2]
                    p2 = pq[0]; p3 = pq[1]
                    import os
                    QS = int(os.environ.get("QSTEPS", "99"))
                    steps = [
                        lambda: k.act(qsq[:, :n], ps[:, :n], AF.Square, [pk], ["qsq"]),
                        lambda: k.act(qw[:, :n], ps[:, :n], AF.Identity, [pk, "qkn"], ["qw"], scale=nw),
                        lambda: k.mm(p2[:, :n], blk1[:], qsq[:, :n], True, True, ["blk1", "qsq"], ["pq0"]),
                        lambda: k.mm(p3[:, :n], rm2[:], qw[:, :n], True, True, ["rm2", "qw"], ["pq1"]),
                        lambda: k.act(qrs[:, :n], p2[:, :n], AF.Sqrt, ["pq0"], ["qrs"], bias=EPS, scale=1.0 / 64),
                        lambda: k.recip(qri[:, :n], qrs[:, :n], ["qrs"], ["qri"]),
                        lambda: k.tt(qt1[:, :n], qw[:, :n], cosT[:, t0:t0 + n], ALU.mult, ["qw", "cosT"], ["qt1"]),
                        lambda: k.tt(qt2[:, :n], p3[:, :n], sinT[:, t0:t0 + n], ALU.mult, ["pq1", "sinT"], ["qt2"]),
                        lambda: k.tt(qt1[:, :n], qt1[:, :n], qt2[:, :n], ALU.add, ["qt1", "qt2"], ["qt1"]),
                        lambda: k.tt(st[:, :n], qt1[:, :n], qri[:, :n], ALU.mult, ["qt1", "qri"], [sk]),
                    ]
                    for f_ in steps[:QS]:
                        f_()
                S.dma("sync", fmT_o[c0:c0 + M, gcol(t0):gcol(t0) + n], st[:M, :n], reads=[sk])
            if dbg in ("fm", "fmc", "fmq", "fmg"):
                break
            for s0 in range(0, n, 128):
                st = stg_t[ntm % 2]; sk = f"stgt{ntm % 2}"; ntm += 1
                for g in range(4):
                    ps = pt[(ntm * 4 + g) % 2]; pk = f"pt{(ntm * 4 + g) % 2}"
                    for kc in range(8):
                        k.mm(ps[:, :], hb[:, kc, s0:s0 + 128], Wtm[:, kc, g * 512:(g + 1) * 512], kc == 0, kc == 7,
                             WTM_KEYS + hkeys, [pk])
                    k.copy("scalar" if g % 2 == 0 else "vector", st[:, g * 512:(g + 1) * 512], ps[:, :], [pk], [sk + f"_{g}"])
                S.dma("sync", tm_o[grow(t0 + s0):grow(t0 + s0) + 128, :], st[:], reads=[sk + f"_{g}" for g in range(4)])
        S.emit()


def rope_tables():
    t = np.arange(SEQ)
    row = (t // 64).astype(np.float64)
    col = (t % 64).astype(np.float64)
    nf = 16
    inv = 10000.0 ** (-np.arange(nf, dtype=np.float64) / nf)
    cos = np.zeros((64, SEQ), np.float32)
    sin = np.zeros((64, SEQ), np.float32)
    for a, pos in enumerate((row, col)):
        ang = (pos[None, :].astype(np.float32) * inv[:, None].astype(np.float32)).astype(np.float32)
        for half in range(2):
            cos[a * 32 + half * 16:a * 32 + half * 16 + 16] = np.cos(ang)
            sin[a * 32 + half * 16:a * 32 + half * 16 + 16] = np.sin(ang)
    return cos, sin


def rope_consts():
    rm = np.zeros((64, 64), np.float32)
    for d in range(64):
        if (d % 32) < 16:
            rm[d + 16, d] = -1.0
        else:
            rm[d - 16, d] = 1.0
    rm2 = np.zeros((128, 128), np.float32)
    rm2[:64, :64] = rm
    rm2[64:, 64:] = rm
    blk = np.zeros((128, 128), np.float32)
    blk[:64, :64] = 1.0
    blk[64:, 64:] = 1.0
    return rm2, blk


def pcol(v):
    v = np.asarray(v, np.float32)
    j = v.shape[0] // 128
    return np.ascontiguousarray(np.moveaxis(v.reshape(j, 128, *v.shape[1:]), 0, 1))


def core_bh(core):
    return core // 2, core % 2


def a_inputs(l, core, x_cur, ctx_cur, inp, consts):
    b, hh = core_bh(core)
    cos, sin = consts["rope"]
    xT = np.concatenate([x_cur[b, hh * NLAT:(hh + 1) * NLAT, :].T, ctx_cur[b, hh * NCTX:(hh + 1) * NCTX, :].T], axis=1)
    cosT = np.ones((128, NT), np.float32)
    sinT = np.zeros((128, NT), np.float32)
    cosT[:64, :NLAT] = cos[:, hh * NLAT:(hh + 1) * NLAT]; cosT[64:, :NLAT] = cosT[:64, :NLAT]
    sinT[:64, :NLAT] = sin[:, hh * NLAT:(hh + 1) * NLAT]; sinT[64:, :NLAT] = sinT[:64, :NLAT]
    return {
        "xT": np.ascontiguousarray(xT, dtype=np.float32),
        "sc": pcol(np.stack([inp["c"][b], inp["c_ctx"]], axis=1)).reshape(128, 16),
        "w_mod": inp["w_mod"][l], "b_mod": pcol(inp["b_mod"][l]), "norm1_w": pcol(inp["norm1_w"][l]), "w_in": inp["w_in"][l],
        "qkn": np.ascontiguousarray(np.stack([np.tile(inp["q_norm_w"][l], 2), np.tile(inp["k_norm_w"][l], 2)], axis=1)),
        "gate_b": np.ascontiguousarray(inp["ml_gate_b"][l].reshape(16, 1)),
        "cosT": cosT, "sinT": sinT, "rm2": consts["rm2"], "blk1": consts["blk1"],
    }


def emit_E(nc, bank, pfx, io, gcols, dbg=None):
    mix_d, mod_d, n2_d = io["mix"], io["modT"], io["norm2_w"]
    wo_d, wg_d, wu_d, wd_d = io["w_out"], io["w_gate"], io["w_up"], io["w_down"]
    with ExitStack() as es:
        S = Sched(nc, es, bank, pfx)
        k = K(S)
        modT = S.sb("modT", [128, 48, 2]); S.dma("sync", modT[:].rearrange("p j c -> p (j c)"), mod_d, writes=["modT"])
        n2 = S.sb("n2", [128, 8]); S.dma("sync", n2[:], n2_d, writes=["n2"])
        ones = S.sb("ones", [128, 128], BF16); k.memset("gpsimd", ones[:], 1.0, ["ones"])
        A2 = S.sb("A2", [128, 8, 2])
        for c in range(2):
            k.stt(A2[:, :, c], modT[:, 32:40, c], 1.0, n2[:], ALU.add, ALU.mult, ["modT", "n2"], ["A2"])
        Wo = S.sb("Wo", [128, 8, D], BF16)
        Wg = S.sb("Wg", [128, 8, D_FF], BF16)
        Wu = S.sb("Wu", [128, 8, D_FF], BF16)
        Wd = S.sb("Wd", [128, NFF, D], BF16)
        S.dma("gpsimd", Wo[:], wo_d.rearrange("(kc p) n -> p kc n", p=128), writes=["Wo"])
        wgv = wg_d.rearrange("(kc p) n -> p kc n", p=128)
        wuv = wu_d.rearrange("(kc p) n -> p kc n", p=128)
        wdv = wd_d.rearrange("(m p) n -> p m n", p=128)
        WG_KEYS = []; WU_KEYS = []; WD_KEYS = []
        for h in range(2):
            S.dma("gpsimd", Wg[:, :, h * 1408:(h + 1) * 1408], wgv[:, :, h * 1408:(h + 1) * 1408], writes=[f"Wg{h}"]); WG_KEYS.append(f"Wg{h}")
            S.dma("gpsimd", Wu[:, :, h * 1408:(h + 1) * 1408], wuv[:, :, h * 1408:(h + 1) * 1408], writes=[f"Wu{h}"]); WU_KEYS.append(f"Wu{h}")
            S.dma("gpsimd", Wd[:, h * 11:(h + 1) * 11, :], wdv[:, h * 11:(h + 1) * 11, :], writes=[f"Wd{h}"]); WD_KEYS.append(f"Wd{h}")
        TW = 256
        mv = mix_d.rearrange("(j p) t -> p j t", p=128)
        xt = [S.sb(f"xt{i}", [128, 8, TW]) for i in range(2)]
        mx = [S.sb(f"mx{i}", [128, 8, TW], BF16) for i in range(2)]
        sq = S.sb("sq", [128, 8, TW], BF16)
        tmp = S.sb("tmpn", [128, 2, TW])
        ev = S.sb("ev", [128, 2, TW])
        hT = S.sb("h2T", [128, 8, TW], BF16)
        aT = S.sb("aT", [128, NFF, TW], BF16)
        sg = S.sb("sg", [128, 2, TW], BF16); uc = S.sb("uc", [128, 2, TW], BF16)
        rs = S.sb("rs", [128, TW]); rstd = S.sb("rstd", [128, TW])
        ss_ps = S.ps("ss_ps", [128, 512])
        py = [S.ps(f"py{i}", [128, 512]) for i in range(2)]
        pg = [S.ps(f"pg{i}", [128, 512]) for i in range(2)]
        pu = [S.ps(f"pu{i}", [128, 512]) for i in range(2)]
        ny = 0; nev = 0
        tiles_all = [(u_, t0, n, which) for u_ in range(len(io["xT"])) for (t0, n, which) in token_tiles(TW)]
        def load_xm(tj):
            uj, tj0, nj, _ = tiles_all[tj]
            S.dma("sync", xt[tj % 2][:, :, :nj], io["xT"][uj].rearrange("(j p) t -> p j t", p=128)[:, :, tj0:tj0 + nj],
                  writes=[f"xt{tj % 2}_{j}" for j in range(8)])
            S.dma("gpsimd", mx[tj % 2][:, :, :nj], mv[:, :, gcols[uj](tj0):gcols[uj](tj0) + nj], writes=[f"mx{tj % 2}"])

        load_xm(0)
        for ti, (u_, t0, n, which) in enumerate(tiles_all):
            if t0 == 0:
                xov = io["xT_out"][u_].rearrange("(j p) t -> p j t", p=128)
            xb = xt[ti % 2]; xk = f"xt{ti % 2}"; mb = mx[ti % 2]; mk = f"mx{ti % 2}"
            if ti + 1 < len(tiles_all):
                load_xm(ti + 1)
            for j in range(8):
                ps = py[ny % 2]; pk = f"py{ny % 2}"; ny += 1
                for kc in range(8):
                    k.mm(ps[:, :n], Wo[:, kc, j * 128:(j + 1) * 128], mb[:, kc, :n], kc == 0, kc == 7, ["Wo", mk], [pk])
                e_ = ev[:, nev % 2, :n]; ek = f"ev{nev % 2}"; nev += 1
                k.act(e_, ps[:, :n], AF.Identity, [pk, "modT"], [ek], scale=modT[:, 16 + j, which:which + 1])
                k.tt(xb[:, j, :n], xb[:, j, :n], e_, ALU.add, [xk + f"_{j}", ek], [xk + f"_{j}"], eng="gpsimd")
            xkeys = [xk + f"_{j}" for j in range(8)]
            k.act(sq[:, :, :n], xb[:, :, :n], AF.Square, xkeys, ["sq"])
            for j in range(8):
                k.mm(ss_ps[:, :n], ones[:], sq[:, j, :n], j == 0, j == 7, ["ones", "sq"], ["ss_ps"])
            k.act(rs[:, :n], ss_ps[:, :n], AF.Sqrt, ["ss_ps"], ["rs"], bias=EPS, scale=1.0 / D)
            k.recip(rstd[:, :n], rs[:, :n], ["rs"], ["rstd"])
            for j in range(8):
                k.stt(tmp[:, j % 2, :n], xb[:, j, :n], A2[:, j, which:which + 1], rstd[:, :n], ALU.mult, ALU.mult,
                      [xk + f"_{j}", "A2", "rstd"], [f"tmpn{j % 2}"])
                k.act(hT[:, j, :n], tmp[:, j % 2, :n], AF.Identity, [f"tmpn{j % 2}", "modT"], [f"h2T_{j}"],
                      bias=modT[:, 24 + j, which:which + 1], scale=1.0)
            hkeys = [f"h2T_{j}" for j in range(8)]
            for m in range(NFF):
                g_ = pg[m % 2]; gk = f"pg{m % 2}"; u_ = pu[m % 2]; uk = f"pu{m % 2}"
                for kc in range(8):
                    k.mm(g_[:, :n], Wg[:, kc, m * 128:(m + 1) * 128], hT[:, kc, :n], kc == 0, kc == 7, WG_KEYS + hkeys, [gk])
                for kc in range(8):
                    k.mm(u_[:, :n], Wu[:, kc, m * 128:(m + 1) * 128], hT[:, kc, :n], kc == 0, kc == 7, WU_KEYS + hkeys, [uk])
                k.act(sg[:, m % 2, :n], g_[:, :n], AF.Silu, [gk], [f"sg{m % 2}"])
                k.act(uc[:, m % 2, :n], u_[:, :n], AF.Copy, [uk], [f"uc{m % 2}"])
                k.tt(aT[:, m, :n], sg[:, m % 2, :n], uc[:, m % 2, :n], ALU.mult, [f"sg{m % 2}", f"uc{m % 2}"], [f"aT{m}"])
            akeys = [f"aT{m}" for m in range(NFF)]
            for j in range(8):
                ps = py[ny % 2]; pk = f"py{ny % 2}"; ny += 1
                for m in range(NFF):
                    k.mm(ps[:, :n], Wd[:, m, j * 128:(j + 1) * 128], aT[:, m, :n], m == 0, m == NFF - 1, WD_KEYS + akeys, [pk])
                e_ = ev[:, nev % 2, :n]; ek = f"ev{nev % 2}"; nev += 1
                k.act(e_, ps[:, :n], AF.Identity, [pk, "modT"], [ek], scale=modT[:, 40 + j, which:which + 1])
                k.tt(xb[:, j, :n], xb[:, j, :n], e_, ALU.add, [xk + f"_{j}", ek], [xk + f"_{j}"], eng="gpsimd")
            S.dma("sync", xov[:, :, t0:t0 + n], xb[:, :, :n], reads=xkeys)
        S.emit()


def e_inputs(l, core, x_cur, ctx_cur, mixT_core, modT_core, inp):
    b, hh = core_bh(core)
    xT = np.concatenate([x_cur[b, hh * NLAT:(hh + 1) * NLAT, :].T, ctx_cur[b, hh * NCTX:(hh + 1) * NCTX, :].T], axis=1)
    return {"xT": np.ascontiguousarray(xT, dtype=np.float32), "mixT": np.ascontiguousarray(mixT_core, dtype=np.float32),
            "modT": modT_core, "norm2_w": pcol(inp["norm2_w"][l]), "w_out": inp["w_out"][l],
            "w_gate": inp["ffn_w_gate"][l], "w_up": inp["ffn_w_up"][l], "w_down": inp["ffn_w_down"][l]}


def band_mask():
    s = np.arange(128)[:, None]
    t = np.arange(384)[None, :]
    return ((t - s >= 0) & (t - s <= 256)).astype(np.float32)


def emit_att(nc, bank, pfx, io, dbg=None):
    qT_d, kT_d, sink_d, mask_d, out_d = io["qT"], io["kT"], io["sink"], io["mask"], io["attT"]
    NB = SEQ // 128
    with ExitStack() as es:
        S = Sched(nc, es, bank, pfx)
        k = K(S)
        kT = S.sb("kT", [64, NTOK], BF16); S.dma("gpsimd", kT[:], kT_d, writes=["kT"])
        qT = S.sb("qT", [64, 3, NTOK], BF16)
        S.dma("gpsimd", qT[:], qT_d.rearrange("(h d) t -> d h t", d=64), writes=["qT"])
        V = S.sb("V", [128, 34, 64], BF16)
        S.dma("gpsimd", V[:, 0:NB, :], io["v_lat"].rearrange("(j p) d -> p j d", p=128), writes=["V"])
        S.dma("gpsimd", V[:, NB:NB + 2, :], io["v_ctx"].rearrange("(j p) d -> p j d", p=128), writes=["V"])
        ones64 = S.sb("ones64", [128, 64], BF16); k.memset("gpsimd", ones64[:], 1.0, ["ones64"])
        mask = S.sb("mask", [128, 384], BF16); S.dma("gpsimd", mask[:], mask_d, writes=["mask"])
        sink = S.sb("sink", [64, 3]); S.dma("sync", sink[:], sink_d, writes=["sink"])
        esink = S.sb("esink", [64, 3]); k.act(esink[:], sink[:], AF.Exp, ["sink"], ["esink"])
        P = [S.sb(f"P{r}", [128, 3, 384], BF16) for r in range(4)]
        Pe = [S.sb(f"Pe{r}", [128, 384], BF16) for r in range(2)]
        Pc = [S.sb(f"Pc{r}", [128, 384], BF16) for r in range(4)]
        psc = [S.ps(f"psc{i}", [128, 512]) for i in range(3)]
        po = [S.ps(f"po{i}", [128, 512]) for i in range(2)]
        pd = [S.ps(f"pd{i}", [128, 512]) for i in range(2)]
        den = S.sb("den", [64, 384]); rden = S.sb("rden", [64, 384])
        ost = [S.sb(f"ost{i}", [64, 384]) for i in range(2)]
        outv = out_d.rearrange("(h d) t -> d h t", d=64)
        cnt = {"sc": 0, "pe": 0, "pv": 0, "pc": 0}

        def local_scores(j):
            b0 = max(j - 1, 0); b1 = min(j + 1, NB - 1)
            q0 = b0 * 128; ln = (b1 - b0 + 1) * 128; mc0 = (b0 - (j - 1)) * 128
            Pj = P[j % 4]; pk_ = f"P{j % 4}"
            for h in range(3):
                ps = psc[cnt["sc"] % 3]; sk = f"psc{cnt['sc'] % 3}"; cnt["sc"] += 1
                k.mm(ps[:, :ln], kT[:, j * 128:(j + 1) * 128], qT[:, h, q0:q0 + ln], True, True, ["kT", "qT"], [sk])
                pe = Pe[cnt["pe"] % 2]; ek = f"Pe{cnt['pe'] % 2}"; cnt["pe"] += 1
                k.act(pe[:, :ln], ps[:, :ln], AF.Exp, [sk], [ek], scale=0.125)
                k.tt(Pj[:, h, mc0:mc0 + ln], pe[:, :ln], mask[:, mc0:mc0 + ln], ALU.mult, [ek, "mask"], [pk_ + f"_{h}"], eng="gpsimd")

        def ctx_scores(qtok0):
            res = []
            for c in range(2):
                ps = psc[cnt["sc"] % 3]; sk = f"psc{cnt['sc'] % 3}"; cnt["sc"] += 1
                k.mm(ps[:, :384].rearrange("p (h t) -> p h t", h=3), kT[:, SEQ + c * 128:SEQ + (c + 1) * 128],
                     qT[:, :, qtok0:qtok0 + 128], True, True, ["kT", "qT"], [sk])
                pc = Pc[cnt["pc"] % 4]; ck = f"Pc{cnt['pc'] % 4}"; cnt["pc"] += 1
                k.act(pc[:, :], ps[:, :384], AF.Exp, [sk], [ck], scale=0.125)
                res.append((pc[:, :].rearrange("p (h t) -> p h t", h=3), NB + c, [ck]))
            return res

        def pv(qtok0, terms):
            i_ = cnt["pv"]; cnt["pv"] += 1
            o_ = po[i_ % 2]; ok = f"po{i_ % 2}"; d_ = pd[i_ % 2]; dk = f"pd{i_ % 2}"
            ov = o_[:64, :384].rearrange("p (h t) -> p h t", h=3)
            dv = d_[:64, :384].rearrange("p (h t) -> p h t", h=3)
            nt = len(terms)
            for ti, (pap, vb, keys) in enumerate(terms):
                k.mm(ov, V[:, vb, :], pap, ti == 0, ti == nt - 1, ["V"] + keys, [ok])
            for ti, (pap, vb, keys) in enumerate(terms):
                k.mm(dv, ones64[:], pap, ti == 0, ti == nt - 1, ["ones64"] + keys, [dk])
            for h in range(3):
                k.ts(den[:, h * 128:(h + 1) * 128], d_[:64, h * 128:(h + 1) * 128], esink[:, h:h + 1], None, ALU.add, None,
                     [dk, "esink"], ["den"])
            k.recip(rden[:], den[:], ["den"], ["rden"])
            st = ost[i_ % 2]; sk = f"ost{i_ % 2}"
            k.tt(st[:], o_[:64, :384], rden[:], ALU.mult, [ok, "rden"], [sk])
            S.dma("sync", outv[:, :, qtok0:qtok0 + 128], st[:].rearrange("p (h t) -> p h t", h=3), reads=[sk])

        def local_terms(i):
            terms = []
            for j, c0 in ((i - 1, 256), (i, 128), (i + 1, 0)):
                if 0 <= j < NB:
                    terms.append((P[j % 4][:, :, c0:c0 + 128], j, [f"P{j % 4}_{h}" for h in range(3)]))
            return terms

        nblk = NB if dbg is None else 3
        for j in range(nblk):
            local_scores(j)
            if j >= 1:
                i = j - 1
                pv(i * 128, local_terms(i) + ctx_scores(i * 128))
        if dbg is None:
            i = NB - 1
            pv(i * 128, local_terms(i) + ctx_scores(i * 128))
            for cq in range(2):
                pv(SEQ + cq * 128, ctx_scores(SEQ + cq * 128))
        S.emit()


def att_inputs(l, core, fm_all, tm_all, inp, consts):
    b, g = core_bh(core)
    f0, f1 = fm_all[2 * b], fm_all[2 * b + 1]
    t0, t1 = tm_all[2 * b], tm_all[2 * b + 1]

    def seq_fm(rows):
        return np.concatenate([f0[rows, :NLAT], f1[rows, :NLAT], f0[rows, NLAT:], f1[rows, NLAT:]], axis=1)

    def seq_tm(cols):
        return np.concatenate([t0[:NLAT, cols], t1[:NLAT, cols], t0[NLAT:, cols], t1[NLAT:, cols]], axis=0)

    return {"qT": np.ascontiguousarray(seq_fm(slice(192 * g, 192 * g + 192))),
            "kT": np.ascontiguousarray(seq_fm(slice(384 + 64 * g, 384 + 64 * g + 64))),
            "v": np.ascontiguousarray(seq_tm(slice(64 * g, 64 * g + 64))),
            "sink": np.ascontiguousarray(np.tile(inp["attn_sink"][l][3 * g:3 * g + 3][None, :], (64, 1)).astype(np.float32)),
            "mask": consts["mask"]}


NCH = NTOK // 128


def ml_consts():
    sel = np.zeros((34, 4, 128), np.float32)
    for i, r in enumerate((0, 1, 32, 33)):
        sel[r, i, :] = 1.0
    ident = np.eye(128, dtype=np.float32)
    s = np.arange(128)[:, None]; t = np.arange(128)[None, :]
    return {"sel": sel.reshape(34, 512), "ident": ident, "maskf": (s <= t).astype(np.float32), "maskb": (s >= t).astype(np.float32)}


def proc_chunk(dr, c):
    if dr == 0:
        return 32 + c if c < 2 else c - 2
    return 33 - c if c < 2 else 31 - (c - 2)


def emit_ml(nc, bank, pfx, io, dbg=None):
    qT_d, kT_d, nw_d, sel_d, id_d, mf_d, mb_d, out_d, scr = (io["qT"], io["kT"], io["nw_bc"], io["sel"], io["ident"],
                                                            io["maskf"], io["maskb"], io["mloT"], io["scr"])
    g4 = io["g4"]
    with ExitStack() as es:
        S = Sched(nc, es, bank, pfx)
        k = K(S)
        qT = S.sb("qT", [128, NTOK], BF16); S.dma("gpsimd", qT[:], qT_d, writes=["qT"])
        kT = S.sb("kT", [128, NTOK], BF16); S.dma("gpsimd", kT[:], kT_d, writes=["kT"])
        ktok = S.sb("ktok", [128, NCH, 128], BF16)
        S.dma("gpsimd", ktok[:, 0:32, :], io["ktok"][0].rearrange("(c p) d -> p c d", p=128), writes=["ktok"])
        S.dma("gpsimd", ktok[:, 32:34, :], io["ktok"][1].rearrange("(c p) d -> p c d", p=128), writes=["ktok"])
        Vaug = S.sb("Vaug", [128, NCH, 2, 65], BF16)
        k.memset("gpsimd", Vaug[:], 1.0, ["Vaug"])
        for h_ in range(2):
            S.dma("gpsimd", Vaug[:, 0:32, h_, 0:64], io["vtok"][0][:, h_ * 64:(h_ + 1) * 64].rearrange("(c p) d -> p c d", p=128), writes=["Vaug"])
            S.dma("gpsimd", Vaug[:, 32:34, h_, 0:64], io["vtok"][1][:, h_ * 64:(h_ + 1) * 64].rearrange("(c p) d -> p c d", p=128), writes=["Vaug"])
        otok = S.sb("otok", [128, NCH, 128])
        S.dma("sync", otok[:, 0:32, :], io["otok"][0].rearrange("(c p) d -> p c d", p=128), writes=["otok"])
        S.dma("sync", otok[:, 32:34, :], io["otok"][1].rearrange("(c p) d -> p c d", p=128), writes=["otok"])
        nwb = S.sb("nwb", [128, 128]); S.dma("sync", nwb[:], nw_d, writes=["nwb"])
        sel = S.sb("sel", [34, 512]); S.dma("sync", sel[:], sel_d, writes=["sel"])
        ident = S.sb("ident", [128, 128]); S.dma("sync", ident[:], id_d, writes=["ident"])
        masks = []
        for nm, d_ in (("maskf", mf_d), ("maskb", mb_d)):
            m_ = S.sb(nm, [128, 128], BF16); S.dma("gpsimd", m_[:], d_, writes=[nm]); masks.append(m_)
        LI = S.sb("LI", [34, NTOK]); FF = S.sb("FF", [34, NTOK]); TM = S.sb("TM", [34, NTOK]); BN = S.sb("BN", [34, NTOK])
        for t_, nm in ((LI, "LI"), (FF, "FF"), (TM, "TM")):
            k.memset("gpsimd", t_[:], 0.0, [nm])
        S.dma("sync", LI[0:2, 0:CTX], g4[0][:, SEQ:NTOK], writes=["LI"]); S.dma("sync", LI[0:2, CTX:NTOK], g4[0][:, 0:SEQ], writes=["LI"], reads=["LI"])
        S.dma("sync", FF[0:2, 0:CTX], g4[1][:, SEQ:NTOK], writes=["FF"]); S.dma("sync", FF[0:2, CTX:NTOK], g4[1][:, 0:SEQ], writes=["FF"], reads=["FF"])

        def rev_rows(t_):
            return bass.AP(t_, 32 * NTOK + NTOK - 1, [[NTOK, 2], [-1, NTOK]])

        def revc_rows(t_):
            return bass.AP(t_, 32 * NTOK + 127, [[NTOK, 2], [128, NCH], [-1, 128]])

        S.dma("sync", TM[32:34, :], g4[2], writes=["TM"], reads=["TM"])
        k.copy("vector", LI[32:34, :], rev_rows(TM), ["TM", "LI"], ["LI"])
        S.dma("sync", TM[32:34, :], g4[3], writes=["TM"], reads=["TM"])
        k.copy("vector", FF[32:34, :], rev_rows(TM), ["TM", "FF"], ["FF"])
        k.act(FF[:], FF[:], AF.Exp, ["FF"], ["FF"], scale=-1.0)
        k.act(FF[:], FF[:], AF.Ln, ["FF"], ["FF"], bias=1.0, scale=1.0)
        S.op("vector", lambda e: e.tensor_tensor_scan(out=BN[:], data0=FF[:], data1=FF[:], initial=0.0, op0=ALU.add, op1=ALU.bypass),
             reads=["FF"], writes=["BN"])
        k.tt(LI[:], LI[:], BN[:], ALU.add, ["LI", "BN"], ["LI"])
        S.op("vector", lambda e: e.tensor_tensor_scan(out=TM[:], data0=LI[:], data1=LI[:], initial=0.0, op0=ALU.max, op1=ALU.bypass),
             reads=["LI", "TM"], writes=["TM"])
        MC = S.sb("MC", [34, NCH]); MP = S.sb("MP", [34, NCH]); DEC = S.sb("DEC", [34, NCH])
        k.copy("vector", MC[:], TM[:, 127::128], ["TM"], ["MC"])
        k.memset("vector", MP[:], 0.0, ["MP"])
        k.copy("vector", MP[:, 1:NCH], MC[:, 0:NCH - 1], ["MC", "MP"], ["MP"])
        k.tt(DEC[:], MP[:], MC[:], ALU.subtract, ["MP", "MC"], ["DEC"])
        k.act(DEC[:], DEC[:], AF.Exp, ["DEC"], ["DEC"])
        mcb = MC[:, :].unsqueeze(2).broadcast_to([34, NCH, 128])
        li3 = LI[:, :].rearrange("p (c s) -> p c s", s=128); bn3 = BN[:, :].rearrange("p (c s) -> p c s", s=128)
        k.tt(li3, li3, mcb, ALU.subtract, ["LI", "MC"], ["LI"])
        k.act(LI[:], LI[:], AF.Exp, ["LI"], ["LI"])
        k.tt(bn3, bn3, mcb, ALU.subtract, ["BN", "MC"], ["BN"])
        k.act(BN[:], BN[:], AF.Exp, ["BN"], ["BN"])
        COL = S.sb("COL", [128, 2, 4, NCH])
        ptr = S.ps("ptr", [128, 512])
        T68 = S.sb("T68", [68, 128])
        for qi, (src, nm) in enumerate(((LI, "LI"), (BN, "BN"))):
            k.copy("vector", FF[32:34, :].rearrange("p (c s) -> p c s", s=128), revc_rows(src), [nm, "FF"], ["FF"])
            S.dma("sync", scr[qi, 0:2, :], src[0:2, :], reads=[nm], writes=[f"scr{qi}"])
            S.dma("sync", scr[qi, 2:4, :], FF[32:34, :], reads=["FF", f"scr{qi}"], writes=[f"scr{qi}"])
            for dr in range(2):
                S.dma("sync", T68[:], scr[qi, 2 * dr:2 * dr + 2, :].rearrange("r (c s) -> (r c) s", s=128), reads=[f"scr{qi}"], writes=["T68"])
                S.op("tensor", lambda e: e.transpose(ptr[:, 0:68], T68[:], ident[0:68, 0:68]), reads=["T68", "ident"], writes=["ptr"])
                k.copy("vector", COL[:, qi, 2 * dr:2 * dr + 2, :], ptr[:, 0:68].rearrange("p (r c) -> p r c", r=2), ["ptr"], ["COL"])
        DECB = S.sb("DECB", [128, 4, NCH])
        for i in range(4):
            k.mm(ptr[:, 0:NCH], sel[:, i * 128:(i + 1) * 128], DEC[:], True, True, ["sel", "DEC", "ptr"], ["ptr"])
            k.copy("vector", DECB[:, i, :], ptr[:, 0:NCH], ["ptr"], ["DECB"])
        Cst = S.sb("Cst", [128, 4, 65]); Cd = S.sb("Cd", [128, 4, 65]); Cdb = S.sb("Cdb", [128, 4, 65], BF16)
        k.memset("vector", Cd[:], 0.0, ["Cd0", "Cd1", "Cd2", "Cd3"])
        k.memset("vector", Cdb[:], 0.0, ["Cdb0", "Cdb1", "Cdb2", "Cdb3"])
        HH = [S.sb("HF", [128, NCH, 128]), S.sb("HB", [128, NCH, 128])]
        pA = [S.ps(f"pA{i}", [128, 512]) for i in range(2)]
        pN = [S.ps(f"pN{i}", [128, 512]) for i in range(2)]
        pC = [S.ps(f"pC{i}", [128, 512]) for i in range(2)]
        pt1 = [S.sb(f"pt1_{i}", [128, 128], BF16) for i in range(2)]
        PT = [S.sb(f"PT{i}", [128, 128], BF16) for i in range(2)]
        VU = [S.sb(f"VU{i}", [128, 65], BF16) for i in range(2)]
        dcol = S.sb("dcol", [128, 2]); rcol = S.sb("rcol", [128, 2]); dabs = S.sb("dabs", [128, 2])
        n = 0
        nproc = NCH if dbg is None else 4
        for c in range(nproc):
            for i in range(4):
                dr, h = i // 2, i % 2
                ncn = proc_chunk(dr, c)
                tok0 = ncn * 128
                hs = slice(h * 64, (h + 1) * 64)
                r = n % 2; n += 1
                qc = qT[hs, tok0:tok0 + 128]; kc = kT[hs, tok0:tok0 + 128]
                k.mm(pA[r][:, 0:128], kc, qc, True, True, ["kT", "qT"], [f"pA{r}"])
                k.act(pt1[r][:], pA[r][:, 0:128], AF.Identity, [f"pA{r}", "COL"], [f"pt1_{r}"], scale=COL[:, 0, i, c:c + 1])
                k.tt(PT[r][:], pt1[r][:], masks[dr][:], ALU.mult, [f"pt1_{r}", "maskf", "maskb"], [f"PT{r}"], eng="gpsimd")
                k.act(VU[r][:], Vaug[:, ncn, h, :], AF.Identity, ["Vaug", "COL"], [f"VU{r}"], scale=COL[:, 0, i, c:c + 1])
                k.mm(pN[r][:, 0:65], PT[r][:], Vaug[:, ncn, h, :], True, False, [f"PT{r}", "Vaug"], [f"pN{r}"])
                k.mm(pN[r][:, 0:65], qc, Cdb[hs, i, :], False, True, ["qT", f"Cdb{i}"], [f"pN{r}"])
                k.act(dabs[:, r:r + 1], pN[r][:, 64:65], AF.Abs, [f"pN{r}"], [f"dabs{r}"])
                k.tt(dcol[:, r:r + 1], dabs[:, r:r + 1], COL[:, 1, i, c:c + 1], ALU.max, [f"dabs{r}", "COL"], [f"dcol{r}"])
                k.recip(rcol[:, r:r + 1], dcol[:, r:r + 1], [f"dcol{r}"], [f"rcol{r}"])
                k.act(HH[dr][:, ncn, hs], pN[r][:, 0:64], AF.Identity, [f"pN{r}", f"rcol{r}"], [f"H{dr}_{ncn}_{h}"], scale=rcol[:, r:r + 1])
                k.mm(pC[r][:, 0:65], ktok[:, ncn, :], VU[r][:], True, True, ["ktok", f"VU{r}"], [f"pC{r}"])
                k.tt(Cst[hs, i, :], Cd[hs, i, :], pC[r][hs, 0:65], ALU.add, [f"Cd{i}", f"pC{r}"], [f"Cst{i}"])
                if c + 1 < nproc:
                    k.ts(Cd[hs, i, :], Cst[hs, i, :], DECB[hs, i, c + 1:c + 2], None, ALU.mult, None, [f"Cst{i}", "DECB"], [f"Cd{i}"])
                    k.copy("scalar", Cdb[hs, i, :], Cd[hs, i, :], [f"Cd{i}"], [f"Cdb{i}"])
        hsum = [S.sb(f"hsum{i}", [128, 128]) for i in range(2)]
        hsq = S.sb("hsq", [128, 128]); ssq = S.sb("ssq", [128, 2]); rsq = S.sb("rsq", [128, 2]); rin = S.sb("rin", [128, 2])
        hn = S.sb("hn", [128, 128]); sgo = S.sb("sgo", [128, 128])
        ost = [S.sb(f"ost{i}", [128, 128]) for i in range(2)]
        ostT = [S.sb(f"ostT{i}", [128, 128]) for i in range(2)]
        chunks = range(NCH) if dbg is None else [32, 33, 0, 1]
        for j, ncn in enumerate(chunks):
            r = j % 2
            hk = [f"H{dr}_{ncn}_{h}" for dr in range(2) for h in range(2)]
            k.tt(hsum[r][:], HH[0][:, ncn, :], HH[1][:, ncn, :], ALU.add, hk, [f"hsum{r}"], eng="gpsimd")
            k.tt(hsq[:], hsum[r][:], hsum[r][:], ALU.mult, [f"hsum{r}"], ["hsq"], eng="gpsimd")
            S.op("vector", lambda e, a=hsq, b=ssq: e.tensor_reduce(out=b[:], in_=a[:].rearrange("p (h d) -> p h d", h=2), axis=AX.X, op=ALU.add),
                 reads=["hsq"], writes=["ssq"])
            k.act(rsq[:], ssq[:], AF.Sqrt, ["ssq"], ["rsq"], bias=EPS, scale=1.0 / 64)
            k.recip(rin[:], rsq[:], ["rsq"], ["rin"])
            for h in range(2):
                k.act(hn[:, h * 64:(h + 1) * 64], hsum[r][:, h * 64:(h + 1) * 64], AF.Identity, [f"hsum{r}", "rin"], [f"hn{h}"], scale=rin[:, h:h + 1])
            k.act(sgo[:], otok[:, ncn, :], AF.Sigmoid, ["otok"], ["sgo"])
            k.tt(hn[:], hn[:], nwb[:], ALU.mult, ["hn0", "hn1", "nwb"], ["hn0", "hn1"])
            k.tt(ost[r][:], hn[:], sgo[:], ALU.mult, ["hn0", "hn1", "sgo"], [f"ost{r}"])
            S.op("tensor", lambda e, r=r: e.transpose(ptr[:, 0:128], ost[r][:], ident[:]), reads=[f"ost{r}", "ident", "ptr"], writes=["ptr"])
            k.copy("scalar", ostT[r][:], ptr[:, 0:128], ["ptr"], [f"ostT{r}"])
            S.dma("sync", out_d[:, ncn * 128:(ncn + 1) * 128], ostT[r][:], reads=[f"ostT{r}"])
        S.emit()


def seq_fm(fm_all, b, rows):
    f0, f1 = fm_all[2 * b], fm_all[2 * b + 1]
    return np.ascontiguousarray(np.concatenate([f0[rows, :NLAT], f1[rows, :NLAT], f0[rows, NLAT:], f1[rows, NLAT:]], axis=1))


def seq_tm(tm_all, b, cols):
    t0, t1 = tm_all[2 * b], tm_all[2 * b + 1]
    return np.ascontiguousarray(np.concatenate([t0[:NLAT, cols], t1[:NLAT, cols], t0[NLAT:, cols], t1[NLAT:, cols]], axis=0))


def ml_inputs(l, core, fm_all, tm_all, inp, consts):
    b, hp = core_bh(core)
    grow = [1024 + 2 * hp, 1024 + 2 * hp + 1, 1028 + 2 * hp, 1028 + 2 * hp + 1, 1032 + 2 * hp, 1032 + 2 * hp + 1, 1036 + 2 * hp, 1036 + 2 * hp + 1]
    d = {"qT": seq_fm(fm_all, b, slice(512 + 128 * hp, 512 + 128 * hp + 128)),
         "kT": seq_fm(fm_all, b, slice(768 + 128 * hp, 768 + 128 * hp + 128)),
         "ktok": seq_tm(tm_all, b, slice(128 + 128 * hp, 128 + 128 * hp + 128)),
         "vtok": seq_tm(tm_all, b, slice(384 + 128 * hp, 384 + 128 * hp + 128)),
         "otok": seq_tm(tm_all, b, slice(640 + 128 * hp, 640 + 128 * hp + 128)),
         "gT": seq_fm(fm_all, b, grow),
         "nw_bc": np.ascontiguousarray(np.tile(inp["ml_norm_w"][l][128 * hp:128 * hp + 128][None, :], (128, 1)).astype(np.float32))}
    d.update(consts["ml"])
    return d


HY_CFG = {"lat": dict(L=SEQ, B=1024, nb=4), "ctx": dict(L=CTX, B=256, nb=1)}


def dft_tables(B):
    t = np.arange(B, dtype=np.float64)[:, None]
    om = np.pi * (2 * np.arange(B, dtype=np.float64)[None, :] + 1) / (2 * B)
    return np.cos(t * om).astype(np.float32), np.sin(t * om).astype(np.float32)


def pos_feats(L):
    t = np.linspace(0.0, 1.0, L, dtype=np.float32)[:, None]
    ang = (np.float32(2.0 * math.pi / L) * np.arange(L, dtype=np.float32))[:, None]
    bands = np.linspace(1e-4, 15, 16, dtype=np.float32)[None, :]
    feats = np.concatenate([t, np.cos(bands * ang), -np.sin(bands * ang)], axis=-1).astype(np.float32)
    return feats, t[:, 0]


def hy_consts():
    c = {}
    for nm, cfg in HY_CFG.items():
        L, B = cfg["L"], cfg["B"]
        TC, TS = dft_tables(B)
        feats, t = pos_feats(L)
        c[nm] = {"TC": TC, "TS": TS, "TCT": np.ascontiguousarray(TC.T), "TST": np.ascontiguousarray(TS.T),
                 "featsT": np.ascontiguousarray(feats.T), "featsTr": np.ascontiguousarray(feats[::-1].T),
                 "negt": np.ascontiguousarray((-t).reshape(L // 128, 128).T), "negtr": np.ascontiguousarray((-t[::-1]).reshape(L // 128, 128).T)}
    alt = np.where(np.arange(128) % 2 == 0, 1.0, -1.0).astype(np.float32).reshape(128, 1)
    c["alt"] = alt
    return c


def emit_F(nc, bank, pfx, io0):
    w1_d, w2_d, fb_d, alt_d = io0["w1"], io0["w2"], io0["fb"], io0["alt"]
    io = io0
    with ExitStack() as es:
        S = Sched(nc, es, bank, pfx)
        k = K(S)
        w1 = S.sb("w1", [33, 64]); S.dma("sync", w1[:], w1_d, writes=["w1"])
        w2 = S.sb("w2", [64, 64]); S.dma("sync", w2[:], w2_d, writes=["w2"])
        w3 = S.sb("w3", [64, 768])
        fb = S.sb("fb", [64, 4]); S.dma("sync", fb[:], fb_d, writes=["fb"])
        fbb = S.sb("fbb", [64, 2])
        k.ts(fbb[:], fb[:, 1:3], fb[:, 0:1], None, ALU.mult, None, ["fb"], ["fbb"])
        adec = S.sb("adec", [128, 768])
        alt = S.sb("alt", [128, 1]); S.dma("sync", alt[:], alt_d, writes=["alt"])
        pz = [S.ps(f"pz{i}", [128, 512]) for i in range(2)]
        pP = [S.ps(f"pP{i}", [128, 512]) for i in range(4)]
        TWO_PI = 2.0 * math.pi
        LM, BM, NBM = SEQ, 1024, 4
        altB = S.sb("altB", [128, 1])
        wm = S.sb("wm", [64, 512])
        negt_s = S.sb("negt", [128, LM // 128]); negtr_s = S.sb("negtr", [128, LM // 128])
        TC_s = S.sb("TC", [128, BM // 128, BM], BF16); TS_s = S.sb("TS", [128, BM // 128, BM], BF16)
        Gt_s = [S.sb(f"Gt{dr}", [128, LM // 128, 384], BF16) for dr in range(2)]
        feats_s = S.sb("feats", [33, LM]); z1_s = S.sb("z1", [64, LM]); z2_s = [S.sb(f"z2_{dr}", [64, LM]) for dr in range(2)]
        arg = S.sb("arg", [64, 512]); fsb = S.sb("fsb", [128, 384]); dct = S.sb("dct", [128, 384])
        XY_s = S.sb("XY", [128, 2 * NBM, 4, 384])
        gst = [S.sb(f"gst{i}", [128, 384]) for i in range(1)]
        for nm, cfg in HY_CFG.items():
            L, B, nb = cfg["L"], cfg["B"], cfg["nb"]
            d = io[nm]
            nt = L // 128; ntb = B // 128
            k.ts(altB[:], alt[:], 1.0 / B, None, ALU.mult, None, ["alt"], ["altB"])
            negt = negt_s[:, 0:nt]; negtr = negtr_s[:, 0:nt]
            S.dma("sync", negt, d["negt"], writes=["negt"]); S.dma("sync", negtr, d["negtr"], writes=["negtr"])
            TC = TC_s[:, 0:ntb, 0:B]; TS = TS_s[:, 0:ntb, 0:B]
            S.dma("gpsimd", TC, d["TC"].rearrange("(a p) k -> p a k", p=128), writes=["TC"])
            S.dma("gpsimd", TS, d["TS"].rearrange("(a p) k -> p a k", p=128), writes=["TS"])
            Gt = [Gt_s[dr][:, 0:nt, :] for dr in range(2)]
            feats = feats_s[:, 0:L]; z1 = z1_s[:, 0:L]
            XY = XY_s[:, 0:2 * nb]
            for dr in range(2):
                z2 = z2_s[dr][:, 0:L]
                S.dma("sync", feats, d["featsr" if dr else "feats"], writes=["feats"])
                for si, (src, w_, bcol, dst, K_) in enumerate(((feats, w1, 0, z1, 33), (z1, w2, 1, z2, 64))):
                    for c0 in range(0, L, 512):
                        n = min(512, L - c0)
                        ps = pz[(c0 // 512) % 2]; pk = f"pz{(c0 // 512) % 2}"
                        k.mm(ps[:64, :n], w_[:K_, :], src[:K_, c0:c0 + n], True, True, ["w1", "w2", "feats", "z1"], [pk])
                        k.act(arg[:, :n], ps[:64, :n], AF.Identity, [pk, "fb", "fbb"], ["arg"], scale=fb[:, 0:1], bias=fbb[:, bcol:bcol + 1])
                        for _ in range(2):
                            for (cmp_, val, sh) in ((ALU.is_gt, math.pi, -TWO_PI), (ALU.is_lt, -math.pi, TWO_PI)):
                                k.ts(wm[:, :n], arg[:, :n], val, None, cmp_, None, ["arg"], ["wm"])
                                k.stt(arg[:, :n], wm[:, :n], sh, arg[:, :n], ALU.mult, ALU.add, ["wm", "arg"], ["arg"])
                        k.act(dst[:, c0:c0 + n], arg[:, :n], AF.Sin, ["arg"], ["z1" if si == 0 else f"z2_{dr}"])
            for hh in range(len(io0["w3c"])):
                S.dma("sync", w3[:], io0["w3c"][hh], writes=["w3"])
                S.dma("sync", adec[:], io0["decay_bc"][hh], writes=["adec"])
                k.act(adec[:], adec[:], AF.Abs, ["adec"], ["adec"])
                for dr in range(2):
                    z2 = z2_s[dr][:, 0:L]
                    tcol = negtr if dr else negt
                    for mt in range(nt):
                        ps = pz[mt % 2]; pk = f"pz{mt % 2}"
                        k.mm(ps[:, :384], z2[:, mt * 128:(mt + 1) * 128], w3[:, dr * 384:(dr + 1) * 384], True, True, [f"z2_{dr}", "w3"], [pk])
                        k.act(dct[:], adec[:, dr * 384:(dr + 1) * 384], AF.Exp, ["adec", "negt", "negtr"], ["dct"], scale=tcol[:, mt:mt + 1])
                        k.copy("scalar", fsb[:], ps[:, :384], [pk], ["fsb"])
                        k.tt(Gt[dr][:, mt, :], fsb[:], dct[:], ALU.mult, ["fsb", "dct"], [f"Gt{dr}_{mt // ntb}"], eng="gpsimd")
                npp = 0; ng = 0
                for kt in range(ntb):
                    for ei in range(2 * nb):
                        src = Gt[1] if ei < nb else Gt[0]
                        base = (ei if ei < nb else ei - nb) * ntb
                        bk = f"Gt{1 if ei < nb else 0}_{ei if ei < nb else ei - nb}"
                        for ti, (T_, tk) in enumerate(((TC, "TC"), (TS, "TS"))):
                            ps = pP[npp % 4]; pk = f"pP{npp % 4}"; npp += 1
                            for tt in range(ntb):
                                k.mm(ps[:, :384], T_[:, tt, kt * 128:(kt + 1) * 128], src[:, base + tt, :], tt == 0, tt == ntb - 1, [tk, bk], [pk])
                            k.act(XY[:, ei, ti, :], ps[:, :384], AF.Identity, [pk], [f"XY{ei}_{ti}"], scale=1.0 / B)
                            k.act(XY[:, ei, 2 + ti, :], ps[:, :384], AF.Identity, [pk, "altB"], [f"XY{ei}_{2 + ti}"], scale=altB[:, 0:1])
                    for dd in range(-(nb - 1), nb):
                        ei = dd + nb; q = (nb - 1) - dd
                        for rj in range(2):
                            g_, gk = ((gst[0], "gst0"), (fsb, "fsb"), (dct, "dct"))[ng % 3]; ng += 1
                            if rj == 0:
                                k.tt(g_[:], XY[:, ei, 0, :], XY[:, ei - 1, 3, :], ALU.add, [f"XY{ei}_0", f"XY{ei - 1}_3"], [gk], eng="gpsimd")
                            else:
                                k.tt(g_[:], XY[:, ei, 1, :], XY[:, ei - 1, 2, :], ALU.subtract, [f"XY{ei}_1", f"XY{ei - 1}_2"], [gk], eng="gpsimd")
                            S.dma("sync", d["G"][hh][:, kt, rj, :, q, :].rearrange("o p c -> p o c"), g_[:].rearrange("p (o c) -> p o c", o=2), reads=[gk])
        S.emit()


def f_inputs(core, inp, consts):
    l, hh = core // 2, core % 2
    cols = []
    for dr in range(2):
        for o in range(2):
            c0 = dr * 768 + o * 384 + hh * 192
            cols.extend(range(c0, c0 + 192))
    cols = np.array(cols)
    hc = consts["hy"]
    d = {"w1": inp["hy_w1"][l], "w2": inp["hy_w2"][l], "w3c": np.ascontiguousarray(inp["hy_w3"][l][:, cols]),
         "fb": np.ascontiguousarray(np.stack([inp["hy_freq"][l], inp["hy_b1"][l], inp["hy_b2"][l], inp["hy_b2"][l]], axis=1)),
         "decay_bc": np.ascontiguousarray(np.tile(inp["hy_decay"][l][cols][None, :], (128, 1))), "alt": hc["alt"]}
    for nm in HY_CFG:
        d[f"featsT_{nm}"] = hc[nm]["featsT"]; d[f"featsTr_{nm}"] = hc[nm]["featsTr"]
        d[f"negt_{nm}"] = hc[nm]["negt"]; d[f"negtr_{nm}"] = hc[nm]["negtr"]
        d[f"TC_{nm}"] = hc[nm]["TC"]; d[f"TS_{nm}"] = hc[nm]["TS"]
    return d


def emit_hy(nc, bank, pfx, io, dbg=None):
    cw_d, sk_d, Gd, Td, out_d, id_d = io["cw_bc"], io["sk_bc"], io["G"], io["T"], io["hyoT"], io["ident"]
    with ExitStack() as es:
        S = Sched(nc, es, bank, pfx)
        k = K(S)
        cw = S.sb("cw", [128, 4, 576]); S.dma("sync", cw[:].rearrange("p a c -> p (a c)"), cw_d, writes=["cw"])
        sk = S.sb("sk", [128, 2, 192]); S.dma("sync", sk[:].rearrange("p a c -> p (a c)"), sk_d, writes=["sk"])
        VXX = S.sb("VXX", [128, NCH, 576], BF16)
        Z = S.sb("Z", [128, NCH, 192], BF16)
        tabs = {}
        for nm, cfg in HY_CFG.items():
            B = cfg["B"]; ntb = B // 128
            tabs[nm] = []
            for ti, tname in enumerate(("TC", "TS", "TCT", "TST")):
                t_ = S.sb(f"{tname}_{nm}", [128, ntb, B], BF16)
                S.dma("gpsimd", t_[:], Td[nm][ti].rearrange("(a p) k -> p a k", p=128), writes=[f"{tname}_{nm}"])
                tabs[nm].append((t_, f"{tname}_{nm}"))
        stg = [S.sb(f"stg{i}", [128, 3, 576]) for i in range(2)]
        pr = [S.sb(f"pr{i}", [128, 4, 192]) for i in range(2)]
        pb = [S.sb(f"pb{i}", [128, 4, 192], BF16) for i in range(4)]
        ct0 = pr[0][:, :, :].rearrange("p a c -> p (a c)")[:, 0:576]; ct1 = pr[1][:, :, :].rearrange("p a c -> p (a c)")[:, 0:576]
        tile_base = {"lat": 0, "ctx": SEQ // 128}
        for nm, cfg in HY_CFG.items():
            L = cfg["L"]
            for tt in range(L // 128):
                g = tile_base[nm] + tt
                s_ = stg[g % 2]; skey = f"stg{g % 2}"
                tmt, pitch = io["tmT"]
                for part in range(3):
                    src = bass.AP(tmt, (io["row0"][nm] + tt * 128) * pitch + io["col0"] + part * 384, [[pitch, 128], [pitch, 3], [1, 192]])
                    S.dma("sync", s_[:, :, part * 192:(part + 1) * 192], src, writes=[skey])
                k.tt(ct0, s_[:, 0, :], cw[:, 0, :], ALU.mult, [skey, "cw"], ["pr0"])
                k.tt(ct1, s_[:, 1, :], cw[:, 1, :], ALU.mult, [skey, "cw"], ["pr1"], eng="gpsimd")
                k.tt(ct0, ct0, ct1, ALU.add, ["pr0", "pr1"], ["pr0"])
                k.tt(ct1, s_[:, 2, :], cw[:, 2, :], ALU.mult, [skey, "cw"], ["pr1"], eng="gpsimd")
                k.tt(ct0, ct0, ct1, ALU.add, ["pr0", "pr1"], ["pr0"])
                k.tt(VXX[:, g, :], ct0, cw[:, 3, :], ALU.add, ["pr0", "cw"], [f"VXX{g}"])
        psR = [S.ps(f"psR{i}", [128, 512]) for i in range(2)]
        psJ = [S.ps(f"psJ{i}", [128, 512]) for i in range(2)]
        pI = [S.ps(f"pI{i}", [128, 512]) for i in range(2)]
        NBM = 4
        RJ = [S.sb(f"RJ{i}", [128, 2, NBM, 192]) for i in range(2)]
        Gb = [S.sb(f"Gb{i}", [128, 2, 2 * NBM - 1, 192]) for i in range(1)]
        YAB = S.sb("YAB", [128, 8, 2, NBM, 192], BF16)
        et = [S.sb(f"et{i}", [128, 192]) for i in range(2)]
        ost = [S.sb(f"ost{i}", [128, 192]) for i in range(2)]
        oTs = [S.sb(f"oT{i}", [128, 128]) for i in range(2)]; ptp = S.ps("ptp", [128, 512])
        ident = S.sb("ident", [128, 128]); S.dma("sync", ident[:], id_d, writes=["ident"])
        identb = S.sb("identb", [128, 128], BF16); nidentb = S.sb("nidentb", [128, 128], BF16)
        k.act(identb[:], ident[:], AF.Identity, ["ident"], ["identb"], scale=1.0)
        k.act(nidentb[:], ident[:], AF.Identity, ["ident"], ["identb"], scale=-1.0)
        psY = S.ps("psY", [128, 512])
        cnt = {"f": 0, "g": 0, "i": 0, "e": 0}

        def conv(nm, o, src_fn, src_keys, epi):
            cfg = HY_CFG[nm]; B, nb = cfg["B"], cfg["nb"]; ntb = B // 128; nq = 2 * nb - 1
            base = tile_base[nm]
            (TC, kTC), (TS, kTS), (TCT, kTCT), (TST, kTST) = tabs[nm]
            for kt in range(ntb):
                rj = RJ[kt % 2]; rk = f"RJ{kt % 2}"
                for jp in range(0, nb, 2):
                    nj = min(2, nb - jp)
                    a_ = cnt["f"] % 2; cnt["f"] += 1
                    pR, pJ = psR[a_], psJ[a_]
                    for jj in range(nj):
                        for tt in range(ntb):
                            g = base + (jp + jj) * ntb + tt
                            k.mm(pR[:, jj * 192:(jj + 1) * 192], TC[:, tt, kt * 128:(kt + 1) * 128], src_fn(g), tt == 0, tt == ntb - 1,
                                 [kTC] + src_keys(g), [f"psR{a_}"], inc=(tt == ntb - 1 and jj == nj - 1))
                    for jj in range(nj):
                        for tt in range(ntb):
                            g = base + (jp + jj) * ntb + tt
                            k.mm(pJ[:, jj * 192:(jj + 1) * 192], TS[:, tt, kt * 128:(kt + 1) * 128], src_fn(g), tt == 0, tt == ntb - 1,
                                 [kTS] + src_keys(g), [f"psJ{a_}"], inc=(tt == ntb - 1 and jj == nj - 1))
                    k.copy("scalar", rj[:, 0, jp:jp + nj, :], pR[:, 0:nj * 192].rearrange("p (j c) -> p j c", j=nj), [f"psR{a_}"], [rk + f"_0_{jp}"])
                    k.copy("scalar", rj[:, 1, jp:jp + nj, :], pJ[:, 0:nj * 192].rearrange("p (j c) -> p j c", j=nj), [f"psJ{a_}"], [rk + f"_1_{jp}"])
                rkeys = [rk + f"_{a}_{jp}" for a in range(2) for jp in range(0, nb, 2)]
                gb = Gb[0]; gk = "Gb0"; cnt["g"] += 1
                for a in range(2):
                    S.dma("sync", gb[:, a, 0:nq, :], Gd[nm][o, kt, a], writes=[gk + f"_{a}"])
                gkeys = [gk + "_0", gk + "_1"]
                for i in range(nb):
                    qs = nb - 1 - i
                    Rv = rj[:, 0, 0:nb, :]; Jv = rj[:, 1, 0:nb, :]; GRs = gb[:, 0, qs:qs + nb, :]; GJs = gb[:, 1, qs:qs + nb, :]
                    b0, b1, b2, b3 = [p_[:, 0:nb, :] for p_ in pb]
                    k.tt(b0, Rv, GRs, ALU.mult, rkeys + gkeys, ["pb0"])
                    k.tt(b1, Jv, GJs, ALU.mult, rkeys + gkeys, ["pb1"], eng="gpsimd")
                    k.tt(b2, Rv, GJs, ALU.mult, rkeys + gkeys, ["pb2"], eng="gpsimd")
                    k.tt(b3, Jv, GRs, ALU.mult, rkeys + gkeys, ["pb3"])
                    terms = [(0, identb, "pb0"), (1, nidentb, "pb1")]
                    for half, tl in ((0, [(pb[0], identb, "pb0"), (pb[1], nidentb, "pb1")]), (1, [(pb[2], identb, "pb2"), (pb[3], identb, "pb3")])):
                        nmm = 2 * nb; cmm = 0
                        for (pp, idm, pk_) in tl:
                            for j in range(nb):
                                k.mm(psY[:, half * 192:(half + 1) * 192], idm[:], pp[:, j, :], cmm == 0, cmm == nmm - 1, [pk_, "identb"], ["psY"],
                                     inc=(cmm == nmm - 1))
                                cmm += 1
                    for a in range(2):
                        k.copy("scalar", YAB[:, kt, a, i, :], psY[:, a * 192:(a + 1) * 192], ["psY"], [f"YAB{kt}_{a}_{i}"])
            for pt in range(ntb):
                for ip in range(0, nb, 2):
                    ni = min(2, nb - ip)
                    a_ = cnt["i"] % 2; cnt["i"] += 1
                    ps = pI[a_]; pk = f"pI{a_}"
                    for ii in range(ni):
                        for kt in range(ntb):
                            last = (kt == ntb - 1 and ii == ni - 1)
                            k.mm(ps[:, ii * 192:(ii + 1) * 192], TCT[:, kt, pt * 128:(pt + 1) * 128], YAB[:, kt, 0, ip + ii, :], kt == 0, False,
                                 [kTCT, f"YAB{kt}_0_{ip + ii}"], [pk], inc=False)
                            k.mm(ps[:, ii * 192:(ii + 1) * 192], TST[:, kt, pt * 128:(pt + 1) * 128], YAB[:, kt, 1, ip + ii, :], False, kt == ntb - 1,
                                 [kTST, f"YAB{kt}_1_{ip + ii}"], [pk], inc=last)
                    for ii in range(ni):
                        g = base + (ip + ii) * ntb + pt
                        epi(g, ps[:, ii * 192:(ii + 1) * 192], pk)

        def epi1(g, y, pk):
            e_ = et[cnt["e"] % 2]; ek = f"et{cnt['e'] % 2}"; cnt["e"] += 1
            k.tt(e_[:], VXX[:, g, 0:192], sk[:, 0, :], ALU.mult, [f"VXX{g}", "sk"], [ek], eng="gpsimd")
            k.tt(e_[:], y, e_[:], ALU.add, [pk, ek], [ek])
            k.tt(Z[:, g, :], e_[:], VXX[:, g, 192:384], ALU.mult, [ek, f"VXX{g}"], [f"Z{g}"], eng="gpsimd")

        def epi2(g, y, pk):
            e_ = et[cnt["e"] % 2]; ek = f"et{cnt['e'] % 2}"; cnt["e"] += 1
            o_ = ost[cnt["e"] % 2]; ok = f"ost{cnt['e'] % 2}"
            k.tt(e_[:], Z[:, g, :], sk[:, 1, :], ALU.mult, [f"Z{g}", "sk"], [ek], eng="gpsimd")
            k.tt(e_[:], y, e_[:], ALU.add, [pk, ek], [ek])
            k.tt(o_[:], e_[:], VXX[:, g, 384:576], ALU.mult, [ek, f"VXX{g}"], [ok], eng="gpsimd")
            for bi, (c0_, cn) in enumerate(((0, 128), (128, 64))):
                oT = oTs[bi]
                S.op("tensor", lambda e, o_=o_, c0_=c0_, cn=cn: e.transpose(ptp[:cn, 0:128], o_[:, c0_:c0_ + cn], ident[:]),
                     reads=[ok, "ident", "ptp"], writes=["ptp"])
                k.copy("scalar", oT[:cn, :], ptp[:cn, 0:128], ["ptp"], [f"oT{bi}"])
                S.dma("sync", out_d[c0_:c0_ + cn, g * 128:(g + 1) * 128], oT[:cn, :], reads=[f"oT{bi}"])

        for nm in (("ctx",) if dbg == "ctx" else ("lat", "ctx")):
            conv(nm, 0, lambda g: VXX[:, g, 0:192], lambda g: [f"VXX{g}"], epi1)
            conv(nm, 1, lambda g: Z[:, g, :], lambda g: [f"Z{g}"], epi2)
        S.emit()


def hy_inputs(l, core, tm_all, G_core, inp, consts):
    b, hh = core_bh(core)
    cols = np.concatenate([896 + part * 384 + hh * 192 + np.arange(192) for part in range(3)])
    hy = seq_tm(tm_all, b, cols)
    z = np.zeros((1, 576), np.float32)
    ccols = np.concatenate([part * 384 + hh * 192 + np.arange(192) for part in range(3)])
    cwb = np.concatenate([inp["hy_conv_w"][l][:, ccols].reshape(-1), inp["hy_conv_b"][l][ccols]])
    d = {"hyp_lat": np.ascontiguousarray(np.concatenate([z, hy[:SEQ], z], 0)),
         "hyp_ctx": np.ascontiguousarray(np.concatenate([z, hy[SEQ:], z], 0)),
         "cw_bc": np.ascontiguousarray(np.tile(cwb[None, :], (128, 1)).astype(np.float32)),
         "sk_bc": np.ascontiguousarray(np.tile(inp["hy_skip"][l][:, hh * 192:(hh + 1) * 192].reshape(1, -1), (128, 1)).astype(np.float32))}
    for nm in HY_CFG:
        d[f"G_{nm}"] = G_core[nm]
        for t in ("TC", "TS", "TCT", "TST"):
            d[f"{t}_{nm}"] = consts["hy"][nm][t]
    return d


NROW = SEQ + CTX + 4
LAT0, CTX0 = 1, SEQ + 3


def ext_specs():
    sp = {"xT_in": [2, D, NT], "sc": [128, 16], "w_mod": [DEPTH, D, 6 * D], "b_mod_p": [DEPTH, 128, 48],
          "n1_p": [DEPTH, 128, 8], "n2_p": [DEPTH, 128, 8], "w_in": [DEPTH, D, P_IN], "qkn": [DEPTH, 128, 2],
          "gate_b": [DEPTH, 16, 1], "cosT": [2, 128, NT], "sinT": [2, 128, NT], "rm2": [128, 128], "blk1": [128, 128],
          "w_out": [DEPTH, D, D], "w_gate": [DEPTH, D, D_FF], "w_up": [DEPTH, D, D_FF], "w_down": [DEPTH, D_FF, D],
          "sink": [DEPTH, 2, 64, 3], "mask": [128, 384], "nw_bc": [DEPTH, 2, 128, 128],
          "sel": [34, 512], "ident": [128, 128], "maskf": [128, 128], "maskb": [128, 128],
          "cw_bc": [DEPTH, 2, 128, 4 * 576], "sk_bc": [DEPTH, 2, 128, 384],
          "f_w1": [DEPTH, 33, 64], "f_w2": [DEPTH, 64, 64], "f_w3c": [DEPTH, 2, 64, 768], "f_fb": [DEPTH, 64, 4],
          "f_dec": [DEPTH, 2, 128, 768], "alt": [128, 1]}
    for nm, cfg in HY_CFG.items():
        L, B = cfg["L"], cfg["B"]
        for t in ("TC", "TS", "TCT", "TST"):
            sp[f"{t}_{nm}"] = [B, B]
        sp[f"featsT_{nm}"] = [33, L]; sp[f"featsTr_{nm}"] = [33, L]
        sp[f"negt_{nm}"] = [128, L // 128]; sp[f"negtr_{nm}"] = [128, L // 128]
    return sp


def build_fused(depth=DEPTH):
    nc = bass.Bass("TRN2", target_bir_lowering=False)
    X = {n: dram_in(nc, n, shp) for n, shp in ext_specs().items()}
    OUT = dram_out(nc, "xT_out", [2, D, NT])
    FMS = nc.dram_tensor("FMS", [1424, NTOK], F32).ap()
    TMT = nc.dram_tensor("TMSP", [NROW, 2048], F32)
    TMS = TMT.ap()
    MIXS = nc.dram_tensor("MIXS", [D, NTOK], F32).ap()
    MODT = nc.dram_tensor("MODT", [128, 96], F32).ap()
    XTS = nc.dram_tensor("XTS", [2, D, NT], F32).ap()
    GS = {nm: nc.dram_tensor(f"GS_{nm}", [DEPTH, 2, 2, cfg["B"] // 128, 2, 128, 2 * cfg["nb"] - 1, 192], F32).ap()
          for nm, cfg in HY_CFG.items()}
    MLSCR = nc.dram_tensor("ml_scr", [2, 4, NTOK], F32).ap()
    import os
    PH = os.environ.get("FPH", "F,A,att,ml,hy,E").split(",")
    with ExitStack() as es0:
        bank = SemBank(nc, es0, nsets=1)
        with ExitStack() as es:
            S = Sched(nc, es, bank, "init_")
            z = S.sb("z", [4, 2048])
            S.op("vector", lambda e: e.memset(z[:], 0.0), writes=["z"])
            for i_, row in enumerate((0, SEQ + 1, SEQ + 2, NROW - 1)):
                S.dma("sync", TMS[row:row + 1, :], z[i_:i_ + 1, :], reads=["z"])
            S.emit()
        for l in range(depth):
            io = {"w1": X["f_w1"][l], "w2": X["f_w2"][l], "w3c": [X["f_w3c"][l, hh] for hh in range(2)], "fb": X["f_fb"][l],
                  "decay_bc": [X["f_dec"][l, hh] for hh in range(2)], "alt": X["alt"]}
            for nm in HY_CFG:
                io[nm] = dict(feats=X[f"featsT_{nm}"], featsr=X[f"featsTr_{nm}"], negt=X[f"negt_{nm}"], negtr=X[f"negtr_{nm}"],
                              TC=X[f"TC_{nm}"], TS=X[f"TS_{nm}"], G=[GS[nm][l, hh] for hh in range(2)])
            if "F" in PH:
                emit_F(nc, bank, f"F{l}_", io)
        for l in range(depth):
            xsrc = X["xT_in"] if l == 0 else XTS
            xdst = OUT if l == depth - 1 else XTS
            gcols = [(lambda t, u=u: u * NLAT + t if t < NLAT else SEQ + u * NCTX + (t - NLAT)) for u in range(2)]
            grows = [(lambda t, u=u: LAT0 + u * NLAT + t if t < NLAT else CTX0 + u * NCTX + (t - NLAT)) for u in range(2)]
            io = {"xT": [xsrc[0], xsrc[1]], "sc": X["sc"], "w_mod": X["w_mod"][l], "b_mod": X["b_mod_p"][l], "norm1_w": X["n1_p"][l],
                  "w_in": X["w_in"][l], "qkn": X["qkn"][l], "gate_b": X["gate_b"][l], "cosT": X["cosT"], "sinT": X["sinT"],
                  "rm2": X["rm2"], "blk1": X["blk1"], "modT": MODT, "fm": FMS, "tm": TMS}
            if "A" in PH:
                emit_A(nc, bank, f"A{l}_", io, gcols, grows)
            for g in range(2):
                io = {"qT": FMS[192 * g:192 * g + 192, :], "kT": FMS[384 + 64 * g:384 + 64 * g + 64, :],
                      "v_lat": TMS[LAT0:LAT0 + SEQ, 64 * g:64 * g + 64], "v_ctx": TMS[CTX0:CTX0 + CTX, 64 * g:64 * g + 64],
                      "sink": X["sink"][l, g], "mask": X["mask"], "attT": MIXS[192 * g:192 * g + 192, :]}
                if "att" in PH:
                    emit_att(nc, bank, f"T{l}{g}_", io)
            for hp in range(2):
                def tmp_(c0):
                    return (TMS[LAT0:LAT0 + SEQ, c0:c0 + 128], TMS[CTX0:CTX0 + CTX, c0:c0 + 128])
                io = {"qT": FMS[512 + 128 * hp:512 + 128 * hp + 128, :], "kT": FMS[768 + 128 * hp:768 + 128 * hp + 128, :],
                      "ktok": tmp_(128 + 128 * hp), "vtok": tmp_(384 + 128 * hp), "otok": tmp_(640 + 128 * hp),
                      "g4": [FMS[1024 + 4 * q + 2 * hp:1024 + 4 * q + 2 * hp + 2, :] for q in range(4)],
                      "nw_bc": X["nw_bc"][l, hp], "sel": X["sel"], "ident": X["ident"], "maskf": X["maskf"], "maskb": X["maskb"],
                      "mloT": MIXS[768 + 128 * hp:768 + 128 * hp + 128, :], "scr": MLSCR}
                if "ml" in PH:
                    emit_ml(nc, bank, f"L{l}{hp}_", io)
            for hh in range(2):
                io = {"tmT": (TMT, 2048), "row0": {"lat": LAT0 - 1, "ctx": CTX0 - 1}, "col0": 896 + 192 * hh,
                      "cw_bc": X["cw_bc"][l, hh], "sk_bc": X["sk_bc"][l, hh], "ident": X["ident"],
                      "G": {nm: GS[nm][l, hh] for nm in HY_CFG},
                      "T": {nm: [X[f"{t}_{nm}"] for t in ("TC", "TS", "TCT", "TST")] for nm in HY_CFG},
                      "hyoT": MIXS[384 + 192 * hh:384 + 192 * hh + 192, :]}
                if "hy" in PH:
                    emit_hy(nc, bank, f"H{l}{hh}_", io)
            io = {"xT": [xsrc[0], xsrc[1]], "mix": MIXS, "modT": MODT, "norm2_w": X["n2_p"][l], "w_out": X["w_out"][l],
                  "w_gate": X["w_gate"][l], "w_up": X["w_up"][l], "w_down": X["w_down"][l], "xT_out": [xdst[0], xdst[1]]}
            if "E" in PH:
                emit_E(nc, bank, f"E{l}_", io, gcols)
    return nc


def fused_inputs(b, inp, consts):
    cos, sin = consts["rope"]
    d = {}
    xT = np.empty((2, D, NT), np.float32)
    cosT = np.ones((2, 128, NT), np.float32)
    sinT = np.zeros((2, 128, NT), np.float32)
    for u in range(2):
        xT[u, :, :NLAT] = inp["x"][b, u * NLAT:(u + 1) * NLAT, :].T
        xT[u, :, NLAT:] = inp["ctx"][b, u * NCTX:(u + 1) * NCTX, :].T
        for hd in range(2):
            cosT[u, 64 * hd:64 * hd + 64, :NLAT] = cos[:, u * NLAT:(u + 1) * NLAT]
            sinT[u, 64 * hd:64 * hd + 64, :NLAT] = sin[:, u * NLAT:(u + 1) * NLAT]
    d["xT_in"] = xT; d["cosT"] = cosT; d["sinT"] = sinT
    d["sc"] = pcol(np.stack([inp["c"][b], inp["c_ctx"]], axis=1)).reshape(128, 16)
    for k_ in ("w_mod", "w_in", "w_out"):
        d[k_] = inp[k_]
    d["w_gate"], d["w_up"], d["w_down"] = inp["ffn_w_gate"], inp["ffn_w_up"], inp["ffn_w_down"]
    d["b_mod_p"] = np.stack([pcol(inp["b_mod"][l]) for l in range(DEPTH)])
    d["n1_p"] = np.stack([pcol(inp["norm1_w"][l]) for l in range(DEPTH)])
    d["n2_p"] = np.stack([pcol(inp["norm2_w"][l]) for l in range(DEPTH)])
    d["qkn"] = np.stack([np.stack([np.tile(inp["q_norm_w"][l], 2), np.tile(inp["k_norm_w"][l], 2)], axis=1) for l in range(DEPTH)])
    d["gate_b"] = inp["ml_gate_b"].reshape(DEPTH, 16, 1)
    d["rm2"], d["blk1"], d["mask"] = consts["rm2"], consts["blk1"], consts["mask"]
    d["sink"] = np.stack([np.stack([np.tile(inp["attn_sink"][l][3 * g:3 * g + 3][None, :], (64, 1)) for g in range(2)]) for l in range(DEPTH)])
    d["nw_bc"] = np.stack([np.stack([np.tile(inp["ml_norm_w"][l][128 * hp:128 * hp + 128][None, :], (128, 1)) for hp in range(2)]) for l in range(DEPTH)])
    d.update(consts["ml"])
    cw = np.empty((DEPTH, 2, 128, 4 * 576), np.float32); sk = np.empty((DEPTH, 2, 128, 384), np.float32)
    w3c = np.empty((DEPTH, 2, 64, 768), np.float32); dec = np.empty((DEPTH, 2, 128, 768), np.float32)
    for l in range(DEPTH):
        for hh in range(2):
            ccols = np.concatenate([part * 384 + hh * 192 + np.arange(192) for part in range(3)])
            cw[l, hh] = np.concatenate([inp["hy_conv_w"][l][:, ccols].reshape(-1), inp["hy_conv_b"][l][ccols]])[None, :]
            sk[l, hh] = inp["hy_skip"][l][:, hh * 192:(hh + 1) * 192].reshape(1, -1)
            cols = np.concatenate([dr * 768 + o * 384 + hh * 192 + np.arange(192) for dr in range(2) for o in range(2)])
            w3c[l, hh] = inp["hy_w3"][l][:, cols]
            dec[l, hh] = inp["hy_decay"][l][cols][None, :]
    d["cw_bc"], d["sk_bc"], d["f_w3c"], d["f_dec"] = cw, sk, w3c, dec
    d["f_w1"], d["f_w2"] = inp["hy_w1"], inp["hy_w2"]
    d["f_fb"] = np.stack([np.stack([inp["hy_freq"][l], inp["hy_b1"][l], inp["hy_b2"][l], inp["hy_b2"][l]], axis=1) for l in range(DEPTH)])
    hc = consts["hy"]
    d["alt"] = hc["alt"]
    for nm in HY_CFG:
        for t in ("TC", "TS", "TCT", "TST", "featsT", "featsTr", "negt", "negtr"):
            d[f"{t}_{nm}"] = hc[nm][t]
    sp = ext_specs()
    return {k_: np.ascontiguousarray(np.asarray(v, np.float32).reshape(sp[k_])) for k_, v in d.items()}


def kernel(**inputs):
    inp = {k_: np.asarray(v, dtype=np.float32) for k_, v in inputs.items()}
    consts = {"rope": rope_tables(), "mask": band_mask(), "ml": ml_consts(), "hy": hy_consts()}
    consts["rm2"], consts["blk1"] = rope_consts()
    nc = build_fused()
    maps = [fused_inputs(c // 2, inp, consts) for c in range(8)]
    res = run_bass_kernel_spmd(nc, maps, core_ids=list(range(8))).results
    out = np.empty((4, SEQ, D), np.float32)
    for b in range(4):
        xo = res[2 * b]["xT_out"]
        for u in range(2):
            out[b, u * NLAT:(u + 1) * NLAT, :] = xo[u][:, :NLAT].T
    return out
```
